# Optimizing a Trainium2 kernel written in Bass

```python
import math
import jax
import jax.numpy as jnp
from jax import lax
import numpy as np

D_MODEL = 1024
BATCH = 1
SEQ = 16384
DEPTH = 1
DEC_BATCH = 128
DEC_SEQ = 4
PAST_LEN = 8192
PAGE_SIZE = 128

HEAD_DIM = 64
N_HEADS_A = 8
PATTERNS = ((128, 1), (512, 4), (2048, 16))
MAX_WINDOW = 2048
N_HEADS_B = 4
DK_B = 64
DV_B = 64
CONV_B = 4
CHUNK_B = 64
N_HEADS_C = 4
N_MEM = 256
D_FF = 2816
CONV_FFN = 3
EPS = 1e-6
WIDTH_A = N_HEADS_A * HEAD_DIM
WIDTH_B = N_HEADS_B * DV_B
WIDTH_C = N_HEADS_C * HEAD_DIM
D_MIX = WIDTH_A + WIDTH_B + WIDTH_C
QKV_B = N_HEADS_B * (2 * DK_B + DV_B)
IN_SPLITS = (WIDTH_A, WIDTH_A, WIDTH_A, QKV_B, WIDTH_B, N_HEADS_B, N_HEADS_B, WIDTH_C)
N_IN = sum(IN_SPLITS)

kernel_name = 'hybrid_dilated_gdn_memory_decoder_step'

F32 = jnp.float32


def rms_norm(x, g):
    xf = x.astype(F32)
    y = xf * lax.rsqrt(jnp.mean(xf * xf, axis=-1, keepdims=True) + EPS)
    return (y * g.astype(F32)).astype(x.dtype)


def l2_normalize(x):
    xf = x.astype(F32)
    return xf * lax.rsqrt(jnp.sum(xf * xf, axis=-1, keepdims=True) + EPS)


def split_heads(a, n):
    return a.reshape(*a.shape[:-1], n, a.shape[-1] // n)


def causal_dwconv(x_ext, w):
    return lax.conv_general_dilated(x_ext, w[:, None, :].astype(x_ext.dtype), window_strides=(1,),
                                    padding='VALID', dimension_numbers=('NWC', 'WIO', 'NWC'),
                                    feature_group_count=x_ext.shape[-1])


def alibi_slopes(n):
    return jnp.exp2(-8.0 * jnp.arange(1, n + 1, dtype=F32) / n)


def dilated_band_prompt(q, k, v, slopes, window, dil):
    b, t, h, e = q.shape
    nband = window // dil
    span = dil * nband
    t_pad = -(-t // span) * span
    sub_len = t_pad // dil
    nblk = sub_len // nband

    def to_sub(a):
        a = jnp.pad(a, ((0, 0), (0, t_pad - t), (0, 0), (0, 0)))
        a = jnp.moveaxis(a.reshape(b, sub_len, dil, h, e), 1, 3)
        return a.reshape(b, dil, h, nblk, nband, e)

    def with_prev(a):
        prev = jnp.pad(a, ((0, 0), (0, 0), (0, 0), (1, 0), (0, 0), (0, 0)))[:, :, :, :-1]
        return jnp.concatenate([prev, a], axis=4)

    def from_sub(a):
        rest = a.shape[5:]
        a = a.reshape(b, dil, h, sub_len, *rest)
        return jnp.moveaxis(a, 3, 1).reshape(b, t_pad, h, *rest)[:, :t]

    qs = to_sub(q)
    ks = with_prev(to_sub(k))
    vs = with_prev(to_sub(v))
    s = jnp.einsum('brhnqe,brhnke->brhnqk', qs, ks, preferred_element_type=F32) * (e ** -0.5)
    qi = jnp.arange(nband)[:, None]
    kj = jnp.arange(2 * nband)[None, :]
    delta = qi + nband - kj
    in_band = (delta >= 0) & (delta <= nband)
    exists = (jnp.arange(nblk)[:, None, None] > 0) | (kj >= nband)[None]
    valid = in_band[None] & exists
    s = s - slopes[:, None, None, None] * (delta * dil).astype(F32)
    s = jnp.where(valid, s, -jnp.inf)
    mx = jnp.max(s, axis=-1)
    p = jnp.exp(s - mx[..., None])
    den = jnp.sum(p, axis=-1)
    num = jnp.einsum('brhnqk,brhnke->brhnqe', p, vs.astype(F32))
    return from_sub(num), from_sub(den), from_sub(mx)


def dilated_gather_sample(q, k_all, v_all, slopes, window, dil):
    b, s_len, h, e = q.shape
    past = k_all.shape[1] - s_len
    steps = jnp.arange(window // dil + 1)
    idx = past + jnp.arange(s_len)[:, None] - steps[None, :] * dil
    valid = idx >= 0
    idx = jnp.maximum(idx, 0)
    kg = k_all[:, idx]
    vg = v_all[:, idx]
    s = jnp.einsum('bshe,bsjhe->bshj', q, kg, preferred_element_type=F32) * (e ** -0.5)
    s = s - slopes[:, None] * (steps * dil).astype(F32)[None, :]
    s = jnp.where(valid[:, None, :], s, -jnp.inf)
    mx = jnp.max(s, axis=-1)
    p = jnp.exp(s - mx[..., None])
    den = jnp.sum(p, axis=-1)
    num = jnp.einsum('bshj,bsjhe->bshe', p, vg.astype(F32))
    return num, den, mx


def combine_dilations(parts):
    mx = jnp.max(jnp.stack([m for _, _, m in parts]), axis=0)
    num = sum(n * jnp.exp(m - mx)[..., None] for n, _, m in parts)
    den = sum(d * jnp.exp(m - mx) for _, d, m in parts)
    return num / den[..., None]


def gated_delta_chunked(q, k, v, g, beta, s0, chunk):
    b, t, h, _ = q.shape
    dv = v.shape[-1]
    n = t // chunk

    def blocks(a):
        a = a.reshape(b, n, chunk, h, *a.shape[3:])
        return jnp.swapaxes(jnp.moveaxis(a, 3, 2), 0, 1)

    qc, kc, vc = blocks(q), blocks(k), blocks(v)
    gc = jnp.cumsum(blocks(g), axis=-1)
    bc = blocks(beta)
    causal = jnp.tril(jnp.ones((chunk, chunk), dtype=bool))
    strict = jnp.tril(jnp.ones((chunk, chunk), dtype=bool), -1)
    decay = jnp.exp(jnp.where(causal, gc[..., :, None] - gc[..., None, :], -jnp.inf))
    kb = kc * bc[..., None]
    m = jnp.where(strict, jnp.einsum('nbhid,nbhjd->nbhij', kb, kc) * decay, 0.0)
    eye = jnp.eye(chunk, dtype=F32)
    tinv = lax.linalg.triangular_solve(eye + m, jnp.broadcast_to(eye, m.shape), left_side=True,
                                       lower=True, unit_diagonal=True)
    u = tinv @ (vc * bc[..., None])
    w = tinv @ (kb * jnp.exp(gc)[..., None])
    attn = jnp.einsum('nbhid,nbhjd->nbhij', qc, kc) * decay
    q_dec = qc * jnp.exp(gc)[..., None]
    g_last = gc[..., -1]
    k_dec = kc * jnp.exp(g_last[..., None] - gc)[..., None]

    def step(s, xs):
        u_n, w_n, attn_n, qd_n, kd_n, gl_n = xs
        v_new = u_n - w_n @ s
        o = qd_n @ s + attn_n @ v_new
        s = jnp.exp(gl_n)[..., None, None] * s + jnp.swapaxes(kd_n, -1, -2) @ v_new
        return s, o

    s_fin, o = lax.scan(step, s0, (u, w, attn, q_dec, k_dec, g_last))
    o = jnp.moveaxis(jnp.swapaxes(o, 0, 1), 2, 3).reshape(b, t, h, dv)
    return o, s_fin


def mixer_inputs(x, norm1_g, w_in, q_norm_a, k_norm_a, q_norm_c):
    z = rms_norm(x, norm1_g) @ w_in
    offs = [int(i) for i in np.cumsum(IN_SPLITS)[:-1]]
    qa, ka, va, qkv_b, gate_b, a_b, b_b, qc = jnp.split(z, offs, axis=-1)
    qa = rms_norm(split_heads(qa, N_HEADS_A), q_norm_a)
    ka = rms_norm(split_heads(ka, N_HEADS_A), k_norm_a)
    va = split_heads(va, N_HEADS_A)
    qc = rms_norm(split_heads(qc, N_HEADS_C), q_norm_c)
    return qa, ka, va, qkv_b, gate_b, a_b, b_b, qc


def gated_delta_mixer(qkv_ext, gate_b, a_b, b_b, conv_w, a_log, dt_bias, out_norm, s0):
    qkv = jax.nn.silu(causal_dwconv(qkv_ext, conv_w).astype(F32))
    q, k, v = jnp.split(qkv, [N_HEADS_B * DK_B, 2 * N_HEADS_B * DK_B], axis=-1)
    q = l2_normalize(split_heads(q, N_HEADS_B)) * (DK_B ** -0.5)
    k = l2_normalize(split_heads(k, N_HEADS_B))
    v = split_heads(v, N_HEADS_B)
    beta = jax.nn.sigmoid(b_b.astype(F32))
    g = -jnp.exp(a_log.astype(F32)) * jax.nn.softplus(a_b.astype(F32) + dt_bias.astype(F32))
    t = q.shape[1]
    chunk = CHUNK_B if t % CHUNK_B == 0 else t
    o, s_new = gated_delta_chunked(q, k, v, g, beta, s0.astype(F32), chunk)
    o = rms_norm(o, out_norm) * jax.nn.silu(split_heads(gate_b, N_HEADS_B).astype(F32))
    return o.reshape(*o.shape[:2], WIDTH_B), s_new


def memory_kv(mem, mem_norm_g, w_mem_kv, k_norm_c):
    mk, mv = jnp.split(rms_norm(mem, mem_norm_g) @ w_mem_kv, 2, axis=-1)
    return rms_norm(split_heads(mk, N_HEADS_C), k_norm_c), split_heads(mv, N_HEADS_C)


def memory_attend(q, mk, mv):
    s = jnp.einsum('bthe,bmhe->bhtm', q, mk, preferred_element_type=F32) * (q.shape[-1] ** -0.5)
    p = jax.nn.softmax(s, axis=-1)
    return jnp.einsum('bhtm,bmhe->bthe', p, mv.astype(F32))


def merge_groups(x, oa, ob, oc, w_out):
    b, t = x.shape[:2]
    mix = jnp.concatenate([oa.reshape(b, t, WIDTH_A), ob, oc.reshape(b, t, WIDTH_C)], axis=-1)
    return x + mix.astype(x.dtype) @ w_out


def conv_ffn(up_ext, conv_w, w_down):
    g, u = jnp.split(causal_dwconv(up_ext, conv_w), 2, axis=-1)
    return (jax.nn.silu(g) * u) @ w_down


def setup_inputs(seed: int = 0) -> dict:
    key = jax.random.key(seed)
    keys = iter(jax.random.split(key, 40))

    def nrm(shape, scale=1.0):
        return jax.random.normal(next(keys), shape, F32) * scale

    def gain(shape):
        return 1.0 + nrm(shape, 0.05)

    L = DEPTH
    wbuf = min(MAX_WINDOW, PAST_LEN)
    a_log_b = jnp.log(jax.random.uniform(next(keys), (L, N_HEADS_B), F32, 1.0, 16.0))
    dt = jnp.exp(jax.random.uniform(next(keys), (L, N_HEADS_B), F32, math.log(1e-3), math.log(1e-1)))
    dt_bias_b = dt + jnp.log(-jnp.expm1(-dt))
    return {
        'x_prompt': nrm((BATCH, SEQ, D_MODEL)),
        'x_sample': nrm((DEC_BATCH, DEC_SEQ, D_MODEL)),
        'cache_win_k': nrm((L, DEC_BATCH, wbuf, N_HEADS_A, HEAD_DIM)),
        'cache_win_v': nrm((L, DEC_BATCH, wbuf, N_HEADS_A, HEAD_DIM)),
        'state_gdn': nrm((L, DEC_BATCH, N_HEADS_B, DK_B, DV_B), 0.5),
        'state_gdn_conv': nrm((L, DEC_BATCH, CONV_B - 1, QKV_B)),
        'state_ffn_conv': nrm((L, DEC_BATCH, CONV_FFN - 1, 2 * D_FF)),
        'cache_mem_k': nrm((L, DEC_BATCH, N_MEM, N_HEADS_C, HEAD_DIM)),
        'cache_mem_v': nrm((L, DEC_BATCH, N_MEM, N_HEADS_C, HEAD_DIM)),
        'mem_prompt': nrm((BATCH, N_MEM, D_MODEL)),
        'norm1_g': gain((L, D_MODEL)),
        'w_in': nrm((L, D_MODEL, N_IN), D_MODEL ** -0.5),
        'q_norm_a': gain((L, HEAD_DIM)),
        'k_norm_a': gain((L, HEAD_DIM)),
        'conv_b_w': nrm((L, CONV_B, QKV_B), CONV_B ** -0.5),
        'a_log_b': a_log_b,
        'dt_bias_b': dt_bias_b,
        'out_norm_b': gain((L, DV_B)),
        'mem_norm_g': gain((L, D_MODEL)),
        'w_mem_kv': nrm((L, D_MODEL, 2 * WIDTH_C), D_MODEL ** -0.5),
        'q_norm_c': gain((L, HEAD_DIM)),
        'k_norm_c': gain((L, HEAD_DIM)),
        'w_out': nrm((L, D_MIX, D_MODEL), D_MIX ** -0.5),
        'norm2_g': gain((L, D_MODEL)),
        'w_up': nrm((L, D_MODEL, 2 * D_FF), D_MODEL ** -0.5),
        'conv_ffn_w': nrm((L, CONV_FFN, 2 * D_FF), CONV_FFN ** -0.5),
        'w_down': nrm((L, D_FF, D_MODEL), D_FF ** -0.5),
    }


def reference(x_prompt, x_sample, cache_win_k, cache_win_v, state_gdn, state_gdn_conv, state_ffn_conv,
              cache_mem_k, cache_mem_v, mem_prompt, norm1_g, w_in, q_norm_a, k_norm_a, conv_b_w, a_log_b,
              dt_bias_b, out_norm_b, mem_norm_g, w_mem_kv, q_norm_c, k_norm_c, w_out, norm2_g, w_up,
              conv_ffn_w, w_down):
    slopes = alibi_slopes(N_HEADS_A)
    xp, xs = x_prompt, x_sample
    b_p, t_p = xp.shape[:2]
    (win_k_p, win_v_p, win_k_s, win_v_s, gdn_p, gdn_s, gconv_p, gconv_s,
     fconv_p, fconv_s, mem_k_p, mem_v_p) = ([] for _ in range(12))
    for l in range(DEPTH):
        qa, ka, va, qkv_b, gate_b, a_b, b_b, qc = mixer_inputs(xp, norm1_g[l], w_in[l], q_norm_a[l],
                                                              k_norm_a[l], q_norm_c[l])
        oa = combine_dilations([dilated_band_prompt(qa, ka, va, slopes, w, d) for w, d in PATTERNS])
        n_keep = min(MAX_WINDOW, t_p)
        win_k_p.append(ka[:, t_p - n_keep:])
        win_v_p.append(va[:, t_p - n_keep:])
        qkv_ext = jnp.pad(qkv_b, ((0, 0), (CONV_B - 1, 0), (0, 0)))
        gconv_p.append(qkv_ext[:, -(CONV_B - 1):])
        ob, s_fin = gated_delta_mixer(qkv_ext, gate_b, a_b, b_b, conv_b_w[l], a_log_b[l], dt_bias_b[l],
                                      out_norm_b[l], jnp.zeros((b_p, N_HEADS_B, DK_B, DV_B), F32))
        gdn_p.append(s_fin)
        mk, mv = memory_kv(mem_prompt, mem_norm_g[l], w_mem_kv[l], k_norm_c[l])
        mem_k_p.append(mk)
        mem_v_p.append(mv)
        oc = memory_attend(qc, mk, mv)
        xp = merge_groups(xp, oa, ob, oc, w_out[l])
        up = rms_norm(xp, norm2_g[l]) @ w_up[l]
        up_ext = jnp.pad(up, ((0, 0), (CONV_FFN - 1, 0), (0, 0)))
        fconv_p.append(up_ext[:, -(CONV_FFN - 1):])
        xp = xp + conv_ffn(up_ext, conv_ffn_w[l], w_down[l])

        qa, ka, va, qkv_b, gate_b, a_b, b_b, qc = mixer_inputs(xs, norm1_g[l], w_in[l], q_norm_a[l],
                                                              k_norm_a[l], q_norm_c[l])
        k_all = jnp.concatenate([cache_win_k[l].astype(ka.dtype), ka], axis=1)
        v_all = jnp.concatenate([cache_win_v[l].astype(va.dtype), va], axis=1)
        oa = combine_dilations([dilated_gather_sample(qa, k_all, v_all, slopes, w, d) for w, d in PATTERNS])
        win_k_s.append(ka)
        win_v_s.append(va)
        qkv_ext = jnp.concatenate([state_gdn_conv[l].astype(qkv_b.dtype), qkv_b], axis=1)
        gconv_s.append(qkv_ext[:, -(CONV_B - 1):])
        ob, s_new = gated_delta_mixer(qkv_ext, gate_b, a_b, b_b, conv_b_w[l], a_log_b[l], dt_bias_b[l],
                                      out_norm_b[l], state_gdn[l])
        gdn_s.append(s_new)
        oc = memory_attend(qc, cache_mem_k[l], cache_mem_v[l])
        xs = merge_groups(xs, oa, ob, oc, w_out[l])
        up = rms_norm(xs, norm2_g[l]) @ w_up[l]
        up_ext = jnp.concatenate([state_ffn_conv[l].astype(up.dtype), up], axis=1)
        fconv_s.append(up_ext[:, -(CONV_FFN - 1):])
        xs = xs + conv_ffn(up_ext, conv_ffn_w[l], w_down[l])
    return (xp, xs, jnp.stack(win_k_p), jnp.stack(win_v_p), jnp.stack(win_k_s), jnp.stack(win_v_s),
            jnp.stack(gdn_p), jnp.stack(gdn_s), jnp.stack(gconv_p), jnp.stack(gconv_s),
            jnp.stack(fconv_p), jnp.stack(fconv_s), jnp.stack(mem_k_p), jnp.stack(mem_v_p))
```

```python
import numpy as np
import concourse.bass as bass
import concourse.mybir as mybir
from concourse.bass_utils import run_bass_kernel_spmd

F32 = mybir.dt.float32
BF16 = mybir.dt.bfloat16
AF = mybir.ActivationFunctionType
ALU = mybir.AluOpType
AX = mybir.AxisListType

PE, ACT, DVE, POOL, SP = 0, 1, 2, 3, 4
ENG_NAMES = ["tensor", "scalar", "vector", "gpsimd", "sync"]
NDSEM = 12
PE_NOSELF = False
ACC_NOSELF = True
MM_NOSELF = False

NCORES = 8
D = 1024
SEQ = 16384
OWN = 2048
N_IN = 2824
C_QA, C_KA, C_VA, C_QKV, C_GATE, C_A, C_B, C_QC = 0, 512, 1024, 1536, 2304, 2560, 2564, 2568
D_FF = 2816
EPS = 1e-6
NEGV = -30000.0


class Buf:
    __slots__ = ("name", "w", "r", "excl")

    def __init__(self, name=""):
        self.name = name
        self.w = None
        self.r = []
        self.excl = False


class Prog:
    def __init__(self, nc):
        self.nc = nc
        self.q = [[] for _ in range(5)]
        self.cnt = [0] * 5
        self.dcnt = [0] * 5
        self.seen = [dict() for _ in range(5)]
        self.sems = {}
        self._semstack = []
        self._stack = []
        self.ninstr = 0

    def sb(self, name, shape, dt=F32):
        g = self.nc.sbuf_tensor(name, list(shape), dt)
        t = g.__enter__()
        self._stack.append(g)
        return t

    def ps(self, name, shape, dt=F32):
        g = self.nc.psum_tensor(name, list(shape), dt)
        t = g.__enter__()
        self._stack.append(g)
        return t

    def mark(self):
        return len(self._stack)

    def release(self, mark):
        while len(self._stack) > mark:
            g = self._stack.pop()
            g.__exit__(None, None, None)

    def sem(self, key):
        if key not in self.sems:
            g = self.nc.semaphore("s_" + "_".join(str(k) for k in key))
            self.sems[key] = g.__enter__()
            self._semstack.append(g)
        return self.sems[key]

    def _deps(self, eng, reads, writes, noself=False):
        need = {}
        for b in reads:
            if b.w is not None:
                k, v = b.w
                if need.get(k, 0) < v:
                    need[k] = v
        for b in writes:
            if b.w is not None:
                k, v = b.w
                if need.get(k, 0) < v:
                    need[k] = v
            for (k, v) in b.r:
                if need.get(k, 0) < v:
                    need[k] = v
        waits = []
        seen = self.seen[eng]
        for k, v in need.items():
            if (PE_NOSELF or noself) and eng == PE and k == ("e", PE):
                continue
            if seen.get(k, 0) >= v:
                continue
            seen[k] = v
            waits.append((k, v))
        return waits

    def _commit(self, tok, reads, writes):
        for b in reads:
            b.r = [x for x in b.r if x[0] != tok[0]]
            b.r.append(tok)
        for b in writes:
            b.w = tok
            b.r = []

    def op(self, eng, fn, reads=(), writes=(), noself=False):
        if eng != PE:
            ex = [b for b in reads if b.excl]
            if ex:
                reads = [b for b in reads if not b.excl]
                writes = list(writes) + [b for b in ex if b not in writes]
        waits = self._deps(eng, reads, writes, noself)
        self.cnt[eng] += 1
        key = ("e", eng)
        tok = (key, self.cnt[eng])
        self.q[eng].append((fn, waits, (key, 1)))
        self._commit(tok, reads, writes)
        self.ninstr += 1
        return tok

    def dma(self, eng, fn, reads=(), writes=()):
        i = self.dcnt[eng]
        self.dcnt[eng] += 1
        slot = i % NDSEM
        key = ("d", eng, slot)
        val = 16 * (i // NDSEM + 1)
        waits = self._deps(eng, reads, writes)
        if i >= NDSEM:
            pv = val - 16
            if self.seen[eng].get(key, 0) < pv:
                self.seen[eng][key] = pv
                waits.append((key, pv))
        tok = (key, val)
        self.q[eng].append((fn, waits, (key, 16)))
        self._commit(tok, reads, writes)
        self.ninstr += 1
        return tok

    def barrier(self):
        targets = []
        for e in range(5):
            if self.cnt[e] > 0:
                targets.append((("e", e), self.cnt[e]))
            n = self.dcnt[e]
            for slot in range(min(n, NDSEM)):
                last = ((n - 1 - slot) // NDSEM) * NDSEM + slot
                targets.append((("d", e, slot), 16 * (last // NDSEM + 1)))
        for e in range(5):
            waits = []
            for (k, v) in targets:
                if k == ("e", e):
                    continue
                if self.seen[e].get(k, 0) < v:
                    self.seen[e][k] = v
                    waits.append((k, v))
            self.q[e].append((None, waits, None))

    def wait_all(self, eng, bufs):
        waits = self._deps(eng, bufs, ())
        self.q[eng].append((None, waits, None))

    def emit(self):
        nc = self.nc
        for e in range(5):
            for (fn, waits, inc) in self.q[e]:
                for (k, v) in waits:
                    self.sem(k)
                if inc is not None:
                    self.sem(inc[0])
        with nc.Block() as block:
            def mk(e):
                def body(engine):
                    for (fn, waits, inc) in self.q[e]:
                        for (k, v) in waits:
                            engine.wait_ge(self.sems[k], v)
                        if fn is not None:
                            ins = fn(engine)
                            ins.then_inc(self.sems[inc[0]], inc[1])
                return body
            for e, nm in enumerate(ENG_NAMES):
                if self.q[e]:
                    getattr(block, nm)(mk(e))
        self.q = [[] for _ in range(5)]

    def close(self):
        self.release(0)
        while self._semstack:
            self._semstack.pop().__exit__(None, None, None)


class T:
    __slots__ = ("t", "b")

    def __init__(self, t, name=""):
        self.t = t
        self.b = Buf(name)


def _consts():
    c = {}
    c["ident"] = np.eye(128, dtype=np.float32)
    j = np.arange(128)[:, None]
    i = np.arange(128)[None, :]
    c["triu"] = (j <= i).astype(np.float32)
    c["trils"] = (j > i).astype(np.float32)
    c["negc"] = np.where(i >= j, 0.0, NEGV).astype(np.float32)
    c["strict"] = (i > j).astype(np.float32)
    c["ones"] = np.ones((128, 128), np.float32)
    bd = np.zeros((128, 128), np.float32)
    bd[:64, :64] = 1
    bd[64:, 64:] = 1
    c["bd"] = bd
    return c


class Builder:
    skip = ()

    def __init__(self, NT=128, dbg=False):
        self.NT = NT
        self.L = NT * 128
        self.dbg = dbg
        self.nc = bass.Bass("TRN2", target_bir_lowering=False)
        self.P = Prog(self.nc)
        self.ins = {}
        self.outs = {}
        self.outbufs = []

    def inp(self, name, shape, dt=F32):
        t = self.nc.dram_tensor(name, list(shape), dt, kind="ExternalInput")
        self.ins[name] = T(t, name)
        return self.ins[name]

    def outp(self, name, shape, dt=F32):
        t = self.nc.dram_tensor(name, list(shape), dt, kind="ExternalOutput")
        self.outs[name] = T(t, name)
        self.outbufs.append(self.outs[name].b)
        return self.outs[name]

    def scratch(self, name, shape, dt=F32):
        t = self.nc.dram_tensor(name, list(shape), dt)
        return T(t, name)

    def sbt(self, name, shape, dt=F32):
        return T(self.P.sb(name, shape, dt), name)

    def pst(self, name, shape, dt=F32):
        t = T(self.P.ps(name, shape, dt), name)
        t.b.excl = True
        return t

    def dma(self, q, out, in_, r, w, **kw):
        self.P.dma(q, lambda e: e.dma_start(out=out, in_=in_, **kw), [x.b for x in r], [x.b for x in w])

    def mm(self, out, lhsT, rhs, r, w, start=True, stop=True):
        self.P.op(PE, lambda e: e.matmul(out, lhsT=lhsT, rhs=rhs, start=start, stop=stop),
                  [x.b for x in r], [x.b for x in w], noself=((not start) and ACC_NOSELF) or MM_NOSELF)

    def tr(self, out, in_, ident, r, w):
        self.P.op(PE, lambda e: e.transpose(out=out, in_=in_, identity=ident), [x.b for x in r], [x.b for x in w])

    def act(self, out, in_, func, r, w, eng=ACT, **kw):
        self.P.op(ACT, lambda e: e.activation(out=out, in_=in_, func=func, **kw), [x.b for x in r], [x.b for x in w])

    def cp(self, eng, out, in_, r, w):
        if eng == ACT:
            self.P.op(ACT, lambda e: e.activation(out=out, in_=in_, func=AF.Copy), [x.b for x in r], [x.b for x in w])
        else:
            self.P.op(eng, lambda e: e.tensor_copy(out=out, in_=in_), [x.b for x in r], [x.b for x in w])

    def tt(self, eng, out, in0, in1, op, r, w):
        if eng == POOL and getattr(self, "nopool", False):
            eng = DVE
        self.P.op(eng, lambda e: e.tensor_tensor(out=out, in0=in0, in1=in1, op=op), [x.b for x in r], [x.b for x in w])

    def ts(self, eng, out, in0, s1, s2, op0, op1, r, w):
        if eng == POOL and getattr(self, "nopool", False):
            eng = DVE
        if op1 is None:
            self.P.op(eng, lambda e: e.tensor_scalar(out=out, in0=in0, scalar1=s1, scalar2=None, op0=op0),
                      [x.b for x in r], [x.b for x in w])
        else:
            self.P.op(eng, lambda e: e.tensor_scalar(out=out, in0=in0, scalar1=s1, scalar2=s2, op0=op0, op1=op1),
                      [x.b for x in r], [x.b for x in w])

    def stt(self, eng, out, in0, scalar, in1, op0, op1, r, w):
        eng = DVE
        self.P.op(eng, lambda e: e.scalar_tensor_tensor(out=out, in0=in0, scalar=scalar, in1=in1, op0=op0, op1=op1),
                  [x.b for x in r], [x.b for x in w])

    def memset(self, eng, ap, val, w):
        self.P.op(eng, lambda e: e.memset(ap, val), [], [x.b for x in w])

    def rsqrt_inplace(self, t, ap, scale, r_extra=()):
        self.act(ap, ap, AF.Sqrt, [t, self.epsc] + list(r_extra), [t], scale=scale, bias=self.epsc.t[0:ap.shape[0], :])
        self.P.op(DVE, lambda e: e.reciprocal(out=ap, in_=ap), [t.b], [t.b])


def bc(ap, shape, axis):
    return ap.unsqueeze(axis).to_broadcast(list(shape))


class Builder(Builder):
    def setup_consts(self):
        B = self
        cn = _consts()
        self.cnp = cn
        for k, v in cn.items():
            B.inp("c_" + k, v.shape)
        B.epsc = B.sbt("epsc", [128, 1])
        B.memset(POOL, B.epsc.t[:], EPS, [B.epsc])
        B.onec = B.sbt("onec", [128, 1])
        B.memset(POOL, B.onec.t[:], 1.0, [B.onec])
        def ld(name, dt=F32):
            t = B.sbt("k_" + name + ("b" if dt == BF16 else "f"), [128, 128], dt)
            q = POOL if dt == BF16 else SP
            B.dma(q, t.t[:], B.ins["c_" + name].t.ap(), [B.ins["c_" + name]], [t])
            return t
        B.identf = ld("ident")
        B.identb = ld("ident", BF16)
        B.triu = ld("triu")
        B.trils = ld("trils")
        B.negc = ld("negc")
        B.strictf = ld("strict")
        B.onesf = ld("ones")
        B.bdb = ld("bd", BF16)
        B.gm = {"triu": B.triu, "trils": B.trils, "negc": B.negc, "strictf": B.strictf}

    def setup_weights_in(self):
        B = self
        w_in = B.inp("w_in", [D, N_IN])
        g1 = B.inp("norm1_g", [D])
        B.Wb = B.sbt("Wb", [128, 8, N_IN], BF16)
        B.g1 = B.sbt("g1", [128, 8])
        B.dma(SP, B.g1.t[:], g1.t.ap().rearrange("(k p) -> p k", p=128), [g1], [B.g1], allow_slow_non_contiguous=True)
        m = B.P.mark()
        st = [B.sbt("wst%d" % i, [128, 512]) for i in range(2)]
        i = 0
        for kc in range(8):
            for c0 in range(0, N_IN, 512):
                cn = min(512, N_IN - c0)
                s_ = st[i % 2]
                B.dma(SP if i % 2 == 0 else POOL, s_.t[:, 0:cn], w_in.t.ap()[kc * 128:(kc + 1) * 128, c0:c0 + cn], [w_in], [s_])
                if i % 2 == 0:
                    B.act(B.Wb.t[:, kc, c0:c0 + cn], s_.t[:, 0:cn], AF.Copy, [s_, B.g1], [B.Wb], scale=B.g1.t[:, kc:kc + 1])
                else:
                    B.ts(DVE, B.Wb.t[:, kc, c0:c0 + cn], s_.t[:, 0:cn], B.g1.t[:, kc:kc + 1], None, ALU.mult, None, [s_, B.g1], [B.Wb])
                i += 1
        return m

    def gdn_consts(self):
        B = self
        cw = B.inp("conv_b_w", [4, 768])
        alog = B.inp("a_log_b", [128, 4])
        dtb = B.inp("dt_bias_b", [128, 4])
        onb = B.inp("out_norm_b", [128, 64])
        B.cw = B.sbt("cw", [128, 6, 4])
        for k in range(4):
            B.dma(SP, B.cw.t[:, :, k], cw.t.ap()[k, :].rearrange("(c p) -> p c", p=128), [cw], [B.cw], allow_slow_non_contiguous=True)
        B.nega = B.sbt("nega", [128, 4])
        B.dma(SP, B.nega.t[:], alog.t.ap(), [alog], [B.nega])
        B.act(B.nega.t[:], B.nega.t[:], AF.Exp, [B.nega], [B.nega])
        B.ts(DVE, B.nega.t[:], B.nega.t[:], -1.0, None, ALU.mult, None, [B.nega], [B.nega])
        B.dtb = B.sbt("dtb", [128, 4])
        B.dma(SP, B.dtb.t[:], dtb.t.ap(), [dtb], [B.dtb])
        B.onw = B.sbt("onw", [128, 64])
        B.dma(SP, B.onw.t[:], onb.t.ap(), [onb], [B.onw])

    def gdn_alloc(self, pfx="g_"):
        B = self
        s = {}
        def a(name, shape, dt=F32):
            s[name] = B.sbt(pfx + name, shape, dt)
        a("qkvx", [128, 6, 131]); a("ca", [128, 6, 128]); a("cs", [128, 6, 128])
        a("sqb", [128, 4, 128], BF16); a("rn", [128, 4, 128])
        a("qT", [128, 2, 128], BF16); a("kT", [128, 2, 128], BF16); a("vTb", [128, 2, 128], BF16)
        a("ktok", [128, 4, 64]); a("vtok", [128, 4, 64])
        a("ab", [128, 8]); a("beta", [128, 4]); a("g", [128, 4]); a("ng", [128, 4])
        a("gc", [128, 4]); a("rev", [128, 4]); a("egc", [128, 4]); a("erev", [128, 4])
        a("G1", [128, 4, 128]); a("G2", [128, 4, 128])
        a("Ec", [128, 4, 128]); a("EsB", [128, 4, 128])
        a("X", [128, 4, 128]); a("Nn", [128, 4, 128]); a("IX", [128, 4, 128]); a("AT", [128, 4, 128], BF16)
        a("Y", [128, 4, 128])
        a("ub", [128, 4, 64]); a("wb", [128, 4, 64], BF16); a("wT", [128, 2, 128], BF16)
        a("kd", [128, 4, 64], BF16); a("vnew", [128, 4, 64], BF16)
        a("S", [128, 2, 64]); a("Sb", [128, 2, 64], BF16); a("eglp", [128, 2])
        a("o", [128, 4, 64]); a("o2", [128, 4, 64]); a("ms", [128, 4]); a("gs", [128, 256]); a("obf", [128, 256], BF16)
        B.gs = s
        if not hasattr(B, "gsets"):
            B.gsets = {}
        B.memset(POOL, s["qkvx"].t[:], 0.0, [s["qkvx"]])
        B.memset(POOL, s["ca"].t[:], 0.0, [s["ca"]])
        B.memset(POOL, s["qT"].t[:], 0.0, [s["qT"]])
        B.memset(POOL, s["S"].t[:], 0.0, [s["S"]])
        B.memset(POOL, s["Sb"].t[:], 0.0, [s["Sb"]])

    def gdn_tile(self, psG, psAB, psA, psB, psC, psD, want_out, gate_fn=None, smode=None, par=0, first=False):
        B = self
        s = B.gsets[par]
        n = 128
        qkvx, ca, cs = s["qkvx"], s["ca"], s["cs"]
        if smode is None:
            if first:
                B.memset(POOL, qkvx.t[:, :, 0:3], 0.0, [qkvx])
            else:
                pq_ = B.gsets[1 - par]["qkvx"]
                B.cp(POOL, qkvx.t[:, :, 0:3], pq_.t[:, :, 128:131], [pq_], [qkvx])
            B.cp(ACT, qkvx.t[:, 0:3, 3:131], psG[0].t[:], [psG[0]], [qkvx])
            B.cp(DVE, qkvx.t[:, 3:6, 3:131], psG[1].t[:], [psG[1]], [qkvx])
            B.cp(DVE, s["ab"].t[:], psAB.t[:, 0:8], [psAB], [s["ab"]])
        if getattr(B, 'stop_at', 99) <= 1:
            return
        yield
        if smode is not None:
            smode["conv_fn"]()
        for c in range(6 if smode is None else 0):
            if c < 2 and not want_out:
                continue
            eng = DVE if c < 3 else POOL
            B.ts(eng, ca.t[:, c, :], qkvx.t[:, c, 0:128], B.cw.t[:, c, 0:1], None, ALU.mult, None, [qkvx, B.cw], [ca])
            for k in range(1, 4):
                B.stt(eng, ca.t[:, c, :], qkvx.t[:, c, k:k + 128], B.cw.t[:, c, k:k + 1], ca.t[:, c, :], ALU.mult, ALU.add,
                      [qkvx, B.cw, ca], [ca])
        B.act(cs.t[:], ca.t[:], AF.Silu, [ca], [cs])
        if getattr(B, 'stop_at', 99) <= 2:
            return
        yield
        B.tt(POOL, s["sqb"].t[:], cs.t[:, 0:4, :], cs.t[:, 0:4, :], ALU.mult, [cs], [s["sqb"]])
        B.mm(psA.t[:], B.bdb.t[:], s["sqb"].t[:].rearrange("p a b -> p (a b)"), [B.bdb, s["sqb"]], [psA])
        B.act(s["rn"].t[:], psA.t[:].rearrange("p (a b) -> p a b", a=4), AF.Sqrt, [psA, B.epsc], [s["rn"]], bias=B.epsc.t[:], scale=1.0)
        B.P.op(DVE, lambda e: e.reciprocal(out=s["rn"].t[:], in_=s["rn"].t[:]), [s["rn"].b], [s["rn"].b])
        if want_out:
            B.stt(DVE, s["qT"].t[:], cs.t[:, 0:2, :], 0.125, s["rn"].t[:, 0:2, :], ALU.mult, ALU.mult, [cs, s["rn"]], [s["qT"]])
        B.tt(POOL, s["kT"].t[:], cs.t[:, 2:4, :], s["rn"].t[:, 2:4, :], ALU.mult, [cs, s["rn"]], [s["kT"]])
        B.cp(ACT, s["vTb"].t[:], cs.t[:, 4:6, :], [cs], [s["vTb"]])
        if getattr(B, 'stop_at', 99) <= 3:
            return
        yield
        pTb = B.psTb
        for a_ in range(2):
            B.tr(pTb.t[:, a_, :], s["kT"].t[:, a_, :], B.identb.t[:], [s["kT"], B.identb], [pTb])
            B.tr(pTb.t[:, 2 + a_, :], s["vTb"].t[:, a_, :], B.identb.t[:], [s["vTb"], B.identb], [pTb])
        if getattr(B, 'stop_at', 99) <= 3.1:
            return
        B.cp(DVE, s["ktok"].t[:].rearrange("p a b -> p (a b)"), pTb.t[:, 0:2, :].rearrange("p a b -> p (a b)"), [pTb], [s["ktok"]])
        if getattr(B, 'stop_at', 99) <= 3.2:
            return
        B.cp(DVE, s["vtok"].t[:].rearrange("p a b -> p (a b)"), pTb.t[:, 2:4, :].rearrange("p a b -> p (a b)"), [pTb], [s["vtok"]])
        if getattr(B, 'stop_at', 99) <= 4:
            return
        yield
        ab = s["ab"]
        B.act(s["beta"].t[:], ab.t[:, 4:8], AF.Sigmoid, [ab], [s["beta"]])
        B.tt(DVE, s["g"].t[:], ab.t[:, 0:4], B.dtb.t[:], ALU.add, [ab, B.dtb], [s["g"]])
        B.act(s["g"].t[:], s["g"].t[:], AF.Exp, [s["g"]], [s["g"]])
        B.act(s["g"].t[:], s["g"].t[:], AF.Ln, [s["g"], B.onec], [s["g"]], bias=B.onec.t[:], scale=1.0)
        B.tt(DVE, s["g"].t[:], s["g"].t[:], B.nega.t[:], ALU.mult, [s["g"], B.nega], [s["g"]])
        B.ts(DVE, s["ng"].t[:], s["g"].t[:], -1.0, None, ALU.mult, None, [s["g"]], [s["ng"]])
        if getattr(B, 'stop_at', 99) <= 5:
            return
        yield
        B.mm(psD.t[:, 0:4], B.gm["triu"].t[:], s["g"].t[:], [B.gm["triu"], s["g"]], [psD])
        B.mm(psD.t[:, 4:8], B.gm["trils"].t[:], s["g"].t[:], [B.gm["trils"], s["g"]], [psD])
        B.mm(psD.t[:, 8:12], B.onesf.t[:], s["g"].t[:], [B.onesf, s["g"]], [psD])
        B.act(s["egc"].t[:], psD.t[:, 0:4], AF.Exp, [psD], [s["egc"]])
        B.act(s["erev"].t[:], psD.t[:, 4:8], AF.Exp, [psD], [s["erev"]])
        B.act(s["eglp"].t[0:64, :], psD.t[0:64, 8:12:2], AF.Exp, [psD], [s["eglp"]])
        B.act(s["eglp"].t[64:128, :], psD.t[64:128, 9:12:2], AF.Exp, [psD], [s["eglp"]])
        if getattr(B, 'stop_at', 99) <= 6:
            return
        yield
        B.tt(DVE, s["G1"].t[:], bc(B.onesf.t[:], [128, 4, 128], 1), bc(s["g"].t[:], [128, 4, 128], 2), ALU.mult, [B.onesf, s["g"]], [s["G1"]])
        B.tt(POOL, s["G2"].t[:], bc(B.gm["triu"].t[:], [128, 4, 128], 1), bc(s["ng"].t[:], [128, 4, 128], 2), ALU.mult, [B.gm["triu"], s["ng"]], [s["G2"]])
        pC = psC.t[:].rearrange("p (a b) -> p a b", a=4)
        for h in range(4):
            B.mm(pC[:, h, :], s["G1"].t[:, h, :], B.gm["triu"].t[:], [s["G1"], B.gm["triu"]], [psC], start=True, stop=False)
            B.mm(pC[:, h, :], s["G2"].t[:, h, :], B.onesf.t[:], [s["G2"], B.onesf], [psC], start=False, stop=False)
            B.mm(pC[:, h, :], B.identf.t[:], B.gm["negc"].t[:], [B.identf, B.gm["negc"]], [psC], start=False, stop=True)
        B.act(s["Ec"].t[:], pC, AF.Exp, [psC], [s["Ec"]])
        if getattr(B, 'stop_at', 99) <= 7:
            return
        yield
        B.tt(POOL, s["EsB"].t[:], s["Ec"].t[:], bc(B.gm["strictf"].t[:], [128, 4, 128], 1), ALU.mult, [s["Ec"], B.gm["strictf"]], [s["EsB"]])
        B.tt(POOL, s["EsB"].t[:], s["EsB"].t[:], bc(s["beta"].t[:], [128, 4, 128], 2), ALU.mult, [s["EsB"], s["beta"]], [s["EsB"]])
        if getattr(B, 'stop_at', 99) <= 8:
            return
        yield
        pA = psA.t[:].rearrange("p (a b) -> p a b", a=4)
        pB = psB.t[:].rearrange("p (a b) -> p a b", a=4)
        for h in range(4):
            lo = 64 * (h % 2)
            kTh = s["kT"].t[lo:lo + 64, h // 2, :]
            B.mm(pA[:, h, :], kTh, kTh, [s["kT"]], [psA])
            if want_out:
                B.mm(pB[:, h, :], kTh, s["qT"].t[lo:lo + 64, h // 2, :], [s["kT"], s["qT"]], [psB])
        B.tt(DVE, s["X"].t[:], pA, s["EsB"].t[:], ALU.mult, [psA, s["EsB"]], [s["X"]])
        if want_out:
            B.tt(DVE, s["AT"].t[:], pB, s["Ec"].t[:], ALU.mult, [psB, s["Ec"]], [s["AT"]])
        yield
        Y = s["Y"]
        B.cp(ACT, Y.t[:, :, 0:64], s["vtok"].t[:], [s["vtok"]], [Y])
        B.tt(POOL, Y.t[:, :, 64:128], s["ktok"].t[:], bc(s["egc"].t[:], [128, 4, 64], 2), ALU.mult, [s["ktok"], s["egc"]], [Y])
        B.tt(POOL, s["kd"].t[:], s["ktok"].t[:], bc(s["erev"].t[:], [128, 4, 64], 2), ALU.mult, [s["ktok"], s["erev"]], [s["kd"]])
        if getattr(B, 'stop_at', 99) <= 10:
            return
        yield
        pTb = B.psTb
        for h in range(4):
            B.mm(pB[:, h, :], s["X"].t[:, h, :], B.identf.t[:], [s["X"], B.identf], [psB])
        B.cp(ACT, s["Nn"].t[:], pB, [psB], [s["Nn"]])
        identb4 = bc(B.identf.t[:], [128, 4, 128], 1)
        B.stt(POOL, s["IX"].t[:], s["X"].t[:], -1.0, identb4, ALU.mult, ALU.add, [s["X"], B.identf], [s["IX"]])
        if getattr(B, 'stop_at', 99) <= 11:
            return
        nlev = getattr(B, 'nlev', 7)
        for lev in range(nlev):
            for h in range(4):
                B.mm(pC[:, h, :], s["IX"].t[:, h, :], Y.t[:, h, :], [s["IX"], Y], [psC])
                if lev < nlev - 1:
                    B.mm(pA[:, h, :], s["Nn"].t[:, h, :], s["X"].t[:, h, :], [s["Nn"], s["X"]], [psA])
                    if lev < nlev - 2:
                        B.mm(pB[:, h, :], s["X"].t[:, h, :], s["Nn"].t[:, h, :], [s["Nn"], s["X"]], [psB])
            if lev < nlev - 1:
                B.cp(DVE, Y.t[:], pC, [psC], [Y])
                B.cp(ACT, s["X"].t[:], pA, [psA], [s["X"]])
                B.tt(POOL, s["IX"].t[:], s["X"].t[:], identb4, ALU.add, [s["X"], B.identf], [s["IX"]])
                if lev < nlev - 2:
                    B.cp(ACT, s["Nn"].t[:], pB, [psB], [s["Nn"]])
                yield
        if getattr(B, 'stop_at', 99) <= 12:
            return
        B.tt(DVE, s["ub"].t[:], pC[:, :, 0:64], bc(s["beta"].t[:], [128, 4, 64], 2), ALU.mult, [psC, s["beta"]], [s["ub"]])
        B.tt(DVE, s["wb"].t[:], pC[:, :, 64:128], bc(s["beta"].t[:], [128, 4, 64], 2), ALU.mult, [psC, s["beta"]], [s["wb"]])
        for p_ in range(2):
            B.tr(pTb.t[:, 4 + p_, :], s["wb"].t[:, 2 * p_:2 * p_ + 2, :].rearrange("p a b -> p (a b)"), B.identb.t[:], [s["wb"], B.identb], [pTb])
        B.cp(DVE, s["wT"].t[:], pTb.t[:, 4:6, :], [pTb], [s["wT"]])
        if getattr(B, 'stop_at', 99) <= 13:
            return
        if smode is not None:
            smode["state_fn"](psB, psD, gate_fn)
            return
        yield
        pD = psD.t[:, 0:256].rearrange("p (a b) -> p a b", a=4)
        pD2 = psD.t[:, 256:512].rearrange("p (a b) -> p a b", a=4)
        pQ = psB.t[:, 0:256].rearrange("p (a b) -> p a b", a=4)
        pV = psB.t[:, 256:512].rearrange("p (a b) -> p a b", a=4)
        for h in range(4):
            lo = 64 * (h % 2)
            B.mm(pD[:, h, :], s["wT"].t[lo:lo + 64, h // 2, :], s["Sb"].t[lo:lo + 64, h // 2, :], [s["wT"], s["Sb"]], [psD])
            if want_out:
                B.mm(pQ[:, h, :], s["qT"].t[lo:lo + 64, h // 2, :], s["Sb"].t[lo:lo + 64, h // 2, :], [s["qT"], s["Sb"]], [psB])
        B.tt(DVE, s["vnew"].t[:], s["ub"].t[:], pD, ALU.subtract, [s["ub"], psD], [s["vnew"]])
        if want_out:
            B.tt(DVE, s["o"].t[:], pQ, bc(s["egc"].t[:], [128, 4, 64], 2), ALU.mult, [psB, s["egc"]], [s["o"]])
        for h in range(4):
            if want_out:
                B.mm(pV[:, h, :], s["AT"].t[:, h, :], s["vnew"].t[:, h, :], [s["AT"], s["vnew"]], [psB])
            B.mm(pD2[:, h, :], s["kd"].t[:, 2 * (h // 2):2 * (h // 2) + 2, :].rearrange("p a b -> p (a b)"), s["vnew"].t[:, h, :], [s["kd"], s["vnew"]], [psD])
        if want_out:
            B.tt(DVE, s["o"].t[:], s["o"].t[:], pV, ALU.add, [psB, s["o"]], [s["o"]])
        B.tt(POOL, s["S"].t[:], s["S"].t[:], bc(s["eglp"].t[:], [128, 2, 64], 2), ALU.mult, [s["S"], s["eglp"]], [s["S"]])
        B.tt(DVE, s["S"].t[0:64, :, :], s["S"].t[0:64, :, :], pD2[0:64, 0:4:2, :], ALU.add, [s["S"], psD], [s["S"]])
        B.tt(DVE, s["S"].t[64:128, :, :], s["S"].t[64:128, :, :], pD2[64:128, 1:4:2, :], ALU.add, [s["S"], psD], [s["S"]])
        B.cp(ACT, s["Sb"].t[:], s["S"].t[:], [s["S"]], [s["Sb"]])
        if want_out:
            yield
            B.tt(POOL, s["o2"].t[:], s["o"].t[:], s["o"].t[:], ALU.mult, [s["o"]], [s["o2"]])
            B.P.op(DVE, lambda e: e.tensor_reduce(out=s["ms"].t[:], in_=s["o2"].t[:], axis=AX.X, op=ALU.add), [s["o2"].b], [s["ms"].b])
            B.act(s["ms"].t[:], s["ms"].t[:], AF.Sqrt, [s["ms"], B.epsc], [s["ms"]], bias=B.epsc.t[:], scale=1.0 / 64)
            B.P.op(DVE, lambda e: e.reciprocal(out=s["ms"].t[:], in_=s["ms"].t[:]), [s["ms"].b], [s["ms"].b])
            B.tt(DVE, s["o"].t[:], s["o"].t[:], bc(s["ms"].t[:], [128, 4, 64], 2), ALU.mult, [s["o"], s["ms"]], [s["o"]])
            B.tt(POOL, s["o"].t[:], s["o"].t[:], bc(B.onw.t[:], [128, 4, 64], 1), ALU.mult, [s["o"], B.onw], [s["o"]])
            psGate = gate_fn()
            B.act(s["gs"].t[:], psGate.t[:, 0:256], AF.Silu, [psGate], [s["gs"]])
            B.tt(DVE, s["obf"].t[:], s["o"].t[:].rearrange("p a b -> p (a b)"), s["gs"].t[:], ALU.mult, [s["o"], s["gs"]], [s["obf"]])


def alibi_masks():
    slopes = 2.0 ** (-8.0 * np.arange(1, 9) / 8.0)
    k = np.arange(128)[:, None]
    q = np.arange(128)[None, :]
    m = np.zeros((4, 128, 17, 2, 128), np.float64)
    for o in range(17):
        d = 128 * o + q - k
        c = ((d >= 0) & (d <= 128)).astype(np.float64) + ((d >= 0) & (d <= 512) & (d % 4 == 0)) + ((d >= 0) & (d <= 2048) & (d % 16 == 0))
        for h in range(8):
            m[h // 2, :, o, h % 2, :] = c * np.exp(-slopes[h] * np.maximum(d, 0))
    return m.astype(np.float32)


class Builder(Builder):
    def headnorm_fm(self, ps, npairs, ntok, gcol, dst, dstT, ps2, tmp_sq, tmp_rn):
        B = self
        n = npairs * ntok
        B.act(tmp_sq.t[:, 0:n], ps.t[:, 0:n], AF.Square, [ps], [tmp_sq])
        B.mm(ps2.t[:, 0:n], B.bdb.t[:], tmp_sq.t[:, 0:n], [B.bdb, tmp_sq], [ps2])
        B.act(tmp_rn.t[:, 0:n], ps2.t[:, 0:n], AF.Sqrt, [ps2, B.epsc], [tmp_rn], bias=B.epsc.t[:], scale=1.0 / 64)
        B.P.op(DVE, lambda e: e.reciprocal(out=tmp_rn.t[:, 0:n], in_=tmp_rn.t[:, 0:n]), [tmp_rn.b], [tmp_rn.b])
        B.stt(DVE, dst, ps.t[:, 0:n], gcol, tmp_rn.t[:, 0:n], ALU.mult, ALU.mult, [ps, tmp_rn] + [v[1] for v in B.gw.values()], [dstT])

    def rmsnorm_tile(self, x, sqj, ss, xb, gate_scale=None):
        B = self
        B.act(sqj.t[:], x.t[:], AF.Square, [x], [sqj, ss], accum_out=ss.t[:])
        np_ = x.t.shape[0]
        B.act(ss.t[:], ss.t[:], AF.Sqrt, [ss, B.epsc], [ss], bias=B.epsc.t[0:np_, :], scale=1.0 / D)
        B.P.op(DVE, lambda e: e.reciprocal(out=ss.t[:], in_=ss.t[:]), [ss.b], [ss.b])
        B.act(xb.t[:], x.t[:], AF.Copy, [x, ss], [xb], scale=ss.t[:])

    def tok_headnorm(self, ps, nh, gw, out, kss, np_=128):
        B = self
        n = nh * 64
        P_ = slice(0, np_)
        B.act(out.t[P_, 0:n], ps.t[P_, 0:n], AF.Square, [ps], [out])
        B.P.op(DVE, lambda e: e.tensor_reduce(out=kss.t[P_, 0:nh], in_=out.t[P_, 0:n].rearrange("p (a b) -> p a b", a=nh), axis=AX.X, op=ALU.add), [out.b], [kss.b])
        B.act(kss.t[P_, 0:nh], kss.t[P_, 0:nh], AF.Sqrt, [kss, B.epsc], [kss], bias=B.epsc.t[P_, :], scale=1.0 / 64)
        B.P.op(DVE, lambda e: e.reciprocal(out=kss.t[P_, 0:nh], in_=kss.t[P_, 0:nh]), [kss.b], [kss.b])
        o3 = out.t[P_, 0:n].rearrange("p (a b) -> p a b", a=nh)
        B.tt(DVE, o3, ps.t[P_, 0:n].rearrange("p (a b) -> p a b", a=nh), bc(kss.t[P_, 0:nh], [np_, nh, 64], 2), ALU.mult, [ps, kss], [out])
        B.tt(DVE, o3, o3, bc(gw.t[P_, :], [np_, nh, 64], 1), ALU.mult, [out, gw], [out])

    def prompt_setup(self, NOWN, sample=False):
        B = self
        NT = B.NT
        B.NOWN = NOWN
        B.th = NT - NOWN - 1
        assert B.th >= 0
        B.NF = NOWN + 1
        B.tw0 = max(0, B.th - 16)
        B.KW = NT - B.tw0
        B.xloc = B.inp("xloc", [B.L, D])
        B.valid = B.inp("valid", [128, NT])
        B.flag = B.inp("flag", [128, 1])
        B.amask = B.inp("amask", [4, 128, 17 * 2 * 128])
        for nm in ["q_norm_a", "k_norm_a", "q_norm_c", "k_norm_c"]:
            B.inp(nm, [128, 64])
            B.inp(nm + "_col", [128, 1])
        B.winK = B.outp("win_k_p", [NOWN * 128, 512])
        B.winV = B.outp("win_v_p", [NOWN * 128, 512])
        B.gst = B.outp("gdn_state_p", [4, 64, 64])
        B.gcv = B.outp("gdn_conv_p", [3, 768])
        B.memk_o = B.outp("mem_k_p", [256, 256])
        B.memv_o = B.outp("mem_v_p", [256, 256])
        B.y_o = B.outp("y_p", [NOWN * 128, D])
        B.fcv = B.outp("ffn_conv_p", [2, 2 * D_FF])
        B.kTs = B.scratch("kTs", [4, 128, B.KW * 128], BF16)
        B.qTs = B.scratch("qTs", [4, 128, B.NF * 128], BF16)
        B.vaug = B.scratch("vaug", [B.KW, 128, 8 * 80], BF16)
        B.x1s = B.scratch("x1s", [B.NF * 128, D])
        B.gw = {}
        for nm in ["q_norm_a", "k_norm_a", "q_norm_c", "k_norm_c"]:
            t = B.sbt("gw_" + nm, [128, 64])
            B.dma(SP, t.t[:], B.ins[nm].t.ap(), [B.ins[nm]], [t])
            c = B.sbt("gc_" + nm, [128, 1])
            B.dma(SP, c.t[:], B.ins[nm + "_col"].t.ap(), [B.ins[nm + "_col"]], [c])
            B.gw[nm] = (t, c)
        B.validsb = B.sbt("validsb", [128, NT])
        B.dma(SP, B.validsb.t[:], B.valid.t.ap(), [B.valid], [B.validsb])
        B.flagsb = B.sbt("flagsb", [128, 1])
        B.dma(SP, B.flagsb.t[:], B.flag.t.ap(), [B.flag], [B.flagsb])
        B.pb = [None] + [B.pst("pb%d" % i, [128, 512]) for i in range(1, 7)]
        B.psTb = B.pst("psTb", [128, 8, 128], BF16)
        B.psT = B.pst("psT", [128, 8, 128], BF16)
        if sample:
            B.sample_setup()
        B.m_mix = B.P.mark()
        B.mixTs = B.scratch("mixTs", [8, 128, B.NF * 128], BF16)

    def mem_kv(self):
        B = self
        pb = B.pb
        mem = B.inp("mem_prompt", [256, D])
        wm = B.inp("w_mem_kv", [D, 512])
        gm = B.inp("mem_norm_g", [D])
        B.mkT = B.sbt("mkT", [128, 2, 256], BF16)
        B.mvaug = B.sbt("mvaug", [128, 2, 4, 80], BF16)
        m = B.P.mark()
        B.Wm = B.sbt("Wm", [128, 8, 512], BF16)
        gmc = B.sbt("gmc", [128, 8])
        B.dma(SP, gmc.t[:], gm.t.ap().rearrange("(k p) -> p k", p=128), [gm], [gmc], allow_slow_non_contiguous=True)
        st = B.sbt("wmst", [128, 512])
        for kc in range(8):
            B.dma(SP, st.t[:], wm.t.ap()[kc * 128:(kc + 1) * 128, :], [wm], [st])
            B.ts(DVE, B.Wm.t[:, kc, :], st.t[:], gmc.t[:, kc:kc + 1], None, ALU.mult, None, [st, gmc], [B.Wm])
        x = B.sbt("memx", [128, D]); sqj = B.sbt("memsq", [128, D]); ss = B.sbt("memss", [128, 1]); xb = B.sbt("memxb", [128, D], BF16)
        mT = B.sbt("memT", [128, 8, 256], BF16)
        tsq = B.sbt("mem_tsq", [128, 512], BF16); trn = B.sbt("mem_trn", [128, 512])
        tokf = B.sbt("mem_tokf", [128, 256]); kss = B.sbt("mem_kss", [128, 8])
        for t in range(2):
            B.dma(SP, x.t[:], mem.t.ap()[t * 128:(t + 1) * 128, :], [mem], [x])
            B.rmsnorm_tile(x, sqj, ss, xb)
            for c in range(8):
                B.tr(B.psT.t[:, c, :], xb.t[:, c * 128:(c + 1) * 128], B.identb.t[:], [xb, B.identb], [B.psT])
            B.cp(DVE, mT.t[:, :, t * 128:(t + 1) * 128], B.psT.t[:], [B.psT], [mT])
        for p_ in range(2):
            for kc in range(8):
                B.mm(pb[3].t[:, p_ * 256:(p_ + 1) * 256], B.Wm.t[:, kc, p_ * 128:(p_ + 1) * 128], mT.t[:, kc, :], [B.Wm, mT], [pb[3]], start=(kc == 0), stop=(kc == 7))
        B.headnorm_fm(pb[3], 2, 256, B.gw["k_norm_c"][1].t[:], B.mkT.t[:].rearrange("p a b -> p (a b)"), B.mkT, pb[4], tsq, trn)
        B.memset(POOL, B.mvaug.t[:], 0.0, [B.mvaug])
        B.memset(POOL, B.mvaug.t[:, :, :, 64:65], 1.0, [B.mvaug])
        for t in range(2):
            for kc in range(8):
                B.mm(pb[5].t[:, 0:512], mT.t[:, kc, t * 128:(t + 1) * 128], B.Wm.t[:, kc, :], [B.Wm, mT], [pb[5]], start=(kc == 0), stop=(kc == 7))
            pk = T(pb[5].t[:, 0:256]); pk.b = pb[5].b
            B.tok_headnorm(pk, 4, B.gw["k_norm_c"][0], tokf, kss)
            B.dma(SP, B.memk_o.t.ap()[t * 128:(t + 1) * 128, :], tokf.t[:], [tokf], [B.memk_o])
            tokv = B.sbt("mem_tokv%d" % t, [128, 256])
            B.cp(ACT, tokv.t[:], pb[5].t[:, 256:512], [pb[5]], [tokv])
            B.dma(SP, B.memv_o.t.ap()[t * 128:(t + 1) * 128, :], tokv.t[:], [tokv], [B.memv_o])
            B.cp(DVE, B.mvaug.t[:, t, :, 0:64], pb[5].t[:, 256:512].rearrange("p (a b) -> p a b", a=4), [pb[5]], [B.mvaug])
        return m

    def prompt_loop(self):
        B = self
        NT, NOWN, th, tw0 = B.NT, B.NOWN, B.th, B.tw0
        pb = B.pb
        psT = B.psT
        xt = [B.sbt("xt%d" % i, [128, D]) for i in range(2)]
        sqj = B.sbt("sqj", [128, D], BF16)
        ssL = [B.sbt("ss%d" % i, [128, 1]) for i in range(2)]
        xbL = [B.sbt("xb%d" % i, [128, D], BF16) for i in range(2)]
        xnTL = [B.sbt("xnT%d" % i, [128, 8, 128], BF16) for i in range(2)]
        kv = B.sbt("kvtok", [128, 512]); kv2 = B.sbt("kv2", [128, 512]); kss = B.sbt("kss", [128, 8])
        tsq = B.sbt("tsq", [128, 512], BF16); trn = B.sbt("trn", [128, 512])
        kst = B.sbt("kst", [128, 4, 128], BF16); qst = B.sbt("qst", [128, 4, 128], BF16)
        vst = B.sbt("vst", [128, 8, 80], BF16)
        qcT = B.sbt("qcT", [128, 2, 128], BF16)
        pcs = B.sbt("pcs", [128, 4, 128], BF16)
        rc = B.sbt("rc", [128, 2]); ocp = B.sbt("ocp", [128, 2, 64], BF16)
        mixst = B.sbt("mixst", [128, 4, 128], BF16)
        B.memset(POOL, vst.t[:], 0.0, [vst])
        B.gdn_alloc("g0_"); B.gsets[0] = B.gs
        B.gdn_alloc("g1_"); B.gsets[1] = B.gs
        B.gsets[1]["S"] = B.gsets[0]["S"]; B.gsets[1]["Sb"] = B.gsets[0]["Sb"]

        def tile_gen(t):
            par = t % 2
            s = B.gsets[par]
            ss = ssL[par]; xb = xbL[par]; xnT = xnTL[par]
            win = t >= tw0
            F = t >= th
            own = t > th
            f = t - th
            kw = t - tw0
            x = xt[t % 2]
            B.dma(SP, x.t[:], B.xloc.t.ap()[t * 128:(t + 1) * 128, :], [B.xloc], [x])
            B.rmsnorm_tile(x, sqj, ss, xb)
            for c in range(8):
                B.tr(psT.t[:, c, :], xb.t[:, c * 128:(c + 1) * 128], B.identb.t[:], [xb, B.identb], [psT])
            B.cp(DVE, xnT.t[:], psT.t[:], [psT], [xnT])
            if win and 'win' not in B.skip:
                for p_ in range(4):
                    if 'kproj' in B.skip:
                        break
                    for kc in range(8):
                        B.mm(pb[3].t[:, p_ * 128:(p_ + 1) * 128], B.Wb.t[:, kc, C_KA + p_ * 128:C_KA + (p_ + 1) * 128], xnT.t[:, kc, :],
                             [B.Wb, xnT], [pb[3]], start=(kc == 0), stop=(kc == 7))
                if 'kproj' not in B.skip:
                  B.headnorm_fm(pb[3], 4, 128, B.gw["k_norm_a"][1].t[:], kst.t[:].rearrange("p a b -> p (a b)"), kst, pb[4], tsq, trn)
                if 'ktsdma' not in B.skip:
                    B.dma(SP, B.kTs.t.ap()[:, :, kw * 128:(kw + 1) * 128].rearrange("a p n -> p a n"), kst.t[:], [kst], [B.kTs])
                for kc in range(8):
                    B.mm(pb[4].t[:], xnT.t[:, kc, :], B.Wb.t[:, kc, C_VA:C_VA + 512], [B.Wb, xnT], [pb[4]], start=(kc == 0), stop=(kc == 7))
                if 'vts' not in B.skip:
                    if 'novalid' in B.skip:
                        B.cp(DVE, vst.t[:, :, 0:64], pb[4].t[:].rearrange("p (a b) -> p a b", a=8), [pb[4]], [vst])
                    elif 'actv' in B.skip:
                        B.act(vst.t[:, :, 0:64], pb[4].t[:].rearrange("p (a b) -> p a b", a=8), AF.Copy, [pb[4], B.validsb], [vst], scale=B.validsb.t[:, t:t + 1])
                    else:
                        B.tt(DVE, vst.t[:, :, 0:64], pb[4].t[:].rearrange("p (a b) -> p a b", a=8),
                             B.validsb.t[:, t:t + 1].unsqueeze(1).to_broadcast([128, 8, 64]), ALU.mult, [pb[4], B.validsb], [vst])
                if 'vstcp' not in B.skip:
                    B.cp(DVE, vst.t[:, :, 64:65], bc(B.validsb.t[:, t:t + 1], [128, 8, 1], 1), [B.validsb], [vst])
                if 'vaugdma' not in B.skip:
                    B.dma(SP, B.vaug.t.ap()[kw, :, :], vst.t[:].rearrange("p a b -> p (a b)"), [vst], [B.vaug])
                if own:
                    r0 = (f - 1) * 128
                    B.cp(ACT, kv2.t[:], pb[4].t[:], [pb[4]], [kv2])
                    B.dma(SP, B.winV.t.ap()[r0:r0 + 128, :], kv2.t[:], [kv2], [B.winV])
                    for kc in range(8):
                        B.mm(pb[5].t[:], xnT.t[:, kc, :], B.Wb.t[:, kc, C_KA:C_KA + 512], [B.Wb, xnT], [pb[5]], start=(kc == 0), stop=(kc == 7))
                    B.tok_headnorm(pb[5], 8, B.gw["k_norm_a"][0], kv, kss)
                    B.dma(SP, B.winK.t.ap()[r0:r0 + 128, :], kv.t[:], [kv], [B.winK])
            if F and 'fq' not in B.skip:
                for p_ in range(4):
                    for kc in range(8):
                        B.mm(pb[3].t[:, p_ * 128:(p_ + 1) * 128], B.Wb.t[:, kc, C_QA + p_ * 128:C_QA + (p_ + 1) * 128], xnT.t[:, kc, :],
                             [B.Wb, xnT], [pb[3]], start=(kc == 0), stop=(kc == 7))
                B.headnorm_fm(pb[3], 4, 128, B.gw["q_norm_a"][1].t[:], qst.t[:].rearrange("p a b -> p (a b)"), qst, pb[4], tsq, trn)
                B.dma(SP, B.qTs.t.ap()[:, :, f * 128:(f + 1) * 128].rearrange("a p n -> p a n"), qst.t[:], [qst], [B.qTs])
            if F and 'cross' not in B.skip:
                for p_ in range(2):
                    for kc in range(8):
                        B.mm(pb[3].t[:, p_ * 128:(p_ + 1) * 128], B.Wb.t[:, kc, C_QC + p_ * 128:C_QC + (p_ + 1) * 128], xnT.t[:, kc, :],
                             [B.Wb, xnT], [pb[3]], start=(kc == 0), stop=(kc == 7))
                B.headnorm_fm(pb[3], 2, 128, B.gw["q_norm_c"][1].t[:], qcT.t[:].rearrange("p a b -> p (a b)"), qcT, pb[4], tsq, trn)
                for p_ in range(2):
                    p5 = pb[5].t[:].rearrange("p (a b) -> p a b", a=4)
                    for mt in range(2):
                        for h2 in range(2):
                            lo = 64 * h2
                            B.mm(p5[:, mt * 2 + h2, :], B.mkT.t[lo:lo + 64, p_, mt * 128:(mt + 1) * 128], qcT.t[lo:lo + 64, p_, :],
                                 [B.mkT, qcT], [pb[5]])
                    B.act(pcs.t[:], p5, AF.Exp, [pb[5]], [pcs], scale=0.125)
                    for h2 in range(2):
                        for mt in range(2):
                            B.mm(pb[6].t[:, h2 * 128:h2 * 128 + 65], pcs.t[:, mt * 2 + h2, :], B.mvaug.t[:, mt, 2 * p_ + h2, 0:65],
                                 [pcs, B.mvaug], [pb[6]], start=(mt == 0), stop=(mt == 1))
                    B.P.op(DVE, lambda e: e.reciprocal(out=rc.t[:], in_=pb[6].t[:, 64:256:128]), [pb[6].b], [rc.b])
                    B.tt(DVE, ocp.t[:], pb[6].t[:, 0:256].rearrange("p (a b) -> p a b", a=2)[:, :, 0:64], bc(rc.t[:], [128, 2, 64], 2), ALU.mult,
                         [pb[6], rc], [ocp])
                    B.tr(B.psTb.t[:, 0, :], ocp.t[:].rearrange("p a b -> p (a b)"), B.identb.t[:], [ocp, B.identb], [B.psTb])
                    B.cp(DVE, mixst.t[:, 2 + p_, :], B.psTb.t[:, 0, :], [B.psTb], [mixst])
                B.dma(SP, B.mixTs.t.ap()[6:8, :, f * 128:(f + 1) * 128].rearrange("a p n -> p a n"), mixst.t[:, 2:4, :], [mixst], [B.mixTs])
            for ct in (0, 3, 1, 4, 2, 5):
                dst = pb[1 + ct // 3]
                for kc in range(8):
                    B.mm(dst.t[:, (ct % 3) * 128:(ct % 3 + 1) * 128], B.Wb.t[:, kc, C_QKV + ct * 128:C_QKV + (ct + 1) * 128],
                         xnT.t[:, kc, :], [B.Wb, xnT], [dst], start=(kc == 0), stop=(kc == 7))
            for kc in range(8):
                B.mm(pb[1].t[:, 384:392], xnT.t[:, kc, :], B.Wb.t[:, kc, C_A:C_A + 8], [B.Wb, xnT], [pb[1]], start=(kc == 0), stop=(kc == 7))
            psG = [T(pb[1].t[:, 0:384].rearrange("p (a b) -> p a b", a=3)), T(pb[2].t[:, 0:384].rearrange("p (a b) -> p a b", a=3))]
            psG[0].b = pb[1].b
            psG[1].b = pb[2].b
            psAB = T(pb[1].t[:, 384:392])
            psAB.b = pb[1].b

            def gate_fn():
                for kc in range(8):
                    B.mm(pb[3].t[:, 0:256], xnT.t[:, kc, :], B.Wb.t[:, kc, C_GATE:C_GATE + 256], [B.Wb, xnT], [pb[3]], start=(kc == 0), stop=(kc == 7))
                return pb[3]
            yield from B.gdn_tile(psG, psAB, pb[3], pb[4], pb[5], pb[6], want_out=F and 'gout' not in B.skip, gate_fn=gate_fn, par=par, first=(t == 0))
            if F and 'gout' not in B.skip:
                for p_ in range(2):
                    B.tr(B.psTb.t[:, p_, :], s["obf"].t[:, p_ * 128:(p_ + 1) * 128], B.identb.t[:], [s["obf"], B.identb], [B.psTb])
                B.cp(DVE, mixst.t[:, 0:2, :], B.psTb.t[:, 0:2, :], [B.psTb], [mixst])
                B.dma(SP, B.mixTs.t.ap()[4:6, :, f * 128:(f + 1) * 128].rearrange("a p n -> p a n"), mixst.t[:, 0:2, :], [mixst], [B.mixTs])

        active = []
        nxt = 0
        while active or nxt < NT:
            if len(active) < 2 and nxt < NT:
                active.append(tile_gen(nxt)); nxt += 1
            for g_ in list(active):
                try:
                    next(g_)
                except StopIteration:
                    active.remove(g_)
        s = B.gsets[(NT - 1) % 2]
        B.dma(SP, B.gst.t.ap().rearrange("(p h) k v -> (h k) p v", h=2), s["S"].t[:], [s["S"]], [B.gst])
        gct = B.sbt("gct", [3, 768])
        for c in range(4):
            B.mm(pb[3].t[0:3, c * 128:(c + 1) * 128], s["qkvx"].t[:, c, 128:131], B.identf.t[:], [s["qkvx"], B.identf], [pb[3]])
        B.cp(DVE, gct.t[:, 0:512], pb[3].t[0:3, 0:512], [pb[3]], [gct])
        for c in range(4, 6):
            B.mm(pb[4].t[0:3, (c - 4) * 128:(c - 3) * 128], s["qkvx"].t[:, c, 128:131], B.identf.t[:], [s["qkvx"], B.identf], [pb[4]])
        B.cp(DVE, gct.t[:, 512:768], pb[4].t[0:3, 0:256], [pb[4]], [gct])
        B.dma(SP, B.gcv.t.ap(), gct.t[:], [gct], [B.gcv])

    def attention_phase(self):
        B = self
        NF, th, tw0, KW = B.NF, B.th, B.tw0, B.KW
        pb = B.pb
        vall = B.sbt("vall", [128, KW, 8 * 80], BF16)
        B.dma(SP, vall.t[:], B.vaug.t.ap().rearrange("k p n -> p k n"), [B.vaug], [vall])
        kTb = [B.sbt("kTb%d" % i, [128, KW * 128], BF16) for i in range(2)]
        qTb = [B.sbt("qTb%d" % i, [128, NF * 128], BF16) for i in range(2)]
        msk = [B.sbt("msk%d" % i, [128, 17 * 2 * 128]) for i in range(2)]
        Eb = [B.sbt("Eb%d" % i, [128, 4, 128]) for i in range(2)]
        Pm = [B.sbt("Pm%d" % i, [128, 4, 128], BF16) for i in range(2)]
        rc = B.sbt("arc", [128, 2]); oap = B.sbt("oap", [128, 2, 64], BF16)
        amix = [B.sbt("amix%d" % i, [128, 128], BF16) for i in range(2)]
        it = 0
        for hp in range(4):
            kT = kTb[hp % 2]; qT = qTb[hp % 2]; mk = msk[hp % 2]
            B.dma(SP, kT.t[:], B.kTs.t.ap()[hp], [B.kTs], [kT])
            B.dma(SP, qT.t[:], B.qTs.t.ap()[hp], [B.qTs], [qT])
            B.dma(POOL, mk.t[:], B.amask.t.ap()[hp], [B.amask], [mk])
            mk4 = mk.t[:].rearrange("p (o h q) -> p o h q", o=17, h=2)
            jobs = []
            for f in range(NF):
                tq = th + f
                offs = [o for o in range(17) if tq - o >= tw0]
                groups = [offs[i:i + 2] for i in range(0, len(offs), 2)]
                for gi, g in enumerate(groups):
                    jobs.append((f, g, offs, gi == len(groups) - 1))

            def stageA(n):
                f, g, offs, last = jobs[n]
                tq = th + f
                psS = pb[1 + n % 2]
                pS = psS.t[:].rearrange("p (a b) -> p a b", a=4)
                for oi, o in enumerate(g):
                    kw = tq - o - tw0
                    for h2 in range(2):
                        lo = 64 * h2
                        B.mm(pS[:, oi * 2 + h2, :], kT.t[lo:lo + 64, kw * 128:(kw + 1) * 128], qT.t[lo:lo + 64, f * 128:(f + 1) * 128],
                             [kT, qT], [psS])

            def stageB(n):
                f, g, offs, last = jobs[n]
                psS = pb[1 + n % 2]
                pS = psS.t[:].rearrange("p (a b) -> p a b", a=4)
                E = Eb[n % 2]; P_ = Pm[n % 2]
                ns = 2 * len(g)
                B.act(E.t[:, 0:ns, :], pS[:, 0:ns, :], AF.Exp, [psS], [E], scale=0.125)
                B.tt(DVE, P_.t[:, 0:ns, :], E.t[:, 0:ns, :], mk4[:, g[0]:g[0] + len(g), :, :].rearrange("p o h q -> p (o h) q"), ALU.mult, [E, mk], [P_])

            def stageC(n):
                f, g, offs, last = jobs[n]
                tq = th + f
                P_ = Pm[n % 2]
                pab = 5 if f % 2 == 0 else 3
                for oi, o in enumerate(g):
                    kw = tq - o - tw0
                    for h2 in range(2):
                        h = 2 * hp + h2
                        pa_ = pb[pab + h2]
                        B.mm(pa_.t[:, 0:65], P_.t[:, oi * 2 + h2, :], vall.t[:, kw, h * 80:h * 80 + 65], [P_, vall], [pa_],
                             start=(o == offs[0]), stop=(o == offs[-1]))
                if not last:
                    return
                for h2 in range(2):
                    pa_ = pb[pab + h2]
                    B.ts(DVE, rc.t[:, h2:h2 + 1], pa_.t[:, 64:65], 1e-30, None, ALU.add, None, [pa_], [rc])
                B.P.op(DVE, lambda e: e.reciprocal(out=rc.t[:], in_=rc.t[:]), [rc.b], [rc.b])
                for h2 in range(2):
                    pa_ = pb[pab + h2]
                    B.tt(DVE, oap.t[:, h2, :], pa_.t[:, 0:64], rc.t[:, h2:h2 + 1].to_broadcast([128, 64]), ALU.mult, [pa_, rc], [oap])
                B.tr(B.psTb.t[:, 0, :], oap.t[:].rearrange("p a b -> p (a b)"), B.identb.t[:], [oap, B.identb], [B.psTb])
                ms_ = amix[f % 2]
                B.cp(DVE, ms_.t[:], B.psTb.t[:, 0, :], [B.psTb], [ms_])
                B.dma(SP, B.mixTs.t.ap()[hp, :, f * 128:(f + 1) * 128], ms_.t[:], [ms_], [B.mixTs])

            stageA(0)
            for n in range(len(jobs)):
                if n + 1 < len(jobs):
                    stageA(n + 1)
                stageB(n)
                stageC(n)

    def outproj_phase(self):
        B = self
        NF, th = B.NF, B.th
        pb = B.pb
        w_out = B.inp("w_out", [D, D])
        B.x1nTs = B.scratch("x1nTs", [8, 128, NF * 128], BF16)
        Wo = B.sbt("Wo", [128, 8, D], BF16)
        st = [B.sbt("wost%d" % i, [128, D]) for i in range(2)]
        for kc in range(8):
            B.dma(SP if kc % 2 == 0 else POOL, st[kc % 2].t[:], w_out.t.ap()[kc * 128:(kc + 1) * 128, :], [w_out], [st[kc % 2]])
            B.cp(ACT if kc % 2 == 0 else DVE, Wo.t[:, kc, :], st[kc % 2].t[:], [st[kc % 2]], [Wo])
        xt = [B.sbt("oxt%d" % i, [128, D]) for i in range(2)]
        x1 = [B.sbt("ox1%d" % i, [128, D]) for i in range(2)]
        sqj = B.sbt("osqj", [128, D]); ss = B.sbt("oss", [128, 1]); xb = B.sbt("oxb", [128, D], BF16)
        xnT = B.sbt("oxnT", [128, 8, 128], BF16)
        omix = [B.sbt("omix%d" % i, [128, 8, 128], BF16) for i in range(2)]
        for f in range(NF):
            t = th + f
            x = xt[f % 2]; y = x1[f % 2]
            B.dma(SP, x.t[:], B.xloc.t.ap()[t * 128:(t + 1) * 128, :], [B.xloc], [x])
            mx_ = omix[f % 2]
            B.dma(POOL, mx_.t[:], B.mixTs.t.ap()[:, :, f * 128:(f + 1) * 128].rearrange("a p n -> p a n"), [B.mixTs], [mx_])
            for half in range(2):
                ps = pb[1 + half]
                for kc in range(8):
                    B.mm(ps.t[:], mx_.t[:, kc, :], Wo.t[:, kc, half * 512:(half + 1) * 512], [mx_, Wo], [ps],
                         start=(kc == 0), stop=(kc == 7))
                B.tt(DVE, y.t[:, half * 512:(half + 1) * 512], x.t[:, half * 512:(half + 1) * 512], ps.t[:], ALU.add, [x, ps], [y])
            B.dma(SP, B.x1s.t.ap()[f * 128:(f + 1) * 128, :], y.t[:], [y], [B.x1s])
            B.rmsnorm_tile(y, sqj, ss, xb)
            for c in range(8):
                B.tr(B.psT.t[:, c, :], xb.t[:, c * 128:(c + 1) * 128], B.identb.t[:], [xb, B.identb], [B.psT])
            B.cp(DVE, xnT.t[:], B.psT.t[:], [B.psT], [xnT])
            B.dma(SP, B.x1nTs.t.ap()[:, :, f * 128:(f + 1) * 128].rearrange("a p n -> p a n"), xnT.t[:], [xnT], [B.x1nTs])
        if hasattr(B, "z"):
            z = B.z
            WoA = B.sbt("WoA", [64, 8, D], BF16); WoC = B.sbt("WoC", [64, 4, D], BF16)
            for h in range(12):
                s_ = st[h % 2]
                r0 = h * 64 if h < 8 else 768 + (h - 8) * 64
                B.dma(SP if h % 2 == 0 else POOL, s_.t[0:64, :], w_out.t.ap()[r0:r0 + 64, :], [w_out], [s_])
                dst = WoA.t[:, h, :] if h < 8 else WoC.t[:, h - 8, :]
                B.cp(ACT if h % 2 == 0 else DVE, dst, s_.t[0:64, :], [s_], [WoA if h < 8 else WoC])
            xs_ = B.sbt("oxs", [64, D]); sqs = B.sbt("osqs", [64, D], BF16); sss = B.sbt("osss", [64, 1]); xbs = B.sbt("oxbs", [64, D], BF16)
            B.dma(SP, xs_.t[:], B.xs.t.ap(), [B.xs], [xs_])
            for half in range(2):
                ps = pb[1 + half]
                hs = slice(half * 512, (half + 1) * 512)
                for h in range(8):
                    B.mm(ps.t[0:64, :], z["oaT"].t[:, h, :], WoA.t[:, h, hs], [z["oaT"], WoA], [ps], start=(h == 0), stop=False)
                for p_ in range(2):
                    B.mm(ps.t[0:64, :], z["obT"].t[:, p_, :], Wo.t[:, 4 + p_, hs], [z["obT"], Wo], [ps], start=False, stop=False)
                for h in range(4):
                    B.mm(ps.t[0:64, :], z["ocT"].t[:, h, :], WoC.t[:, h, hs], [z["ocT"], WoC], [ps], start=False, stop=(h == 3))
                B.tt(DVE, z["x1"].t[:, hs], xs_.t[:, hs], ps.t[0:64, :], ALU.add, [xs_, ps], [z["x1"]])
            B.rmsnorm_tile(z["x1"], sqs, sss, xbs)
            for c in range(8):
                B.tr(B.psT.t[:, c, 0:64], xbs.t[:, c * 128:(c + 1) * 128], B.identb.t[0:64, 0:64], [xbs, B.identb], [B.psT])
            B.cp(DVE, z["x1nT"].t[:], B.psT.t[:, :, 0:64], [B.psT], [z["x1nT"]])

    def ffn_phase(self):
        B = self
        NF, NOWN = B.NF, B.NOWN
        pb = B.pb
        g2 = B.inp("norm2_g", [D])
        w_up = B.inp("w_up", [D, 2 * D_FF])
        cfw = B.inp("conv_ffn_w", [3, 2 * D_FF])
        w_down = B.inp("w_down", [D_FF, D])
        g2c = B.sbt("g2c", [128, 8])
        B.dma(SP, g2c.t[:], g2.t.ap().rearrange("(k p) -> p k", p=128), [g2], [g2c], allow_slow_non_contiguous=True)
        cfc = B.sbt("cfc", [128, 44, 3])
        for k in range(3):
            B.dma(SP, cfc.t[:, :, k], cfw.t.ap()[k, :].rearrange("(c p) -> p c", p=128), [cfw], [cfc], allow_slow_non_contiguous=True)
        Wd = B.sbt("Wd", [128, 22, D], BF16)
        x1t = [B.sbt("fx1t%d" % i, [128, D]) for i in range(2)]
        yt = [B.sbt("fyt%d" % i, [128, D]) for i in range(2)]
        st = x1t
        for j in range(22):
            B.dma(SP if j % 2 == 0 else POOL, st[j % 2].t[:], w_down.t.ap()[j * 128:(j + 1) * 128, :], [w_down], [st[j % 2]])
            B.cp(ACT if j % 2 == 0 else DVE, Wd.t[:, j, :], st[j % 2].t[:], [st[j % 2]], [Wd])
        n0 = (NF + 2) // 3
        groups = [(0, min(n0, NF)), (min(n0, NF), min(2 * n0, NF)), (min(2 * n0, NF), NF)]
        NG = n0 * 128
        x1nT = B.sbt("fx1nT", [128, 8, NG], BF16)
        hT = B.sbt("fhT", [128, 22, NG], BF16)
        ug = [B.sbt("fug%d" % i, [128, NG + 2]) for i in range(2)]
        cg = [B.sbt("fcg%d" % i, [128, NG]) for i in range(2)]
        sg = B.sbt("fsg", [128, NG])
        wst = [[B.sbt("fwst%d_%d" % (i, k), [128, 8, 128]) for k in range(2)] for i in range(2)]
        wbf = [[B.sbt("fwbf%d_%d" % (i, k), [128, 8, 128], BF16) for k in range(2)] for i in range(2)]
        hsave = B.sbt("fhsave", [128, 44, 2])
        B.memset(POOL, hsave.t[:], 0.0, [hsave])
        SAMP = hasattr(B, "z")
        if SAMP:
            z = B.z
            fst = B.sbt("s_fst", [128, 44, 16, 2])
            fstg = [B.sbt("s_fstg%d" % i, [32, 512]) for i in range(2)]
            sfv = B.sfconv.t.ap().rearrange("b r n -> (b r) n")
            for c4 in range(11):
                sg_ = fstg[c4 % 2]
                B.dma(SP, sg_.t[:], sfv[:, c4 * 512:(c4 + 1) * 512], [B.sfconv], [sg_])
                ps = pb[5 + c4 % 2]
                for i in range(4):
                    B.mm(ps.t[:, i * 32:(i + 1) * 32], sg_.t[:, i * 128:(i + 1) * 128], B.identf.t[0:32, 0:32], [sg_, B.identf], [ps])
                B.cp(DVE, fst.t[:, c4 * 4:(c4 + 1) * 4, :, :], ps.t[:, 0:128].rearrange("p (c b r) -> p c b r", c=4, b=16), [ps], [fst])
            us = [B.sbt("s_us%d" % i, [128, 16, 6]) for i in range(2)]
            cs_ = [B.sbt("s_cs%d" % i, [128, 16, 4]) for i in range(2)]
            sgs = B.sbt("s_sgs", [128, 64])
            hTs = B.sbt("s_hTs", [128, 22, 64], BF16)
            usel = B.sbt("s_usel", [128, 16, 2])
            tks = [B.sbt("s_tks%d" % i, [32, 128]) for i in range(2)]
            ofv = B.o_fcs.t.ap().rearrange("b r n -> (b r) n")
        it = 0
        for gi, (f0, f1) in enumerate(groups):
            ntok = (f1 - f0) * 128
            if ntok == 0:
                continue
            B.dma(SP, x1nT.t[:, :, 0:ntok], B.x1nTs.t.ap()[:, :, f0 * 128:f1 * 128].rearrange("a p n -> p a n"), [B.x1nTs], [x1nT])
            chunks = [(c0, min(512, ntok - c0)) for c0 in range(0, ntok, 512)]
            for j in range(22):
                for k in range(2):
                    col = k * D_FF + j * 128
                    ws = wst[it % 2][k]; wb_ = wbf[it % 2][k]
                    B.dma(SP if k == 0 else POOL, ws.t[:], w_up.t.ap()[:, col:col + 128].rearrange("(a p) n -> p a n", p=128), [w_up], [ws])
                    B.tt(DVE, wb_.t[:], ws.t[:], bc(g2c.t[:], [128, 8, 128], 2), ALU.mult, [ws, g2c], [wb_])
                    u = ug[k]
                    B.cp(POOL, u.t[:, 0:2], hsave.t[:, k * 22 + j, :], [hsave], [u])
                    for ci, (c0, cn) in enumerate(chunks):
                        ps = pb[1 + (ci + k) % 2]
                        for kc in range(8):
                            B.mm(ps.t[:, 0:cn], wb_.t[:, kc, :], x1nT.t[:, kc, c0:c0 + cn], [wb_, x1nT], [ps], start=(kc == 0), stop=(kc == 7))
                        B.cp(ACT, u.t[:, 2 + c0:2 + c0 + cn], ps.t[:, 0:cn], [ps], [u])
                    if gi == 0:
                        B.ts(DVE, u.t[:, 2 + 126:2 + 128], u.t[:, 2 + 126:2 + 128], B.flagsb.t[:, 0:1], None, ALU.mult, None, [u, B.flagsb], [u])
                    B.cp(POOL, hsave.t[:, k * 22 + j, :], u.t[:, ntok:ntok + 2], [u], [hsave])
                    c_ = cg[k]
                    cc = k * 22 + j
                    B.ts(DVE, c_.t[:, 0:ntok], u.t[:, 0:ntok], cfc.t[:, cc, 0:1], None, ALU.mult, None, [u, cfc], [c_])
                    B.stt(DVE, c_.t[:, 0:ntok], u.t[:, 1:ntok + 1], cfc.t[:, cc, 1:2], c_.t[:, 0:ntok], ALU.mult, ALU.add, [u, cfc, c_], [c_])
                    B.stt(DVE, c_.t[:, 0:ntok], u.t[:, 2:ntok + 2], cfc.t[:, cc, 2:3], c_.t[:, 0:ntok], ALU.mult, ALU.add, [u, cfc, c_], [c_])
                    if SAMP and gi == 0:
                        ps = pb[5 + k]
                        for kc in range(8):
                            B.mm(ps.t[:, 0:64], wb_.t[:, kc, :], z["x1nT"].t[:, kc, :], [wb_, z["x1nT"]], [ps], start=(kc == 0), stop=(kc == 7))
                        us_ = us[k]
                        B.cp(POOL, us_.t[:, :, 0:2], fst.t[:, cc, :, :], [fst], [us_])
                        B.cp(ACT, us_.t[:, :, 2:6], ps.t[:, 0:64].rearrange("p (b s) -> p b s", b=16), [ps], [us_])
                        B.cp(POOL, usel.t[:], us_.t[:, :, 4:6], [us_], [usel])
                        tk_ = tks[cc % 2]
                        pq = pb[3 + cc % 2]
                        B.mm(pq.t[0:32, 0:128], usel.t[:].rearrange("p b r -> p (b r)"), B.identf.t[:], [usel, B.identf], [pq])
                        B.cp(DVE, tk_.t[:], pq.t[0:32, 0:128], [pq], [tk_])
                        B.dma(SP, ofv[:, k * D_FF + j * 128:k * D_FF + (j + 1) * 128], tk_.t[:], [tk_], [B.o_fcs])
                        cq = cs_[k]
                        B.ts(DVE, cq.t[:], us_.t[:, :, 0:4], cfc.t[:, cc, 0:1], None, ALU.mult, None, [us_, cfc], [cq])
                        B.stt(DVE, cq.t[:], us_.t[:, :, 1:5], cfc.t[:, cc, 1:2], cq.t[:], ALU.mult, ALU.add, [us_, cfc, cq], [cq])
                        B.stt(DVE, cq.t[:], us_.t[:, :, 2:6], cfc.t[:, cc, 2:3], cq.t[:], ALU.mult, ALU.add, [us_, cfc, cq], [cq])
                if SAMP and gi == 0:
                    B.act(sgs.t[:], cs_[0].t[:].rearrange("p b s -> p (b s)"), AF.Silu, [cs_[0]], [sgs])
                    B.tt(DVE, hTs.t[:, j, :], sgs.t[:], cs_[1].t[:].rearrange("p b s -> p (b s)"), ALU.mult, [sgs, cs_[1]], [hTs])
                it += 1
                B.act(sg.t[:, 0:ntok], cg[0].t[:, 0:ntok], AF.Silu, [cg[0]], [sg])
                B.tt(DVE, hT.t[:, j, 0:ntok], sg.t[:, 0:ntok], cg[1].t[:, 0:ntok], ALU.mult, [sg, cg[1]], [hT])
            for f in range(f0, f1):
                if f == 0:
                    continue
                lt = f - f0
                xx = x1t[f % 2]; yy = yt[f % 2]
                B.dma(POOL, xx.t[:], B.x1s.t.ap()[f * 128:(f + 1) * 128, :], [B.x1s], [xx])
                for half in range(2):
                    ps = pb[3 + half]
                    for j in range(22):
                        B.mm(ps.t[:], hT.t[:, j, lt * 128:(lt + 1) * 128], Wd.t[:, j, half * 512:(half + 1) * 512], [hT, Wd], [ps],
                             start=(j == 0), stop=(j == 21))
                    B.tt(DVE, yy.t[:, half * 512:(half + 1) * 512], xx.t[:, half * 512:(half + 1) * 512], ps.t[:], ALU.add, [xx, ps], [yy])
                B.dma(SP, B.y_o.t.ap()[(f - 1) * 128:f * 128, :], yy.t[:], [yy], [B.y_o])
        if SAMP:
            ys_ = B.sbt("s_ys", [64, D])
            for half in range(2):
                ps = pb[3 + half]
                hs = slice(half * 512, (half + 1) * 512)
                for j in range(22):
                    B.mm(ps.t[0:64, :], hTs.t[:, j, :], Wd.t[:, j, hs], [hTs, Wd], [ps], start=(j == 0), stop=(j == 21))
                B.tt(DVE, ys_.t[:, hs], z["x1"].t[:, hs], ps.t[0:64, :], ALU.add, [z["x1"], ps], [ys_])
            B.dma(SP, B.o_ys.t.ap(), ys_.t[:], [ys_], [B.o_ys])
        fct = [B.sbt("fct%d" % i, [2, 512]) for i in range(2)]
        for c4 in range(11):
            ps = pb[5 + c4 % 2]
            for i in range(4):
                c = c4 * 4 + i
                B.mm(ps.t[0:2, i * 128:(i + 1) * 128], hsave.t[:, c, :], B.identf.t[:], [hsave, B.identf], [ps])
            B.cp(DVE, fct[c4 % 2].t[:], ps.t[0:2, :], [ps], [fct[c4 % 2]])
            B.dma(SP, B.fcv.t.ap()[:, c4 * 512:(c4 + 1) * 512], fct[c4 % 2].t[:], [fct[c4 % 2]], [B.fcv])

    def finish(self):
        B = self
        B.P.wait_all(SP, B.outbufs)
        B.P.emit()
        B.P.close()

    def build_prompt(self, NOWN, sample=False):
        B = self
        B.setup_consts()
        m0 = B.P.mark()
        B.prompt_setup(NOWN, sample=sample)
        m1 = B.P.mark()
        mw = B.setup_weights_in()
        if sample:
            B.sample_inproj()
        B.P.barrier(); B.P.emit(); B.P.release(mw)
        if 'memkv' not in B.skip:
            mm_ = B.mem_kv()
            B.P.barrier(); B.P.emit(); B.P.release(mm_)
        B.gdn_consts()
        B.prompt_loop()
        if getattr(B, "nphase", 9) < 2:
            return
        B.P.barrier(); B.P.emit(); B.P.release(m1)
        B.attention_phase()
        if getattr(B, "nphase", 9) < 3:
            return
        if sample:
            B.P.barrier(); B.P.emit(); B.P.release(m1)
            B.sample_phase()
        B.P.barrier(); B.P.emit(); B.P.release(m1)
        B.outproj_phase()
        if getattr(B, "nphase", 9) < 4:
            return
        B.P.barrier(); B.P.emit(); B.P.release(B.m_mix)
        B.ffn_phase()


_CACHE = {}


def _build():
    if "B" in _CACHE:
        return _CACHE["B"]
    B = Builder(NT=128)
    B.build_prompt(16, sample=True)
    B.finish()
    _CACHE["B"] = B
    return B


def kernel(x_prompt, x_sample, cache_win_k, cache_win_v, state_gdn, state_gdn_conv, state_ffn_conv,
           cache_mem_k, cache_mem_v, mem_prompt, norm1_g, w_in, q_norm_a, k_norm_a, conv_b_w, a_log_b,
           dt_bias_b, out_norm_b, mem_norm_g, w_mem_kv, q_norm_c, k_norm_c, w_out, norm2_g, w_up,
           conv_ffn_w, w_down):
    f = lambda a: np.ascontiguousarray(np.asarray(a, dtype=np.float32))
    B = _build()
    xp = f(x_prompt)[0]
    rep = lambda v, n: np.ascontiguousarray(np.broadcast_to(f(v).reshape(1, -1), (128, n)))
    col = lambda v: np.ascontiguousarray(np.tile(f(v).reshape(-1), 2).reshape(128, 1))
    am = alibi_masks().reshape(4, 128, -1)
    shared = {
        "amask": am, "w_in": f(w_in)[0], "norm1_g": f(norm1_g)[0], "conv_b_w": f(conv_b_w)[0],
        "a_log_b": rep(a_log_b, 4), "dt_bias_b": rep(dt_bias_b, 4), "out_norm_b": rep(out_norm_b, 64),
        "mem_prompt": f(mem_prompt)[0], "w_mem_kv": f(w_mem_kv)[0], "mem_norm_g": f(mem_norm_g)[0],
        "w_out": f(w_out)[0], "norm2_g": f(norm2_g)[0], "w_up": f(w_up)[0], "conv_ffn_w": f(conv_ffn_w)[0], "w_down": f(w_down)[0],
    }
    for nm, v in (("q_norm_a", q_norm_a), ("k_norm_a", k_norm_a), ("q_norm_c", q_norm_c), ("k_norm_c", k_norm_c)):
        shared[nm] = rep(v, 64)
        shared[nm + "_col"] = col(v)
    for k, v in B.cnp.items():
        shared["c_" + k] = v
    for k, v in B.scn.items():
        shared["c_" + k] = v
    xs_all = f(x_sample)
    cwk_all = np.asarray(cache_win_k, dtype=np.float32)[0]
    cwv_all = np.asarray(cache_win_v, dtype=np.float32)[0]
    sg_all = f(state_gdn)[0]; sgc_all = f(state_gdn_conv)[0]; sfc_all = f(state_ffn_conv)[0]
    cmk_all = f(cache_mem_k)[0]; cmv_all = f(cache_mem_v)[0]
    in_maps = []
    for c in range(NCORES):
        npad = (NCORES - 1 - c) * OWN
        xloc = np.zeros((SEQ, D), np.float32)
        xloc[npad:] = xp[:(c + 1) * OWN]
        valid = np.zeros((SEQ,), np.float32)
        valid[npad:] = 1.0
        im = dict(shared)
        im["xloc"] = xloc
        im["valid"] = np.ascontiguousarray(valid.reshape(128, 128).T)
        im["flag"] = np.full((128, 1), 0.0 if c == 0 else 1.0, np.float32)
        bs = slice(16 * c, 16 * c + 16)
        im["xs"] = np.ascontiguousarray(xs_all[bs].reshape(64, D))
        im["cwk"] = np.ascontiguousarray(cwk_all[bs].reshape(16, 2048, 512))
        im["cwv"] = np.ascontiguousarray(cwv_all[bs].reshape(16, 2048, 512))
        im["sgdn"] = np.ascontiguousarray(sg_all[bs]); im["sgconv"] = np.ascontiguousarray(sgc_all[bs]); im["sfconv"] = np.ascontiguousarray(sfc_all[bs])
        im["cmk"] = np.ascontiguousarray(cmk_all[bs].reshape(16, 256, 256)); im["cmv"] = np.ascontiguousarray(cmv_all[bs].reshape(16, 256, 256))
        im = {k: v for k, v in im.items() if k in B.ins}
        in_maps.append(im)
    res = run_bass_kernel_spmd(B.nc, in_maps, core_ids=list(range(NCORES)))
    r = res.results
    z = lambda *s: np.zeros(s, np.float32)
    g = lambda c, n: np.asarray(r[c][n], np.float32)
    y_p = np.concatenate([g(c, "y_p") for c in range(NCORES)], 0).reshape(1, SEQ, D)
    win_k_p = g(7, "win_k_p").reshape(1, 1, 2048, 8, 64)
    win_v_p = g(7, "win_v_p").reshape(1, 1, 2048, 8, 64)
    gst_p = g(7, "gdn_state_p").reshape(1, 1, 4, 64, 64)
    gconv_p = g(7, "gdn_conv_p").reshape(1, 1, 3, 768)
    fconv_p = g(7, "ffn_conv_p").reshape(1, 1, 2, 2 * D_FF)
    mem_k_p = g(0, "mem_k_p").reshape(1, 1, 256, 4, 64)
    mem_v_p = g(0, "mem_v_p").reshape(1, 1, 256, 4, 64)
    cat = lambda n: np.concatenate([g(c, n) for c in range(NCORES)], 0)
    y_s = cat("y_s").reshape(128, 4, D)
    win_k_s = cat("win_k_s").reshape(1, 128, 4, 8, 64)
    win_v_s = cat("win_v_s").reshape(1, 128, 4, 8, 64)
    gst_s = cat("gdn_state_s").reshape(1, 128, 4, 64, 64)
    gconv_s = cat("gdn_conv_s").reshape(1, 128, 3, 768)
    fconv_s = cat("ffn_conv_s").reshape(1, 128, 2, 2 * D_FF)
    return (y_p, y_s, win_k_p, win_v_p, win_k_s, win_v_s, gst_p, gst_s, gconv_p, gconv_s, fconv_p, fconv_s, mem_k_p, mem_v_p)


def sample_consts():
    slopes = 2.0 ** (-8.0 * np.arange(1, 9) / 8.0)
    def cmult(d):
        return ((d >= 0) & (d <= 128)).astype(np.float64) + ((d >= 0) & (d <= 512) & (d % 4 == 0)) + ((d >= 0) & (d <= 2048) & (d % 16 == 0))
    p = np.arange(128)
    rows = np.zeros((7, 128), np.int64)
    for j in range(4):
        rows[j] = 1536 + 128 * j + p
    for j in range(4, 7):
        rows[j] = 16 * (32 * (j - 4) + p % 32) + p // 32
    smask = np.zeros((128, 7, 8, 4), np.float64)
    for j in range(7):
        for s in range(4):
            d = 2048 + s - rows[j]
            c = cmult(d)
            for h in range(8):
                smask[:, j, h, s] = c * np.exp(-slopes[h] * np.maximum(d, 0))
    snew = np.zeros((128, 16, 8, 4), np.float64)
    for b in range(16):
        for sp in range(4):
            for s in range(4):
                d = s - sp
                if d >= 0:
                    snew[4 * b + sp, b, :, s] = cmult(np.array(d)) * np.exp(-slopes * d)
    grp = np.where(p < 64, p // 4, 100 + p)
    same = grp[:, None] == grp[None, :]
    j = p[:, None]; i = p[None, :]
    c = {}
    c["s_triu"] = (same & (j <= i)).astype(np.float32)
    c["s_trils"] = (same & (j > i)).astype(np.float32)
    c["s_negc"] = np.where(same & (i >= j), 0.0, NEGV).astype(np.float32)
    c["s_strict"] = (same & (i > j)).astype(np.float32)
    bm = np.zeros((128, 16, 64), np.float32)
    for b in range(16):
        bm[:, b, 4 * b:4 * b + 4] = 1.0
    rm = np.zeros((128, 16), np.float32)
    for b in range(16):
        rm[4 * b:4 * b + 4, b] = 1.0
    c["s_smask"] = smask.reshape(128, -1).astype(np.float32)
    c["s_snew"] = snew.reshape(128, -1).astype(np.float32)
    c["s_bm"] = bm.reshape(128, -1)
    c["s_rm"] = rm
    return c


class Builder(Builder):
    def sample_setup(self):
        B = self
        B.scn = sample_consts()
        for k, v in B.scn.items():
            B.inp("c_" + k, v.shape)
        B.xs = B.inp("xs", [64, D])
        B.cwk = B.inp("cwk", [16, 2048, 512]); B.cwv = B.inp("cwv", [16, 2048, 512])
        B.sgdn = B.inp("sgdn", [16, 4, 64, 64]); B.sgconv = B.inp("sgconv", [16, 3, 768]); B.sfconv = B.inp("sfconv", [16, 2, 2 * D_FF])
        B.cmk = B.inp("cmk", [16, 256, 256]); B.cmv = B.inp("cmv", [16, 256, 256])
        B.o_wks = B.outp("win_k_s", [64, 512]); B.o_wvs = B.outp("win_v_s", [64, 512])
        B.o_gcs = B.outp("gdn_conv_s", [16, 3, 768]); B.o_gss = B.outp("gdn_state_s", [16, 4, 64, 64])
        B.o_fcs = B.outp("ffn_conv_s", [16, 2, 2 * D_FF]); B.o_ys = B.outp("y_s", [64, D])
        z = {}
        def a(name, shape, dt=F32):
            z[name] = B.sbt("z_" + name, shape, dt)
        a("qaT", [128, 4, 64], BF16); a("kaT", [128, 4, 64], BF16); a("qcT", [128, 2, 64], BF16)
        a("qkvT", [128, 6, 64]); a("ab", [128, 8]); a("gate", [128, 256]); a("vaug", [64, 8, 80], BF16)
        a("oaT", [64, 8, 64], BF16); a("obT", [128, 2, 64], BF16); a("ocT", [64, 4, 64], BF16)
        a("x1nT", [128, 8, 64], BF16); a("x1", [64, D])
        B.z = z

    def sample_inproj(self):
        B = self
        pb = B.pb
        z = B.z
        x = B.sbt("sx", [64, D]); sqj = B.sbt("ssqj", [64, D], BF16); ss = B.sbt("sss", [64, 1]); xb = B.sbt("sxb", [64, D], BF16)
        xnT = B.sbt("sxnT", [128, 8, 64], BF16)
        tsq = B.sbt("stsq", [128, 512], BF16); trn = B.sbt("strn", [128, 512])
        tok = B.sbt("stok", [64, 768]); tok2 = B.sbt("stok2", [64, 512]); kss = B.sbt("skss", [64, 8])
        B.dma(SP, x.t[:], B.xs.t.ap(), [B.xs], [x])
        B.rmsnorm_tile(x, sqj, ss, xb)
        for c in range(8):
            B.tr(B.psT.t[:, c, 0:64], xb.t[:, c * 128:(c + 1) * 128], B.identb.t[0:64, 0:64], [xb, B.identb], [B.psT])
        B.cp(DVE, xnT.t[:], B.psT.t[:, :, 0:64], [B.psT], [xnT])

        def fm(col0, npairs, dst_ps):
            for p_ in range(npairs):
                for kc in range(8):
                    B.mm(dst_ps.t[:, p_ * 64:(p_ + 1) * 64], B.Wb.t[:, kc, col0 + p_ * 128:col0 + (p_ + 1) * 128], xnT.t[:, kc, :],
                         [B.Wb, xnT], [dst_ps], start=(kc == 0), stop=(kc == 7))
        fm(C_QA, 4, pb[3])
        B.headnorm_fm(pb[3], 4, 64, B.gw["q_norm_a"][1].t[:], z["qaT"].t[:].rearrange("p a b -> p (a b)"), z["qaT"], pb[4], tsq, trn)
        fm(C_KA, 4, pb[3])
        B.headnorm_fm(pb[3], 4, 64, B.gw["k_norm_a"][1].t[:], z["kaT"].t[:].rearrange("p a b -> p (a b)"), z["kaT"], pb[4], tsq, trn)
        fm(C_QC, 2, pb[3])
        B.headnorm_fm(pb[3], 2, 64, B.gw["q_norm_c"][1].t[:], z["qcT"].t[:].rearrange("p a b -> p (a b)"), z["qcT"], pb[4], tsq, trn)
        fm(C_QKV, 6, pb[3])
        B.cp(DVE, z["qkvT"].t[:].rearrange("p a b -> p (a b)"), pb[3].t[:, 0:384], [pb[3]], [z["qkvT"]])

        def tm(col0, ncol, dst_ps, c_off=0):
            for kc in range(8):
                B.mm(dst_ps.t[0:64, c_off:c_off + ncol], xnT.t[:, kc, :], B.Wb.t[:, kc, col0:col0 + ncol], [B.Wb, xnT], [dst_ps], start=(kc == 0), stop=(kc == 7))
        tm(C_KA, 512, pb[5])
        B.tok_headnorm(pb[5], 8, B.gw["k_norm_a"][0], tok2, kss, np_=64)
        B.dma(SP, B.o_wks.t.ap(), tok2.t[:], [tok2], [B.o_wks])
        tm(C_VA, 512, pb[6])
        tokv = B.sbt("stokv", [64, 512])
        B.cp(ACT, tokv.t[:], pb[6].t[0:64, :], [pb[6]], [tokv])
        B.dma(SP, B.o_wvs.t.ap(), tokv.t[:], [tokv], [B.o_wvs])
        B.memset(POOL, z["vaug"].t[:], 0.0, [z["vaug"]])
        B.memset(POOL, z["vaug"].t[:, :, 64:65], 1.0, [z["vaug"]])
        B.cp(DVE, z["vaug"].t[:, :, 0:64], tokv.t[:].rearrange("p (a b) -> p a b", a=8), [tokv], [z["vaug"]])
        tm(C_QKV, 512, pb[5])
        tm(C_QKV + 512, 256, pb[6])
        B.cp(ACT, tok.t[:, 0:512], pb[5].t[0:64, :], [pb[5]], [tok])
        B.cp(ACT, tok.t[:, 512:768], pb[6].t[0:64, 0:256], [pb[6]], [tok])
        for b in range(16):
            B.dma(SP if b % 2 == 0 else POOL, B.o_gcs.t.ap()[b, :, :], tok.t[4 * b + 1:4 * b + 4, :], [tok], [B.o_gcs])
        B.memset(POOL, z["ab"].t[:], 0.0, [z["ab"]])
        B.memset(POOL, z["gate"].t[:], 0.0, [z["gate"]])
        tm(C_A, 8, pb[5])
        B.cp(DVE, z["ab"].t[0:64, :], pb[5].t[0:64, 0:8], [pb[5]], [z["ab"]])
        tm(C_GATE, 256, pb[6])
        B.cp(DVE, z["gate"].t[0:64, :], pb[6].t[0:64, 0:256], [pb[6]], [z["gate"]])

    def sample_phase(self):
        B = self
        pb = B.pb
        z = B.z
        sc = {}
        for k in ["s_triu", "s_trils", "s_negc", "s_strict", "s_rm"]:
            shp = list(B.scn[k].shape)
            sc[k] = B.sbt("k_" + k, shp)
            B.dma(SP, sc[k].t[:], B.ins["c_" + k].t.ap(), [B.ins["c_" + k]], [sc[k]])
        smask = B.sbt("k_smask", [128, 7 * 32]); B.dma(SP, smask.t[:], B.ins["c_s_smask"].t.ap(), [B.ins["c_s_smask"]], [smask])
        snew = B.sbt("k_snew", [128, 16 * 32]); B.dma(SP, snew.t[:], B.ins["c_s_snew"].t.ap(), [B.ins["c_s_snew"]], [snew])
        bm = B.sbt("k_bm", [128, 16 * 64], BF16); B.dma(POOL, bm.t[:], B.ins["c_s_bm"].t.ap(), [B.ins["c_s_bm"]], [bm])
        kf = [B.sbt("a_kf%d" % i, [128, 512]) for i in range(2)]
        kb = [B.sbt("a_kb%d" % i, [128, 512], BF16) for i in range(2)]
        vf = [B.sbt("a_vf%d" % i, [128, 512]) for i in range(2)]
        kT = B.sbt("a_kT", [128, 4, 7 * 128], BF16)
        va = B.sbt("a_va", [128, 7, 8, 80], BF16)
        E = B.sbt("a_E", [128, 7 * 32]); Pb = B.sbt("a_P", [128, 7 * 32], BF16)
        En = B.sbt("a_En", [64, 32]); Pn = B.sbt("a_Pn", [64, 32], BF16)
        accs = B.sbt("a_acc", [65, 8, 64])
        B.memset(POOL, va.t[:], 0.0, [va])
        B.memset(POOL, va.t[:, :, :, 64:65], 1.0, [va])
        pacc = pb[6]
        pa3 = pacc.t[0:65, :].rearrange("p (h t) -> p h t", h=8)
        it = 0
        for b in range(16):
            for j in range(7):
                k_ = kf[it % 2]; v_ = vf[it % 2]; kb_ = kb[it % 2]
                it += 1
                if j < 4:
                    B.dma(SP, k_.t[:], B.cwk.t.ap()[b, 1536 + 128 * j:1536 + 128 * (j + 1), :], [B.cwk], [k_])
                    B.dma(POOL, v_.t[:], B.cwv.t.ap()[b, 1536 + 128 * j:1536 + 128 * (j + 1), :], [B.cwv], [v_])
                else:
                    m0 = 32 * (j - 4)
                    for r_ in range(4):
                        B.dma(SP, k_.t[32 * r_:32 * r_ + 32, :], B.cwk.t.ap()[b, 16 * m0 + r_:16 * (m0 + 32):16, :], [B.cwk], [k_])
                        B.dma(POOL, v_.t[32 * r_:32 * r_ + 32, :], B.cwv.t.ap()[b, 16 * m0 + r_:16 * (m0 + 32):16, :], [B.cwv], [v_])
                B.cp(ACT, kb_.t[:], k_.t[:], [k_], [kb_])
                B.cp(POOL, va.t[:, j, :, 0:64], v_.t[:].rearrange("p (a b) -> p a b", a=8), [v_], [va])
                for p_ in range(4):
                    B.tr(B.psTb.t[:, p_, :], kb_.t[:, p_ * 128:(p_ + 1) * 128], B.identb.t[:], [kb_, B.identb], [B.psTb])
                B.cp(DVE, kT.t[:, :, j * 128:(j + 1) * 128], B.psTb.t[:, 0:4, :], [B.psTb], [kT])
            pS = pb[1 + b % 2]
            for j in range(7):
                for h in range(8):
                    lo = 64 * (h % 2)
                    B.mm(pS.t[:, j * 32 + h * 4:j * 32 + h * 4 + 4], kT.t[lo:lo + 64, h // 2, j * 128:(j + 1) * 128],
                         z["qaT"].t[lo:lo + 64, h // 2, 4 * b:4 * b + 4], [kT, z["qaT"]], [pS])
            B.act(E.t[:], pS.t[:, 0:224], AF.Exp, [pS], [E], scale=0.125)
            B.tt(DVE, Pb.t[:], E.t[:], smask.t[:], ALU.mult, [E, smask], [Pb])
            pN = pb[3 + b % 2]
            for h in range(8):
                lo = 64 * (h % 2)
                B.mm(pN.t[0:64, h * 4:h * 4 + 4], z["kaT"].t[lo:lo + 64, h // 2, :], z["qaT"].t[lo:lo + 64, h // 2, 4 * b:4 * b + 4],
                     [z["kaT"], z["qaT"]], [pN])
            B.act(En.t[:], pN.t[0:64, 0:32], AF.Exp, [pN], [En], scale=0.125)
            B.tt(DVE, Pn.t[:], En.t[:], snew.t[0:64, b * 32:(b + 1) * 32], ALU.mult, [En, snew], [Pn])
            for h in range(8):
                for j in range(7):
                    B.mm(pa3[:, h, 4 * b:4 * b + 4], va.t[:, j, h, 0:65], Pb.t[:, j * 32 + h * 4:j * 32 + h * 4 + 4], [va, Pb], [pacc],
                         start=(j == 0), stop=False)
                B.mm(pa3[:, h, 4 * b:4 * b + 4], z["vaug"].t[:, h, 0:65], Pn.t[:, h * 4:h * 4 + 4], [z["vaug"], Pn], [pacc], start=False, stop=True)
        B.cp(DVE, accs.t[:], pa3, [pacc], [accs])
        B.mm(pb[5].t[0:64, :], B.onesf.t[64:65, 0:64], accs.t[64:65, :, :].rearrange("p h t -> p (h t)"), [B.onesf, accs], [pb[5]])
        rec = B.sbt("a_rec", [64, 512])
        B.P.op(DVE, lambda e: e.reciprocal(out=rec.t[:], in_=pb[5].t[0:64, :]), [pb[5].b], [rec.b])
        B.tt(DVE, z["oaT"].t[:].rearrange("p h t -> p (h t)"), accs.t[0:64, :, :].rearrange("p h t -> p (h t)"), rec.t[:], ALU.mult, [accs, rec], [z["oaT"]])
        mf = [B.sbt("c_mf%d" % i, [128, 256]) for i in range(2)]
        mb = [B.sbt("c_mb%d" % i, [128, 256], BF16) for i in range(2)]
        mvf = [B.sbt("c_mvf%d" % i, [128, 256]) for i in range(2)]
        mkT = B.sbt("c_mkT", [128, 2, 256], BF16)
        mva = B.sbt("c_mva", [128, 2, 4, 80], BF16)
        Ec = B.sbt("c_E", [128, 32], BF16)
        accc = B.sbt("c_acc", [65, 4, 64])
        B.memset(POOL, mva.t[:], 0.0, [mva])
        B.memset(POOL, mva.t[:, :, :, 64:65], 1.0, [mva])
        pacc = pb[6]
        pc3 = pacc.t[0:65, 0:256].rearrange("p (h t) -> p h t", h=4)
        it = 0
        for b in range(16):
            for mt in range(2):
                f_ = mf[it % 2]; b_ = mb[it % 2]; v_ = mvf[it % 2]
                it += 1
                B.dma(SP, f_.t[:], B.cmk.t.ap()[b, mt * 128:(mt + 1) * 128, :], [B.cmk], [f_])
                B.dma(POOL, v_.t[:], B.cmv.t.ap()[b, mt * 128:(mt + 1) * 128, :], [B.cmv], [v_])
                B.cp(ACT, b_.t[:], f_.t[:], [f_], [b_])
                B.cp(POOL, mva.t[:, mt, :, 0:64], v_.t[:].rearrange("p (a b) -> p a b", a=4), [v_], [mva])
                for p_ in range(2):
                    B.tr(B.psTb.t[:, 4 + p_, :], b_.t[:, p_ * 128:(p_ + 1) * 128], B.identb.t[:], [b_, B.identb], [B.psTb])
                B.cp(DVE, mkT.t[:, :, mt * 128:(mt + 1) * 128], B.psTb.t[:, 4:6, :], [B.psTb], [mkT])
            pS = pb[1 + b % 2]
            for mt in range(2):
                for h in range(4):
                    lo = 64 * (h % 2)
                    B.mm(pS.t[:, mt * 16 + h * 4:mt * 16 + h * 4 + 4], mkT.t[lo:lo + 64, h // 2, mt * 128:(mt + 1) * 128],
                         z["qcT"].t[lo:lo + 64, h // 2, 4 * b:4 * b + 4], [mkT, z["qcT"]], [pS])
            B.act(Ec.t[:], pS.t[:, 0:32], AF.Exp, [pS], [Ec], scale=0.125)
            for h in range(4):
                for mt in range(2):
                    B.mm(pc3[:, h, 4 * b:4 * b + 4], mva.t[:, mt, h, 0:65], Ec.t[:, mt * 16 + h * 4:mt * 16 + h * 4 + 4], [mva, Ec], [pacc],
                         start=(mt == 0), stop=(mt == 1))
        B.cp(DVE, accc.t[:], pc3, [pacc], [accc])
        B.mm(pb[5].t[0:64, 0:256], B.onesf.t[64:65, 0:64], accc.t[64:65, :, :].rearrange("p h t -> p (h t)"), [B.onesf, accc], [pb[5]])
        B.P.op(DVE, lambda e: e.reciprocal(out=rec.t[:, 0:256], in_=pb[5].t[0:64, 0:256]), [pb[5].b], [rec.b])
        B.tt(DVE, z["ocT"].t[:].rearrange("p h t -> p (h t)"), accc.t[0:64, :, :].rearrange("p h t -> p (h t)"), rec.t[:, 0:256], ALU.mult, [accc, rec], [z["ocT"]])
        B.gdn_consts_s()
        B.gdn_alloc("h_")
        B.gsets = {0: B.gs}
        s = B.gs
        B.gm = {"triu": sc["s_triu"], "trils": sc["s_trils"], "negc": sc["s_negc"], "strictf": sc["s_strict"]}
        stf = B.sbt("h_stf", [48, 768])
        qx = B.sbt("h_qx", [128, 6, 16, 7])
        Ss = B.sbt("h_Ss", [128, 16, 2, 64]); Ssb = B.sbt("h_Ssb", [128, 16, 2, 64], BF16)
        wTm = B.sbt("h_wTm", [128, 2, 16, 64], BF16); qTm = B.sbt("h_qTm", [128, 2, 16, 64], BF16)
        kdm = B.sbt("h_kdm", [128, 16, 256], BF16)
        gmk = B.sbt("h_gmk", [128, 16, 4]); egls = B.sbt("h_egls", [128, 16, 2])
        for p_ in range(2):
            B.dma(SP, Ss.t[:, :, p_, :], B.sgdn.t.ap()[:, 2 * p_:2 * p_ + 2, :, :].rearrange("b h k v -> (h k) b v"), [B.sgdn], [Ss])
        B.cp(ACT, Ssb.t[:], Ss.t[:], [Ss], [Ssb])
        B.dma(SP, stf.t[:], B.sgconv.t.ap().rearrange("b r n -> (b r) n"), [B.sgconv], [stf])

        def conv_fn():
            for c in range(6):
                B.mm(pb[3].t[:, c * 48:(c + 1) * 48], stf.t[:, c * 128:(c + 1) * 128], B.identf.t[0:48, 0:48], [stf, B.identf], [pb[3]])
            B.cp(DVE, qx.t[:, :, :, 0:3], pb[3].t[:, 0:288].rearrange("p (c b r) -> p c b r", c=6, b=16), [pb[3]], [qx])
            B.cp(POOL, qx.t[:, :, :, 3:7], z["qkvT"].t[:].rearrange("p c (b s) -> p c b s", b=16), [z["qkvT"]], [qx])
            B.memset(POOL, s["ca"].t[:], 0.0, [s["ca"]])
            for c in range(6):
                cav = s["ca"].t[:, c, 0:64].rearrange("p (b s) -> p b s", b=16)
                B.ts(DVE, cav, qx.t[:, c, :, 0:4], B.cw.t[:, c, 0:1], None, ALU.mult, None, [qx, B.cw], [s["ca"]])
                for k in range(1, 4):
                    B.stt(DVE, cav, qx.t[:, c, :, k:k + 4], B.cw.t[:, c, k:k + 1], cav, ALU.mult, ALU.add, [qx, B.cw, s["ca"]], [s["ca"]])
            B.cp(DVE, s["ab"].t[:], z["ab"].t[:], [z["ab"]], [s["ab"]])

        def state_fn(psB, psD, gate_fn):
            bm4 = bm.t[:].rearrange("p (b t) -> p b t", b=16)
            for p_ in range(2):
                B.tt(DVE, wTm.t[:, p_, :, :], bc(s["wT"].t[:, p_, 0:64], [128, 16, 64], 1), bm4, ALU.mult, [s["wT"], bm], [wTm])
                B.tt(DVE, qTm.t[:, p_, :, :], bc(s["qT"].t[:, p_, 0:64], [128, 16, 64], 1), bm4, ALU.mult, [s["qT"], bm], [qTm])
            pD = psD.t[0:64, 0:256].rearrange("p (a b) -> p a b", a=4)
            for h in range(4):
                lo = 64 * (h % 2)
                for b in range(16):
                    B.mm(pD[:, h, :], wTm.t[lo:lo + 64, h // 2, b, :], Ssb.t[lo:lo + 64, b, h // 2, :], [wTm, Ssb], [psD], start=(b == 0), stop=(b == 15))
            B.memset(POOL, s["vnew"].t[:], 0.0, [s["vnew"]])
            B.tt(DVE, s["vnew"].t[0:64, :, :], s["ub"].t[0:64, :, :], pD, ALU.subtract, [s["ub"], psD], [s["vnew"]])
            pQ = psB.t[0:64, 0:256].rearrange("p (a b) -> p a b", a=4)
            for h in range(4):
                lo = 64 * (h % 2)
                for b in range(16):
                    B.mm(pQ[:, h, :], qTm.t[lo:lo + 64, h // 2, b, :], Ssb.t[lo:lo + 64, b, h // 2, :], [qTm, Ssb], [psB], start=(b == 0), stop=(b == 15))
            B.memset(POOL, s["o"].t[:], 0.0, [s["o"]])
            B.tt(DVE, s["o"].t[0:64, :, :], pQ, bc(s["egc"].t[0:64, :], [64, 4, 64], 2), ALU.mult, [psB, s["egc"]], [s["o"]])
            pV = psD.t[:, 256:512].rearrange("p (a b) -> p a b", a=4)
            for h in range(4):
                B.mm(pV[:, h, :], s["AT"].t[:, h, :], s["vnew"].t[:, h, :], [s["AT"], s["vnew"]], [psD])
            B.tt(DVE, s["o"].t[0:64, :, :], s["o"].t[0:64, :, :], pV[0:64, :, :], ALU.add, [psD, s["o"]], [s["o"]])
            B.tt(DVE, gmk.t[:], bc(s["g"].t[:], [128, 16, 4], 1), bc(sc["s_rm"].t[:], [128, 16, 4], 2), ALU.mult, [s["g"], sc["s_rm"]], [gmk])
            B.mm(pb[3].t[:, 0:64], B.onesf.t[:], gmk.t[:].rearrange("p b h -> p (b h)"), [B.onesf, gmk], [pb[3]])
            g3 = pb[3].t[:, 0:64].rearrange("p (b h) -> p b h", b=16)
            B.act(egls.t[0:64, :, :], g3[0:64, :, 0:4:2], AF.Exp, [pb[3]], [egls])
            B.act(egls.t[64:128, :, :], g3[64:128, :, 1:4:2], AF.Exp, [pb[3]], [egls])
            B.tt(DVE, kdm.t[:], bc(s["kd"].t[:].rearrange("p a b -> p (a b)"), [128, 16, 256], 1), bc(sc["s_rm"].t[:], [128, 16, 256], 2), ALU.mult,
                 [s["kd"], sc["s_rm"]], [kdm])
            B.tt(DVE, Ss.t[:].rearrange("p b q v -> p (b q) v"), Ss.t[:].rearrange("p b q v -> p (b q) v"),
                 bc(egls.t[:].rearrange("p b q -> p (b q)"), [128, 32, 64], 2), ALU.mult, [Ss, egls], [Ss])
            for b in range(16):
                pk = pb[4 + b % 2]
                pk3 = pk.t[:, 0:256].rearrange("p (a b) -> p a b", a=4)
                for h in range(4):
                    B.mm(pk3[:, h, :], kdm.t[:, b, (h // 2) * 128:(h // 2 + 1) * 128], s["vnew"].t[:, h, :], [kdm, s["vnew"]], [pk])
                B.tt(DVE, Ss.t[0:64, b, :, :], Ss.t[0:64, b, :, :], pk3[0:64, 0:4:2, :], ALU.add, [Ss, pk], [Ss])
                B.tt(DVE, Ss.t[64:128, b, :, :], Ss.t[64:128, b, :, :], pk3[64:128, 1:4:2, :], ALU.add, [Ss, pk], [Ss])
            for p_ in range(2):
                B.dma(SP, B.o_gss.t.ap()[:, 2 * p_:2 * p_ + 2, :, :].rearrange("b h k v -> (h k) b v"), Ss.t[:, :, p_, :], [Ss], [B.o_gss])
            B.tt(DVE, s["o2"].t[:], s["o"].t[:], s["o"].t[:], ALU.mult, [s["o"]], [s["o2"]])
            B.P.op(DVE, lambda e: e.tensor_reduce(out=s["ms"].t[:], in_=s["o2"].t[:], axis=AX.X, op=ALU.add), [s["o2"].b], [s["ms"].b])
            B.act(s["ms"].t[:], s["ms"].t[:], AF.Sqrt, [s["ms"], B.epsc], [s["ms"]], bias=B.epsc.t[:], scale=1.0 / 64)
            B.P.op(DVE, lambda e: e.reciprocal(out=s["ms"].t[:], in_=s["ms"].t[:]), [s["ms"].b], [s["ms"].b])
            B.tt(DVE, s["o"].t[:], s["o"].t[:], bc(s["ms"].t[:], [128, 4, 64], 2), ALU.mult, [s["o"], s["ms"]], [s["o"]])
            B.tt(DVE, s["o"].t[:], s["o"].t[:], bc(B.onw.t[:], [128, 4, 64], 1), ALU.mult, [s["o"], B.onw], [s["o"]])
            B.act(s["gs"].t[:], z["gate"].t[:], AF.Silu, [z["gate"]], [s["gs"]])
            B.tt(DVE, s["obf"].t[:], s["o"].t[:].rearrange("p a b -> p (a b)"), s["gs"].t[:], ALU.mult, [s["o"], s["gs"]], [s["obf"]])
            for p_ in range(2):
                B.tr(B.psTb.t[:, p_, :], s["obf"].t[:, p_ * 128:(p_ + 1) * 128], B.identb.t[:], [s["obf"], B.identb], [B.psTb])
            B.cp(DVE, z["obT"].t[:], B.psTb.t[:, 0:2, 0:64], [B.psTb], [z["obT"]])

        for _ in B.gdn_tile(None, None, pb[3], pb[4], pb[5], pb[6], want_out=True, gate_fn=None, smode={"conv_fn": conv_fn, "state_fn": state_fn}, par=0):
            pass
        B.gm = {"triu": B.triu, "trils": B.trils, "negc": B.negc, "strictf": B.strictf}

    def gdn_consts_s(self):
        B = self
        cw = B.ins["conv_b_w"]; alog = B.ins["a_log_b"]; dtb = B.ins["dt_bias_b"]; onb = B.ins["out_norm_b"]
        B.cw = B.sbt("s_cw", [128, 6, 4])
        for k in range(4):
            B.dma(SP, B.cw.t[:, :, k], cw.t.ap()[k, :].rearrange("(c p) -> p c", p=128), [cw], [B.cw], allow_slow_non_contiguous=True)
        B.nega = B.sbt("s_nega", [128, 4])
        B.dma(SP, B.nega.t[:], alog.t.ap(), [alog], [B.nega])
        B.act(B.nega.t[:], B.nega.t[:], AF.Exp, [B.nega], [B.nega])
        B.ts(DVE, B.nega.t[:], B.nega.t[:], -1.0, None, ALU.mult, None, [B.nega], [B.nega])
        B.dtb = B.sbt("s_dtb", [128, 4])
        B.dma(SP, B.dtb.t[:], dtb.t.ap(), [dtb], [B.dtb])
        B.onw = B.sbt("s_onw", [128, 64])
        B.dma(SP, B.onw.t[:], onb.t.ap(), [onb], [B.onw])
```

```python
import numpy as np
import concourse.bass as bass
import concourse.mybir as mybir
from concourse.bass_utils import run_bass_kernel_spmd

F32 = mybir.dt.float32
BF16 = mybir.dt.bfloat16
AF = mybir.ActivationFunctionType
ALU = mybir.AluOpType
AX = mybir.AxisListType

PE, ACT, DVE, POOL, SP = 0, 1, 2, 3, 4
ENG_NAMES = ["tensor", "scalar", "vector", "gpsimd", "sync"]
NDSEM = 12
PE_NOSELF = False
ACC_NOSELF = True
MM_NOSELF = False

NCORES = 8
D = 1024
SEQ = 16384
OWN = 2048
N_IN = 2824
C_QA, C_KA, C_VA, C_QKV, C_GATE, C_A, C_B, C_QC = 0, 512, 1024, 1536, 2304, 2560, 2564, 2568
D_FF = 2816
EPS = 1e-6
NEGV = -30000.0


class Buf:
    __slots__ = ("name", "w", "r", "excl")

    def __init__(self, name=""):
        self.name = name
        self.w = None
        self.r = []
        self.excl = False


class Prog:
    def __init__(self, nc):
        self.nc = nc
        self.q = [[] for _ in range(5)]
        self.cnt = [0] * 5
        self.dcnt = [0] * 5
        self.seen = [dict() for _ in range(5)]
        self.sems = {}
        self._semstack = []
        self._stack = []
        self.ninstr = 0

    def sb(self, name, shape, dt=F32):
        g = self.nc.sbuf_tensor(name, list(shape), dt)
        t = g.__enter__()
        self._stack.append(g)
        return t

    def ps(self, name, shape, dt=F32):
        g = self.nc.psum_tensor(name, list(shape), dt)
        t = g.__enter__()
        self._stack.append(g)
        return t

    def mark(self):
        return len(self._stack)

    def release(self, mark):
        while len(self._stack) > mark:
            g = self._stack.pop()
            g.__exit__(None, None, None)

    def sem(self, key):
        if key not in self.sems:
            g = self.nc.semaphore("s_" + "_".join(str(k) for k in key))
            self.sems[key] = g.__enter__()
            self._semstack.append(g)
        return self.sems[key]

    def _deps(self, eng, reads, writes, noself=False):
        need = {}
        for b in reads:
            if b.w is not None:
                k, v = b.w
                if need.get(k, 0) < v:
                    need[k] = v
        for b in writes:
            if b.w is not None:
                k, v = b.w
                if need.get(k, 0) < v:
                    need[k] = v
            for (k, v) in b.r:
                if need.get(k, 0) < v:
                    need[k] = v
        waits = []
        seen = self.seen[eng]
        for k, v in need.items():
            if (PE_NOSELF or noself) and eng == PE and k == ("e", PE):
                continue
            if seen.get(k, 0) >= v:
                continue
            seen[k] = v
            waits.append((k, v))
        return waits

    def _commit(self, tok, reads, writes):
        for b in reads:
            b.r = [x for x in b.r if x[0] != tok[0]]
            b.r.append(tok)
        for b in writes:
            b.w = tok
            b.r = []

    def op(self, eng, fn, reads=(), writes=(), noself=False):
        if eng != PE:
            ex = [b for b in reads if b.excl]
            if ex:
                reads = [b for b in reads if not b.excl]
                writes = list(writes) + [b for b in ex if b not in writes]
        waits = self._deps(eng, reads, writes, noself)
        self.cnt[eng] += 1
        key = ("e", eng)
        tok = (key, self.cnt[eng])
        self.q[eng].append((fn, waits, (key, 1)))
        self._commit(tok, reads, writes)
        self.ninstr += 1
        return tok

    def dma(self, eng, fn, reads=(), writes=()):
        i = self.dcnt[eng]
        self.dcnt[eng] += 1
        slot = i % NDSEM
        key = ("d", eng, slot)
        val = 16 * (i // NDSEM + 1)
        waits = self._deps(eng, reads, writes)
        if i >= NDSEM:
            pv = val - 16
            if self.seen[eng].get(key, 0) < pv:
                self.seen[eng][key] = pv
                waits.append((key, pv))
        tok = (key, val)
        self.q[eng].append((fn, waits, (key, 16)))
        self._commit(tok, reads, writes)
        self.ninstr += 1
        return tok

    def barrier(self):
        targets = []
        for e in range(5):
            if self.cnt[e] > 0:
                targets.append((("e", e), self.cnt[e]))
            n = self.dcnt[e]
            for slot in range(min(n, NDSEM)):
                last = ((n - 1 - slot) // NDSEM) * NDSEM + slot
                targets.append((("d", e, slot), 16 * (last // NDSEM + 1)))
        for e in range(5):
            waits = []
            for (k, v) in targets:
                if k == ("e", e):
                    continue
                if self.seen[e].get(k, 0) < v:
                    self.seen[e][k] = v
                    waits.append((k, v))
            self.q[e].append((None, waits, None))

    def wait_all(self, eng, bufs):
        waits = self._deps(eng, bufs, ())
        self.q[eng].append((None, waits, None))

    def emit(self):
        nc = self.nc
        for e in range(5):
            for (fn, waits, inc) in self.q[e]:
                for (k, v) in waits:
                    self.sem(k)
                if inc is not None:
                    self.sem(inc[0])
        with nc.Block() as block:
            def mk(e):
                def body(engine):
                    for (fn, waits, inc) in self.q[e]:
                        for (k, v) in waits:
                            engine.wait_ge(self.sems[k], v)
                        if fn is not None:
                            ins = fn(engine)
                            ins.then_inc(self.sems[inc[0]], inc[1])
                return body
            for e, nm in enumerate(ENG_NAMES):
                if self.q[e]:
                    getattr(block, nm)(mk(e))
        self.q = [[] for _ in range(5)]

    def close(self):
        self.release(0)
        while self._semstack:
            self._semstack.pop().__exit__(None, None, None)


class T:
    __slots__ = ("t", "b")

    def __init__(self, t, name=""):
        self.t = t
        self.b = Buf(name)


def _consts():
    c = {}
    c["ident"] = np.eye(128, dtype=np.float32)
    j = np.arange(128)[:, None]
    i = np.arange(128)[None, :]
    c["triu"] = (j <= i).astype(np.float32)
    c["trils"] = (j > i).astype(np.float32)
    c["negc"] = np.where(i >= j, 0.0, NEGV).astype(np.float32)
    c["strict"] = (i > j).astype(np.float32)
    c["ones"] = np.ones((128, 128), np.float32)
    bd = np.zeros((128, 128), np.float32)
    bd[:64, :64] = 1
    bd[64:, 64:] = 1
    c["bd"] = bd
    return c


class Builder:
    skip = ()
    NSETS = 3

    def __init__(self, NT=128, dbg=False):
        self.NT = NT
        self.L = NT * 128
        self.dbg = dbg
        self.nc = bass.Bass("TRN2", target_bir_lowering=False)
        self.P = Prog(self.nc)
        self.ins = {}
        self.outs = {}
        self.outbufs = []

    def inp(self, name, shape, dt=F32):
        t = self.nc.dram_tensor(name, list(shape), dt, kind="ExternalInput")
        self.ins[name] = T(t, name)
        return self.ins[name]

    def outp(self, name, shape, dt=F32):
        t = self.nc.dram_tensor(name, list(shape), dt, kind="ExternalOutput")
        self.outs[name] = T(t, name)
        self.outbufs.append(self.outs[name].b)
        return self.outs[name]

    def scratch(self, name, shape, dt=F32):
        t = self.nc.dram_tensor(name, list(shape), dt)
        return T(t, name)

    def sbt(self, name, shape, dt=F32):
        return T(self.P.sb(name, shape, dt), name)

    def pst(self, name, shape, dt=F32):
        t = T(self.P.ps(name, shape, dt), name)
        t.b.excl = True
        return t

    def dma(self, q, out, in_, r, w, **kw):
        self.P.dma(q, lambda e: e.dma_start(out=out, in_=in_, **kw), [x.b for x in r], [x.b for x in w])

    def mm(self, out, lhsT, rhs, r, w, start=True, stop=True):
        self.P.op(PE, lambda e: e.matmul(out, lhsT=lhsT, rhs=rhs, start=start, stop=stop),
                  [x.b for x in r], [x.b for x in w], noself=((not start) and ACC_NOSELF) or MM_NOSELF)

    def tr(self, out, in_, ident, r, w):
        self.P.op(PE, lambda e: e.transpose(out=out, in_=in_, identity=ident), [x.b for x in r], [x.b for x in w])

    def act(self, out, in_, func, r, w, eng=ACT, **kw):
        self.P.op(ACT, lambda e: e.activation(out=out, in_=in_, func=func, **kw), [x.b for x in r], [x.b for x in w])

    def cp(self, eng, out, in_, r, w):
        if eng == ACT:
            self.P.op(ACT, lambda e: e.activation(out=out, in_=in_, func=AF.Copy), [x.b for x in r], [x.b for x in w])
        else:
            self.P.op(eng, lambda e: e.tensor_copy(out=out, in_=in_), [x.b for x in r], [x.b for x in w])

    def tt(self, eng, out, in0, in1, op, r, w):
        if eng == POOL and getattr(self, "nopool", False):
            eng = DVE
        self.P.op(eng, lambda e: e.tensor_tensor(out=out, in0=in0, in1=in1, op=op), [x.b for x in r], [x.b for x in w])

    def ts(self, eng, out, in0, s1, s2, op0, op1, r, w):
        if eng == POOL and getattr(self, "nopool", False):
            eng = DVE
        if op1 is None:
            self.P.op(eng, lambda e: e.tensor_scalar(out=out, in0=in0, scalar1=s1, scalar2=None, op0=op0),
                      [x.b for x in r], [x.b for x in w])
        else:
            self.P.op(eng, lambda e: e.tensor_scalar(out=out, in0=in0, scalar1=s1, scalar2=s2, op0=op0, op1=op1),
                      [x.b for x in r], [x.b for x in w])

    def stt(self, eng, out, in0, scalar, in1, op0, op1, r, w):
        eng = DVE
        self.P.op(eng, lambda e: e.scalar_tensor_tensor(out=out, in0=in0, scalar=scalar, in1=in1, op0=op0, op1=op1),
                  [x.b for x in r], [x.b for x in w])

    def memset(self, eng, ap, val, w):
        self.P.op(eng, lambda e: e.memset(ap, val), [], [x.b for x in w])

    def rsqrt_inplace(self, t, ap, scale, r_extra=()):
        self.act(ap, ap, AF.Sqrt, [t, self.epsc] + list(r_extra), [t], scale=scale, bias=self.epsc.t[0:ap.shape[0], :])
        self.P.op(DVE, lambda e: e.reciprocal(out=ap, in_=ap), [t.b], [t.b])


def bc(ap, shape, axis):
    return ap.unsqueeze(axis).to_broadcast(list(shape))


class Builder(Builder):
    def setup_consts(self):
        B = self
        cn = _consts()
        self.cnp = cn
        for k, v in cn.items():
            B.inp("c_" + k, v.shape)
        B.epsc = B.sbt("epsc", [128, 1])
        B.memset(POOL, B.epsc.t[:], EPS, [B.epsc])
        B.onec = B.sbt("onec", [128, 1])
        B.memset(POOL, B.onec.t[:], 1.0, [B.onec])
        def ld(name, dt=F32):
            t = B.sbt("k_" + name + ("b" if dt == BF16 else "f"), [128, 128], dt)
            q = POOL if dt == BF16 else SP
            B.dma(q, t.t[:], B.ins["c_" + name].t.ap(), [B.ins["c_" + name]], [t])
            return t
        B.identf = ld("ident")
        B.identb = ld("ident", BF16)
        B.triu = ld("triu")
        B.trils = ld("trils")
        B.negc = ld("negc")
        B.strictf = ld("strict")
        B.onesf = ld("ones")
        B.bdb = ld("bd", BF16)
        B.gm = {"triu": B.triu, "trils": B.trils, "negc": B.negc, "strictf": B.strictf}

    def setup_weights_in(self):
        B = self
        w_in = B.inp("w_in", [D, N_IN])
        g1 = B.inp("norm1_g", [D])
        B.Wb = B.sbt("Wb", [128, 8, N_IN], BF16)
        B.g1 = B.sbt("g1", [128, 8])
        B.dma(SP, B.g1.t[:], g1.t.ap().rearrange("(k p) -> p k", p=128), [g1], [B.g1], allow_slow_non_contiguous=True)
        m = B.P.mark()
        st = [B.sbt("wst%d" % i, [128, 512]) for i in range(2)]
        i = 0
        for kc in range(8):
            for c0 in range(0, N_IN, 512):
                cn = min(512, N_IN - c0)
                s_ = st[i % 2]
                B.dma(SP if i % 2 == 0 else POOL, s_.t[:, 0:cn], w_in.t.ap()[kc * 128:(kc + 1) * 128, c0:c0 + cn], [w_in], [s_])
                if i % 2 == 0:
                    B.act(B.Wb.t[:, kc, c0:c0 + cn], s_.t[:, 0:cn], AF.Copy, [s_, B.g1], [B.Wb], scale=B.g1.t[:, kc:kc + 1])
                else:
                    B.ts(DVE, B.Wb.t[:, kc, c0:c0 + cn], s_.t[:, 0:cn], B.g1.t[:, kc:kc + 1], None, ALU.mult, None, [s_, B.g1], [B.Wb])
                i += 1
        return m

    def gdn_consts(self):
        B = self
        cw = B.inp("conv_b_w", [4, 768])
        alog = B.inp("a_log_b", [128, 4])
        dtb = B.inp("dt_bias_b", [128, 4])
        onb = B.inp("out_norm_b", [128, 64])
        B.cw = B.sbt("cw", [128, 6, 4])
        for k in range(4):
            B.dma(SP, B.cw.t[:, :, k], cw.t.ap()[k, :].rearrange("(c p) -> p c", p=128), [cw], [B.cw], allow_slow_non_contiguous=True)
        B.nega = B.sbt("nega", [128, 4])
        B.dma(SP, B.nega.t[:], alog.t.ap(), [alog], [B.nega])
        B.act(B.nega.t[:], B.nega.t[:], AF.Exp, [B.nega], [B.nega])
        B.ts(DVE, B.nega.t[:], B.nega.t[:], -1.0, None, ALU.mult, None, [B.nega], [B.nega])
        B.dtb = B.sbt("dtb", [128, 4])
        B.dma(SP, B.dtb.t[:], dtb.t.ap(), [dtb], [B.dtb])
        B.onw = B.sbt("onw", [128, 64])
        B.dma(SP, B.onw.t[:], onb.t.ap(), [onb], [B.onw])

    def gdn_alloc(self, pfx="g_", share=None):
        B = self
        s = {}
        def a(name, shape, dt=F32):
            if share is not None and name in share:
                s[name] = share[name]
            else:
                s[name] = B.sbt(pfx + name, shape, dt)
        a("qkvx", [128, 6, 131]); a("ca", [128, 6, 128]); a("cs", [128, 6, 128])
        a("sqb", [128, 4, 128], BF16); a("rn", [128, 4, 128])
        a("qT", [128, 2, 128], BF16); a("kT", [128, 2, 128], BF16); a("vTb", [128, 2, 128], BF16)
        a("ktok", [128, 4, 64]); a("vtok", [128, 4, 64])
        a("ab", [128, 8]); a("beta", [128, 4]); a("g", [128, 4]); a("ng", [128, 4])
        a("gc", [128, 4]); a("rev", [128, 4]); a("egc", [128, 4]); a("erev", [128, 4])
        a("G1", [128, 4, 128]); a("G2", [128, 4, 128])
        a("Ec", [128, 4, 128]); a("EsB", [128, 4, 128])
        a("X", [128, 4, 128]); a("AT", [128, 4, 128], BF16)
        s["Nn"] = s["G1"]; s["IX"] = s["G2"]
        yv = T(s["ca"].t[:, 0:4, :]); yv.b = s["ca"].b
        s["Y"] = yv
        a("ub", [128, 4, 64]); a("wb", [128, 4, 64], BF16); a("wT", [128, 2, 128], BF16)
        a("kd", [128, 4, 64], BF16); a("vnew", [128, 4, 64], BF16)
        a("S", [128, 2, 64]); a("Sb", [128, 2, 64], BF16); a("eglp", [128, 2])
        a("o", [128, 4, 64]); a("o2", [128, 4, 64]); a("ms", [128, 4]); a("gs", [128, 256]); a("obf", [128, 256], BF16)
        B.gs = s
        if not hasattr(B, "gsets"):
            B.gsets = {}
        B.memset(POOL, s["qkvx"].t[:], 0.0, [s["qkvx"]])
        B.memset(POOL, s["ca"].t[:], 0.0, [s["ca"]])
        B.memset(POOL, s["qT"].t[:], 0.0, [s["qT"]])
        if share is None or "S" not in share:
            B.memset(POOL, s["S"].t[:], 0.0, [s["S"]])
            B.memset(POOL, s["Sb"].t[:], 0.0, [s["Sb"]])

    def gdn_tile(self, psG, psAB, psA, psB, psC, psD, want_out, gate_fn=None, smode=None, par=0, first=False):
        B = self
        s = B.gsets[par]
        n = 128
        qkvx, ca, cs = s["qkvx"], s["ca"], s["cs"]
        if smode is None:
            if first:
                B.memset(POOL, qkvx.t[:, :, 0:3], 0.0, [qkvx])
            else:
                pq_ = B.gsets[(par - 1) % len(B.gsets)]["qkvx"]
                B.cp(POOL, qkvx.t[:, :, 0:3], pq_.t[:, :, 128:131], [pq_], [qkvx])
            B.cp(ACT, qkvx.t[:, 0:3, 3:131], psG[0].t[:], [psG[0]], [qkvx])
            B.cp(DVE, qkvx.t[:, 3:6, 3:131], psG[1].t[:], [psG[1]], [qkvx])
            B.cp(DVE, s["ab"].t[:], psAB.t[:, 0:8], [psAB], [s["ab"]])
        if getattr(B, 'stop_at', 99) <= 1:
            return
        yield
        if smode is not None:
            smode["conv_fn"]()
        for c in range(6 if smode is None else 0):
            if c < 2 and not want_out:
                continue
            eng = DVE if c < 3 else POOL
            B.ts(eng, ca.t[:, c, :], qkvx.t[:, c, 0:128], B.cw.t[:, c, 0:1], None, ALU.mult, None, [qkvx, B.cw], [ca])
            for k in range(1, 4):
                B.stt(eng, ca.t[:, c, :], qkvx.t[:, c, k:k + 128], B.cw.t[:, c, k:k + 1], ca.t[:, c, :], ALU.mult, ALU.add,
                      [qkvx, B.cw, ca], [ca])
        B.act(cs.t[:], ca.t[:], AF.Silu, [ca], [cs])
        if getattr(B, 'stop_at', 99) <= 2:
            return
        yield
        B.tt(POOL, s["sqb"].t[:], cs.t[:, 0:4, :], cs.t[:, 0:4, :], ALU.mult, [cs], [s["sqb"]])
        B.mm(psA.t[:], B.bdb.t[:], s["sqb"].t[:].rearrange("p a b -> p (a b)"), [B.bdb, s["sqb"]], [psA])
        B.act(s["rn"].t[:], psA.t[:].rearrange("p (a b) -> p a b", a=4), AF.Sqrt, [psA, B.epsc], [s["rn"]], bias=B.epsc.t[:], scale=1.0)
        B.P.op(DVE, lambda e: e.reciprocal(out=s["rn"].t[:], in_=s["rn"].t[:]), [s["rn"].b], [s["rn"].b])
        if want_out:
            B.stt(DVE, s["qT"].t[:], cs.t[:, 0:2, :], 0.125, s["rn"].t[:, 0:2, :], ALU.mult, ALU.mult, [cs, s["rn"]], [s["qT"]])
        B.tt(POOL, s["kT"].t[:], cs.t[:, 2:4, :], s["rn"].t[:, 2:4, :], ALU.mult, [cs, s["rn"]], [s["kT"]])
        B.cp(ACT, s["vTb"].t[:], cs.t[:, 4:6, :], [cs], [s["vTb"]])
        if getattr(B, 'stop_at', 99) <= 3:
            return
        yield
        pTb = B.psTb
        for a_ in range(2):
            B.tr(pTb.t[:, a_, :], s["kT"].t[:, a_, :], B.identb.t[:], [s["kT"], B.identb], [pTb])
            B.tr(pTb.t[:, 2 + a_, :], s["vTb"].t[:, a_, :], B.identb.t[:], [s["vTb"], B.identb], [pTb])
        if getattr(B, 'stop_at', 99) <= 3.1:
            return
        B.cp(DVE, s["ktok"].t[:].rearrange("p a b -> p (a b)"), pTb.t[:, 0:2, :].rearrange("p a b -> p (a b)"), [pTb], [s["ktok"]])
        if getattr(B, 'stop_at', 99) <= 3.2:
            return
        B.cp(DVE, s["vtok"].t[:].rearrange("p a b -> p (a b)"), pTb.t[:, 2:4, :].rearrange("p a b -> p (a b)"), [pTb], [s["vtok"]])
        if getattr(B, 'stop_at', 99) <= 4:
            return
        yield
        ab = s["ab"]
        B.act(s["beta"].t[:], ab.t[:, 4:8], AF.Sigmoid, [ab], [s["beta"]])
        B.tt(DVE, s["g"].t[:], ab.t[:, 0:4], B.dtb.t[:], ALU.add, [ab, B.dtb], [s["g"]])
        B.act(s["g"].t[:], s["g"].t[:], AF.Exp, [s["g"]], [s["g"]])
        B.act(s["g"].t[:], s["g"].t[:], AF.Ln, [s["g"], B.onec], [s["g"]], bias=B.onec.t[:], scale=1.0)
        B.tt(DVE, s["g"].t[:], s["g"].t[:], B.nega.t[:], ALU.mult, [s["g"], B.nega], [s["g"]])
        B.ts(DVE, s["ng"].t[:], s["g"].t[:], -1.0, None, ALU.mult, None, [s["g"]], [s["ng"]])
        if getattr(B, 'stop_at', 99) <= 5:
            return
        yield
        B.mm(psD.t[:, 0:4], B.gm["triu"].t[:], s["g"].t[:], [B.gm["triu"], s["g"]], [psD])
        B.mm(psD.t[:, 4:8], B.gm["trils"].t[:], s["g"].t[:], [B.gm["trils"], s["g"]], [psD])
        B.mm(psD.t[:, 8:12], B.onesf.t[:], s["g"].t[:], [B.onesf, s["g"]], [psD])
        B.act(s["egc"].t[:], psD.t[:, 0:4], AF.Exp, [psD], [s["egc"]])
        B.act(s["erev"].t[:], psD.t[:, 4:8], AF.Exp, [psD], [s["erev"]])
        B.act(s["eglp"].t[0:64, :], psD.t[0:64, 8:12:2], AF.Exp, [psD], [s["eglp"]])
        B.act(s["eglp"].t[64:128, :], psD.t[64:128, 9:12:2], AF.Exp, [psD], [s["eglp"]])
        if getattr(B, 'stop_at', 99) <= 6:
            return
        yield
        B.tt(DVE, s["G1"].t[:], bc(B.onesf.t[:], [128, 4, 128], 1), bc(s["g"].t[:], [128, 4, 128], 2), ALU.mult, [B.onesf, s["g"]], [s["G1"]])
        B.tt(POOL, s["G2"].t[:], bc(B.gm["triu"].t[:], [128, 4, 128], 1), bc(s["ng"].t[:], [128, 4, 128], 2), ALU.mult, [B.gm["triu"], s["ng"]], [s["G2"]])
        pC = psC.t[:].rearrange("p (a b) -> p a b", a=4)
        for h in range(4):
            B.mm(pC[:, h, :], s["G1"].t[:, h, :], B.gm["triu"].t[:], [s["G1"], B.gm["triu"]], [psC], start=True, stop=False)
            B.mm(pC[:, h, :], s["G2"].t[:, h, :], B.onesf.t[:], [s["G2"], B.onesf], [psC], start=False, stop=False)
            B.mm(pC[:, h, :], B.identf.t[:], B.gm["negc"].t[:], [B.identf, B.gm["negc"]], [psC], start=False, stop=True)
        B.act(s["Ec"].t[:], pC, AF.Exp, [psC], [s["Ec"]])
        if getattr(B, 'stop_at', 99) <= 7:
            return
        yield
        B.tt(POOL, s["EsB"].t[:], s["Ec"].t[:], bc(B.gm["strictf"].t[:], [128, 4, 128], 1), ALU.mult, [s["Ec"], B.gm["strictf"]], [s["EsB"]])
        B.tt(POOL, s["EsB"].t[:], s["EsB"].t[:], bc(s["beta"].t[:], [128, 4, 128], 2), ALU.mult, [s["EsB"], s["beta"]], [s["EsB"]])
        if getattr(B, 'stop_at', 99) <= 8:
            return
        yield
        pA = psA.t[:].rearrange("p (a b) -> p a b", a=4)
        pB = psB.t[:].rearrange("p (a b) -> p a b", a=4)
        for h in range(4):
            lo = 64 * (h % 2)
            kTh = s["kT"].t[lo:lo + 64, h // 2, :]
            B.mm(pA[:, h, :], kTh, kTh, [s["kT"]], [psA])
            if want_out:
                B.mm(pB[:, h, :], kTh, s["qT"].t[lo:lo + 64, h // 2, :], [s["kT"], s["qT"]], [psB])
        B.tt(DVE, s["X"].t[:], pA, s["EsB"].t[:], ALU.mult, [psA, s["EsB"]], [s["X"]])
        if want_out:
            B.tt(DVE, s["AT"].t[:], pB, s["Ec"].t[:], ALU.mult, [psB, s["Ec"]], [s["AT"]])
        yield
        Y = s["Y"]
        B.cp(ACT, Y.t[:, :, 0:64], s["vtok"].t[:], [s["vtok"]], [Y])
        B.tt(POOL, Y.t[:, :, 64:128], s["ktok"].t[:], bc(s["egc"].t[:], [128, 4, 64], 2), ALU.mult, [s["ktok"], s["egc"]], [Y])
        B.tt(POOL, s["kd"].t[:], s["ktok"].t[:], bc(s["erev"].t[:], [128, 4, 64], 2), ALU.mult, [s["ktok"], s["erev"]], [s["kd"]])
        if getattr(B, 'stop_at', 99) <= 10:
            return
        yield
        pTb = B.psTb
        for h in range(4):
            B.mm(pB[:, h, :], s["X"].t[:, h, :], B.identf.t[:], [s["X"], B.identf], [psB])
        B.cp(ACT, s["Nn"].t[:], pB, [psB], [s["Nn"]])
        identb4 = bc(B.identf.t[:], [128, 4, 128], 1)
        B.stt(POOL, s["IX"].t[:], s["X"].t[:], -1.0, identb4, ALU.mult, ALU.add, [s["X"], B.identf], [s["IX"]])
        if getattr(B, 'stop_at', 99) <= 11:
            return
        nlev = getattr(B, 'nlev', 7)
        for lev in range(nlev):
            for h in range(4):
                B.mm(pC[:, h, :], s["IX"].t[:, h, :], Y.t[:, h, :], [s["IX"], Y], [psC])
                if lev < nlev - 1:
                    B.mm(pA[:, h, :], s["Nn"].t[:, h, :], s["X"].t[:, h, :], [s["Nn"], s["X"]], [psA])
                    if lev < nlev - 2:
                        B.mm(pB[:, h, :], s["X"].t[:, h, :], s["Nn"].t[:, h, :], [s["Nn"], s["X"]], [psB])
            if lev < nlev - 1:
                B.cp(DVE, Y.t[:], pC, [psC], [Y])
                B.cp(ACT, s["X"].t[:], pA, [psA], [s["X"]])
                B.tt(POOL, s["IX"].t[:], s["X"].t[:], identb4, ALU.add, [s["X"], B.identf], [s["IX"]])
                if lev < nlev - 2:
                    B.cp(ACT, s["Nn"].t[:], pB, [psB], [s["Nn"]])
                yield
        if getattr(B, 'stop_at', 99) <= 12:
            return
        B.tt(DVE, s["ub"].t[:], pC[:, :, 0:64], bc(s["beta"].t[:], [128, 4, 64], 2), ALU.mult, [psC, s["beta"]], [s["ub"]])
        B.tt(DVE, s["wb"].t[:], pC[:, :, 64:128], bc(s["beta"].t[:], [128, 4, 64], 2), ALU.mult, [psC, s["beta"]], [s["wb"]])
        for p_ in range(2):
            B.tr(pTb.t[:, 4 + p_, :], s["wb"].t[:, 2 * p_:2 * p_ + 2, :].rearrange("p a b -> p (a b)"), B.identb.t[:], [s["wb"], B.identb], [pTb])
        B.cp(DVE, s["wT"].t[:], pTb.t[:, 4:6, :], [pTb], [s["wT"]])
        if getattr(B, 'stop_at', 99) <= 13:
            return
        if smode is not None:
            smode["state_fn"](psB, psD, gate_fn)
            return
        yield
        pD = psD.t[:, 0:256].rearrange("p (a b) -> p a b", a=4)
        pD2 = psD.t[:, 256:512].rearrange("p (a b) -> p a b", a=4)
        pQ = psB.t[:, 0:256].rearrange("p (a b) -> p a b", a=4)
        pV = psB.t[:, 256:512].rearrange("p (a b) -> p a b", a=4)
        for h in range(4):
            lo = 64 * (h % 2)
            B.mm(pD[:, h, :], s["wT"].t[lo:lo + 64, h // 2, :], s["Sb"].t[lo:lo + 64, h // 2, :], [s["wT"], s["Sb"]], [psD])
            if want_out:
                B.mm(pQ[:, h, :], s["qT"].t[lo:lo + 64, h // 2, :], s["Sb"].t[lo:lo + 64, h // 2, :], [s["qT"], s["Sb"]], [psB])
        B.tt(DVE, s["vnew"].t[:], s["ub"].t[:], pD, ALU.subtract, [s["ub"], psD], [s["vnew"]])
        if want_out:
            B.tt(DVE, s["o"].t[:], pQ, bc(s["egc"].t[:], [128, 4, 64], 2), ALU.mult, [psB, s["egc"]], [s["o"]])
        for h in range(4):
            if want_out:
                B.mm(pV[:, h, :], s["AT"].t[:, h, :], s["vnew"].t[:, h, :], [s["AT"], s["vnew"]], [psB])
            B.mm(pD2[:, h, :], s["kd"].t[:, 2 * (h // 2):2 * (h // 2) + 2, :].rearrange("p a b -> p (a b)"), s["vnew"].t[:, h, :], [s["kd"], s["vnew"]], [psD])
        if want_out:
            B.tt(DVE, s["o"].t[:], s["o"].t[:], pV, ALU.add, [psB, s["o"]], [s["o"]])
        B.tt(POOL, s["S"].t[:], s["S"].t[:], bc(s["eglp"].t[:], [128, 2, 64], 2), ALU.mult, [s["S"], s["eglp"]], [s["S"]])
        B.tt(DVE, s["S"].t[0:64, :, :], s["S"].t[0:64, :, :], pD2[0:64, 0:4:2, :], ALU.add, [s["S"], psD], [s["S"]])
        B.tt(DVE, s["S"].t[64:128, :, :], s["S"].t[64:128, :, :], pD2[64:128, 1:4:2, :], ALU.add, [s["S"], psD], [s["S"]])
        B.cp(ACT, s["Sb"].t[:], s["S"].t[:], [s["S"]], [s["Sb"]])
        if want_out:
            B.tt(POOL, s["o2"].t[:], s["o"].t[:], s["o"].t[:], ALU.mult, [s["o"]], [s["o2"]])
            B.P.op(DVE, lambda e: e.tensor_reduce(out=s["ms"].t[:], in_=s["o2"].t[:], axis=AX.X, op=ALU.add), [s["o2"].b], [s["ms"].b])
            B.act(s["ms"].t[:], s["ms"].t[:], AF.Sqrt, [s["ms"], B.epsc], [s["ms"]], bias=B.epsc.t[:], scale=1.0 / 64)
            B.P.op(DVE, lambda e: e.reciprocal(out=s["ms"].t[:], in_=s["ms"].t[:]), [s["ms"].b], [s["ms"].b])
            B.tt(DVE, s["o"].t[:], s["o"].t[:], bc(s["ms"].t[:], [128, 4, 64], 2), ALU.mult, [s["o"], s["ms"]], [s["o"]])
            B.tt(POOL, s["o"].t[:], s["o"].t[:], bc(B.onw.t[:], [128, 4, 64], 1), ALU.mult, [s["o"], B.onw], [s["o"]])
            psGate = gate_fn()
            B.act(s["gs"].t[:], psGate.t[:, 0:256], AF.Silu, [psGate], [s["gs"]])
            B.tt(DVE, s["obf"].t[:], s["o"].t[:].rearrange("p a b -> p (a b)"), s["gs"].t[:], ALU.mult, [s["o"], s["gs"]], [s["obf"]])


def alibi_masks():
    slopes = 2.0 ** (-8.0 * np.arange(1, 9) / 8.0)
    k = np.arange(128)[:, None]
    q = np.arange(128)[None, :]
    m = np.zeros((4, 128, 17, 2, 128), np.float64)
    for o in range(17):
        d = 128 * o + q - k
        c = ((d >= 0) & (d <= 128)).astype(np.float64) + ((d >= 0) & (d <= 512) & (d % 4 == 0)) + ((d >= 0) & (d <= 2048) & (d % 16 == 0))
        for h in range(8):
            m[h // 2, :, o, h % 2, :] = c * np.exp(-slopes[h] * np.maximum(d, 0))
    return m.astype(np.float32)


class Builder(Builder):
    def headnorm_fm(self, ps, npairs, ntok, gcol, dst, dstT, ps2, tmp_sq, tmp_rn):
        B = self
        n = npairs * ntok
        B.act(tmp_sq.t[:, 0:n], ps.t[:, 0:n], AF.Square, [ps], [tmp_sq])
        B.mm(ps2.t[:, 0:n], B.bdb.t[:], tmp_sq.t[:, 0:n], [B.bdb, tmp_sq], [ps2])
        B.act(tmp_rn.t[:, 0:n], ps2.t[:, 0:n], AF.Sqrt, [ps2, B.epsc], [tmp_rn], bias=B.epsc.t[:], scale=1.0 / 64)
        B.P.op(DVE, lambda e: e.reciprocal(out=tmp_rn.t[:, 0:n], in_=tmp_rn.t[:, 0:n]), [tmp_rn.b], [tmp_rn.b])
        B.stt(DVE, dst, ps.t[:, 0:n], gcol, tmp_rn.t[:, 0:n], ALU.mult, ALU.mult, [ps, tmp_rn] + [v[1] for v in B.gw.values()], [dstT])

    def rmsnorm_tile(self, x, sqj, ss, xb, gate_scale=None):
        B = self
        B.act(sqj.t[:], x.t[:], AF.Square, [x], [sqj, ss], accum_out=ss.t[:])
        np_ = x.t.shape[0]
        B.act(ss.t[:], ss.t[:], AF.Sqrt, [ss, B.epsc], [ss], bias=B.epsc.t[0:np_, :], scale=1.0 / D)
        B.P.op(DVE, lambda e: e.reciprocal(out=ss.t[:], in_=ss.t[:]), [ss.b], [ss.b])
        B.act(xb.t[:], x.t[:], AF.Copy, [x, ss], [xb], scale=ss.t[:])

    def tok_headnorm(self, ps, nh, gw, out, kss, np_=128):
        B = self
        n = nh * 64
        P_ = slice(0, np_)
        B.act(out.t[P_, 0:n], ps.t[P_, 0:n], AF.Square, [ps], [out])
        B.P.op(DVE, lambda e: e.tensor_reduce(out=kss.t[P_, 0:nh], in_=out.t[P_, 0:n].rearrange("p (a b) -> p a b", a=nh), axis=AX.X, op=ALU.add), [out.b], [kss.b])
        B.act(kss.t[P_, 0:nh], kss.t[P_, 0:nh], AF.Sqrt, [kss, B.epsc], [kss], bias=B.epsc.t[P_, :], scale=1.0 / 64)
        B.P.op(DVE, lambda e: e.reciprocal(out=kss.t[P_, 0:nh], in_=kss.t[P_, 0:nh]), [kss.b], [kss.b])
        o3 = out.t[P_, 0:n].rearrange("p (a b) -> p a b", a=nh)
        B.tt(DVE, o3, ps.t[P_, 0:n].rearrange("p (a b) -> p a b", a=nh), bc(kss.t[P_, 0:nh], [np_, nh, 64], 2), ALU.mult, [ps, kss], [out])
        B.tt(DVE, o3, o3, bc(gw.t[P_, :], [np_, nh, 64], 1), ALU.mult, [out, gw], [out])

    def prompt_setup(self, NOWN, sample=False):
        B = self
        NT = B.NT
        B.NOWN = NOWN
        B.th = NT - NOWN - 1
        assert B.th >= 0
        B.NF = NOWN + 1
        B.tw0 = max(0, B.th - 16)
        B.KW = NT - B.tw0
        B.xloc = B.inp("xloc", [B.L, D])
        B.valid = B.inp("valid", [128, NT])
        B.flag = B.inp("flag", [128, 1])
        B.amask = B.inp("amask", [4, 128, 17 * 2 * 128])
        for nm in ["q_norm_a", "k_norm_a", "q_norm_c", "k_norm_c"]:
            B.inp(nm, [128, 64])
            B.inp(nm + "_col", [128, 1])
        B.winK = B.outp("win_k_p", [NOWN * 128, 512])
        B.winV = B.outp("win_v_p", [NOWN * 128, 512])
        B.gst = B.outp("gdn_state_p", [4, 64, 64])
        B.gcv = B.outp("gdn_conv_p", [3, 768])
        B.memk_o = B.outp("mem_k_p", [256, 256])
        B.memv_o = B.outp("mem_v_p", [256, 256])
        B.y_o = B.outp("y_p", [NOWN * 128, D])
        B.fcv = B.outp("ffn_conv_p", [2, 2 * D_FF])
        B.kTs = B.scratch("kTs", [4, 128, B.KW * 128], BF16)
        B.qTs = B.scratch("qTs", [4, 128, B.NF * 128], BF16)
        B.vaug = B.scratch("vaug", [B.KW, 128, 8 * 80], BF16)
        B.x1s = B.scratch("x1s", [B.NF * 128, D])
        B.gw = {}
        for nm in ["q_norm_a", "k_norm_a", "q_norm_c", "k_norm_c"]:
            t = B.sbt("gw_" + nm, [128, 64])
            B.dma(SP, t.t[:], B.ins[nm].t.ap(), [B.ins[nm]], [t])
            c = B.sbt("gc_" + nm, [128, 1])
            B.dma(SP, c.t[:], B.ins[nm + "_col"].t.ap(), [B.ins[nm + "_col"]], [c])
            B.gw[nm] = (t, c)
        B.validsb = B.sbt("validsb", [128, NT])
        B.dma(SP, B.validsb.t[:], B.valid.t.ap(), [B.valid], [B.validsb])
        B.flagsb = B.sbt("flagsb", [128, 1])
        B.dma(SP, B.flagsb.t[:], B.flag.t.ap(), [B.flag], [B.flagsb])
        B.pb = [None] + [B.pst("pb%d" % i, [128, 512]) for i in range(1, 7)]
        B.psTb = B.pst("psTb", [128, 8, 128], BF16)
        B.psT = B.pst("psT", [128, 8, 128], BF16)
        if sample:
            B.sample_setup()
        B.m_mix = B.P.mark()
        B.mixTs = B.scratch("mixTs", [8, 128, B.NF * 128], BF16)

    def mem_kv(self):
        B = self
        pb = B.pb
        mem = B.inp("mem_prompt", [256, D])
        wm = B.inp("w_mem_kv", [D, 512])
        gm = B.inp("mem_norm_g", [D])
        B.mkT = B.sbt("mkT", [128, 2, 256], BF16)
        B.mvaug = B.sbt("mvaug", [128, 2, 4, 80], BF16)
        m = B.P.mark()
        B.Wm = B.sbt("Wm", [128, 8, 512], BF16)
        gmc = B.sbt("gmc", [128, 8])
        B.dma(SP, gmc.t[:], gm.t.ap().rearrange("(k p) -> p k", p=128), [gm], [gmc], allow_slow_non_contiguous=True)
        st = B.sbt("wmst", [128, 512])
        for kc in range(8):
            B.dma(SP, st.t[:], wm.t.ap()[kc * 128:(kc + 1) * 128, :], [wm], [st])
            B.ts(DVE, B.Wm.t[:, kc, :], st.t[:], gmc.t[:, kc:kc + 1], None, ALU.mult, None, [st, gmc], [B.Wm])
        x = B.sbt("memx", [128, D]); sqj = B.sbt("memsq", [128, D]); ss = B.sbt("memss", [128, 1]); xb = B.sbt("memxb", [128, D], BF16)
        mT = B.sbt("memT", [128, 8, 256], BF16)
        tsq = B.sbt("mem_tsq", [128, 512], BF16); trn = B.sbt("mem_trn", [128, 512])
        tokf = B.sbt("mem_tokf", [128, 256]); kss = B.sbt("mem_kss", [128, 8])
        for t in range(2):
            B.dma(SP, x.t[:], mem.t.ap()[t * 128:(t + 1) * 128, :], [mem], [x])
            B.rmsnorm_tile(x, sqj, ss, xb)
            for c in range(8):
                B.tr(B.psT.t[:, c, :], xb.t[:, c * 128:(c + 1) * 128], B.identb.t[:], [xb, B.identb], [B.psT])
            B.cp(DVE, mT.t[:, :, t * 128:(t + 1) * 128], B.psT.t[:], [B.psT], [mT])
        for p_ in range(2):
            for kc in range(8):
                B.mm(pb[3].t[:, p_ * 256:(p_ + 1) * 256], B.Wm.t[:, kc, p_ * 128:(p_ + 1) * 128], mT.t[:, kc, :], [B.Wm, mT], [pb[3]], start=(kc == 0), stop=(kc == 7))
        B.headnorm_fm(pb[3], 2, 256, B.gw["k_norm_c"][1].t[:], B.mkT.t[:].rearrange("p a b -> p (a b)"), B.mkT, pb[4], tsq, trn)
        B.memset(POOL, B.mvaug.t[:], 0.0, [B.mvaug])
        B.memset(POOL, B.mvaug.t[:, :, :, 64:65], 1.0, [B.mvaug])
        for t in range(2):
            for kc in range(8):
                B.mm(pb[5].t[:, 0:512], mT.t[:, kc, t * 128:(t + 1) * 128], B.Wm.t[:, kc, :], [B.Wm, mT], [pb[5]], start=(kc == 0), stop=(kc == 7))
            pk = T(pb[5].t[:, 0:256]); pk.b = pb[5].b
            B.tok_headnorm(pk, 4, B.gw["k_norm_c"][0], tokf, kss)
            B.dma(SP, B.memk_o.t.ap()[t * 128:(t + 1) * 128, :], tokf.t[:], [tokf], [B.memk_o])
            tokv = B.sbt("mem_tokv%d" % t, [128, 256])
            B.cp(ACT, tokv.t[:], pb[5].t[:, 256:512], [pb[5]], [tokv])
            B.dma(SP, B.memv_o.t.ap()[t * 128:(t + 1) * 128, :], tokv.t[:], [tokv], [B.memv_o])
            B.cp(DVE, B.mvaug.t[:, t, :, 0:64], pb[5].t[:, 256:512].rearrange("p (a b) -> p a b", a=4), [pb[5]], [B.mvaug])
        return m

    def prompt_loop(self):
        B = self
        NT, NOWN, th, tw0 = B.NT, B.NOWN, B.th, B.tw0
        pb = B.pb
        psT = B.psT
        xt = [B.sbt("xt%d" % i, [128, D]) for i in range(B.NSETS)]
        sqj = B.sbt("sqj", [128, D], BF16)
        NS = B.NSETS
        ssL = [B.sbt("ss%d" % i, [128, 1]) for i in range(NS)]
        xbL = [B.sbt("xb%d" % i, [128, D], BF16) for i in range(NS)]
        xnTL = [B.sbt("xnT%d" % i, [128, 8, 128], BF16) for i in range(NS)]
        kv = B.sbt("kvtok", [128, 512]); kv2 = B.sbt("kv2", [128, 512]); kss = B.sbt("kss", [128, 8])
        tsq = B.sbt("tsq", [128, 512], BF16); trn = B.sbt("trn", [128, 512])
        kst = B.sbt("kst", [128, 4, 128], BF16); qst = B.sbt("qst", [128, 4, 128], BF16)
        vst = B.sbt("vst", [128, 8, 80], BF16)
        qcT = B.sbt("qcT", [128, 2, 128], BF16)
        pcs = B.sbt("pcs", [128, 4, 128], BF16)
        rc = B.sbt("rc", [128, 2]); ocp = B.sbt("ocp", [128, 2, 64], BF16)
        mixst = B.sbt("mixst", [128, 4, 128], BF16)
        B.memset(POOL, vst.t[:], 0.0, [vst])
        B.gdn_alloc("g0_"); B.gsets[0] = B.gs
        shared = {k: B.gsets[0][k] for k in ["S", "Sb", "o", "o2", "ms", "gs", "obf"]}
        for i in range(1, NS):
            B.gdn_alloc("g%d_" % i, share=shared); B.gsets[i] = B.gs

        def tile_gen(t):
            par = t % NS
            s = B.gsets[par]
            ss = ssL[par]; xb = xbL[par]; xnT = xnTL[par]
            win = t >= tw0
            F = t >= th
            own = t > th
            f = t - th
            kw = t - tw0
            x = xt[t % NS]
            B.dma(SP, x.t[:], B.xloc.t.ap()[t * 128:(t + 1) * 128, :], [B.xloc], [x])
            B.rmsnorm_tile(x, sqj, ss, xb)
            for c in range(8):
                B.tr(psT.t[:, c, :], xb.t[:, c * 128:(c + 1) * 128], B.identb.t[:], [xb, B.identb], [psT])
            B.cp(DVE, xnT.t[:], psT.t[:], [psT], [xnT])
            if win and 'win' not in B.skip:
                for p_ in range(4):
                    if 'kproj' in B.skip:
                        break
                    for kc in range(8):
                        B.mm(pb[3].t[:, p_ * 128:(p_ + 1) * 128], B.Wb.t[:, kc, C_KA + p_ * 128:C_KA + (p_ + 1) * 128], xnT.t[:, kc, :],
                             [B.Wb, xnT], [pb[3]], start=(kc == 0), stop=(kc == 7))
                if 'kproj' not in B.skip:
                  B.headnorm_fm(pb[3], 4, 128, B.gw["k_norm_a"][1].t[:], kst.t[:].rearrange("p a b -> p (a b)"), kst, pb[4], tsq, trn)
                if 'ktsdma' not in B.skip:
                    B.dma(SP, B.kTs.t.ap()[:, :, kw * 128:(kw + 1) * 128].rearrange("a p n -> p a n"), kst.t[:], [kst], [B.kTs])
                for kc in range(8):
                    B.mm(pb[4].t[:], xnT.t[:, kc, :], B.Wb.t[:, kc, C_VA:C_VA + 512], [B.Wb, xnT], [pb[4]], start=(kc == 0), stop=(kc == 7))
                if 'vts' not in B.skip:
                    if 'novalid' in B.skip:
                        B.cp(DVE, vst.t[:, :, 0:64], pb[4].t[:].rearrange("p (a b) -> p a b", a=8), [pb[4]], [vst])
                    elif 'actv' in B.skip:
                        B.act(vst.t[:, :, 0:64], pb[4].t[:].rearrange("p (a b) -> p a b", a=8), AF.Copy, [pb[4], B.validsb], [vst], scale=B.validsb.t[:, t:t + 1])
                    else:
                        B.tt(DVE, vst.t[:, :, 0:64], pb[4].t[:].rearrange("p (a b) -> p a b", a=8),
                             B.validsb.t[:, t:t + 1].unsqueeze(1).to_broadcast([128, 8, 64]), ALU.mult, [pb[4], B.validsb], [vst])
                if 'vstcp' not in B.skip:
                    B.cp(DVE, vst.t[:, :, 64:65], bc(B.validsb.t[:, t:t + 1], [128, 8, 1], 1), [B.validsb], [vst])
                if 'vaugdma' not in B.skip:
                    B.dma(SP, B.vaug.t.ap()[kw, :, :], vst.t[:].rearrange("p a b -> p (a b)"), [vst], [B.vaug])
                if own:
                    r0 = (f - 1) * 128
                    B.cp(ACT, kv2.t[:], pb[4].t[:], [pb[4]], [kv2])
                    B.dma(SP, B.winV.t.ap()[r0:r0 + 128, :], kv2.t[:], [kv2], [B.winV])
                    for kc in range(8):
                        B.mm(pb[5].t[:], xnT.t[:, kc, :], B.Wb.t[:, kc, C_KA:C_KA + 512], [B.Wb, xnT], [pb[5]], start=(kc == 0), stop=(kc == 7))
                    B.tok_headnorm(pb[5], 8, B.gw["k_norm_a"][0], kv, kss)
                    B.dma(SP, B.winK.t.ap()[r0:r0 + 128, :], kv.t[:], [kv], [B.winK])
            if F and 'fq' not in B.skip:
                for p_ in range(4):
                    for kc in range(8):
                        B.mm(pb[3].t[:, p_ * 128:(p_ + 1) * 128], B.Wb.t[:, kc, C_QA + p_ * 128:C_QA + (p_ + 1) * 128], xnT.t[:, kc, :],
                             [B.Wb, xnT], [pb[3]], start=(kc == 0), stop=(kc == 7))
                B.headnorm_fm(pb[3], 4, 128, B.gw["q_norm_a"][1].t[:], qst.t[:].rearrange("p a b -> p (a b)"), qst, pb[4], tsq, trn)
                B.dma(SP, B.qTs.t.ap()[:, :, f * 128:(f + 1) * 128].rearrange("a p n -> p a n"), qst.t[:], [qst], [B.qTs])
            if F and 'cross' not in B.skip:
                for p_ in range(2):
                    for kc in range(8):
                        B.mm(pb[3].t[:, p_ * 128:(p_ + 1) * 128], B.Wb.t[:, kc, C_QC + p_ * 128:C_QC + (p_ + 1) * 128], xnT.t[:, kc, :],
                             [B.Wb, xnT], [pb[3]], start=(kc == 0), stop=(kc == 7))
                B.headnorm_fm(pb[3], 2, 128, B.gw["q_norm_c"][1].t[:], qcT.t[:].rearrange("p a b -> p (a b)"), qcT, pb[4], tsq, trn)
                for p_ in range(2):
                    p5 = pb[5].t[:].rearrange("p (a b) -> p a b", a=4)
                    for mt in range(2):
                        for h2 in range(2):
                            lo = 64 * h2
                            B.mm(p5[:, mt * 2 + h2, :], B.mkT.t[lo:lo + 64, p_, mt * 128:(mt + 1) * 128], qcT.t[lo:lo + 64, p_, :],
                                 [B.mkT, qcT], [pb[5]])
                    B.act(pcs.t[:], p5, AF.Exp, [pb[5]], [pcs], scale=0.125)
                    for h2 in range(2):
                        for mt in range(2):
                            B.mm(pb[6].t[:, h2 * 128:h2 * 128 + 65], pcs.t[:, mt * 2 + h2, :], B.mvaug.t[:, mt, 2 * p_ + h2, 0:65],
                                 [pcs, B.mvaug], [pb[6]], start=(mt == 0), stop=(mt == 1))
                    B.P.op(DVE, lambda e: e.reciprocal(out=rc.t[:], in_=pb[6].t[:, 64:256:128]), [pb[6].b], [rc.b])
                    B.tt(DVE, ocp.t[:], pb[6].t[:, 0:256].rearrange("p (a b) -> p a b", a=2)[:, :, 0:64], bc(rc.t[:], [128, 2, 64], 2), ALU.mult,
                         [pb[6], rc], [ocp])
                    B.tr(B.psTb.t[:, 0, :], ocp.t[:].rearrange("p a b -> p (a b)"), B.identb.t[:], [ocp, B.identb], [B.psTb])
                    B.cp(DVE, mixst.t[:, 2 + p_, :], B.psTb.t[:, 0, :], [B.psTb], [mixst])
                B.dma(SP, B.mixTs.t.ap()[6:8, :, f * 128:(f + 1) * 128].rearrange("a p n -> p a n"), mixst.t[:, 2:4, :], [mixst], [B.mixTs])
            for ct in (0, 3, 1, 4, 2, 5):
                dst = pb[1 + ct // 3]
                for kc in range(8):
                    B.mm(dst.t[:, (ct % 3) * 128:(ct % 3 + 1) * 128], B.Wb.t[:, kc, C_QKV + ct * 128:C_QKV + (ct + 1) * 128],
                         xnT.t[:, kc, :], [B.Wb, xnT], [dst], start=(kc == 0), stop=(kc == 7))
            for kc in range(8):
                B.mm(pb[1].t[:, 384:392], xnT.t[:, kc, :], B.Wb.t[:, kc, C_A:C_A + 8], [B.Wb, xnT], [pb[1]], start=(kc == 0), stop=(kc == 7))
            psG = [T(pb[1].t[:, 0:384].rearrange("p (a b) -> p a b", a=3)), T(pb[2].t[:, 0:384].rearrange("p (a b) -> p a b", a=3))]
            psG[0].b = pb[1].b
            psG[1].b = pb[2].b
            psAB = T(pb[1].t[:, 384:392])
            psAB.b = pb[1].b

            def gate_fn():
                for kc in range(8):
                    B.mm(pb[3].t[:, 0:256], xnT.t[:, kc, :], B.Wb.t[:, kc, C_GATE:C_GATE + 256], [B.Wb, xnT], [pb[3]], start=(kc == 0), stop=(kc == 7))
                return pb[3]
            yield from B.gdn_tile(psG, psAB, pb[3], pb[4], pb[5], pb[6], want_out=F and 'gout' not in B.skip, gate_fn=gate_fn, par=par, first=(t == 0))
            if F and 'gout' not in B.skip:
                for p_ in range(2):
                    B.tr(B.psTb.t[:, p_, :], s["obf"].t[:, p_ * 128:(p_ + 1) * 128], B.identb.t[:], [s["obf"], B.identb], [B.psTb])
                B.cp(DVE, mixst.t[:, 0:2, :], B.psTb.t[:, 0:2, :], [B.psTb], [mixst])
                B.dma(SP, B.mixTs.t.ap()[4:6, :, f * 128:(f + 1) * 128].rearrange("a p n -> p a n"), mixst.t[:, 0:2, :], [mixst], [B.mixTs])

        active = []
        nxt = 0
        while active or nxt < NT:
            if len(active) < NS and nxt < NT:
                active.append(tile_gen(nxt)); nxt += 1
            for g_ in list(active):
                try:
                    next(g_)
                except StopIteration:
                    active.remove(g_)
        s = B.gsets[(NT - 1) % NS]
        B.dma(SP, B.gst.t.ap().rearrange("(p h) k v -> (h k) p v", h=2), s["S"].t[:], [s["S"]], [B.gst])
        gct = B.sbt("gct", [3, 768])
        for c in range(4):
            B.mm(pb[3].t[0:3, c * 128:(c + 1) * 128], s["qkvx"].t[:, c, 128:131], B.identf.t[:], [s["qkvx"], B.identf], [pb[3]])
        B.cp(DVE, gct.t[:, 0:512], pb[3].t[0:3, 0:512], [pb[3]], [gct])
        for c in range(4, 6):
            B.mm(pb[4].t[0:3, (c - 4) * 128:(c - 3) * 128], s["qkvx"].t[:, c, 128:131], B.identf.t[:], [s["qkvx"], B.identf], [pb[4]])
        B.cp(DVE, gct.t[:, 512:768], pb[4].t[0:3, 0:256], [pb[4]], [gct])
        B.dma(SP, B.gcv.t.ap(), gct.t[:], [gct], [B.gcv])

    def attention_phase(self):
        B = self
        NF, th, tw0, KW = B.NF, B.th, B.tw0, B.KW
        pb = B.pb
        vall = B.sbt("vall", [128, KW, 8 * 80], BF16)
        B.dma(SP, vall.t[:], B.vaug.t.ap().rearrange("k p n -> p k n"), [B.vaug], [vall])
        kTb = [B.sbt("kTb%d" % i, [128, KW * 128], BF16) for i in range(2)]
        qTb = [B.sbt("qTb%d" % i, [128, NF * 128], BF16) for i in range(2)]
        msk = [B.sbt("msk%d" % i, [128, 17 * 2 * 128]) for i in range(2)]
        Eb = [B.sbt("Eb%d" % i, [128, 4, 128]) for i in range(2)]
        Pm = [B.sbt("Pm%d" % i, [128, 4, 128], BF16) for i in range(2)]
        rc = B.sbt("arc", [128, 2]); oap = B.sbt("oap", [128, 2, 64], BF16)
        amix = [B.sbt("amix%d" % i, [128, 128], BF16) for i in range(2)]
        it = 0
        for hp in range(4):
            kT = kTb[hp % 2]; qT = qTb[hp % 2]; mk = msk[hp % 2]
            B.dma(SP, kT.t[:], B.kTs.t.ap()[hp], [B.kTs], [kT])
            B.dma(SP, qT.t[:], B.qTs.t.ap()[hp], [B.qTs], [qT])
            B.dma(POOL, mk.t[:], B.amask.t.ap()[hp], [B.amask], [mk])
            mk4 = mk.t[:].rearrange("p (o h q) -> p o h q", o=17, h=2)
            jobs = []
            for f in range(NF):
                tq = th + f
                offs = [o for o in range(17) if tq - o >= tw0]
                groups = [offs[i:i + 2] for i in range(0, len(offs), 2)]
                for gi, g in enumerate(groups):
                    jobs.append((f, g, offs, gi == len(groups) - 1))

            def stageA(n):
                f, g, offs, last = jobs[n]
                tq = th + f
                psS = pb[1 + n % 2]
                pS = psS.t[:].rearrange("p (a b) -> p a b", a=4)
                for oi, o in enumerate(g):
                    kw = tq - o - tw0
                    for h2 in range(2):
                        lo = 64 * h2
                        B.mm(pS[:, oi * 2 + h2, :], kT.t[lo:lo + 64, kw * 128:(kw + 1) * 128], qT.t[lo:lo + 64, f * 128:(f + 1) * 128],
                             [kT, qT], [psS])

            def stageB(n):
                f, g, offs, last = jobs[n]
                psS = pb[1 + n % 2]
                pS = psS.t[:].rearrange("p (a b) -> p a b", a=4)
                E = Eb[n % 2]; P_ = Pm[n % 2]
                ns = 2 * len(g)
                B.act(E.t[:, 0:ns, :], pS[:, 0:ns, :], AF.Exp, [psS], [E], scale=0.125)
                B.tt(DVE, P_.t[:, 0:ns, :], E.t[:, 0:ns, :], mk4[:, g[0]:g[0] + len(g), :, :].rearrange("p o h q -> p (o h) q"), ALU.mult, [E, mk], [P_])

            def stageC(n):
                f, g, offs, last = jobs[n]
                tq = th + f
                P_ = Pm[n % 2]
                pab = 5 if f % 2 == 0 else 3
                for oi, o in enumerate(g):
                    kw = tq - o - tw0
                    for h2 in range(2):
                        h = 2 * hp + h2
                        pa_ = pb[pab + h2]
                        B.mm(pa_.t[:, 0:65], P_.t[:, oi * 2 + h2, :], vall.t[:, kw, h * 80:h * 80 + 65], [P_, vall], [pa_],
                             start=(o == offs[0]), stop=(o == offs[-1]))
                if not last:
                    return
                for h2 in range(2):
                    pa_ = pb[pab + h2]
                    B.ts(DVE, rc.t[:, h2:h2 + 1], pa_.t[:, 64:65], 1e-30, None, ALU.add, None, [pa_], [rc])
                B.P.op(DVE, lambda e: e.reciprocal(out=rc.t[:], in_=rc.t[:]), [rc.b], [rc.b])
                for h2 in range(2):
                    pa_ = pb[pab + h2]
                    B.tt(DVE, oap.t[:, h2, :], pa_.t[:, 0:64], rc.t[:, h2:h2 + 1].to_broadcast([128, 64]), ALU.mult, [pa_, rc], [oap])
                B.tr(B.psTb.t[:, 0, :], oap.t[:].rearrange("p a b -> p (a b)"), B.identb.t[:], [oap, B.identb], [B.psTb])
                ms_ = amix[f % 2]
                B.cp(DVE, ms_.t[:], B.psTb.t[:, 0, :], [B.psTb], [ms_])
                B.dma(SP, B.mixTs.t.ap()[hp, :, f * 128:(f + 1) * 128], ms_.t[:], [ms_], [B.mixTs])

            stageA(0)
            for n in range(len(jobs)):
                if n + 1 < len(jobs):
                    stageA(n + 1)
                stageB(n)
                stageC(n)

    def outproj_phase(self):
        B = self
        NF, th = B.NF, B.th
        pb = B.pb
        w_out = B.inp("w_out", [D, D])
        B.x1nTs = B.scratch("x1nTs", [8, 128, NF * 128], BF16)
        Wo = B.sbt("Wo", [128, 8, D], BF16)
        st = [B.sbt("wost%d" % i, [128, D]) for i in range(2)]
        for kc in range(8):
            B.dma(SP if kc % 2 == 0 else POOL, st[kc % 2].t[:], w_out.t.ap()[kc * 128:(kc + 1) * 128, :], [w_out], [st[kc % 2]])
            B.cp(ACT if kc % 2 == 0 else DVE, Wo.t[:, kc, :], st[kc % 2].t[:], [st[kc % 2]], [Wo])
        xt = [B.sbt("oxt%d" % i, [128, D]) for i in range(2)]
        x1 = [B.sbt("ox1%d" % i, [128, D]) for i in range(2)]
        sqj = B.sbt("osqj", [128, D]); ss = B.sbt("oss", [128, 1]); xb = B.sbt("oxb", [128, D], BF16)
        xnT = B.sbt("oxnT", [128, 8, 128], BF16)
        omix = [B.sbt("omix%d" % i, [128, 8, 128], BF16) for i in range(2)]
        for f in range(NF):
            t = th + f
            x = xt[f % 2]; y = x1[f % 2]
            B.dma(SP, x.t[:], B.xloc.t.ap()[t * 128:(t + 1) * 128, :], [B.xloc], [x])
            mx_ = omix[f % 2]
            B.dma(POOL, mx_.t[:], B.mixTs.t.ap()[:, :, f * 128:(f + 1) * 128].rearrange("a p n -> p a n"), [B.mixTs], [mx_])
            for half in range(2):
                ps = pb[1 + half]
                for kc in range(8):
                    B.mm(ps.t[:], mx_.t[:, kc, :], Wo.t[:, kc, half * 512:(half + 1) * 512], [mx_, Wo], [ps],
                         start=(kc == 0), stop=(kc == 7))
                B.tt(DVE, y.t[:, half * 512:(half + 1) * 512], x.t[:, half * 512:(half + 1) * 512], ps.t[:], ALU.add, [x, ps], [y])
            B.dma(SP, B.x1s.t.ap()[f * 128:(f + 1) * 128, :], y.t[:], [y], [B.x1s])
            B.rmsnorm_tile(y, sqj, ss, xb)
            for c in range(8):
                B.tr(B.psT.t[:, c, :], xb.t[:, c * 128:(c + 1) * 128], B.identb.t[:], [xb, B.identb], [B.psT])
            B.cp(DVE, xnT.t[:], B.psT.t[:], [B.psT], [xnT])
            B.dma(SP, B.x1nTs.t.ap()[:, :, f * 128:(f + 1) * 128].rearrange("a p n -> p a n"), xnT.t[:], [xnT], [B.x1nTs])
        if hasattr(B, "z"):
            z = B.z
            WoA = B.sbt("WoA", [64, 8, D], BF16); WoC = B.sbt("WoC", [64, 4, D], BF16)
            for h in range(12):
                s_ = st[h % 2]
                r0 = h * 64 if h < 8 else 768 + (h - 8) * 64
                B.dma(SP if h % 2 == 0 else POOL, s_.t[0:64, :], w_out.t.ap()[r0:r0 + 64, :], [w_out], [s_])
                dst = WoA.t[:, h, :] if h < 8 else WoC.t[:, h - 8, :]
                B.cp(ACT if h % 2 == 0 else DVE, dst, s_.t[0:64, :], [s_], [WoA if h < 8 else WoC])
            xs_ = B.sbt("oxs", [64, D]); sqs = B.sbt("osqs", [64, D], BF16); sss = B.sbt("osss", [64, 1]); xbs = B.sbt("oxbs", [64, D], BF16)
            B.dma(SP, xs_.t[:], B.xs.t.ap(), [B.xs], [xs_])
            for half in range(2):
                ps = pb[1 + half]
                hs = slice(half * 512, (half + 1) * 512)
                for h in range(8):
                    B.mm(ps.t[0:64, :], z["oaT"].t[:, h, :], WoA.t[:, h, hs], [z["oaT"], WoA], [ps], start=(h == 0), stop=False)
                for p_ in range(2):
                    B.mm(ps.t[0:64, :], z["obT"].t[:, p_, :], Wo.t[:, 4 + p_, hs], [z["obT"], Wo], [ps], start=False, stop=False)
                for h in range(4):
                    B.mm(ps.t[0:64, :], z["ocT"].t[:, h, :], WoC.t[:, h, hs], [z["ocT"], WoC], [ps], start=False, stop=(h == 3))
                B.tt(DVE, z["x1"].t[:, hs], xs_.t[:, hs], ps.t[0:64, :], ALU.add, [xs_, ps], [z["x1"]])
            B.rmsnorm_tile(z["x1"], sqs, sss, xbs)
            for c in range(8):
                B.tr(B.psT.t[:, c, 0:64], xbs.t[:, c * 128:(c + 1) * 128], B.identb.t[0:64, 0:64], [xbs, B.identb], [B.psT])
            B.cp(DVE, z["x1nT"].t[:], B.psT.t[:, :, 0:64], [B.psT], [z["x1nT"]])

    def ffn_phase(self):
        B = self
        NF, NOWN = B.NF, B.NOWN
        pb = B.pb
        g2 = B.inp("norm2_g", [D])
        w_up = B.inp("w_up", [D, 2 * D_FF])
        cfw = B.inp("conv_ffn_w", [3, 2 * D_FF])
        w_down = B.inp("w_down", [D_FF, D])
        g2c = B.sbt("g2c", [128, 8])
        B.dma(SP, g2c.t[:], g2.t.ap().rearrange("(k p) -> p k", p=128), [g2], [g2c], allow_slow_non_contiguous=True)
        cfc = B.sbt("cfc", [128, 44, 3])
        for k in range(3):
            B.dma(SP, cfc.t[:, :, k], cfw.t.ap()[k, :].rearrange("(c p) -> p c", p=128), [cfw], [cfc], allow_slow_non_contiguous=True)
        Wd = B.sbt("Wd", [128, 22, D], BF16)
        x1t = [B.sbt("fx1t%d" % i, [128, D]) for i in range(2)]
        yt = [B.sbt("fyt%d" % i, [128, D]) for i in range(2)]
        st = x1t
        for j in range(22):
            B.dma(SP if j % 2 == 0 else POOL, st[j % 2].t[:], w_down.t.ap()[j * 128:(j + 1) * 128, :], [w_down], [st[j % 2]])
            B.cp(ACT if j % 2 == 0 else DVE, Wd.t[:, j, :], st[j % 2].t[:], [st[j % 2]], [Wd])
        n0 = (NF + 2) // 3
        groups = [(0, min(n0, NF)), (min(n0, NF), min(2 * n0, NF)), (min(2 * n0, NF), NF)]
        NG = n0 * 128
        x1nT = B.sbt("fx1nT", [128, 8, NG], BF16)
        hT = B.sbt("fhT", [128, 22, NG], BF16)
        ug = [B.sbt("fug%d" % i, [128, NG + 2]) for i in range(2)]
        cg = [B.sbt("fcg%d" % i, [128, NG]) for i in range(2)]
        sg = B.sbt("fsg", [128, NG])
        wst = [[B.sbt("fwst%d_%d" % (i, k), [128, 8, 128]) for k in range(2)] for i in range(2)]
        wbf = [[B.sbt("fwbf%d_%d" % (i, k), [128, 8, 128], BF16) for k in range(2)] for i in range(2)]
        hsave = B.sbt("fhsave", [128, 44, 2])
        B.memset(POOL, hsave.t[:], 0.0, [hsave])
        SAMP = hasattr(B, "z")
        if SAMP:
            z = B.z
            fst = B.sbt("s_fst", [128, 44, 16, 2])
            fstg = [B.sbt("s_fstg%d" % i, [32, 512]) for i in range(2)]
            sfv = B.sfconv.t.ap().rearrange("b r n -> (b r) n")
            for c4 in range(11):
                sg_ = fstg[c4 % 2]
                B.dma(SP, sg_.t[:], sfv[:, c4 * 512:(c4 + 1) * 512], [B.sfconv], [sg_])
                ps = pb[5 + c4 % 2]
                for i in range(4):
                    B.mm(ps.t[:, i * 32:(i + 1) * 32], sg_.t[:, i * 128:(i + 1) * 128], B.identf.t[0:32, 0:32], [sg_, B.identf], [ps])
                B.cp(DVE, fst.t[:, c4 * 4:(c4 + 1) * 4, :, :], ps.t[:, 0:128].rearrange("p (c b r) -> p c b r", c=4, b=16), [ps], [fst])
            us = [B.sbt("s_us%d" % i, [128, 16, 6]) for i in range(2)]
            cs_ = [B.sbt("s_cs%d" % i, [128, 16, 4]) for i in range(2)]
            sgs = B.sbt("s_sgs", [128, 64])
            hTs = B.sbt("s_hTs", [128, 22, 64], BF16)
            usel = B.sbt("s_usel", [128, 16, 2])
            tks = [B.sbt("s_tks%d" % i, [32, 128]) for i in range(2)]
            ofv = B.o_fcs.t.ap().rearrange("b r n -> (b r) n")
        it = 0
        for gi, (f0, f1) in enumerate(groups):
            ntok = (f1 - f0) * 128
            if ntok == 0:
                continue
            B.dma(SP, x1nT.t[:, :, 0:ntok], B.x1nTs.t.ap()[:, :, f0 * 128:f1 * 128].rearrange("a p n -> p a n"), [B.x1nTs], [x1nT])
            chunks = [(c0, min(512, ntok - c0)) for c0 in range(0, ntok, 512)]
            for j in range(22):
                for k in range(2):
                    col = k * D_FF + j * 128
                    ws = wst[it % 2][k]; wb_ = wbf[it % 2][k]
                    B.dma(SP if k == 0 else POOL, ws.t[:], w_up.t.ap()[:, col:col + 128].rearrange("(a p) n -> p a n", p=128), [w_up], [ws])
                    B.tt(DVE, wb_.t[:], ws.t[:], bc(g2c.t[:], [128, 8, 128], 2), ALU.mult, [ws, g2c], [wb_])
                    u = ug[k]
                    B.cp(POOL, u.t[:, 0:2], hsave.t[:, k * 22 + j, :], [hsave], [u])
                    for ci, (c0, cn) in enumerate(chunks):
                        ps = pb[1 + (ci + k) % 2]
                        for kc in range(8):
                            B.mm(ps.t[:, 0:cn], wb_.t[:, kc, :], x1nT.t[:, kc, c0:c0 + cn], [wb_, x1nT], [ps], start=(kc == 0), stop=(kc == 7))
                        B.cp(ACT, u.t[:, 2 + c0:2 + c0 + cn], ps.t[:, 0:cn], [ps], [u])
                    if gi == 0:
                        B.ts(DVE, u.t[:, 2 + 126:2 + 128], u.t[:, 2 + 126:2 + 128], B.flagsb.t[:, 0:1], None, ALU.mult, None, [u, B.flagsb], [u])
                    B.cp(POOL, hsave.t[:, k * 22 + j, :], u.t[:, ntok:ntok + 2], [u], [hsave])
                    c_ = cg[k]
                    cc = k * 22 + j
                    B.ts(DVE, c_.t[:, 0:ntok], u.t[:, 0:ntok], cfc.t[:, cc, 0:1], None, ALU.mult, None, [u, cfc], [c_])
                    B.stt(DVE, c_.t[:, 0:ntok], u.t[:, 1:ntok + 1], cfc.t[:, cc, 1:2], c_.t[:, 0:ntok], ALU.mult, ALU.add, [u, cfc, c_], [c_])
                    B.stt(DVE, c_.t[:, 0:ntok], u.t[:, 2:ntok + 2], cfc.t[:, cc, 2:3], c_.t[:, 0:ntok], ALU.mult, ALU.add, [u, cfc, c_], [c_])
                    if SAMP and gi == 0:
                        ps = pb[5 + k]
                        for kc in range(8):
                            B.mm(ps.t[:, 0:64], wb_.t[:, kc, :], z["x1nT"].t[:, kc, :], [wb_, z["x1nT"]], [ps], start=(kc == 0), stop=(kc == 7))
                        us_ = us[k]
                        B.cp(POOL, us_.t[:, :, 0:2], fst.t[:, cc, :, :], [fst], [us_])
                        B.cp(ACT, us_.t[:, :, 2:6], ps.t[:, 0:64].rearrange("p (b s) -> p b s", b=16), [ps], [us_])
                        B.cp(POOL, usel.t[:], us_.t[:, :, 4:6], [us_], [usel])
                        tk_ = tks[cc % 2]
                        pq = pb[3 + cc % 2]
                        B.mm(pq.t[0:32, 0:128], usel.t[:].rearrange("p b r -> p (b r)"), B.identf.t[:], [usel, B.identf], [pq])
                        B.cp(DVE, tk_.t[:], pq.t[0:32, 0:128], [pq], [tk_])
                        B.dma(SP, ofv[:, k * D_FF + j * 128:k * D_FF + (j + 1) * 128], tk_.t[:], [tk_], [B.o_fcs])
                        cq = cs_[k]
                        B.ts(DVE, cq.t[:], us_.t[:, :, 0:4], cfc.t[:, cc, 0:1], None, ALU.mult, None, [us_, cfc], [cq])
                        B.stt(DVE, cq.t[:], us_.t[:, :, 1:5], cfc.t[:, cc, 1:2], cq.t[:], ALU.mult, ALU.add, [us_, cfc, cq], [cq])
                        B.stt(DVE, cq.t[:], us_.t[:, :, 2:6], cfc.t[:, cc, 2:3], cq.t[:], ALU.mult, ALU.add, [us_, cfc, cq], [cq])
                if SAMP and gi == 0:
                    B.act(sgs.t[:], cs_[0].t[:].rearrange("p b s -> p (b s)"), AF.Silu, [cs_[0]], [sgs])
                    B.tt(DVE, hTs.t[:, j, :], sgs.t[:], cs_[1].t[:].rearrange("p b s -> p (b s)"), ALU.mult, [sgs, cs_[1]], [hTs])
                it += 1
                B.act(sg.t[:, 0:ntok], cg[0].t[:, 0:ntok], AF.Silu, [cg[0]], [sg])
                B.tt(DVE, hT.t[:, j, 0:ntok], sg.t[:, 0:ntok], cg[1].t[:, 0:ntok], ALU.mult, [sg, cg[1]], [hT])
            for f in range(f0, f1):
                if f == 0:
                    continue
                lt = f - f0
                xx = x1t[f % 2]; yy = yt[f % 2]
                B.dma(POOL, xx.t[:], B.x1s.t.ap()[f * 128:(f + 1) * 128, :], [B.x1s], [xx])
                for half in range(2):
                    ps = pb[3 + half]
                    for j in range(22):
                        B.mm(ps.t[:], hT.t[:, j, lt * 128:(lt + 1) * 128], Wd.t[:, j, half * 512:(half + 1) * 512], [hT, Wd], [ps],
                             start=(j == 0), stop=(j == 21))
                    B.tt(DVE, yy.t[:, half * 512:(half + 1) * 512], xx.t[:, half * 512:(half + 1) * 512], ps.t[:], ALU.add, [xx, ps], [yy])
                B.dma(SP, B.y_o.t.ap()[(f - 1) * 128:f * 128, :], yy.t[:], [yy], [B.y_o])
        if SAMP:
            ys_ = B.sbt("s_ys", [64, D])
            for half in range(2):
                ps = pb[3 + half]
                hs = slice(half * 512, (half + 1) * 512)
                for j in range(22):
                    B.mm(ps.t[0:64, :], hTs.t[:, j, :], Wd.t[:, j, hs], [hTs, Wd], [ps], start=(j == 0), stop=(j == 21))
                B.tt(DVE, ys_.t[:, hs], z["x1"].t[:, hs], ps.t[0:64, :], ALU.add, [z["x1"], ps], [ys_])
            B.dma(SP, B.o_ys.t.ap(), ys_.t[:], [ys_], [B.o_ys])
        fct = [B.sbt("fct%d" % i, [2, 512]) for i in range(2)]
        for c4 in range(11):
            ps = pb[5 + c4 % 2]
            for i in range(4):
                c = c4 * 4 + i
                B.mm(ps.t[0:2, i * 128:(i + 1) * 128], hsave.t[:, c, :], B.identf.t[:], [hsave, B.identf], [ps])
            B.cp(DVE, fct[c4 % 2].t[:], ps.t[0:2, :], [ps], [fct[c4 % 2]])
            B.dma(SP, B.fcv.t.ap()[:, c4 * 512:(c4 + 1) * 512], fct[c4 % 2].t[:], [fct[c4 % 2]], [B.fcv])

    def finish(self):
        B = self
        B.P.wait_all(SP, B.outbufs)
        B.P.emit()
        B.P.close()

    def build_prompt(self, NOWN, sample=False):
        B = self
        B.setup_consts()
        m0 = B.P.mark()
        B.prompt_setup(NOWN, sample=sample)
        m1 = B.P.mark()
        mw = B.setup_weights_in()
        if sample:
            B.sample_inproj()
        B.P.barrier(); B.P.emit(); B.P.release(mw)
        if 'memkv' not in B.skip:
            mm_ = B.mem_kv()
            B.P.barrier(); B.P.emit(); B.P.release(mm_)
        B.gdn_consts()
        B.prompt_loop()
        if getattr(B, "nphase", 9) < 2:
            return
        B.P.barrier(); B.P.emit(); B.P.release(m1)
        B.attention_phase()
        if getattr(B, "nphase", 9) < 3:
            return
        if sample:
            B.P.barrier(); B.P.emit(); B.P.release(m1)
            B.sample_phase()
        B.P.barrier(); B.P.emit(); B.P.release(m1)
        B.outproj_phase()
        if getattr(B, "nphase", 9) < 4:
            return
        B.P.barrier(); B.P.emit(); B.P.release(B.m_mix)
        B.ffn_phase()


_CACHE = {}


def _build():
    if "B" in _CACHE:
        return _CACHE["B"]
    B = Builder(NT=128)
    B.build_prompt(16, sample=True)
    B.finish()
    _CACHE["B"] = B
    return B


def kernel(x_prompt, x_sample, cache_win_k, cache_win_v, state_gdn, state_gdn_conv, state_ffn_conv,
           cache_mem_k, cache_mem_v, mem_prompt, norm1_g, w_in, q_norm_a, k_norm_a, conv_b_w, a_log_b,
           dt_bias_b, out_norm_b, mem_norm_g, w_mem_kv, q_norm_c, k_norm_c, w_out, norm2_g, w_up,
           conv_ffn_w, w_down):
    f = lambda a: np.ascontiguousarray(np.asarray(a, dtype=np.float32))
    B = _build()
    xp = f(x_prompt)[0]
    rep = lambda v, n: np.ascontiguousarray(np.broadcast_to(f(v).reshape(1, -1), (128, n)))
    col = lambda v: np.ascontiguousarray(np.tile(f(v).reshape(-1), 2).reshape(128, 1))
    am = alibi_masks().reshape(4, 128, -1)
    shared = {
        "amask": am, "w_in": f(w_in)[0], "norm1_g": f(norm1_g)[0], "conv_b_w": f(conv_b_w)[0],
        "a_log_b": rep(a_log_b, 4), "dt_bias_b": rep(dt_bias_b, 4), "out_norm_b": rep(out_norm_b, 64),
        "mem_prompt": f(mem_prompt)[0], "w_mem_kv": f(w_mem_kv)[0], "mem_norm_g": f(mem_norm_g)[0],
        "w_out": f(w_out)[0], "norm2_g": f(norm2_g)[0], "w_up": f(w_up)[0], "conv_ffn_w": f(conv_ffn_w)[0], "w_down": f(w_down)[0],
    }
    for nm, v in (("q_norm_a", q_norm_a), ("k_norm_a", k_norm_a), ("q_norm_c", q_norm_c), ("k_norm_c", k_norm_c)):
        shared[nm] = rep(v, 64)
        shared[nm + "_col"] = col(v)
    for k, v in B.cnp.items():
        shared["c_" + k] = v
    for k, v in B.scn.items():
        shared["c_" + k] = v
    xs_all = f(x_sample)
    cwk_all = np.asarray(cache_win_k, dtype=np.float32)[0]
    cwv_all = np.asarray(cache_win_v, dtype=np.float32)[0]
    sg_all = f(state_gdn)[0]; sgc_all = f(state_gdn_conv)[0]; sfc_all = f(state_ffn_conv)[0]
    cmk_all = f(cache_mem_k)[0]; cmv_all = f(cache_mem_v)[0]
    in_maps = []
    for c in range(NCORES):
        npad = (NCORES - 1 - c) * OWN
        xloc = np.zeros((SEQ, D), np.float32)
        xloc[npad:] = xp[:(c + 1) * OWN]
        valid = np.zeros((SEQ,), np.float32)
        valid[npad:] = 1.0
        im = dict(shared)
        im["xloc"] = xloc
        im["valid"] = np.ascontiguousarray(valid.reshape(128, 128).T)
        im["flag"] = np.full((128, 1), 0.0 if c == 0 else 1.0, np.float32)
        bs = slice(16 * c, 16 * c + 16)
        im["xs"] = np.ascontiguousarray(xs_all[bs].reshape(64, D))
        im["cwk"] = np.ascontiguousarray(cwk_all[bs].reshape(16, 2048, 512))
        im["cwv"] = np.ascontiguousarray(cwv_all[bs].reshape(16, 2048, 512))
        im["sgdn"] = np.ascontiguousarray(sg_all[bs]); im["sgconv"] = np.ascontiguousarray(sgc_all[bs]); im["sfconv"] = np.ascontiguousarray(sfc_all[bs])
        im["cmk"] = np.ascontiguousarray(cmk_all[bs].reshape(16, 256, 256)); im["cmv"] = np.ascontiguousarray(cmv_all[bs].reshape(16, 256, 256))
        im = {k: v for k, v in im.items() if k in B.ins}
        in_maps.append(im)
    res = run_bass_kernel_spmd(B.nc, in_maps, core_ids=list(range(NCORES)))
    r = res.results
    z = lambda *s: np.zeros(s, np.float32)
    g = lambda c, n: np.asarray(r[c][n], np.float32)
    y_p = np.concatenate([g(c, "y_p") for c in range(NCORES)], 0).reshape(1, SEQ, D)
    win_k_p = g(7, "win_k_p").reshape(1, 1, 2048, 8, 64)
    win_v_p = g(7, "win_v_p").reshape(1, 1, 2048, 8, 64)
    gst_p = g(7, "gdn_state_p").reshape(1, 1, 4, 64, 64)
    gconv_p = g(7, "gdn_conv_p").reshape(1, 1, 3, 768)
    fconv_p = g(7, "ffn_conv_p").reshape(1, 1, 2, 2 * D_FF)
    mem_k_p = g(0, "mem_k_p").reshape(1, 1, 256, 4, 64)
    mem_v_p = g(0, "mem_v_p").reshape(1, 1, 256, 4, 64)
    cat = lambda n: np.concatenate([g(c, n) for c in range(NCORES)], 0)
    y_s = cat("y_s").reshape(128, 4, D)
    win_k_s = cat("win_k_s").reshape(1, 128, 4, 8, 64)
    win_v_s = cat("win_v_s").reshape(1, 128, 4, 8, 64)
    gst_s = cat("gdn_state_s").reshape(1, 128, 4, 64, 64)
    gconv_s = cat("gdn_conv_s").reshape(1, 128, 3, 768)
    fconv_s = cat("ffn_conv_s").reshape(1, 128, 2, 2 * D_FF)
    return (y_p, y_s, win_k_p, win_v_p, win_k_s, win_v_s, gst_p, gst_s, gconv_p, gconv_s, fconv_p, fconv_s, mem_k_p, mem_v_p)


def sample_consts():
    slopes = 2.0 ** (-8.0 * np.arange(1, 9) / 8.0)
    def cmult(d):
        return ((d >= 0) & (d <= 128)).astype(np.float64) + ((d >= 0) & (d <= 512) & (d % 4 == 0)) + ((d >= 0) & (d <= 2048) & (d % 16 == 0))
    p = np.arange(128)
    rows = np.zeros((7, 128), np.int64)
    for j in range(4):
        rows[j] = 1536 + 128 * j + p
    for j in range(4, 7):
        rows[j] = 16 * (32 * (j - 4) + p % 32) + p // 32
    smask = np.zeros((128, 7, 8, 4), np.float64)
    for j in range(7):
        for s in range(4):
            d = 2048 + s - rows[j]
            c = cmult(d)
            for h in range(8):
                smask[:, j, h, s] = c * np.exp(-slopes[h] * np.maximum(d, 0))
    snew = np.zeros((128, 16, 8, 4), np.float64)
    for b in range(16):
        for sp in range(4):
            for s in range(4):
                d = s - sp
                if d >= 0:
                    snew[4 * b + sp, b, :, s] = cmult(np.array(d)) * np.exp(-slopes * d)
    grp = np.where(p < 64, p // 4, 100 + p)
    same = grp[:, None] == grp[None, :]
    j = p[:, None]; i = p[None, :]
    c = {}
    c["s_triu"] = (same & (j <= i)).astype(np.float32)
    c["s_trils"] = (same & (j > i)).astype(np.float32)
    c["s_negc"] = np.where(same & (i >= j), 0.0, NEGV).astype(np.float32)
    c["s_strict"] = (same & (i > j)).astype(np.float32)
    bm = np.zeros((128, 16, 64), np.float32)
    for b in range(16):
        bm[:, b, 4 * b:4 * b + 4] = 1.0
    rm = np.zeros((128, 16), np.float32)
    for b in range(16):
        rm[4 * b:4 * b + 4, b] = 1.0
    c["s_smask"] = smask.reshape(128, -1).astype(np.float32)
    c["s_snew"] = snew.reshape(128, -1).astype(np.float32)
    c["s_bm"] = bm.reshape(128, -1)
    c["s_rm"] = rm
    return c


class Builder(Builder):
    def sample_setup(self):
        B = self
        B.scn = sample_consts()
        for k, v in B.scn.items():
            B.inp("c_" + k, v.shape)
        B.xs = B.inp("xs", [64, D])
        B.cwk = B.inp("cwk", [16, 2048, 512]); B.cwv = B.inp("cwv", [16, 2048, 512])
        B.sgdn = B.inp("sgdn", [16, 4, 64, 64]); B.sgconv = B.inp("sgconv", [16, 3, 768]); B.sfconv = B.inp("sfconv", [16, 2, 2 * D_FF])
        B.cmk = B.inp("cmk", [16, 256, 256]); B.cmv = B.inp("cmv", [16, 256, 256])
        B.o_wks = B.outp("win_k_s", [64, 512]); B.o_wvs = B.outp("win_v_s", [64, 512])
        B.o_gcs = B.outp("gdn_conv_s", [16, 3, 768]); B.o_gss = B.outp("gdn_state_s", [16, 4, 64, 64])
        B.o_fcs = B.outp("ffn_conv_s", [16, 2, 2 * D_FF]); B.o_ys = B.outp("y_s", [64, D])
        z = {}
        def a(name, shape, dt=F32):
            z[name] = B.sbt("z_" + name, shape, dt)
        a("qaT", [128, 4, 64], BF16); a("kaT", [128, 4, 64], BF16); a("qcT", [128, 2, 64], BF16)
        a("qkvT", [128, 6, 64]); a("ab", [128, 8]); a("gate", [128, 256]); a("vaug", [64, 8, 80], BF16)
        a("oaT", [64, 8, 64], BF16); a("obT", [128, 2, 64], BF16); a("ocT", [64, 4, 64], BF16)
        a("x1nT", [128, 8, 64], BF16); a("x1", [64, D])
        B.z = z

    def sample_inproj(self):
        B = self
        pb = B.pb
        z = B.z
        x = B.sbt("sx", [64, D]); sqj = B.sbt("ssqj", [64, D], BF16); ss = B.sbt("sss", [64, 1]); xb = B.sbt("sxb", [64, D], BF16)
        xnT = B.sbt("sxnT", [128, 8, 64], BF16)
        tsq = B.sbt("stsq", [128, 512], BF16); trn = B.sbt("strn", [128, 512])
        tok = B.sbt("stok", [64, 768]); tok2 = B.sbt("stok2", [64, 512]); kss = B.sbt("skss", [64, 8])
        B.dma(SP, x.t[:], B.xs.t.ap(), [B.xs], [x])
        B.rmsnorm_tile(x, sqj, ss, xb)
        for c in range(8):
            B.tr(B.psT.t[:, c, 0:64], xb.t[:, c * 128:(c + 1) * 128], B.identb.t[0:64, 0:64], [xb, B.identb], [B.psT])
        B.cp(DVE, xnT.t[:], B.psT.t[:, :, 0:64], [B.psT], [xnT])

        def fm(col0, npairs, dst_ps):
            for p_ in range(npairs):
                for kc in range(8):
                    B.mm(dst_ps.t[:, p_ * 64:(p_ + 1) * 64], B.Wb.t[:, kc, col0 + p_ * 128:col0 + (p_ + 1) * 128], xnT.t[:, kc, :],
                         [B.Wb, xnT], [dst_ps], start=(kc == 0), stop=(kc == 7))
        fm(C_QA, 4, pb[3])
        B.headnorm_fm(pb[3], 4, 64, B.gw["q_norm_a"][1].t[:], z["qaT"].t[:].rearrange("p a b -> p (a b)"), z["qaT"], pb[4], tsq, trn)
        fm(C_KA, 4, pb[3])
        B.headnorm_fm(pb[3], 4, 64, B.gw["k_norm_a"][1].t[:], z["kaT"].t[:].rearrange("p a b -> p (a b)"), z["kaT"], pb[4], tsq, trn)
        fm(C_QC, 2, pb[3])
        B.headnorm_fm(pb[3], 2, 64, B.gw["q_norm_c"][1].t[:], z["qcT"].t[:].rearrange("p a b -> p (a b)"), z["qcT"], pb[4], tsq, trn)
        fm(C_QKV, 6, pb[3])
        B.cp(DVE, z["qkvT"].t[:].rearrange("p a b -> p (a b)"), pb[3].t[:, 0:384], [pb[3]], [z["qkvT"]])

        def tm(col0, ncol, dst_ps, c_off=0):
            for kc in range(8):
                B.mm(dst_ps.t[0:64, c_off:c_off + ncol], xnT.t[:, kc, :], B.Wb.t[:, kc, col0:col0 + ncol], [B.Wb, xnT], [dst_ps], start=(kc == 0), stop=(kc == 7))
        tm(C_KA, 512, pb[5])
        B.tok_headnorm(pb[5], 8, B.gw["k_norm_a"][0], tok2, kss, np_=64)
        B.dma(SP, B.o_wks.t.ap(), tok2.t[:], [tok2], [B.o_wks])
        tm(C_VA, 512, pb[6])
        tokv = B.sbt("stokv", [64, 512])
        B.cp(ACT, tokv.t[:], pb[6].t[0:64, :], [pb[6]], [tokv])
        B.dma(SP, B.o_wvs.t.ap(), tokv.t[:], [tokv], [B.o_wvs])
        B.memset(POOL, z["vaug"].t[:], 0.0, [z["vaug"]])
        B.memset(POOL, z["vaug"].t[:, :, 64:65], 1.0, [z["vaug"]])
        B.cp(DVE, z["vaug"].t[:, :, 0:64], tokv.t[:].rearrange("p (a b) -> p a b", a=8), [tokv], [z["vaug"]])
        tm(C_QKV, 512, pb[5])
        tm(C_QKV + 512, 256, pb[6])
        B.cp(ACT, tok.t[:, 0:512], pb[5].t[0:64, :], [pb[5]], [tok])
        B.cp(ACT, tok.t[:, 512:768], pb[6].t[0:64, 0:256], [pb[6]], [tok])
        for b in range(16):
            B.dma(SP if b % 2 == 0 else POOL, B.o_gcs.t.ap()[b, :, :], tok.t[4 * b + 1:4 * b + 4, :], [tok], [B.o_gcs])
        B.memset(POOL, z["ab"].t[:], 0.0, [z["ab"]])
        B.memset(POOL, z["gate"].t[:], 0.0, [z["gate"]])
        tm(C_A, 8, pb[5])
        B.cp(DVE, z["ab"].t[0:64, :], pb[5].t[0:64, 0:8], [pb[5]], [z["ab"]])
        tm(C_GATE, 256, pb[6])
        B.cp(DVE, z["gate"].t[0:64, :], pb[6].t[0:64, 0:256], [pb[6]], [z["gate"]])

    def sample_phase(self):
        B = self
        pb = B.pb
        z = B.z
        sc = {}
        for k in ["s_triu", "s_trils", "s_negc", "s_strict", "s_rm"]:
            shp = list(B.scn[k].shape)
            sc[k] = B.sbt("k_" + k, shp)
            B.dma(SP, sc[k].t[:], B.ins["c_" + k].t.ap(), [B.ins["c_" + k]], [sc[k]])
        smask = B.sbt("k_smask", [128, 7 * 32]); B.dma(SP, smask.t[:], B.ins["c_s_smask"].t.ap(), [B.ins["c_s_smask"]], [smask])
        snew = B.sbt("k_snew", [128, 16 * 32]); B.dma(SP, snew.t[:], B.ins["c_s_snew"].t.ap(), [B.ins["c_s_snew"]], [snew])
        bm = B.sbt("k_bm", [128, 16 * 64], BF16); B.dma(POOL, bm.t[:], B.ins["c_s_bm"].t.ap(), [B.ins["c_s_bm"]], [bm])
        kf = [B.sbt("a_kf%d" % i, [128, 512]) for i in range(2)]
        kb = [B.sbt("a_kb%d" % i, [128, 512], BF16) for i in range(2)]
        vf = [B.sbt("a_vf%d" % i, [128, 512]) for i in range(2)]
        kT = B.sbt("a_kT", [128, 4, 7 * 128], BF16)
        va = B.sbt("a_va", [128, 7, 8, 80], BF16)
        E = B.sbt("a_E", [128, 7 * 32]); Pb = B.sbt("a_P", [128, 7 * 32], BF16)
        En = B.sbt("a_En", [64, 32]); Pn = B.sbt("a_Pn", [64, 32], BF16)
        accs = B.sbt("a_acc", [65, 8, 64])
        B.memset(POOL, va.t[:], 0.0, [va])
        B.memset(POOL, va.t[:, :, :, 64:65], 1.0, [va])
        pacc = pb[6]
        pa3 = pacc.t[0:65, :].rearrange("p (h t) -> p h t", h=8)
        it = 0
        for b in range(16):
            for j in range(7):
                k_ = kf[it % 2]; v_ = vf[it % 2]; kb_ = kb[it % 2]
                it += 1
                if j < 4:
                    B.dma(SP, k_.t[:], B.cwk.t.ap()[b, 1536 + 128 * j:1536 + 128 * (j + 1), :], [B.cwk], [k_])
                    B.dma(POOL, v_.t[:], B.cwv.t.ap()[b, 1536 + 128 * j:1536 + 128 * (j + 1), :], [B.cwv], [v_])
                else:
                    m0 = 32 * (j - 4)
                    for r_ in range(4):
                        B.dma(SP, k_.t[32 * r_:32 * r_ + 32, :], B.cwk.t.ap()[b, 16 * m0 + r_:16 * (m0 + 32):16, :], [B.cwk], [k_])
                        B.dma(POOL, v_.t[32 * r_:32 * r_ + 32, :], B.cwv.t.ap()[b, 16 * m0 + r_:16 * (m0 + 32):16, :], [B.cwv], [v_])
                B.cp(ACT, kb_.t[:], k_.t[:], [k_], [kb_])
                B.cp(POOL, va.t[:, j, :, 0:64], v_.t[:].rearrange("p (a b) -> p a b", a=8), [v_], [va])
                for p_ in range(4):
                    B.tr(B.psTb.t[:, p_, :], kb_.t[:, p_ * 128:(p_ + 1) * 128], B.identb.t[:], [kb_, B.identb], [B.psTb])
                B.cp(DVE, kT.t[:, :, j * 128:(j + 1) * 128], B.psTb.t[:, 0:4, :], [B.psTb], [kT])
            pS = pb[1 + b % 2]
            for j in range(7):
                for h in range(8):
                    lo = 64 * (h % 2)
                    B.mm(pS.t[:, j * 32 + h * 4:j * 32 + h * 4 + 4], kT.t[lo:lo + 64, h // 2, j * 128:(j + 1) * 128],
                         z["qaT"].t[lo:lo + 64, h // 2, 4 * b:4 * b + 4], [kT, z["qaT"]], [pS])
            B.act(E.t[:], pS.t[:, 0:224], AF.Exp, [pS], [E], scale=0.125)
            B.tt(DVE, Pb.t[:], E.t[:], smask.t[:], ALU.mult, [E, smask], [Pb])
            pN = pb[3 + b % 2]
            for h in range(8):
                lo = 64 * (h % 2)
                B.mm(pN.t[0:64, h * 4:h * 4 + 4], z["kaT"].t[lo:lo + 64, h // 2, :], z["qaT"].t[lo:lo + 64, h // 2, 4 * b:4 * b + 4],
                     [z["kaT"], z["qaT"]], [pN])
            B.act(En.t[:], pN.t[0:64, 0:32], AF.Exp, [pN], [En], scale=0.125)
            B.tt(DVE, Pn.t[:], En.t[:], snew.t[0:64, b * 32:(b + 1) * 32], ALU.mult, [En, snew], [Pn])
            for h in range(8):
                for j in range(7):
                    B.mm(pa3[:, h, 4 * b:4 * b + 4], va.t[:, j, h, 0:65], Pb.t[:, j * 32 + h * 4:j * 32 + h * 4 + 4], [va, Pb], [pacc],
                         start=(j == 0), stop=False)
                B.mm(pa3[:, h, 4 * b:4 * b + 4], z["vaug"].t[:, h, 0:65], Pn.t[:, h * 4:h * 4 + 4], [z["vaug"], Pn], [pacc], start=False, stop=True)
        B.cp(DVE, accs.t[:], pa3, [pacc], [accs])
        B.mm(pb[5].t[0:64, :], B.onesf.t[64:65, 0:64], accs.t[64:65, :, :].rearrange("p h t -> p (h t)"), [B.onesf, accs], [pb[5]])
        rec = B.sbt("a_rec", [64, 512])
        B.P.op(DVE, lambda e: e.reciprocal(out=rec.t[:], in_=pb[5].t[0:64, :]), [pb[5].b], [rec.b])
        B.tt(DVE, z["oaT"].t[:].rearrange("p h t -> p (h t)"), accs.t[0:64, :, :].rearrange("p h t -> p (h t)"), rec.t[:], ALU.mult, [accs, rec], [z["oaT"]])
        mf = [B.sbt("c_mf%d" % i, [128, 256]) for i in range(2)]
        mb = [B.sbt("c_mb%d" % i, [128, 256], BF16) for i in range(2)]
        mvf = [B.sbt("c_mvf%d" % i, [128, 256]) for i in range(2)]
        mkT = B.sbt("c_mkT", [128, 2, 256], BF16)
        mva = B.sbt("c_mva", [128, 2, 4, 80], BF16)
        Ec = B.sbt("c_E", [128, 32], BF16)
        accc = B.sbt("c_acc", [65, 4, 64])
        B.memset(POOL, mva.t[:], 0.0, [mva])
        B.memset(POOL, mva.t[:, :, :, 64:65], 1.0, [mva])
        pacc = pb[6]
        pc3 = pacc.t[0:65, 0:256].rearrange("p (h t) -> p h t", h=4)
        it = 0
        for b in range(16):
            for mt in range(2):
                f_ = mf[it % 2]; b_ = mb[it % 2]; v_ = mvf[it % 2]
                it += 1
                B.dma(SP, f_.t[:], B.cmk.t.ap()[b, mt * 128:(mt + 1) * 128, :], [B.cmk], [f_])
                B.dma(POOL, v_.t[:], B.cmv.t.ap()[b, mt * 128:(mt + 1) * 128, :], [B.cmv], [v_])
                B.cp(ACT, b_.t[:], f_.t[:], [f_], [b_])
                B.cp(POOL, mva.t[:, mt, :, 0:64], v_.t[:].rearrange("p (a b) -> p a b", a=4), [v_], [mva])
                for p_ in range(2):
                    B.tr(B.psTb.t[:, 4 + p_, :], b_.t[:, p_ * 128:(p_ + 1) * 128], B.identb.t[:], [b_, B.identb], [B.psTb])
                B.cp(DVE, mkT.t[:, :, mt * 128:(mt + 1) * 128], B.psTb.t[:, 4:6, :], [B.psTb], [mkT])
            pS = pb[1 + b % 2]
            for mt in range(2):
                for h in range(4):
                    lo = 64 * (h % 2)
                    B.mm(pS.t[:, mt * 16 + h * 4:mt * 16 + h * 4 + 4], mkT.t[lo:lo + 64, h // 2, mt * 128:(mt + 1) * 128],
                         z["qcT"].t[lo:lo + 64, h // 2, 4 * b:4 * b + 4], [mkT, z["qcT"]], [pS])
            B.act(Ec.t[:], pS.t[:, 0:32], AF.Exp, [pS], [Ec], scale=0.125)
            for h in range(4):
                for mt in range(2):
                    B.mm(pc3[:, h, 4 * b:4 * b + 4], mva.t[:, mt, h, 0:65], Ec.t[:, mt * 16 + h * 4:mt * 16 + h * 4 + 4], [mva, Ec], [pacc],
                         start=(mt == 0), stop=(mt == 1))
        B.cp(DVE, accc.t[:], pc3, [pacc], [accc])
        B.mm(pb[5].t[0:64, 0:256], B.onesf.t[64:65, 0:64], accc.t[64:65, :, :].rearrange("p h t -> p (h t)"), [B.onesf, accc], [pb[5]])
        B.P.op(DVE, lambda e: e.reciprocal(out=rec.t[:, 0:256], in_=pb[5].t[0:64, 0:256]), [pb[5].b], [rec.b])
        B.tt(DVE, z["ocT"].t[:].rearrange("p h t -> p (h t)"), accc.t[0:64, :, :].rearrange("p h t -> p (h t)"), rec.t[:, 0:256], ALU.mult, [accc, rec], [z["ocT"]])
        B.gdn_consts_s()
        B.gdn_alloc("h_")
        B.gsets = {0: B.gs}
        s = B.gs
        B.gm = {"triu": sc["s_triu"], "trils": sc["s_trils"], "negc": sc["s_negc"], "strictf": sc["s_strict"]}
        stf = B.sbt("h_stf", [48, 768])
        qx = B.sbt("h_qx", [128, 6, 16, 7])
        Ss = B.sbt("h_Ss", [128, 16, 2, 64]); Ssb = B.sbt("h_Ssb", [128, 16, 2, 64], BF16)
        wTm = B.sbt("h_wTm", [128, 2, 16, 64], BF16); qTm = B.sbt("h_qTm", [128, 2, 16, 64], BF16)
        kdm = B.sbt("h_kdm", [128, 16, 256], BF16)
        gmk = B.sbt("h_gmk", [128, 16, 4]); egls = B.sbt("h_egls", [128, 16, 2])
        for p_ in range(2):
            B.dma(SP, Ss.t[:, :, p_, :], B.sgdn.t.ap()[:, 2 * p_:2 * p_ + 2, :, :].rearrange("b h k v -> (h k) b v"), [B.sgdn], [Ss])
        B.cp(ACT, Ssb.t[:], Ss.t[:], [Ss], [Ssb])
        B.dma(SP, stf.t[:], B.sgconv.t.ap().rearrange("b r n -> (b r) n"), [B.sgconv], [stf])

        def conv_fn():
            for c in range(6):
                B.mm(pb[3].t[:, c * 48:(c + 1) * 48], stf.t[:, c * 128:(c + 1) * 128], B.identf.t[0:48, 0:48], [stf, B.identf], [pb[3]])
            B.cp(DVE, qx.t[:, :, :, 0:3], pb[3].t[:, 0:288].rearrange("p (c b r) -> p c b r", c=6, b=16), [pb[3]], [qx])
            B.cp(POOL, qx.t[:, :, :, 3:7], z["qkvT"].t[:].rearrange("p c (b s) -> p c b s", b=16), [z["qkvT"]], [qx])
            B.memset(POOL, s["ca"].t[:], 0.0, [s["ca"]])
            for c in range(6):
                cav = s["ca"].t[:, c, 0:64].rearrange("p (b s) -> p b s", b=16)
                B.ts(DVE, cav, qx.t[:, c, :, 0:4], B.cw.t[:, c, 0:1], None, ALU.mult, None, [qx, B.cw], [s["ca"]])
                for k in range(1, 4):
                    B.stt(DVE, cav, qx.t[:, c, :, k:k + 4], B.cw.t[:, c, k:k + 1], cav, ALU.mult, ALU.add, [qx, B.cw, s["ca"]], [s["ca"]])
            B.cp(DVE, s["ab"].t[:], z["ab"].t[:], [z["ab"]], [s["ab"]])

        def state_fn(psB, psD, gate_fn):
            bm4 = bm.t[:].rearrange("p (b t) -> p b t", b=16)
            for p_ in range(2):
                B.tt(DVE, wTm.t[:, p_, :, :], bc(s["wT"].t[:, p_, 0:64], [128, 16, 64], 1), bm4, ALU.mult, [s["wT"], bm], [wTm])
                B.tt(DVE, qTm.t[:, p_, :, :], bc(s["qT"].t[:, p_, 0:64], [128, 16, 64], 1), bm4, ALU.mult, [s["qT"], bm], [qTm])
            pD = psD.t[0:64, 0:256].rearrange("p (a b) -> p a b", a=4)
            for h in range(4):
                lo = 64 * (h % 2)
                for b in range(16):
                    B.mm(pD[:, h, :], wTm.t[lo:lo + 64, h // 2, b, :], Ssb.t[lo:lo + 64, b, h // 2, :], [wTm, Ssb], [psD], start=(b == 0), stop=(b == 15))
            B.memset(POOL, s["vnew"].t[:], 0.0, [s["vnew"]])
            B.tt(DVE, s["vnew"].t[0:64, :, :], s["ub"].t[0:64, :, :], pD, ALU.subtract, [s["ub"], psD], [s["vnew"]])
            pQ = psB.t[0:64, 0:256].rearrange("p (a b) -> p a b", a=4)
            for h in range(4):
                lo = 64 * (h % 2)
                for b in range(16):
                    B.mm(pQ[:, h, :], qTm.t[lo:lo + 64, h // 2, b, :], Ssb.t[lo:lo + 64, b, h // 2, :], [qTm, Ssb], [psB], start=(b == 0), stop=(b == 15))
            B.memset(POOL, s["o"].t[:], 0.0, [s["o"]])
            B.tt(DVE, s["o"].t[0:64, :, :], pQ, bc(s["egc"].t[0:64, :], [64, 4, 64], 2), ALU.mult, [psB, s["egc"]], [s["o"]])
            pV = psD.t[:, 256:512].rearrange("p (a b) -> p a b", a=4)
            for h in range(4):
                B.mm(pV[:, h, :], s["AT"].t[:, h, :], s["vnew"].t[:, h, :], [s["AT"], s["vnew"]], [psD])
            B.tt(DVE, s["o"].t[0:64, :, :], s["o"].t[0:64, :, :], pV[0:64, :, :], ALU.add, [psD, s["o"]], [s["o"]])
            B.tt(DVE, gmk.t[:], bc(s["g"].t[:], [128, 16, 4], 1), bc(sc["s_rm"].t[:], [128, 16, 4], 2), ALU.mult, [s["g"], sc["s_rm"]], [gmk])
            B.mm(pb[3].t[:, 0:64], B.onesf.t[:], gmk.t[:].rearrange("p b h -> p (b h)"), [B.onesf, gmk], [pb[3]])
            g3 = pb[3].t[:, 0:64].rearrange("p (b h) -> p b h", b=16)
            B.act(egls.t[0:64, :, :], g3[0:64, :, 0:4:2], AF.Exp, [pb[3]], [egls])
            B.act(egls.t[64:128, :, :], g3[64:128, :, 1:4:2], AF.Exp, [pb[3]], [egls])
            B.tt(DVE, kdm.t[:], bc(s["kd"].t[:].rearrange("p a b -> p (a b)"), [128, 16, 256], 1), bc(sc["s_rm"].t[:], [128, 16, 256], 2), ALU.mult,
                 [s["kd"], sc["s_rm"]], [kdm])
            B.tt(DVE, Ss.t[:].rearrange("p b q v -> p (b q) v"), Ss.t[:].rearrange("p b q v -> p (b q) v"),
                 bc(egls.t[:].rearrange("p b q -> p (b q)"), [128, 32, 64], 2), ALU.mult, [Ss, egls], [Ss])
            for b in range(16):
                pk = pb[4 + b % 2]
                pk3 = pk.t[:, 0:256].rearrange("p (a b) -> p a b", a=4)
                for h in range(4):
                    B.mm(pk3[:, h, :], kdm.t[:, b, (h // 2) * 128:(h // 2 + 1) * 128], s["vnew"].t[:, h, :], [kdm, s["vnew"]], [pk])
                B.tt(DVE, Ss.t[0:64, b, :, :], Ss.t[0:64, b, :, :], pk3[0:64, 0:4:2, :], ALU.add, [Ss, pk], [Ss])
                B.tt(DVE, Ss.t[64:128, b, :, :], Ss.t[64:128, b, :, :], pk3[64:128, 1:4:2, :], ALU.add, [Ss, pk], [Ss])
            for p_ in range(2):
                B.dma(SP, B.o_gss.t.ap()[:, 2 * p_:2 * p_ + 2, :, :].rearrange("b h k v -> (h k) b v"), Ss.t[:, :, p_, :], [Ss], [B.o_gss])
            B.tt(DVE, s["o2"].t[:], s["o"].t[:], s["o"].t[:], ALU.mult, [s["o"]], [s["o2"]])
            B.P.op(DVE, lambda e: e.tensor_reduce(out=s["ms"].t[:], in_=s["o2"].t[:], axis=AX.X, op=ALU.add), [s["o2"].b], [s["ms"].b])
            B.act(s["ms"].t[:], s["ms"].t[:], AF.Sqrt, [s["ms"], B.epsc], [s["ms"]], bias=B.epsc.t[:], scale=1.0 / 64)
            B.P.op(DVE, lambda e: e.reciprocal(out=s["ms"].t[:], in_=s["ms"].t[:]), [s["ms"].b], [s["ms"].b])
            B.tt(DVE, s["o"].t[:], s["o"].t[:], bc(s["ms"].t[:], [128, 4, 64], 2), ALU.mult, [s["o"], s["ms"]], [s["o"]])
            B.tt(DVE, s["o"].t[:], s["o"].t[:], bc(B.onw.t[:], [128, 4, 64], 1), ALU.mult, [s["o"], B.onw], [s["o"]])
            B.act(s["gs"].t[:], z["gate"].t[:], AF.Silu, [z["gate"]], [s["gs"]])
            B.tt(DVE, s["obf"].t[:], s["o"].t[:].rearrange("p a b -> p (a b)"), s["gs"].t[:], ALU.mult, [s["o"], s["gs"]], [s["obf"]])
            for p_ in range(2):
                B.tr(B.psTb.t[:, p_, :], s["obf"].t[:, p_ * 128:(p_ + 1) * 128], B.identb.t[:], [s["obf"], B.identb], [B.psTb])
            B.cp(DVE, z["obT"].t[:], B.psTb.t[:, 0:2, 0:64], [B.psTb], [z["obT"]])

        for _ in B.gdn_tile(None, None, pb[3], pb[4], pb[5], pb[6], want_out=True, gate_fn=None, smode={"conv_fn": conv_fn, "state_fn": state_fn}, par=0):
            pass
        B.gm = {"triu": B.triu, "trils": B.trils, "negc": B.negc, "strictf": B.strictf}

    def gdn_consts_s(self):
        B = self
        cw = B.ins["conv_b_w"]; alog = B.ins["a_log_b"]; dtb = B.ins["dt_bias_b"]; onb = B.ins["out_norm_b"]
        B.cw = B.sbt("s_cw", [128, 6, 4])
        for k in range(4):
            B.dma(SP, B.cw.t[:, :, k], cw.t.ap()[k, :].rearrange("(c p) -> p c", p=128), [cw], [B.cw], allow_slow_non_contiguous=True)
        B.nega = B.sbt("s_nega", [128, 4])
        B.dma(SP, B.nega.t[:], alog.t.ap(), [alog], [B.nega])
        B.act(B.nega.t[:], B.nega.t[:], AF.Exp, [B.nega], [B.nega])
        B.ts(DVE, B.nega.t[:], B.nega.t[:], -1.0, None, ALU.mult, None, [B.nega], [B.nega])
        B.dtb = B.sbt("s_dtb", [128, 4])
        B.dma(SP, B.dtb.t[:], dtb.t.ap(), [dtb], [B.dtb])
        B.onw = B.sbt("s_onw", [128, 64])
        B.dma(SP, B.onw.t[:], onb.t.ap(), [onb], [B.onw])
```

```python
import numpy as np
import concourse.bass as bass
import concourse.mybir as mybir
from concourse.bass_utils import run_bass_kernel_spmd

F32 = mybir.dt.float32
BF16 = mybir.dt.bfloat16
AF = mybir.ActivationFunctionType
ALU = mybir.AluOpType
AX = mybir.AxisListType

PE, ACT, DVE, POOL, SP = 0, 1, 2, 3, 4
ENG_NAMES = ["tensor", "scalar", "vector", "gpsimd", "sync"]
NDSEM = 12
PE_NOSELF = False
ACC_NOSELF = True
MM_NOSELF = False

NCORES = 8
D = 1024
SEQ = 16384
OWN = 2048
N_IN = 2824
C_QA, C_KA, C_VA, C_QKV, C_GATE, C_A, C_B, C_QC = 0, 512, 1024, 1536, 2304, 2560, 2564, 2568
D_FF = 2816
EPS = 1e-6
NEGV = -30000.0


class Buf:
    __slots__ = ("name", "w", "r", "excl")

    def __init__(self, name=""):
        self.name = name
        self.w = None
        self.r = []
        self.excl = False


class Prog:
    def __init__(self, nc):
        self.nc = nc
        self.q = [[] for _ in range(5)]
        self.cnt = [0] * 5
        self.dcnt = [0] * 5
        self.seen = [dict() for _ in range(5)]
        self.sems = {}
        self._semstack = []
        self._stack = []
        self.ninstr = 0

    def sb(self, name, shape, dt=F32):
        g = self.nc.sbuf_tensor(name, list(shape), dt)
        t = g.__enter__()
        self._stack.append(g)
        return t

    def ps(self, name, shape, dt=F32):
        g = self.nc.psum_tensor(name, list(shape), dt)
        t = g.__enter__()
        self._stack.append(g)
        return t

    def mark(self):
        return len(self._stack)

    def release(self, mark):
        while len(self._stack) > mark:
            g = self._stack.pop()
            g.__exit__(None, None, None)

    def sem(self, key):
        if key not in self.sems:
            g = self.nc.semaphore("s_" + "_".join(str(k) for k in key))
            self.sems[key] = g.__enter__()
            self._semstack.append(g)
        return self.sems[key]

    def _deps(self, eng, reads, writes, noself=False):
        need = {}
        for b in reads:
            if b.w is not None:
                k, v = b.w
                if need.get(k, 0) < v:
                    need[k] = v
        for b in writes:
            if b.w is not None:
                k, v = b.w
                if need.get(k, 0) < v:
                    need[k] = v
            for (k, v) in b.r:
                if need.get(k, 0) < v:
                    need[k] = v
        waits = []
        seen = self.seen[eng]
        for k, v in need.items():
            if (PE_NOSELF or noself) and eng == PE and k == ("e", PE):
                continue
            if seen.get(k, 0) >= v:
                continue
            seen[k] = v
            waits.append((k, v))
        return waits

    def _commit(self, tok, reads, writes):
        for b in reads:
            b.r = [x for x in b.r if x[0] != tok[0]]
            b.r.append(tok)
        for b in writes:
            b.w = tok
            b.r = []

    def op(self, eng, fn, reads=(), writes=(), noself=False):
        if eng != PE:
            ex = [b for b in reads if b.excl]
            if ex:
                reads = [b for b in reads if not b.excl]
                writes = list(writes) + [b for b in ex if b not in writes]
        waits = self._deps(eng, reads, writes, noself)
        self.cnt[eng] += 1
        key = ("e", eng)
        tok = (key, self.cnt[eng])
        self.q[eng].append((fn, waits, (key, 1)))
        self._commit(tok, reads, writes)
        self.ninstr += 1
        return tok

    def dma(self, eng, fn, reads=(), writes=()):
        i = self.dcnt[eng]
        self.dcnt[eng] += 1
        slot = i % NDSEM
        key = ("d", eng, slot)
        val = 16 * (i // NDSEM + 1)
        waits = self._deps(eng, reads, writes)
        if i >= NDSEM:
            pv = val - 16
            if self.seen[eng].get(key, 0) < pv:
                self.seen[eng][key] = pv
                waits.append((key, pv))
        tok = (key, val)
        self.q[eng].append((fn, waits, (key, 16)))
        self._commit(tok, reads, writes)
        self.ninstr += 1
        return tok

    def barrier(self):
        targets = []
        for e in range(5):
            if self.cnt[e] > 0:
                targets.append((("e", e), self.cnt[e]))
            n = self.dcnt[e]
            for slot in range(min(n, NDSEM)):
                last = ((n - 1 - slot) // NDSEM) * NDSEM + slot
                targets.append((("d", e, slot), 16 * (last // NDSEM + 1)))
        for e in range(5):
            waits = []
            for (k, v) in targets:
                if k == ("e", e):
                    continue
                if self.seen[e].get(k, 0) < v:
                    self.seen[e][k] = v
                    waits.append((k, v))
            self.q[e].append((None, waits, None))

    def wait_all(self, eng, bufs):
        waits = self._deps(eng, bufs, ())
        self.q[eng].append((None, waits, None))

    def emit(self):
        nc = self.nc
        for e in range(5):
            for (fn, waits, inc) in self.q[e]:
                for (k, v) in waits:
                    self.sem(k)
                if inc is not None:
                    self.sem(inc[0])
        with nc.Block() as block:
            def mk(e):
                def body(engine):
                    for (fn, waits, inc) in self.q[e]:
                        for (k, v) in waits:
                            engine.wait_ge(self.sems[k], v)
                        if fn is not None:
                            ins = fn(engine)
                            ins.then_inc(self.sems[inc[0]], inc[1])
                return body
            for e, nm in enumerate(ENG_NAMES):
                if self.q[e]:
                    getattr(block, nm)(mk(e))
        self.q = [[] for _ in range(5)]

    def close(self):
        self.release(0)
        while self._semstack:
            self._semstack.pop().__exit__(None, None, None)


class T:
    __slots__ = ("t", "b")

    def __init__(self, t, name=""):
        self.t = t
        self.b = Buf(name)


def _consts():
    c = {}
    c["ident"] = np.eye(128, dtype=np.float32)
    j = np.arange(128)[:, None]
    i = np.arange(128)[None, :]
    c["triu"] = (j <= i).astype(np.float32)
    c["trils"] = (j > i).astype(np.float32)
    c["negc"] = np.where(i >= j, 0.0, NEGV).astype(np.float32)
    c["strict"] = (i > j).astype(np.float32)
    c["causal"] = (i >= j).astype(np.float32)
    c["ones"] = np.ones((128, 128), np.float32)
    bd = np.zeros((128, 128), np.float32)
    bd[:64, :64] = 1
    bd[64:, 64:] = 1
    c["bd"] = bd
    return c


class Builder:
    skip = ()
    NSETS = 3

    def __init__(self, NT=128, dbg=False):
        self.NT = NT
        self.L = NT * 128
        self.dbg = dbg
        self.nc = bass.Bass("TRN2", target_bir_lowering=False)
        self.P = Prog(self.nc)
        self.ins = {}
        self.outs = {}
        self.outbufs = []

    def inp(self, name, shape, dt=F32):
        t = self.nc.dram_tensor(name, list(shape), dt, kind="ExternalInput")
        self.ins[name] = T(t, name)
        return self.ins[name]

    def outp(self, name, shape, dt=F32):
        t = self.nc.dram_tensor(name, list(shape), dt, kind="ExternalOutput")
        self.outs[name] = T(t, name)
        self.outbufs.append(self.outs[name].b)
        return self.outs[name]

    def scratch(self, name, shape, dt=F32):
        t = self.nc.dram_tensor(name, list(shape), dt)
        return T(t, name)

    def sbt(self, name, shape, dt=F32):
        return T(self.P.sb(name, shape, dt), name)

    def pst(self, name, shape, dt=F32):
        t = T(self.P.ps(name, shape, dt), name)
        t.b.excl = True
        return t

    def dma(self, q, out, in_, r, w, **kw):
        self.P.dma(q, lambda e: e.dma_start(out=out, in_=in_, **kw), [x.b for x in r], [x.b for x in w])

    def mm(self, out, lhsT, rhs, r, w, start=True, stop=True):
        self.P.op(PE, lambda e: e.matmul(out, lhsT=lhsT, rhs=rhs, start=start, stop=stop),
                  [x.b for x in r], [x.b for x in w], noself=((not start) and ACC_NOSELF) or MM_NOSELF)

    def tr(self, out, in_, ident, r, w):
        self.P.op(PE, lambda e: e.transpose(out=out, in_=in_, identity=ident), [x.b for x in r], [x.b for x in w])

    def act(self, out, in_, func, r, w, eng=ACT, **kw):
        self.P.op(ACT, lambda e: e.activation(out=out, in_=in_, func=func, **kw), [x.b for x in r], [x.b for x in w])

    def cp(self, eng, out, in_, r, w):
        if eng == ACT:
            self.P.op(ACT, lambda e: e.activation(out=out, in_=in_, func=AF.Copy), [x.b for x in r], [x.b for x in w])
        else:
            self.P.op(eng, lambda e: e.tensor_copy(out=out, in_=in_), [x.b for x in r], [x.b for x in w])

    def tt(self, eng, out, in0, in1, op, r, w):
        if eng == POOL and getattr(self, "nopool", False):
            eng = DVE
        self.P.op(eng, lambda e: e.tensor_tensor(out=out, in0=in0, in1=in1, op=op), [x.b for x in r], [x.b for x in w])

    def ts(self, eng, out, in0, s1, s2, op0, op1, r, w):
        if eng == POOL and getattr(self, "nopool", False):
            eng = DVE
        if op1 is None:
            self.P.op(eng, lambda e: e.tensor_scalar(out=out, in0=in0, scalar1=s1, scalar2=None, op0=op0),
                      [x.b for x in r], [x.b for x in w])
        else:
            self.P.op(eng, lambda e: e.tensor_scalar(out=out, in0=in0, scalar1=s1, scalar2=s2, op0=op0, op1=op1),
                      [x.b for x in r], [x.b for x in w])

    def stt(self, eng, out, in0, scalar, in1, op0, op1, r, w):
        eng = DVE
        self.P.op(eng, lambda e: e.scalar_tensor_tensor(out=out, in0=in0, scalar=scalar, in1=in1, op0=op0, op1=op1),
                  [x.b for x in r], [x.b for x in w])

    def memset(self, eng, ap, val, w):
        self.P.op(eng, lambda e: e.memset(ap, val), [], [x.b for x in w])

    def rsqrt_inplace(self, t, ap, scale, r_extra=()):
        self.act(ap, ap, AF.Sqrt, [t, self.epsc] + list(r_extra), [t], scale=scale, bias=self.epsc.t[0:ap.shape[0], :])
        self.P.op(DVE, lambda e: e.reciprocal(out=ap, in_=ap), [t.b], [t.b])


def bc(ap, shape, axis):
    return ap.unsqueeze(axis).to_broadcast(list(shape))


class Builder(Builder):
    def setup_consts(self):
        B = self
        cn = _consts()
        self.cnp = cn
        for k, v in cn.items():
            B.inp("c_" + k, v.shape)
        B.epsc = B.sbt("epsc", [128, 1])
        B.memset(POOL, B.epsc.t[:], EPS, [B.epsc])
        B.onec = B.sbt("onec", [128, 1])
        B.memset(POOL, B.onec.t[:], 1.0, [B.onec])
        def ld(name, dt=F32):
            t = B.sbt("k_" + name + ("b" if dt == BF16 else "f"), [128, 128], dt)
            q = POOL if dt == BF16 else SP
            B.dma(q, t.t[:], B.ins["c_" + name].t.ap(), [B.ins["c_" + name]], [t])
            return t
        B.identf = ld("ident")
        B.identb = ld("ident", BF16)
        B.triu = ld("triu")
        B.trils = ld("trils")
        B.negc = ld("negc")
        B.strictf = ld("strict")
        B.causalf = ld("causal")
        B.onesf = ld("ones")
        B.bdb = ld("bd", BF16)
        B.gm = {"triu": B.triu, "trils": B.trils, "negc": B.negc, "strictf": B.strictf, "causal": B.causalf}

    def setup_weights_in(self):
        B = self
        w_in = B.inp("w_in", [D, N_IN])
        g1 = B.inp("norm1_g", [D])
        B.Wb = B.sbt("Wb", [128, 8, N_IN], BF16)
        B.g1 = B.sbt("g1", [128, 8])
        B.dma(SP, B.g1.t[:], g1.t.ap().rearrange("(k p) -> p k", p=128), [g1], [B.g1], allow_slow_non_contiguous=True)
        m = B.P.mark()
        st = [B.sbt("wst%d" % i, [128, 512]) for i in range(2)]
        i = 0
        for kc in range(8):
            for c0 in range(0, N_IN, 512):
                cn = min(512, N_IN - c0)
                s_ = st[i % 2]
                B.dma(SP if i % 2 == 0 else POOL, s_.t[:, 0:cn], w_in.t.ap()[kc * 128:(kc + 1) * 128, c0:c0 + cn], [w_in], [s_])
                if i % 2 == 0:
                    B.act(B.Wb.t[:, kc, c0:c0 + cn], s_.t[:, 0:cn], AF.Copy, [s_, B.g1], [B.Wb], scale=B.g1.t[:, kc:kc + 1])
                else:
                    B.ts(DVE, B.Wb.t[:, kc, c0:c0 + cn], s_.t[:, 0:cn], B.g1.t[:, kc:kc + 1], None, ALU.mult, None, [s_, B.g1], [B.Wb])
                i += 1
        return m

    def gdn_consts(self):
        B = self
        cw = B.inp("conv_b_w", [4, 768])
        alog = B.inp("a_log_b", [128, 4])
        dtb = B.inp("dt_bias_b", [128, 4])
        onb = B.inp("out_norm_b", [128, 64])
        B.cw = B.sbt("cw", [128, 6, 4])
        for k in range(4):
            B.dma(SP, B.cw.t[:, :, k], cw.t.ap()[k, :].rearrange("(c p) -> p c", p=128), [cw], [B.cw], allow_slow_non_contiguous=True)
        B.nega = B.sbt("nega", [128, 4])
        B.dma(SP, B.nega.t[:], alog.t.ap(), [alog], [B.nega])
        B.act(B.nega.t[:], B.nega.t[:], AF.Exp, [B.nega], [B.nega])
        B.ts(DVE, B.nega.t[:], B.nega.t[:], -1.0, None, ALU.mult, None, [B.nega], [B.nega])
        B.dtb = B.sbt("dtb", [128, 4])
        B.dma(SP, B.dtb.t[:], dtb.t.ap(), [dtb], [B.dtb])
        B.onw = B.sbt("onw", [128, 64])
        B.dma(SP, B.onw.t[:], onb.t.ap(), [onb], [B.onw])

    def gdn_alloc(self, pfx="g_", share=None):
        B = self
        s = {}
        def a(name, shape, dt=F32):
            if share is not None and name in share:
                s[name] = share[name]
            else:
                s[name] = B.sbt(pfx + name, shape, dt)
        a("qkvx", [128, 6, 131]); a("ca", [128, 6, 128]); a("cs", [128, 6, 128])
        a("sqb", [128, 4, 128], BF16); a("rn", [128, 4, 128])
        a("qT", [128, 2, 128], BF16); a("kT", [128, 2, 128], BF16); a("vTb", [128, 2, 128], BF16)
        a("ktok", [128, 4, 64]); a("vtok", [128, 4, 64])
        a("ab", [128, 8]); a("beta", [128, 4]); a("g", [128, 4]); a("ng", [128, 4])
        a("gc", [128, 4]); a("rev", [128, 4]); a("egc", [128, 4]); a("erev", [128, 4])
        a("G1", [128, 4, 128]); a("G2", [128, 4, 128])
        a("Ec", [128, 4, 128]); a("EsB", [128, 4, 128])
        a("X", [128, 4, 128]); a("AT", [128, 4, 128], BF16)
        s["Nn"] = s["G1"]; s["IX"] = s["G2"]
        yv = T(s["ca"].t[:, 0:4, :]); yv.b = s["ca"].b
        s["Y"] = yv
        a("ub", [128, 4, 64]); a("wb", [128, 4, 64], BF16); a("wT", [128, 2, 128], BF16)
        a("kd", [128, 4, 64], BF16); a("vnew", [128, 4, 64], BF16)
        a("S", [128, 2, 64]); a("Sb", [128, 2, 64], BF16); a("eglp", [128, 2])
        a("o", [128, 4, 64]); a("o2", [128, 4, 64]); a("ms", [128, 4]); a("gs", [128, 256]); a("obf", [128, 256], BF16)
        B.gs = s
        if not hasattr(B, "gsets"):
            B.gsets = {}
        B.memset(POOL, s["qkvx"].t[:], 0.0, [s["qkvx"]])
        B.memset(POOL, s["ca"].t[:], 0.0, [s["ca"]])
        B.memset(POOL, s["qT"].t[:], 0.0, [s["qT"]])
        if share is None or "S" not in share:
            B.memset(POOL, s["S"].t[:], 0.0, [s["S"]])
            B.memset(POOL, s["Sb"].t[:], 0.0, [s["Sb"]])

    def gdn_tile(self, psG, psAB, psA, psB, psC, psD, want_out, gate_fn=None, smode=None, par=0, first=False):
        B = self
        s = B.gsets[par]
        n = 128
        qkvx, ca, cs = s["qkvx"], s["ca"], s["cs"]
        if smode is None:
            if first:
                B.memset(POOL, qkvx.t[:, :, 0:3], 0.0, [qkvx])
            else:
                pq_ = B.gsets[(par - 1) % len(B.gsets)]["qkvx"]
                B.cp(POOL, qkvx.t[:, :, 0:3], pq_.t[:, :, 128:131], [pq_], [qkvx])
            B.cp(ACT, qkvx.t[:, 0:3, 3:131], psG[0].t[:], [psG[0]], [qkvx])
            B.cp(DVE, qkvx.t[:, 3:6, 3:131], psG[1].t[:], [psG[1]], [qkvx])
            B.cp(DVE, s["ab"].t[:], psAB.t[:, 0:8], [psAB], [s["ab"]])
        if getattr(B, 'stop_at', 99) <= 1:
            return
        yield
        if smode is not None:
            smode["conv_fn"]()
        for c in range(6 if smode is None else 0):
            if c < 2 and not want_out:
                continue
            eng = DVE if c < 3 else POOL
            B.ts(eng, ca.t[:, c, :], qkvx.t[:, c, 0:128], B.cw.t[:, c, 0:1], None, ALU.mult, None, [qkvx, B.cw], [ca])
            for k in range(1, 4):
                B.stt(eng, ca.t[:, c, :], qkvx.t[:, c, k:k + 128], B.cw.t[:, c, k:k + 1], ca.t[:, c, :], ALU.mult, ALU.add,
                      [qkvx, B.cw, ca], [ca])
        B.act(cs.t[:], ca.t[:], AF.Silu, [ca], [cs])
        if getattr(B, 'stop_at', 99) <= 2:
            return
        yield
        B.tt(POOL, s["sqb"].t[:], cs.t[:, 0:4, :], cs.t[:, 0:4, :], ALU.mult, [cs], [s["sqb"]])
        B.mm(psA.t[:], B.bdb.t[:], s["sqb"].t[:].rearrange("p a b -> p (a b)"), [B.bdb, s["sqb"]], [psA])
        B.act(s["rn"].t[:], psA.t[:].rearrange("p (a b) -> p a b", a=4), AF.Sqrt, [psA, B.epsc], [s["rn"]], bias=B.epsc.t[:], scale=1.0)
        B.P.op(DVE, lambda e: e.reciprocal(out=s["rn"].t[:], in_=s["rn"].t[:]), [s["rn"].b], [s["rn"].b])
        if want_out:
            B.stt(DVE, s["qT"].t[:], cs.t[:, 0:2, :], 0.125, s["rn"].t[:, 0:2, :], ALU.mult, ALU.mult, [cs, s["rn"]], [s["qT"]])
        B.tt(POOL, s["kT"].t[:], cs.t[:, 2:4, :], s["rn"].t[:, 2:4, :], ALU.mult, [cs, s["rn"]], [s["kT"]])
        B.cp(ACT, s["vTb"].t[:], cs.t[:, 4:6, :], [cs], [s["vTb"]])
        if getattr(B, 'stop_at', 99) <= 3:
            return
        yield
        pTb = B.psTb
        for a_ in range(2):
            B.tr(pTb.t[:, a_, :], s["kT"].t[:, a_, :], B.identb.t[:], [s["kT"], B.identb], [pTb])
            B.tr(pTb.t[:, 2 + a_, :], s["vTb"].t[:, a_, :], B.identb.t[:], [s["vTb"], B.identb], [pTb])
        if getattr(B, 'stop_at', 99) <= 3.1:
            return
        B.cp(DVE, s["ktok"].t[:].rearrange("p a b -> p (a b)"), pTb.t[:, 0:2, :].rearrange("p a b -> p (a b)"), [pTb], [s["ktok"]])
        if getattr(B, 'stop_at', 99) <= 3.2:
            return
        B.cp(DVE, s["vtok"].t[:].rearrange("p a b -> p (a b)"), pTb.t[:, 2:4, :].rearrange("p a b -> p (a b)"), [pTb], [s["vtok"]])
        if getattr(B, 'stop_at', 99) <= 4:
            return
        yield
        ab = s["ab"]
        B.act(s["beta"].t[:], ab.t[:, 4:8], AF.Sigmoid, [ab], [s["beta"]])
        B.tt(DVE, s["g"].t[:], ab.t[:, 0:4], B.dtb.t[:], ALU.add, [ab, B.dtb], [s["g"]])
        B.act(s["g"].t[:], s["g"].t[:], AF.Exp, [s["g"]], [s["g"]])
        B.act(s["g"].t[:], s["g"].t[:], AF.Ln, [s["g"], B.onec], [s["g"]], bias=B.onec.t[:], scale=1.0)
        B.tt(DVE, s["g"].t[:], s["g"].t[:], B.nega.t[:], ALU.mult, [s["g"], B.nega], [s["g"]])
        B.ts(DVE, s["ng"].t[:], s["g"].t[:], -1.0, None, ALU.mult, None, [s["g"]], [s["ng"]])
        if getattr(B, 'stop_at', 99) <= 5:
            return
        yield
        B.mm(psD.t[:, 0:4], B.gm["triu"].t[:], s["g"].t[:], [B.gm["triu"], s["g"]], [psD])
        B.mm(psD.t[:, 4:8], B.gm["trils"].t[:], s["g"].t[:], [B.gm["trils"], s["g"]], [psD])
        B.mm(psD.t[:, 8:12], B.onesf.t[:], s["g"].t[:], [B.onesf, s["g"]], [psD])
        B.act(s["egc"].t[:], psD.t[:, 0:4], AF.Exp, [psD], [s["egc"]])
        B.cp(ACT, s["gc"].t[:], psD.t[:, 0:4], [psD], [s["gc"]])
        B.act(s["erev"].t[:], psD.t[:, 4:8], AF.Exp, [psD], [s["erev"]])
        B.act(s["eglp"].t[0:64, :], psD.t[0:64, 8:12:2], AF.Exp, [psD], [s["eglp"]])
        B.act(s["eglp"].t[64:128, :], psD.t[64:128, 9:12:2], AF.Exp, [psD], [s["eglp"]])
        if getattr(B, 'stop_at', 99) <= 6:
            return
        yield
        B.tt(DVE, s["G1"].t[:], bc(B.onesf.t[:], [128, 4, 128], 1), bc(s["g"].t[:], [128, 4, 128], 2), ALU.mult, [B.onesf, s["g"]], [s["G1"]])
        pC = psC.t[:].rearrange("p (a b) -> p a b", a=4)
        for h in range(4):
            B.mm(pC[:, h, :], s["G1"].t[:, h, :], B.gm["triu"].t[:], [s["G1"], B.gm["triu"]], [psC], start=True, stop=True)
        B.tt(DVE, s["G2"].t[:], pC, bc(s["gc"].t[:], [128, 4, 128], 2), ALU.subtract, [psC, s["gc"]], [s["G2"]])
        B.ts(DVE, s["G2"].t[:], s["G2"].t[:], 0.0, None, ALU.min, None, [s["G2"]], [s["G2"]])
        B.act(s["Ec"].t[:], s["G2"].t[:], AF.Exp, [s["G2"]], [s["Ec"]])
        B.tt(POOL, s["Ec"].t[:], s["Ec"].t[:], bc(B.gm["causal"].t[:], [128, 4, 128], 1), ALU.mult, [s["Ec"], B.gm["causal"]], [s["Ec"]])
        if getattr(B, 'stop_at', 99) <= 7:
            return
        yield
        B.tt(POOL, s["EsB"].t[:], s["Ec"].t[:], bc(B.gm["strictf"].t[:], [128, 4, 128], 1), ALU.mult, [s["Ec"], B.gm["strictf"]], [s["EsB"]])
        B.tt(POOL, s["EsB"].t[:], s["EsB"].t[:], bc(s["beta"].t[:], [128, 4, 128], 2), ALU.mult, [s["EsB"], s["beta"]], [s["EsB"]])
        if getattr(B, 'stop_at', 99) <= 8:
            return
        yield
        pA = psA.t[:].rearrange("p (a b) -> p a b", a=4)
        pB = psB.t[:].rearrange("p (a b) -> p a b", a=4)
        for h in range(4):
            lo = 64 * (h % 2)
            kTh = s["kT"].t[lo:lo + 64, h // 2, :]
            B.mm(pA[:, h, :], kTh, kTh, [s["kT"]], [psA])
            if want_out:
                B.mm(pB[:, h, :], kTh, s["qT"].t[lo:lo + 64, h // 2, :], [s["kT"], s["qT"]], [psB])
        B.tt(DVE, s["X"].t[:], pA, s["EsB"].t[:], ALU.mult, [psA, s["EsB"]], [s["X"]])
        if want_out:
            B.tt(DVE, s["AT"].t[:], pB, s["Ec"].t[:], ALU.mult, [psB, s["Ec"]], [s["AT"]])
        yield
        Y = s["Y"]
        B.cp(ACT, Y.t[:, :, 0:64], s["vtok"].t[:], [s["vtok"]], [Y])
        B.tt(POOL, Y.t[:, :, 64:128], s["ktok"].t[:], bc(s["egc"].t[:], [128, 4, 64], 2), ALU.mult, [s["ktok"], s["egc"]], [Y])
        B.tt(POOL, s["kd"].t[:], s["ktok"].t[:], bc(s["erev"].t[:], [128, 4, 64], 2), ALU.mult, [s["ktok"], s["erev"]], [s["kd"]])
        if getattr(B, 'stop_at', 99) <= 10:
            return
        yield
        pTb = B.psTb
        for h in range(4):
            B.mm(pB[:, h, :], s["X"].t[:, h, :], B.identf.t[:], [s["X"], B.identf], [psB])
        B.cp(ACT, s["Nn"].t[:], pB, [psB], [s["Nn"]])
        identb4 = bc(B.identf.t[:], [128, 4, 128], 1)
        B.stt(POOL, s["IX"].t[:], s["X"].t[:], -1.0, identb4, ALU.mult, ALU.add, [s["X"], B.identf], [s["IX"]])
        if getattr(B, 'stop_at', 99) <= 11:
            return
        nlev = getattr(B, 'nlev', 7)
        for lev in range(nlev):
            for h in range(4):
                B.mm(pC[:, h, :], s["IX"].t[:, h, :], Y.t[:, h, :], [s["IX"], Y], [psC])
                if lev < nlev - 1:
                    B.mm(pA[:, h, :], s["Nn"].t[:, h, :], s["X"].t[:, h, :], [s["Nn"], s["X"]], [psA])
                    if lev < nlev - 2:
                        B.mm(pB[:, h, :], s["X"].t[:, h, :], s["Nn"].t[:, h, :], [s["Nn"], s["X"]], [psB])
            if lev < nlev - 1:
                B.cp(DVE, Y.t[:], pC, [psC], [Y])
                B.cp(ACT, s["X"].t[:], pA, [psA], [s["X"]])
                B.tt(POOL, s["IX"].t[:], s["X"].t[:], identb4, ALU.add, [s["X"], B.identf], [s["IX"]])
                if lev < nlev - 2:
                    B.cp(ACT, s["Nn"].t[:], pB, [psB], [s["Nn"]])
                yield
        if getattr(B, 'stop_at', 99) <= 12:
            return
        B.tt(DVE, s["ub"].t[:], pC[:, :, 0:64], bc(s["beta"].t[:], [128, 4, 64], 2), ALU.mult, [psC, s["beta"]], [s["ub"]])
        B.tt(DVE, s["wb"].t[:], pC[:, :, 64:128], bc(s["beta"].t[:], [128, 4, 64], 2), ALU.mult, [psC, s["beta"]], [s["wb"]])
        for p_ in range(2):
            B.tr(pTb.t[:, 4 + p_, :], s["wb"].t[:, 2 * p_:2 * p_ + 2, :].rearrange("p a b -> p (a b)"), B.identb.t[:], [s["wb"], B.identb], [pTb])
        B.cp(DVE, s["wT"].t[:], pTb.t[:, 4:6, :], [pTb], [s["wT"]])
        if getattr(B, 'stop_at', 99) <= 13:
            return
        if smode is not None:
            smode["state_fn"](psB, psD, gate_fn)
            return
        yield
        pD = psD.t[:, 0:256].rearrange("p (a b) -> p a b", a=4)
        pD2 = psD.t[:, 256:512].rearrange("p (a b) -> p a b", a=4)
        pQ = psB.t[:, 0:256].rearrange("p (a b) -> p a b", a=4)
        pV = psB.t[:, 256:512].rearrange("p (a b) -> p a b", a=4)
        for h in range(4):
            lo = 64 * (h % 2)
            B.mm(pD[:, h, :], s["wT"].t[lo:lo + 64, h // 2, :], s["Sb"].t[lo:lo + 64, h // 2, :], [s["wT"], s["Sb"]], [psD])
            if want_out:
                B.mm(pQ[:, h, :], s["qT"].t[lo:lo + 64, h // 2, :], s["Sb"].t[lo:lo + 64, h // 2, :], [s["qT"], s["Sb"]], [psB])
        B.tt(DVE, s["vnew"].t[:], s["ub"].t[:], pD, ALU.subtract, [s["ub"], psD], [s["vnew"]])
        if want_out:
            B.tt(DVE, s["o"].t[:], pQ, bc(s["egc"].t[:], [128, 4, 64], 2), ALU.mult, [psB, s["egc"]], [s["o"]])
        for h in range(4):
            if want_out:
                B.mm(pV[:, h, :], s["AT"].t[:, h, :], s["vnew"].t[:, h, :], [s["AT"], s["vnew"]], [psB])
            B.mm(pD2[:, h, :], s["kd"].t[:, 2 * (h // 2):2 * (h // 2) + 2, :].rearrange("p a b -> p (a b)"), s["vnew"].t[:, h, :], [s["kd"], s["vnew"]], [psD])
        if want_out:
            B.tt(DVE, s["o"].t[:], s["o"].t[:], pV, ALU.add, [psB, s["o"]], [s["o"]])
        B.tt(POOL, s["S"].t[:], s["S"].t[:], bc(s["eglp"].t[:], [128, 2, 64], 2), ALU.mult, [s["S"], s["eglp"]], [s["S"]])
        B.tt(DVE, s["S"].t[0:64, :, :], s["S"].t[0:64, :, :], pD2[0:64, 0:4:2, :], ALU.add, [s["S"], psD], [s["S"]])
        B.tt(DVE, s["S"].t[64:128, :, :], s["S"].t[64:128, :, :], pD2[64:128, 1:4:2, :], ALU.add, [s["S"], psD], [s["S"]])
        B.cp(ACT, s["Sb"].t[:], s["S"].t[:], [s["S"]], [s["Sb"]])
        if want_out:
            B.tt(POOL, s["o2"].t[:], s["o"].t[:], s["o"].t[:], ALU.mult, [s["o"]], [s["o2"]])
            B.P.op(DVE, lambda e: e.tensor_reduce(out=s["ms"].t[:], in_=s["o2"].t[:], axis=AX.X, op=ALU.add), [s["o2"].b], [s["ms"].b])
            B.act(s["ms"].t[:], s["ms"].t[:], AF.Sqrt, [s["ms"], B.epsc], [s["ms"]], bias=B.epsc.t[:], scale=1.0 / 64)
            B.P.op(DVE, lambda e: e.reciprocal(out=s["ms"].t[:], in_=s["ms"].t[:]), [s["ms"].b], [s["ms"].b])
            B.tt(DVE, s["o"].t[:], s["o"].t[:], bc(s["ms"].t[:], [128, 4, 64], 2), ALU.mult, [s["o"], s["ms"]], [s["o"]])
            B.tt(POOL, s["o"].t[:], s["o"].t[:], bc(B.onw.t[:], [128, 4, 64], 1), ALU.mult, [s["o"], B.onw], [s["o"]])
            psGate = gate_fn()
            B.act(s["gs"].t[:], psGate.t[:, 0:256], AF.Silu, [psGate], [s["gs"]])
            B.tt(DVE, s["obf"].t[:], s["o"].t[:].rearrange("p a b -> p (a b)"), s["gs"].t[:], ALU.mult, [s["o"], s["gs"]], [s["obf"]])


def alibi_masks():
    slopes = 2.0 ** (-8.0 * np.arange(1, 9) / 8.0)
    k = np.arange(128)[:, None]
    q = np.arange(128)[None, :]
    m = np.zeros((4, 128, 17, 2, 128), np.float64)
    for o in range(17):
        d = 128 * o + q - k
        c = ((d >= 0) & (d <= 128)).astype(np.float64) + ((d >= 0) & (d <= 512) & (d % 4 == 0)) + ((d >= 0) & (d <= 2048) & (d % 16 == 0))
        for h in range(8):
            m[h // 2, :, o, h % 2, :] = c * np.exp(-slopes[h] * np.maximum(d, 0))
    return m.astype(np.float32)


class Builder(Builder):
    def headnorm_fm(self, ps, npairs, ntok, gcol, dst, dstT, ps2, tmp_sq, tmp_rn):
        B = self
        n = npairs * ntok
        B.act(tmp_sq.t[:, 0:n], ps.t[:, 0:n], AF.Square, [ps], [tmp_sq])
        B.mm(ps2.t[:, 0:n], B.bdb.t[:], tmp_sq.t[:, 0:n], [B.bdb, tmp_sq], [ps2])
        B.act(tmp_rn.t[:, 0:n], ps2.t[:, 0:n], AF.Sqrt, [ps2, B.epsc], [tmp_rn], bias=B.epsc.t[:], scale=1.0 / 64)
        B.P.op(DVE, lambda e: e.reciprocal(out=tmp_rn.t[:, 0:n], in_=tmp_rn.t[:, 0:n]), [tmp_rn.b], [tmp_rn.b])
        B.stt(DVE, dst, ps.t[:, 0:n], gcol, tmp_rn.t[:, 0:n], ALU.mult, ALU.mult, [ps, tmp_rn] + [v[1] for v in B.gw.values()], [dstT])

    def rmsnorm_tile(self, x, sqj, ss, xb, gate_scale=None):
        B = self
        B.act(sqj.t[:], x.t[:], AF.Square, [x], [sqj, ss], accum_out=ss.t[:])
        np_ = x.t.shape[0]
        B.act(ss.t[:], ss.t[:], AF.Sqrt, [ss, B.epsc], [ss], bias=B.epsc.t[0:np_, :], scale=1.0 / D)
        B.P.op(DVE, lambda e: e.reciprocal(out=ss.t[:], in_=ss.t[:]), [ss.b], [ss.b])
        B.act(xb.t[:], x.t[:], AF.Copy, [x, ss], [xb], scale=ss.t[:])

    def tok_headnorm(self, ps, nh, gw, out, kss, np_=128):
        B = self
        n = nh * 64
        P_ = slice(0, np_)
        B.act(out.t[P_, 0:n], ps.t[P_, 0:n], AF.Square, [ps], [out])
        B.P.op(DVE, lambda e: e.tensor_reduce(out=kss.t[P_, 0:nh], in_=out.t[P_, 0:n].rearrange("p (a b) -> p a b", a=nh), axis=AX.X, op=ALU.add), [out.b], [kss.b])
        B.act(kss.t[P_, 0:nh], kss.t[P_, 0:nh], AF.Sqrt, [kss, B.epsc], [kss], bias=B.epsc.t[P_, :], scale=1.0 / 64)
        B.P.op(DVE, lambda e: e.reciprocal(out=kss.t[P_, 0:nh], in_=kss.t[P_, 0:nh]), [kss.b], [kss.b])
        o3 = out.t[P_, 0:n].rearrange("p (a b) -> p a b", a=nh)
        B.tt(DVE, o3, ps.t[P_, 0:n].rearrange("p (a b) -> p a b", a=nh), bc(kss.t[P_, 0:nh], [np_, nh, 64], 2), ALU.mult, [ps, kss], [out])
        B.tt(DVE, o3, o3, bc(gw.t[P_, :], [np_, nh, 64], 1), ALU.mult, [out, gw], [out])

    def prompt_setup(self, NOWN, sample=False):
        B = self
        NT = B.NT
        B.NOWN = NOWN
        B.th = NT - NOWN - 1
        assert B.th >= 0
        B.NF = NOWN + 1
        B.tw0 = max(0, B.th - 16)
        B.KW = NT - B.tw0
        B.xloc = B.inp("xloc", [B.L, D])
        B.valid = B.inp("valid", [128, NT])
        B.flag = B.inp("flag", [128, 1])
        B.amask = B.inp("amask", [4, 128, 17 * 2 * 128])
        for nm in ["q_norm_a", "k_norm_a", "q_norm_c", "k_norm_c"]:
            B.inp(nm, [128, 64])
            B.inp(nm + "_col", [128, 1])
        B.winK = B.outp("win_k_p", [NOWN * 128, 512])
        B.winV = B.outp("win_v_p", [NOWN * 128, 512])
        B.gst = B.outp("gdn_state_p", [4, 64, 64])
        B.gcv = B.outp("gdn_conv_p", [3, 768])
        B.memk_o = B.outp("mem_k_p", [256, 256])
        B.memv_o = B.outp("mem_v_p", [256, 256])
        B.y_o = B.outp("y_p", [NOWN * 128, D])
        B.fcv = B.outp("ffn_conv_p", [2, 2 * D_FF])
        B.kTs = B.scratch("kTs", [4, 128, B.KW * 128], BF16)
        B.qTs = B.scratch("qTs", [4, 128, B.NF * 128], BF16)
        B.vaug = B.scratch("vaug", [B.KW, 128, 8 * 80], BF16)
        B.x1s = B.scratch("x1s", [B.NF * 128, D])
        B.gw = {}
        for nm in ["q_norm_a", "k_norm_a", "q_norm_c", "k_norm_c"]:
            t = B.sbt("gw_" + nm, [128, 64])
            B.dma(SP, t.t[:], B.ins[nm].t.ap(), [B.ins[nm]], [t])
            c = B.sbt("gc_" + nm, [128, 1])
            B.dma(SP, c.t[:], B.ins[nm + "_col"].t.ap(), [B.ins[nm + "_col"]], [c])
            B.gw[nm] = (t, c)
        B.validsb = B.sbt("validsb", [128, NT])
        B.dma(SP, B.validsb.t[:], B.valid.t.ap(), [B.valid], [B.validsb])
        B.flagsb = B.sbt("flagsb", [128, 1])
        B.dma(SP, B.flagsb.t[:], B.flag.t.ap(), [B.flag], [B.flagsb])
        B.pb = [None] + [B.pst("pb%d" % i, [128, 512]) for i in range(1, 7)]
        B.psTb = B.pst("psTb", [128, 8, 128], BF16)
        B.psT = B.pst("psT", [128, 8, 128], BF16)
        if sample:
            B.sample_setup()
        B.m_mix = B.P.mark()
        B.mixTs = B.scratch("mixTs", [8, 128, B.NF * 128], BF16)

    def mem_kv(self):
        B = self
        pb = B.pb
        mem = B.inp("mem_prompt", [256, D])
        wm = B.inp("w_mem_kv", [D, 512])
        gm = B.inp("mem_norm_g", [D])
        B.mkT = B.sbt("mkT", [128, 2, 256], BF16)
        B.mvaug = B.sbt("mvaug", [128, 2, 4, 80], BF16)
        m = B.P.mark()
        B.Wm = B.sbt("Wm", [128, 8, 512], BF16)
        gmc = B.sbt("gmc", [128, 8])
        B.dma(SP, gmc.t[:], gm.t.ap().rearrange("(k p) -> p k", p=128), [gm], [gmc], allow_slow_non_contiguous=True)
        st = B.sbt("wmst", [128, 512])
        for kc in range(8):
            B.dma(SP, st.t[:], wm.t.ap()[kc * 128:(kc + 1) * 128, :], [wm], [st])
            B.ts(DVE, B.Wm.t[:, kc, :], st.t[:], gmc.t[:, kc:kc + 1], None, ALU.mult, None, [st, gmc], [B.Wm])
        x = B.sbt("memx", [128, D]); sqj = B.sbt("memsq", [128, D]); ss = B.sbt("memss", [128, 1]); xb = B.sbt("memxb", [128, D], BF16)
        mT = B.sbt("memT", [128, 8, 256], BF16)
        tsq = B.sbt("mem_tsq", [128, 512], BF16); trn = B.sbt("mem_trn", [128, 512])
        tokf = B.sbt("mem_tokf", [128, 256]); kss = B.sbt("mem_kss", [128, 8])
        for t in range(2):
            B.dma(SP, x.t[:], mem.t.ap()[t * 128:(t + 1) * 128, :], [mem], [x])
            B.rmsnorm_tile(x, sqj, ss, xb)
            for c in range(8):
                B.tr(B.psT.t[:, c, :], xb.t[:, c * 128:(c + 1) * 128], B.identb.t[:], [xb, B.identb], [B.psT])
            B.cp(DVE, mT.t[:, :, t * 128:(t + 1) * 128], B.psT.t[:], [B.psT], [mT])
        for p_ in range(2):
            for kc in range(8):
                B.mm(pb[3].t[:, p_ * 256:(p_ + 1) * 256], B.Wm.t[:, kc, p_ * 128:(p_ + 1) * 128], mT.t[:, kc, :], [B.Wm, mT], [pb[3]], start=(kc == 0), stop=(kc == 7))
        B.headnorm_fm(pb[3], 2, 256, B.gw["k_norm_c"][1].t[:], B.mkT.t[:].rearrange("p a b -> p (a b)"), B.mkT, pb[4], tsq, trn)
        B.memset(POOL, B.mvaug.t[:], 0.0, [B.mvaug])
        B.memset(POOL, B.mvaug.t[:, :, :, 64:65], 1.0, [B.mvaug])
        for t in range(2):
            for kc in range(8):
                B.mm(pb[5].t[:, 0:512], mT.t[:, kc, t * 128:(t + 1) * 128], B.Wm.t[:, kc, :], [B.Wm, mT], [pb[5]], start=(kc == 0), stop=(kc == 7))
            pk = T(pb[5].t[:, 0:256]); pk.b = pb[5].b
            B.tok_headnorm(pk, 4, B.gw["k_norm_c"][0], tokf, kss)
            B.dma(SP, B.memk_o.t.ap()[t * 128:(t + 1) * 128, :], tokf.t[:], [tokf], [B.memk_o])
            tokv = B.sbt("mem_tokv%d" % t, [128, 256])
            B.cp(ACT, tokv.t[:], pb[5].t[:, 256:512], [pb[5]], [tokv])
            B.dma(SP, B.memv_o.t.ap()[t * 128:(t + 1) * 128, :], tokv.t[:], [tokv], [B.memv_o])
            B.cp(DVE, B.mvaug.t[:, t, :, 0:64], pb[5].t[:, 256:512].rearrange("p (a b) -> p a b", a=4), [pb[5]], [B.mvaug])
        return m

    def prompt_loop(self):
        B = self
        NT, NOWN, th, tw0 = B.NT, B.NOWN, B.th, B.tw0
        pb = B.pb
        psT = B.psT
        xt = [B.sbt("xt%d" % i, [128, D]) for i in range(B.NSETS)]
        sqj = B.sbt("sqj", [128, D], BF16)
        NS = B.NSETS
        ssL = [B.sbt("ss%d" % i, [128, 1]) for i in range(NS)]
        xbL = [B.sbt("xb%d" % i, [128, D], BF16) for i in range(NS)]
        xnTL = [B.sbt("xnT%d" % i, [128, 8, 128], BF16) for i in range(NS)]
        kv = B.sbt("kvtok", [128, 512]); kv2 = B.sbt("kv2", [128, 512]); kss = B.sbt("kss", [128, 8])
        tsq = B.sbt("tsq", [128, 512], BF16); trn = B.sbt("trn", [128, 512])
        kst = B.sbt("kst", [128, 4, 128], BF16); qst = B.sbt("qst", [128, 4, 128], BF16)
        vst = B.sbt("vst", [128, 8, 80], BF16)
        qcT = B.sbt("qcT", [128, 2, 128], BF16)
        pcs = B.sbt("pcs", [128, 4, 128], BF16)
        rc = B.sbt("rc", [128, 2]); ocp = B.sbt("ocp", [128, 2, 64], BF16)
        mixst = B.sbt("mixst", [128, 4, 128], BF16)
        B.memset(POOL, vst.t[:], 0.0, [vst])
        B.gdn_alloc("g0_"); B.gsets[0] = B.gs
        shared = {k: B.gsets[0][k] for k in ["S", "Sb", "o", "o2", "ms", "gs", "obf"]}
        for i in range(1, NS):
            B.gdn_alloc("g%d_" % i, share=shared); B.gsets[i] = B.gs

        def tile_gen(t):
            par = t % NS
            s = B.gsets[par]
            ss = ssL[par]; xb = xbL[par]; xnT = xnTL[par]
            win = t >= tw0
            F = t >= th
            own = t > th
            f = t - th
            kw = t - tw0
            x = xt[t % NS]
            B.dma(SP, x.t[:], B.xloc.t.ap()[t * 128:(t + 1) * 128, :], [B.xloc], [x])
            B.rmsnorm_tile(x, sqj, ss, xb)
            for c in range(8):
                B.tr(psT.t[:, c, :], xb.t[:, c * 128:(c + 1) * 128], B.identb.t[:], [xb, B.identb], [psT])
            B.cp(DVE, xnT.t[:], psT.t[:], [psT], [xnT])
            if win and 'win' not in B.skip:
                for p_ in range(4):
                    if 'kproj' in B.skip:
                        break
                    for kc in range(8):
                        B.mm(pb[3].t[:, p_ * 128:(p_ + 1) * 128], B.Wb.t[:, kc, C_KA + p_ * 128:C_KA + (p_ + 1) * 128], xnT.t[:, kc, :],
                             [B.Wb, xnT], [pb[3]], start=(kc == 0), stop=(kc == 7))
                if 'kproj' not in B.skip:
                  B.headnorm_fm(pb[3], 4, 128, B.gw["k_norm_a"][1].t[:], kst.t[:].rearrange("p a b -> p (a b)"), kst, pb[4], tsq, trn)
                if 'ktsdma' not in B.skip:
                    B.dma(SP, B.kTs.t.ap()[:, :, kw * 128:(kw + 1) * 128].rearrange("a p n -> p a n"), kst.t[:], [kst], [B.kTs])
                for kc in range(8):
                    B.mm(pb[4].t[:], xnT.t[:, kc, :], B.Wb.t[:, kc, C_VA:C_VA + 512], [B.Wb, xnT], [pb[4]], start=(kc == 0), stop=(kc == 7))
                if 'vts' not in B.skip:
                    if 'novalid' in B.skip:
                        B.cp(DVE, vst.t[:, :, 0:64], pb[4].t[:].rearrange("p (a b) -> p a b", a=8), [pb[4]], [vst])
                    elif 'actv' in B.skip:
                        B.act(vst.t[:, :, 0:64], pb[4].t[:].rearrange("p (a b) -> p a b", a=8), AF.Copy, [pb[4], B.validsb], [vst], scale=B.validsb.t[:, t:t + 1])
                    else:
                        B.tt(DVE, vst.t[:, :, 0:64], pb[4].t[:].rearrange("p (a b) -> p a b", a=8),
                             B.validsb.t[:, t:t + 1].unsqueeze(1).to_broadcast([128, 8, 64]), ALU.mult, [pb[4], B.validsb], [vst])
                if 'vstcp' not in B.skip:
                    B.cp(DVE, vst.t[:, :, 64:65], bc(B.validsb.t[:, t:t + 1], [128, 8, 1], 1), [B.validsb], [vst])
                if 'vaugdma' not in B.skip:
                    B.dma(SP, B.vaug.t.ap()[kw, :, :], vst.t[:].rearrange("p a b -> p (a b)"), [vst], [B.vaug])
                if own:
                    r0 = (f - 1) * 128
                    B.cp(ACT, kv2.t[:], pb[4].t[:], [pb[4]], [kv2])
                    B.dma(SP, B.winV.t.ap()[r0:r0 + 128, :], kv2.t[:], [kv2], [B.winV])
                    for kc in range(8):
                        B.mm(pb[5].t[:], xnT.t[:, kc, :], B.Wb.t[:, kc, C_KA:C_KA + 512], [B.Wb, xnT], [pb[5]], start=(kc == 0), stop=(kc == 7))
                    B.tok_headnorm(pb[5], 8, B.gw["k_norm_a"][0], kv, kss)
                    B.dma(SP, B.winK.t.ap()[r0:r0 + 128, :], kv.t[:], [kv], [B.winK])
            if F and 'fq' not in B.skip:
                for p_ in range(4):
                    for kc in range(8):
                        B.mm(pb[3].t[:, p_ * 128:(p_ + 1) * 128], B.Wb.t[:, kc, C_QA + p_ * 128:C_QA + (p_ + 1) * 128], xnT.t[:, kc, :],
                             [B.Wb, xnT], [pb[3]], start=(kc == 0), stop=(kc == 7))
                B.headnorm_fm(pb[3], 4, 128, B.gw["q_norm_a"][1].t[:], qst.t[:].rearrange("p a b -> p (a b)"), qst, pb[4], tsq, trn)
                B.dma(SP, B.qTs.t.ap()[:, :, f * 128:(f + 1) * 128].rearrange("a p n -> p a n"), qst.t[:], [qst], [B.qTs])
            if F and 'cross' not in B.skip:
                for p_ in range(2):
                    for kc in range(8):
                        B.mm(pb[3].t[:, p_ * 128:(p_ + 1) * 128], B.Wb.t[:, kc, C_QC + p_ * 128:C_QC + (p_ + 1) * 128], xnT.t[:, kc, :],
                             [B.Wb, xnT], [pb[3]], start=(kc == 0), stop=(kc == 7))
                B.headnorm_fm(pb[3], 2, 128, B.gw["q_norm_c"][1].t[:], qcT.t[:].rearrange("p a b -> p (a b)"), qcT, pb[4], tsq, trn)
                for p_ in range(2):
                    p5 = pb[5].t[:].rearrange("p (a b) -> p a b", a=4)
                    for mt in range(2):
                        for h2 in range(2):
                            lo = 64 * h2
                            B.mm(p5[:, mt * 2 + h2, :], B.mkT.t[lo:lo + 64, p_, mt * 128:(mt + 1) * 128], qcT.t[lo:lo + 64, p_, :],
                                 [B.mkT, qcT], [pb[5]])
                    B.act(pcs.t[:], p5, AF.Exp, [pb[5]], [pcs], scale=0.125)
                    for h2 in range(2):
                        for mt in range(2):
                            B.mm(pb[6].t[:, h2 * 128:h2 * 128 + 65], pcs.t[:, mt * 2 + h2, :], B.mvaug.t[:, mt, 2 * p_ + h2, 0:65],
                                 [pcs, B.mvaug], [pb[6]], start=(mt == 0), stop=(mt == 1))
                    B.P.op(DVE, lambda e: e.reciprocal(out=rc.t[:], in_=pb[6].t[:, 64:256:128]), [pb[6].b], [rc.b])
                    B.tt(DVE, ocp.t[:], pb[6].t[:, 0:256].rearrange("p (a b) -> p a b", a=2)[:, :, 0:64], bc(rc.t[:], [128, 2, 64], 2), ALU.mult,
                         [pb[6], rc], [ocp])
                    B.tr(B.psTb.t[:, 0, :], ocp.t[:].rearrange("p a b -> p (a b)"), B.identb.t[:], [ocp, B.identb], [B.psTb])
                    B.cp(DVE, mixst.t[:, 2 + p_, :], B.psTb.t[:, 0, :], [B.psTb], [mixst])
                B.dma(SP, B.mixTs.t.ap()[6:8, :, f * 128:(f + 1) * 128].rearrange("a p n -> p a n"), mixst.t[:, 2:4, :], [mixst], [B.mixTs])
            for ct in (0, 3, 1, 4, 2, 5):
                dst = pb[1 + ct // 3]
                for kc in range(8):
                    B.mm(dst.t[:, (ct % 3) * 128:(ct % 3 + 1) * 128], B.Wb.t[:, kc, C_QKV + ct * 128:C_QKV + (ct + 1) * 128],
                         xnT.t[:, kc, :], [B.Wb, xnT], [dst], start=(kc == 0), stop=(kc == 7))
            for kc in range(8):
                B.mm(pb[1].t[:, 384:392], xnT.t[:, kc, :], B.Wb.t[:, kc, C_A:C_A + 8], [B.Wb, xnT], [pb[1]], start=(kc == 0), stop=(kc == 7))
            psG = [T(pb[1].t[:, 0:384].rearrange("p (a b) -> p a b", a=3)), T(pb[2].t[:, 0:384].rearrange("p (a b) -> p a b", a=3))]
            psG[0].b = pb[1].b
            psG[1].b = pb[2].b
            psAB = T(pb[1].t[:, 384:392])
            psAB.b = pb[1].b

            def gate_fn():
                for kc in range(8):
                    B.mm(pb[3].t[:, 0:256], xnT.t[:, kc, :], B.Wb.t[:, kc, C_GATE:C_GATE + 256], [B.Wb, xnT], [pb[3]], start=(kc == 0), stop=(kc == 7))
                return pb[3]
            yield from B.gdn_tile(psG, psAB, pb[3], pb[4], pb[5], pb[6], want_out=F and 'gout' not in B.skip, gate_fn=gate_fn, par=par, first=(t == 0))
            if F and 'gout' not in B.skip:
                for p_ in range(2):
                    B.tr(B.psTb.t[:, p_, :], s["obf"].t[:, p_ * 128:(p_ + 1) * 128], B.identb.t[:], [s["obf"], B.identb], [B.psTb])
                B.cp(DVE, mixst.t[:, 0:2, :], B.psTb.t[:, 0:2, :], [B.psTb], [mixst])
                B.dma(SP, B.mixTs.t.ap()[4:6, :, f * 128:(f + 1) * 128].rearrange("a p n -> p a n"), mixst.t[:, 0:2, :], [mixst], [B.mixTs])

        active = []
        nxt = 0
        while active or nxt < NT:
            if len(active) < NS and nxt < NT:
                active.append(tile_gen(nxt)); nxt += 1
            for g_ in list(active):
                try:
                    next(g_)
                except StopIteration:
                    active.remove(g_)
        s = B.gsets[(NT - 1) % NS]
        B.dma(SP, B.gst.t.ap().rearrange("(p h) k v -> (h k) p v", h=2), s["S"].t[:], [s["S"]], [B.gst])
        gct = B.sbt("gct", [3, 768])
        for c in range(4):
            B.mm(pb[3].t[0:3, c * 128:(c + 1) * 128], s["qkvx"].t[:, c, 128:131], B.identf.t[:], [s["qkvx"], B.identf], [pb[3]])
        B.cp(DVE, gct.t[:, 0:512], pb[3].t[0:3, 0:512], [pb[3]], [gct])
        for c in range(4, 6):
            B.mm(pb[4].t[0:3, (c - 4) * 128:(c - 3) * 128], s["qkvx"].t[:, c, 128:131], B.identf.t[:], [s["qkvx"], B.identf], [pb[4]])
        B.cp(DVE, gct.t[:, 512:768], pb[4].t[0:3, 0:256], [pb[4]], [gct])
        B.dma(SP, B.gcv.t.ap(), gct.t[:], [gct], [B.gcv])

    def attention_phase(self):
        B = self
        NF, th, tw0, KW = B.NF, B.th, B.tw0, B.KW
        pb = B.pb
        vall = B.sbt("vall", [128, KW, 8 * 80], BF16)
        B.dma(SP, vall.t[:], B.vaug.t.ap().rearrange("k p n -> p k n"), [B.vaug], [vall])
        kTb = [B.sbt("kTb%d" % i, [128, KW * 128], BF16) for i in range(2)]
        qTb = [B.sbt("qTb%d" % i, [128, NF * 128], BF16) for i in range(2)]
        msk = [B.sbt("msk%d" % i, [128, 17 * 2 * 128]) for i in range(2)]
        Eb = [B.sbt("Eb%d" % i, [128, 4, 128]) for i in range(2)]
        Pm = [B.sbt("Pm%d" % i, [128, 4, 128], BF16) for i in range(2)]
        rc = B.sbt("arc", [128, 2]); oap = B.sbt("oap", [128, 2, 64], BF16)
        amix = [B.sbt("amix%d" % i, [128, 128], BF16) for i in range(2)]
        it = 0
        for hp in range(4):
            kT = kTb[hp % 2]; qT = qTb[hp % 2]; mk = msk[hp % 2]
            B.dma(SP, kT.t[:], B.kTs.t.ap()[hp], [B.kTs], [kT])
            B.dma(SP, qT.t[:], B.qTs.t.ap()[hp], [B.qTs], [qT])
            B.dma(POOL, mk.t[:], B.amask.t.ap()[hp], [B.amask], [mk])
            mk4 = mk.t[:].rearrange("p (o h q) -> p o h q", o=17, h=2)
            jobs = []
            for f in range(NF):
                tq = th + f
                offs = [o for o in range(17) if tq - o >= tw0]
                groups = [offs[i:i + 2] for i in range(0, len(offs), 2)]
                for gi, g in enumerate(groups):
                    jobs.append((f, g, offs, gi == len(groups) - 1))

            def stageA(n):
                f, g, offs, last = jobs[n]
                tq = th + f
                psS = pb[1 + n % 2]
                pS = psS.t[:].rearrange("p (a b) -> p a b", a=4)
                for oi, o in enumerate(g):
                    kw = tq - o - tw0
                    for h2 in range(2):
                        lo = 64 * h2
                        B.mm(pS[:, oi * 2 + h2, :], kT.t[lo:lo + 64, kw * 128:(kw + 1) * 128], qT.t[lo:lo + 64, f * 128:(f + 1) * 128],
                             [kT, qT], [psS])

            def stageB(n):
                f, g, offs, last = jobs[n]
                psS = pb[1 + n % 2]
                pS = psS.t[:].rearrange("p (a b) -> p a b", a=4)
                E = Eb[n % 2]; P_ = Pm[n % 2]
                ns = 2 * len(g)
                B.act(E.t[:, 0:ns, :], pS[:, 0:ns, :], AF.Exp, [psS], [E], scale=0.125)
                B.tt(DVE, P_.t[:, 0:ns, :], E.t[:, 0:ns, :], mk4[:, g[0]:g[0] + len(g), :, :].rearrange("p o h q -> p (o h) q"), ALU.mult, [E, mk], [P_])

            def stageC(n):
                f, g, offs, last = jobs[n]
                tq = th + f
                P_ = Pm[n % 2]
                pab = 5 if f % 2 == 0 else 3
                for oi, o in enumerate(g):
                    kw = tq - o - tw0
                    for h2 in range(2):
                        h = 2 * hp + h2
                        pa_ = pb[pab + h2]
                        B.mm(pa_.t[:, 0:65], P_.t[:, oi * 2 + h2, :], vall.t[:, kw, h * 80:h * 80 + 65], [P_, vall], [pa_],
                             start=(o == offs[0]), stop=(o == offs[-1]))
                if not last:
                    return
                for h2 in range(2):
                    pa_ = pb[pab + h2]
                    B.ts(DVE, rc.t[:, h2:h2 + 1], pa_.t[:, 64:65], 1e-30, None, ALU.add, None, [pa_], [rc])
                B.P.op(DVE, lambda e: e.reciprocal(out=rc.t[:], in_=rc.t[:]), [rc.b], [rc.b])
                for h2 in range(2):
                    pa_ = pb[pab + h2]
                    B.tt(DVE, oap.t[:, h2, :], pa_.t[:, 0:64], rc.t[:, h2:h2 + 1].to_broadcast([128, 64]), ALU.mult, [pa_, rc], [oap])
                B.tr(B.psTb.t[:, 0, :], oap.t[:].rearrange("p a b -> p (a b)"), B.identb.t[:], [oap, B.identb], [B.psTb])
                ms_ = amix[f % 2]
                B.cp(DVE, ms_.t[:], B.psTb.t[:, 0, :], [B.psTb], [ms_])
                B.dma(SP, B.mixTs.t.ap()[hp, :, f * 128:(f + 1) * 128], ms_.t[:], [ms_], [B.mixTs])

            stageA(0)
            for n in range(len(jobs)):
                if n + 1 < len(jobs):
                    stageA(n + 1)
                stageB(n)
                stageC(n)

    def outproj_phase(self):
        B = self
        NF, th = B.NF, B.th
        pb = B.pb
        w_out = B.inp("w_out", [D, D])
        B.x1nTs = B.scratch("x1nTs", [8, 128, NF * 128], BF16)
        Wo = B.sbt("Wo", [128, 8, D], BF16)
        st = [B.sbt("wost%d" % i, [128, D]) for i in range(2)]
        for kc in range(8):
            B.dma(SP if kc % 2 == 0 else POOL, st[kc % 2].t[:], w_out.t.ap()[kc * 128:(kc + 1) * 128, :], [w_out], [st[kc % 2]])
            B.cp(ACT if kc % 2 == 0 else DVE, Wo.t[:, kc, :], st[kc % 2].t[:], [st[kc % 2]], [Wo])
        xt = [B.sbt("oxt%d" % i, [128, D]) for i in range(2)]
        x1 = [B.sbt("ox1%d" % i, [128, D]) for i in range(2)]
        sqj = B.sbt("osqj", [128, D]); ss = B.sbt("oss", [128, 1]); xb = B.sbt("oxb", [128, D], BF16)
        xnT = B.sbt("oxnT", [128, 8, 128], BF16)
        omix = [B.sbt("omix%d" % i, [128, 8, 128], BF16) for i in range(2)]
        for f in range(NF):
            t = th + f
            x = xt[f % 2]; y = x1[f % 2]
            B.dma(SP, x.t[:], B.xloc.t.ap()[t * 128:(t + 1) * 128, :], [B.xloc], [x])
            mx_ = omix[f % 2]
            B.dma(POOL, mx_.t[:], B.mixTs.t.ap()[:, :, f * 128:(f + 1) * 128].rearrange("a p n -> p a n"), [B.mixTs], [mx_])
            for half in range(2):
                ps = pb[1 + half]
                for kc in range(8):
                    B.mm(ps.t[:], mx_.t[:, kc, :], Wo.t[:, kc, half * 512:(half + 1) * 512], [mx_, Wo], [ps],
                         start=(kc == 0), stop=(kc == 7))
                B.tt(DVE, y.t[:, half * 512:(half + 1) * 512], x.t[:, half * 512:(half + 1) * 512], ps.t[:], ALU.add, [x, ps], [y])
            B.dma(SP, B.x1s.t.ap()[f * 128:(f + 1) * 128, :], y.t[:], [y], [B.x1s])
            B.rmsnorm_tile(y, sqj, ss, xb)
            for c in range(8):
                B.tr(B.psT.t[:, c, :], xb.t[:, c * 128:(c + 1) * 128], B.identb.t[:], [xb, B.identb], [B.psT])
            B.cp(DVE, xnT.t[:], B.psT.t[:], [B.psT], [xnT])
            B.dma(SP, B.x1nTs.t.ap()[:, :, f * 128:(f + 1) * 128].rearrange("a p n -> p a n"), xnT.t[:], [xnT], [B.x1nTs])
        if hasattr(B, "z"):
            z = B.z
            WoA = B.sbt("WoA", [64, 8, D], BF16); WoC = B.sbt("WoC", [64, 4, D], BF16)
            for h in range(12):
                s_ = st[h % 2]
                r0 = h * 64 if h < 8 else 768 + (h - 8) * 64
                B.dma(SP if h % 2 == 0 else POOL, s_.t[0:64, :], w_out.t.ap()[r0:r0 + 64, :], [w_out], [s_])
                dst = WoA.t[:, h, :] if h < 8 else WoC.t[:, h - 8, :]
                B.cp(ACT if h % 2 == 0 else DVE, dst, s_.t[0:64, :], [s_], [WoA if h < 8 else WoC])
            xs_ = B.sbt("oxs", [64, D]); sqs = B.sbt("osqs", [64, D], BF16); sss = B.sbt("osss", [64, 1]); xbs = B.sbt("oxbs", [64, D], BF16)
            B.dma(SP, xs_.t[:], B.xs.t.ap(), [B.xs], [xs_])
            for half in range(2):
                ps = pb[1 + half]
                hs = slice(half * 512, (half + 1) * 512)
                for h in range(8):
                    B.mm(ps.t[0:64, :], z["oaT"].t[:, h, :], WoA.t[:, h, hs], [z["oaT"], WoA], [ps], start=(h == 0), stop=False)
                for p_ in range(2):
                    B.mm(ps.t[0:64, :], z["obT"].t[:, p_, :], Wo.t[:, 4 + p_, hs], [z["obT"], Wo], [ps], start=False, stop=False)
                for h in range(4):
                    B.mm(ps.t[0:64, :], z["ocT"].t[:, h, :], WoC.t[:, h, hs], [z["ocT"], WoC], [ps], start=False, stop=(h == 3))
                B.tt(DVE, z["x1"].t[:, hs], xs_.t[:, hs], ps.t[0:64, :], ALU.add, [xs_, ps], [z["x1"]])
            B.rmsnorm_tile(z["x1"], sqs, sss, xbs)
            for c in range(8):
                B.tr(B.psT.t[:, c, 0:64], xbs.t[:, c * 128:(c + 1) * 128], B.identb.t[0:64, 0:64], [xbs, B.identb], [B.psT])
            B.cp(DVE, z["x1nT"].t[:], B.psT.t[:, :, 0:64], [B.psT], [z["x1nT"]])

    def ffn_phase(self):
        B = self
        NF, NOWN = B.NF, B.NOWN
        pb = B.pb
        g2 = B.inp("norm2_g", [D])
        w_up = B.inp("w_up", [D, 2 * D_FF])
        cfw = B.inp("conv_ffn_w", [3, 2 * D_FF])
        w_down = B.inp("w_down", [D_FF, D])
        g2c = B.sbt("g2c", [128, 8])
        B.dma(SP, g2c.t[:], g2.t.ap().rearrange("(k p) -> p k", p=128), [g2], [g2c], allow_slow_non_contiguous=True)
        cfc = B.sbt("cfc", [128, 44, 3])
        for k in range(3):
            B.dma(SP, cfc.t[:, :, k], cfw.t.ap()[k, :].rearrange("(c p) -> p c", p=128), [cfw], [cfc], allow_slow_non_contiguous=True)
        Wd = B.sbt("Wd", [128, 22, D], BF16)
        x1t = [B.sbt("fx1t%d" % i, [128, D]) for i in range(2)]
        yt = [B.sbt("fyt%d" % i, [128, D]) for i in range(2)]
        st = x1t
        for j in range(22):
            B.dma(SP if j % 2 == 0 else POOL, st[j % 2].t[:], w_down.t.ap()[j * 128:(j + 1) * 128, :], [w_down], [st[j % 2]])
            B.cp(ACT if j % 2 == 0 else DVE, Wd.t[:, j, :], st[j % 2].t[:], [st[j % 2]], [Wd])
        n0 = (NF + 2) // 3
        groups = [(0, min(n0, NF)), (min(n0, NF), min(2 * n0, NF)), (min(2 * n0, NF), NF)]
        NG = n0 * 128
        x1nT = B.sbt("fx1nT", [128, 8, NG], BF16)
        hT = B.sbt("fhT", [128, 22, NG], BF16)
        ug = [B.sbt("fug%d" % i, [128, NG + 2]) for i in range(2)]
        cg = [B.sbt("fcg%d" % i, [128, NG]) for i in range(2)]
        sg = B.sbt("fsg", [128, NG])
        wst = [[B.sbt("fwst%d_%d" % (i, k), [128, 8, 128]) for k in range(2)] for i in range(2)]
        wbf = [[B.sbt("fwbf%d_%d" % (i, k), [128, 8, 128], BF16) for k in range(2)] for i in range(2)]
        hsave = B.sbt("fhsave", [128, 44, 2])
        B.memset(POOL, hsave.t[:], 0.0, [hsave])
        SAMP = hasattr(B, "z")
        if SAMP:
            z = B.z
            fst = B.sbt("s_fst", [128, 44, 16, 2])
            fstg = [B.sbt("s_fstg%d" % i, [32, 512]) for i in range(2)]
            sfv = B.sfconv.t.ap().rearrange("b r n -> (b r) n")
            for c4 in range(11):
                sg_ = fstg[c4 % 2]
                B.dma(SP, sg_.t[:], sfv[:, c4 * 512:(c4 + 1) * 512], [B.sfconv], [sg_])
                ps = pb[5 + c4 % 2]
                for i in range(4):
                    B.mm(ps.t[:, i * 32:(i + 1) * 32], sg_.t[:, i * 128:(i + 1) * 128], B.identf.t[0:32, 0:32], [sg_, B.identf], [ps])
                B.cp(DVE, fst.t[:, c4 * 4:(c4 + 1) * 4, :, :], ps.t[:, 0:128].rearrange("p (c b r) -> p c b r", c=4, b=16), [ps], [fst])
            us = [B.sbt("s_us%d" % i, [128, 16, 6]) for i in range(2)]
            cs_ = [B.sbt("s_cs%d" % i, [128, 16, 4]) for i in range(2)]
            sgs = B.sbt("s_sgs", [128, 64])
            hTs = B.sbt("s_hTs", [128, 22, 64], BF16)
            usel = B.sbt("s_usel", [128, 16, 2])
            tks = [B.sbt("s_tks%d" % i, [32, 128]) for i in range(2)]
            ofv = B.o_fcs.t.ap().rearrange("b r n -> (b r) n")
        it = 0
        for gi, (f0, f1) in enumerate(groups):
            ntok = (f1 - f0) * 128
            if ntok == 0:
                continue
            B.dma(SP, x1nT.t[:, :, 0:ntok], B.x1nTs.t.ap()[:, :, f0 * 128:f1 * 128].rearrange("a p n -> p a n"), [B.x1nTs], [x1nT])
            chunks = [(c0, min(512, ntok - c0)) for c0 in range(0, ntok, 512)]
            for j in range(22):
                for k in range(2):
                    col = k * D_FF + j * 128
                    ws = wst[it % 2][k]; wb_ = wbf[it % 2][k]
                    B.dma(SP if k == 0 else POOL, ws.t[:], w_up.t.ap()[:, col:col + 128].rearrange("(a p) n -> p a n", p=128), [w_up], [ws])
                    B.tt(DVE, wb_.t[:], ws.t[:], bc(g2c.t[:], [128, 8, 128], 2), ALU.mult, [ws, g2c], [wb_])
                    u = ug[k]
                    B.cp(POOL, u.t[:, 0:2], hsave.t[:, k * 22 + j, :], [hsave], [u])
                    for ci, (c0, cn) in enumerate(chunks):
                        ps = pb[1 + (ci + k) % 2]
                        for kc in range(8):
                            B.mm(ps.t[:, 0:cn], wb_.t[:, kc, :], x1nT.t[:, kc, c0:c0 + cn], [wb_, x1nT], [ps], start=(kc == 0), stop=(kc == 7))
                        B.cp(ACT, u.t[:, 2 + c0:2 + c0 + cn], ps.t[:, 0:cn], [ps], [u])
                    if gi == 0:
                        B.ts(DVE, u.t[:, 2 + 126:2 + 128], u.t[:, 2 + 126:2 + 128], B.flagsb.t[:, 0:1], None, ALU.mult, None, [u, B.flagsb], [u])
                    B.cp(POOL, hsave.t[:, k * 22 + j, :], u.t[:, ntok:ntok + 2], [u], [hsave])
                    c_ = cg[k]
                    cc = k * 22 + j
                    B.ts(DVE, c_.t[:, 0:ntok], u.t[:, 0:ntok], cfc.t[:, cc, 0:1], None, ALU.mult, None, [u, cfc], [c_])
                    B.stt(DVE, c_.t[:, 0:ntok], u.t[:, 1:ntok + 1], cfc.t[:, cc, 1:2], c_.t[:, 0:ntok], ALU.mult, ALU.add, [u, cfc, c_], [c_])
                    B.stt(DVE, c_.t[:, 0:ntok], u.t[:, 2:ntok + 2], cfc.t[:, cc, 2:3], c_.t[:, 0:ntok], ALU.mult, ALU.add, [u, cfc, c_], [c_])
                    if SAMP and gi == 0:
                        ps = pb[5 + k]
                        for kc in range(8):
                            B.mm(ps.t[:, 0:64], wb_.t[:, kc, :], z["x1nT"].t[:, kc, :], [wb_, z["x1nT"]], [ps], start=(kc == 0), stop=(kc == 7))
                        us_ = us[k]
                        B.cp(POOL, us_.t[:, :, 0:2], fst.t[:, cc, :, :], [fst], [us_])
                        B.cp(ACT, us_.t[:, :, 2:6], ps.t[:, 0:64].rearrange("p (b s) -> p b s", b=16), [ps], [us_])
                        B.cp(POOL, usel.t[:], us_.t[:, :, 4:6], [us_], [usel])
                        tk_ = tks[cc % 2]
                        pq = pb[3 + cc % 2]
                        B.mm(pq.t[0:32, 0:128], usel.t[:].rearrange("p b r -> p (b r)"), B.identf.t[:], [usel, B.identf], [pq])
                        B.cp(DVE, tk_.t[:], pq.t[0:32, 0:128], [pq], [tk_])
                        B.dma(SP, ofv[:, k * D_FF + j * 128:k * D_FF + (j + 1) * 128], tk_.t[:], [tk_], [B.o_fcs])
                        cq = cs_[k]
                        B.ts(DVE, cq.t[:], us_.t[:, :, 0:4], cfc.t[:, cc, 0:1], None, ALU.mult, None, [us_, cfc], [cq])
                        B.stt(DVE, cq.t[:], us_.t[:, :, 1:5], cfc.t[:, cc, 1:2], cq.t[:], ALU.mult, ALU.add, [us_, cfc, cq], [cq])
                        B.stt(DVE, cq.t[:], us_.t[:, :, 2:6], cfc.t[:, cc, 2:3], cq.t[:], ALU.mult, ALU.add, [us_, cfc, cq], [cq])
                if SAMP and gi == 0:
                    B.act(sgs.t[:], cs_[0].t[:].rearrange("p b s -> p (b s)"), AF.Silu, [cs_[0]], [sgs])
                    B.tt(DVE, hTs.t[:, j, :], sgs.t[:], cs_[1].t[:].rearrange("p b s -> p (b s)"), ALU.mult, [sgs, cs_[1]], [hTs])
                it += 1
                B.act(sg.t[:, 0:ntok], cg[0].t[:, 0:ntok], AF.Silu, [cg[0]], [sg])
                B.tt(DVE, hT.t[:, j, 0:ntok], sg.t[:, 0:ntok], cg[1].t[:, 0:ntok], ALU.mult, [sg, cg[1]], [hT])
            for f in range(f0, f1):
                if f == 0:
                    continue
                lt = f - f0
                xx = x1t[f % 2]; yy = yt[f % 2]
                B.dma(POOL, xx.t[:], B.x1s.t.ap()[f * 128:(f + 1) * 128, :], [B.x1s], [xx])
                for half in range(2):
                    ps = pb[3 + half]
                    for j in range(22):
                        B.mm(ps.t[:], hT.t[:, j, lt * 128:(lt + 1) * 128], Wd.t[:, j, half * 512:(half + 1) * 512], [hT, Wd], [ps],
                             start=(j == 0), stop=(j == 21))
                    B.tt(DVE, yy.t[:, half * 512:(half + 1) * 512], xx.t[:, half * 512:(half + 1) * 512], ps.t[:], ALU.add, [xx, ps], [yy])
                B.dma(SP, B.y_o.t.ap()[(f - 1) * 128:f * 128, :], yy.t[:], [yy], [B.y_o])
        if SAMP:
            ys_ = B.sbt("s_ys", [64, D])
            for half in range(2):
                ps = pb[3 + half]
                hs = slice(half * 512, (half + 1) * 512)
                for j in range(22):
                    B.mm(ps.t[0:64, :], hTs.t[:, j, :], Wd.t[:, j, hs], [hTs, Wd], [ps], start=(j == 0), stop=(j == 21))
                B.tt(DVE, ys_.t[:, hs], z["x1"].t[:, hs], ps.t[0:64, :], ALU.add, [z["x1"], ps], [ys_])
            B.dma(SP, B.o_ys.t.ap(), ys_.t[:], [ys_], [B.o_ys])
        fct = [B.sbt("fct%d" % i, [2, 512]) for i in range(2)]
        for c4 in range(11):
            ps = pb[5 + c4 % 2]
            for i in range(4):
                c = c4 * 4 + i
                B.mm(ps.t[0:2, i * 128:(i + 1) * 128], hsave.t[:, c, :], B.identf.t[:], [hsave, B.identf], [ps])
            B.cp(DVE, fct[c4 % 2].t[:], ps.t[0:2, :], [ps], [fct[c4 % 2]])
            B.dma(SP, B.fcv.t.ap()[:, c4 * 512:(c4 + 1) * 512], fct[c4 % 2].t[:], [fct[c4 % 2]], [B.fcv])

    def finish(self):
        B = self
        B.P.wait_all(SP, B.outbufs)
        B.P.emit()
        B.P.close()

    def build_prompt(self, NOWN, sample=False):
        B = self
        B.setup_consts()
        m0 = B.P.mark()
        B.prompt_setup(NOWN, sample=sample)
        m1 = B.P.mark()
        mw = B.setup_weights_in()
        if sample:
            B.sample_inproj()
        B.P.barrier(); B.P.emit(); B.P.release(mw)
        if 'memkv' not in B.skip:
            mm_ = B.mem_kv()
            B.P.barrier(); B.P.emit(); B.P.release(mm_)
        B.gdn_consts()
        B.prompt_loop()
        if getattr(B, "nphase", 9) < 2:
            return
        B.P.barrier(); B.P.emit(); B.P.release(m1)
        B.attention_phase()
        if getattr(B, "nphase", 9) < 3:
            return
        if sample:
            B.P.barrier(); B.P.emit(); B.P.release(m1)
            B.sample_phase()
        B.P.barrier(); B.P.emit(); B.P.release(m1)
        B.outproj_phase()
        if getattr(B, "nphase", 9) < 4:
            return
        B.P.barrier(); B.P.emit(); B.P.release(B.m_mix)
        B.ffn_phase()


_CACHE = {}


def _build():
    if "B" in _CACHE:
        return _CACHE["B"]
    B = Builder(NT=128)
    B.build_prompt(16, sample=True)
    B.finish()
    _CACHE["B"] = B
    return B


def kernel(x_prompt, x_sample, cache_win_k, cache_win_v, state_gdn, state_gdn_conv, state_ffn_conv,
           cache_mem_k, cache_mem_v, mem_prompt, norm1_g, w_in, q_norm_a, k_norm_a, conv_b_w, a_log_b,
           dt_bias_b, out_norm_b, mem_norm_g, w_mem_kv, q_norm_c, k_norm_c, w_out, norm2_g, w_up,
           conv_ffn_w, w_down):
    f = lambda a: np.ascontiguousarray(np.asarray(a, dtype=np.float32))
    B = _build()
    xp = f(x_prompt)[0]
    rep = lambda v, n: np.ascontiguousarray(np.broadcast_to(f(v).reshape(1, -1), (128, n)))
    col = lambda v: np.ascontiguousarray(np.tile(f(v).reshape(-1), 2).reshape(128, 1))
    am = alibi_masks().reshape(4, 128, -1)
    shared = {
        "amask": am, "w_in": f(w_in)[0], "norm1_g": f(norm1_g)[0], "conv_b_w": f(conv_b_w)[0],
        "a_log_b": rep(a_log_b, 4), "dt_bias_b": rep(dt_bias_b, 4), "out_norm_b": rep(out_norm_b, 64),
        "mem_prompt": f(mem_prompt)[0], "w_mem_kv": f(w_mem_kv)[0], "mem_norm_g": f(mem_norm_g)[0],
        "w_out": f(w_out)[0], "norm2_g": f(norm2_g)[0], "w_up": f(w_up)[0], "conv_ffn_w": f(conv_ffn_w)[0], "w_down": f(w_down)[0],
    }
    for nm, v in (("q_norm_a", q_norm_a), ("k_norm_a", k_norm_a), ("q_norm_c", q_norm_c), ("k_norm_c", k_norm_c)):
        shared[nm] = rep(v, 64)
        shared[nm + "_col"] = col(v)
    for k, v in B.cnp.items():
        shared["c_" + k] = v
    for k, v in B.scn.items():
        shared["c_" + k] = v
    xs_all = f(x_sample)
    cwk_all = np.asarray(cache_win_k, dtype=np.float32)[0]
    cwv_all = np.asarray(cache_win_v, dtype=np.float32)[0]
    sg_all = f(state_gdn)[0]; sgc_all = f(state_gdn_conv)[0]; sfc_all = f(state_ffn_conv)[0]
    cmk_all = f(cache_mem_k)[0]; cmv_all = f(cache_mem_v)[0]
    in_maps = []
    for c in range(NCORES):
        npad = (NCORES - 1 - c) * OWN
        xloc = np.zeros((SEQ, D), np.float32)
        xloc[npad:] = xp[:(c + 1) * OWN]
        valid = np.zeros((SEQ,), np.float32)
        valid[npad:] = 1.0
        im = dict(shared)
        im["xloc"] = xloc
        im["valid"] = np.ascontiguousarray(valid.reshape(128, 128).T)
        im["flag"] = np.full((128, 1), 0.0 if c == 0 else 1.0, np.float32)
        bs = slice(16 * c, 16 * c + 16)
        im["xs"] = np.ascontiguousarray(xs_all[bs].reshape(64, D))
        im["cwk"] = np.ascontiguousarray(cwk_all[bs].reshape(16, 2048, 512))
        im["cwv"] = np.ascontiguousarray(cwv_all[bs].reshape(16, 2048, 512))
        im["sgdn"] = np.ascontiguousarray(sg_all[bs]); im["sgconv"] = np.ascontiguousarray(sgc_all[bs]); im["sfconv"] = np.ascontiguousarray(sfc_all[bs])
        im["cmk"] = np.ascontiguousarray(cmk_all[bs].reshape(16, 256, 256)); im["cmv"] = np.ascontiguousarray(cmv_all[bs].reshape(16, 256, 256))
        im = {k: v for k, v in im.items() if k in B.ins}
        in_maps.append(im)
    res = run_bass_kernel_spmd(B.nc, in_maps, core_ids=list(range(NCORES)))
    r = res.results
    z = lambda *s: np.zeros(s, np.float32)
    g = lambda c, n: np.asarray(r[c][n], np.float32)
    y_p = np.concatenate([g(c, "y_p") for c in range(NCORES)], 0).reshape(1, SEQ, D)
    win_k_p = g(7, "win_k_p").reshape(1, 1, 2048, 8, 64)
    win_v_p = g(7, "win_v_p").reshape(1, 1, 2048, 8, 64)
    gst_p = g(7, "gdn_state_p").reshape(1, 1, 4, 64, 64)
    gconv_p = g(7, "gdn_conv_p").reshape(1, 1, 3, 768)
    fconv_p = g(7, "ffn_conv_p").reshape(1, 1, 2, 2 * D_FF)
    mem_k_p = g(0, "mem_k_p").reshape(1, 1, 256, 4, 64)
    mem_v_p = g(0, "mem_v_p").reshape(1, 1, 256, 4, 64)
    cat = lambda n: np.concatenate([g(c, n) for c in range(NCORES)], 0)
    y_s = cat("y_s").reshape(128, 4, D)
    win_k_s = cat("win_k_s").reshape(1, 128, 4, 8, 64)
    win_v_s = cat("win_v_s").reshape(1, 128, 4, 8, 64)
    gst_s = cat("gdn_state_s").reshape(1, 128, 4, 64, 64)
    gconv_s = cat("gdn_conv_s").reshape(1, 128, 3, 768)
    fconv_s = cat("ffn_conv_s").reshape(1, 128, 2, 2 * D_FF)
    return (y_p, y_s, win_k_p, win_v_p, win_k_s, win_v_s, gst_p, gst_s, gconv_p, gconv_s, fconv_p, fconv_s, mem_k_p, mem_v_p)


def sample_consts():
    slopes = 2.0 ** (-8.0 * np.arange(1, 9) / 8.0)
    def cmult(d):
        return ((d >= 0) & (d <= 128)).astype(np.float64) + ((d >= 0) & (d <= 512) & (d % 4 == 0)) + ((d >= 0) & (d <= 2048) & (d % 16 == 0))
    p = np.arange(128)
    rows = np.zeros((7, 128), np.int64)
    for j in range(4):
        rows[j] = 1536 + 128 * j + p
    for j in range(4, 7):
        rows[j] = 16 * (32 * (j - 4) + p % 32) + p // 32
    smask = np.zeros((128, 7, 8, 4), np.float64)
    for j in range(7):
        for s in range(4):
            d = 2048 + s - rows[j]
            c = cmult(d)
            for h in range(8):
                smask[:, j, h, s] = c * np.exp(-slopes[h] * np.maximum(d, 0))
    snew = np.zeros((128, 16, 8, 4), np.float64)
    for b in range(16):
        for sp in range(4):
            for s in range(4):
                d = s - sp
                if d >= 0:
                    snew[4 * b + sp, b, :, s] = cmult(np.array(d)) * np.exp(-slopes * d)
    grp = np.where(p < 64, p // 4, 100 + p)
    same = grp[:, None] == grp[None, :]
    j = p[:, None]; i = p[None, :]
    c = {}
    c["s_triu"] = (same & (j <= i)).astype(np.float32)
    c["s_trils"] = (same & (j > i)).astype(np.float32)
    c["s_negc"] = np.where(same & (i >= j), 0.0, NEGV).astype(np.float32)
    c["s_strict"] = (same & (i > j)).astype(np.float32)
    c["s_causal"] = (same & (i >= j)).astype(np.float32)
    bm = np.zeros((128, 16, 64), np.float32)
    for b in range(16):
        bm[:, b, 4 * b:4 * b + 4] = 1.0
    rm = np.zeros((128, 16), np.float32)
    for b in range(16):
        rm[4 * b:4 * b + 4, b] = 1.0
    c["s_smask"] = smask.reshape(128, -1).astype(np.float32)
    c["s_snew"] = snew.reshape(128, -1).astype(np.float32)
    c["s_bm"] = bm.reshape(128, -1)
    c["s_rm"] = rm
    return c


class Builder(Builder):
    def sample_setup(self):
        B = self
        B.scn = sample_consts()
        for k, v in B.scn.items():
            B.inp("c_" + k, v.shape)
        B.xs = B.inp("xs", [64, D])
        B.cwk = B.inp("cwk", [16, 2048, 512]); B.cwv = B.inp("cwv", [16, 2048, 512])
        B.sgdn = B.inp("sgdn", [16, 4, 64, 64]); B.sgconv = B.inp("sgconv", [16, 3, 768]); B.sfconv = B.inp("sfconv", [16, 2, 2 * D_FF])
        B.cmk = B.inp("cmk", [16, 256, 256]); B.cmv = B.inp("cmv", [16, 256, 256])
        B.o_wks = B.outp("win_k_s", [64, 512]); B.o_wvs = B.outp("win_v_s", [64, 512])
        B.o_gcs = B.outp("gdn_conv_s", [16, 3, 768]); B.o_gss = B.outp("gdn_state_s", [16, 4, 64, 64])
        B.o_fcs = B.outp("ffn_conv_s", [16, 2, 2 * D_FF]); B.o_ys = B.outp("y_s", [64, D])
        z = {}
        def a(name, shape, dt=F32):
            z[name] = B.sbt("z_" + name, shape, dt)
        a("qaT", [128, 4, 64], BF16); a("kaT", [128, 4, 64], BF16); a("qcT", [128, 2, 64], BF16)
        a("qkvT", [128, 6, 64]); a("ab", [128, 8]); a("gate", [128, 256]); a("vaug", [64, 8, 80], BF16)
        a("oaT", [64, 8, 64], BF16); a("obT", [128, 2, 64], BF16); a("ocT", [64, 4, 64], BF16)
        a("x1nT", [128, 8, 64], BF16); a("x1", [64, D])
        B.z = z

    def sample_inproj(self):
        B = self
        pb = B.pb
        z = B.z
        x = B.sbt("sx", [64, D]); sqj = B.sbt("ssqj", [64, D], BF16); ss = B.sbt("sss", [64, 1]); xb = B.sbt("sxb", [64, D], BF16)
        xnT = B.sbt("sxnT", [128, 8, 64], BF16)
        tsq = B.sbt("stsq", [128, 512], BF16); trn = B.sbt("strn", [128, 512])
        tok = B.sbt("stok", [64, 768]); tok2 = B.sbt("stok2", [64, 512]); kss = B.sbt("skss", [64, 8])
        B.dma(SP, x.t[:], B.xs.t.ap(), [B.xs], [x])
        B.rmsnorm_tile(x, sqj, ss, xb)
        for c in range(8):
            B.tr(B.psT.t[:, c, 0:64], xb.t[:, c * 128:(c + 1) * 128], B.identb.t[0:64, 0:64], [xb, B.identb], [B.psT])
        B.cp(DVE, xnT.t[:], B.psT.t[:, :, 0:64], [B.psT], [xnT])

        def fm(col0, npairs, dst_ps):
            for p_ in range(npairs):
                for kc in range(8):
                    B.mm(dst_ps.t[:, p_ * 64:(p_ + 1) * 64], B.Wb.t[:, kc, col0 + p_ * 128:col0 + (p_ + 1) * 128], xnT.t[:, kc, :],
                         [B.Wb, xnT], [dst_ps], start=(kc == 0), stop=(kc == 7))
        fm(C_QA, 4, pb[3])
        B.headnorm_fm(pb[3], 4, 64, B.gw["q_norm_a"][1].t[:], z["qaT"].t[:].rearrange("p a b -> p (a b)"), z["qaT"], pb[4], tsq, trn)
        fm(C_KA, 4, pb[3])
        B.headnorm_fm(pb[3], 4, 64, B.gw["k_norm_a"][1].t[:], z["kaT"].t[:].rearrange("p a b -> p (a b)"), z["kaT"], pb[4], tsq, trn)
        fm(C_QC, 2, pb[3])
        B.headnorm_fm(pb[3], 2, 64, B.gw["q_norm_c"][1].t[:], z["qcT"].t[:].rearrange("p a b -> p (a b)"), z["qcT"], pb[4], tsq, trn)
        fm(C_QKV, 6, pb[3])
        B.cp(DVE, z["qkvT"].t[:].rearrange("p a b -> p (a b)"), pb[3].t[:, 0:384], [pb[3]], [z["qkvT"]])

        def tm(col0, ncol, dst_ps, c_off=0):
            for kc in range(8):
                B.mm(dst_ps.t[0:64, c_off:c_off + ncol], xnT.t[:, kc, :], B.Wb.t[:, kc, col0:col0 + ncol], [B.Wb, xnT], [dst_ps], start=(kc == 0), stop=(kc == 7))
        tm(C_KA, 512, pb[5])
        B.tok_headnorm(pb[5], 8, B.gw["k_norm_a"][0], tok2, kss, np_=64)
        B.dma(SP, B.o_wks.t.ap(), tok2.t[:], [tok2], [B.o_wks])
        tm(C_VA, 512, pb[6])
        tokv = B.sbt("stokv", [64, 512])
        B.cp(ACT, tokv.t[:], pb[6].t[0:64, :], [pb[6]], [tokv])
        B.dma(SP, B.o_wvs.t.ap(), tokv.t[:], [tokv], [B.o_wvs])
        B.memset(POOL, z["vaug"].t[:], 0.0, [z["vaug"]])
        B.memset(POOL, z["vaug"].t[:, :, 64:65], 1.0, [z["vaug"]])
        B.cp(DVE, z["vaug"].t[:, :, 0:64], tokv.t[:].rearrange("p (a b) -> p a b", a=8), [tokv], [z["vaug"]])
        tm(C_QKV, 512, pb[5])
        tm(C_QKV + 512, 256, pb[6])
        B.cp(ACT, tok.t[:, 0:512], pb[5].t[0:64, :], [pb[5]], [tok])
        B.cp(ACT, tok.t[:, 512:768], pb[6].t[0:64, 0:256], [pb[6]], [tok])
        for b in range(16):
            B.dma(SP if b % 2 == 0 else POOL, B.o_gcs.t.ap()[b, :, :], tok.t[4 * b + 1:4 * b + 4, :], [tok], [B.o_gcs])
        B.memset(POOL, z["ab"].t[:], 0.0, [z["ab"]])
        B.memset(POOL, z["gate"].t[:], 0.0, [z["gate"]])
        tm(C_A, 8, pb[5])
        B.cp(DVE, z["ab"].t[0:64, :], pb[5].t[0:64, 0:8], [pb[5]], [z["ab"]])
        tm(C_GATE, 256, pb[6])
        B.cp(DVE, z["gate"].t[0:64, :], pb[6].t[0:64, 0:256], [pb[6]], [z["gate"]])

    def sample_phase(self):
        B = self
        pb = B.pb
        z = B.z
        sc = {}
        for k in ["s_triu", "s_trils", "s_negc", "s_strict", "s_causal", "s_rm"]:
            shp = list(B.scn[k].shape)
            sc[k] = B.sbt("k_" + k, shp)
            B.dma(SP, sc[k].t[:], B.ins["c_" + k].t.ap(), [B.ins["c_" + k]], [sc[k]])
        smask = B.sbt("k_smask", [128, 7 * 32]); B.dma(SP, smask.t[:], B.ins["c_s_smask"].t.ap(), [B.ins["c_s_smask"]], [smask])
        snew = B.sbt("k_snew", [128, 16 * 32]); B.dma(SP, snew.t[:], B.ins["c_s_snew"].t.ap(), [B.ins["c_s_snew"]], [snew])
        bm = B.sbt("k_bm", [128, 16 * 64], BF16); B.dma(POOL, bm.t[:], B.ins["c_s_bm"].t.ap(), [B.ins["c_s_bm"]], [bm])
        kf = [B.sbt("a_kf%d" % i, [128, 512]) for i in range(2)]
        kb = [B.sbt("a_kb%d" % i, [128, 512], BF16) for i in range(2)]
        vf = [B.sbt("a_vf%d" % i, [128, 512]) for i in range(2)]
        kT = B.sbt("a_kT", [128, 4, 7 * 128], BF16)
        va = B.sbt("a_va", [128, 7, 8, 80], BF16)
        E = B.sbt("a_E", [128, 7 * 32]); Pb = B.sbt("a_P", [128, 7 * 32], BF16)
        En = B.sbt("a_En", [64, 32]); Pn = B.sbt("a_Pn", [64, 32], BF16)
        accs = B.sbt("a_acc", [65, 8, 64])
        B.memset(POOL, va.t[:], 0.0, [va])
        B.memset(POOL, va.t[:, :, :, 64:65], 1.0, [va])
        pacc = pb[6]
        pa3 = pacc.t[0:65, :].rearrange("p (h t) -> p h t", h=8)
        it = 0
        for b in range(16):
            for j in range(7):
                k_ = kf[it % 2]; v_ = vf[it % 2]; kb_ = kb[it % 2]
                it += 1
                if j < 4:
                    B.dma(SP, k_.t[:], B.cwk.t.ap()[b, 1536 + 128 * j:1536 + 128 * (j + 1), :], [B.cwk], [k_])
                    B.dma(POOL, v_.t[:], B.cwv.t.ap()[b, 1536 + 128 * j:1536 + 128 * (j + 1), :], [B.cwv], [v_])
                else:
                    m0 = 32 * (j - 4)
                    for r_ in range(4):
                        B.dma(SP, k_.t[32 * r_:32 * r_ + 32, :], B.cwk.t.ap()[b, 16 * m0 + r_:16 * (m0 + 32):16, :], [B.cwk], [k_])
                        B.dma(POOL, v_.t[32 * r_:32 * r_ + 32, :], B.cwv.t.ap()[b, 16 * m0 + r_:16 * (m0 + 32):16, :], [B.cwv], [v_])
                B.cp(ACT, kb_.t[:], k_.t[:], [k_], [kb_])
                B.cp(POOL, va.t[:, j, :, 0:64], v_.t[:].rearrange("p (a b) -> p a b", a=8), [v_], [va])
                for p_ in range(4):
                    B.tr(B.psTb.t[:, p_, :], kb_.t[:, p_ * 128:(p_ + 1) * 128], B.identb.t[:], [kb_, B.identb], [B.psTb])
                B.cp(DVE, kT.t[:, :, j * 128:(j + 1) * 128], B.psTb.t[:, 0:4, :], [B.psTb], [kT])
            pS = pb[1 + b % 2]
            for j in range(7):
                for h in range(8):
                    lo = 64 * (h % 2)
                    B.mm(pS.t[:, j * 32 + h * 4:j * 32 + h * 4 + 4], kT.t[lo:lo + 64, h // 2, j * 128:(j + 1) * 128],
                         z["qaT"].t[lo:lo + 64, h // 2, 4 * b:4 * b + 4], [kT, z["qaT"]], [pS])
            B.act(E.t[:], pS.t[:, 0:224], AF.Exp, [pS], [E], scale=0.125)
            B.tt(DVE, Pb.t[:], E.t[:], smask.t[:], ALU.mult, [E, smask], [Pb])
            pN = pb[3 + b % 2]
            for h in range(8):
                lo = 64 * (h % 2)
                B.mm(pN.t[0:64, h * 4:h * 4 + 4], z["kaT"].t[lo:lo + 64, h // 2, :], z["qaT"].t[lo:lo + 64, h // 2, 4 * b:4 * b + 4],
                     [z["kaT"], z["qaT"]], [pN])
            B.act(En.t[:], pN.t[0:64, 0:32], AF.Exp, [pN], [En], scale=0.125)
            B.tt(DVE, Pn.t[:], En.t[:], snew.t[0:64, b * 32:(b + 1) * 32], ALU.mult, [En, snew], [Pn])
            for h in range(8):
                for j in range(7):
                    B.mm(pa3[:, h, 4 * b:4 * b + 4], va.t[:, j, h, 0:65], Pb.t[:, j * 32 + h * 4:j * 32 + h * 4 + 4], [va, Pb], [pacc],
                         start=(j == 0), stop=False)
                B.mm(pa3[:, h, 4 * b:4 * b + 4], z["vaug"].t[:, h, 0:65], Pn.t[:, h * 4:h * 4 + 4], [z["vaug"], Pn], [pacc], start=False, stop=True)
        B.cp(DVE, accs.t[:], pa3, [pacc], [accs])
        B.mm(pb[5].t[0:64, :], B.onesf.t[64:65, 0:64], accs.t[64:65, :, :].rearrange("p h t -> p (h t)"), [B.onesf, accs], [pb[5]])
        rec = B.sbt("a_rec", [64, 512])
        B.P.op(DVE, lambda e: e.reciprocal(out=rec.t[:], in_=pb[5].t[0:64, :]), [pb[5].b], [rec.b])
        B.tt(DVE, z["oaT"].t[:].rearrange("p h t -> p (h t)"), accs.t[0:64, :, :].rearrange("p h t -> p (h t)"), rec.t[:], ALU.mult, [accs, rec], [z["oaT"]])
        mf = [B.sbt("c_mf%d" % i, [128, 256]) for i in range(2)]
        mb = [B.sbt("c_mb%d" % i, [128, 256], BF16) for i in range(2)]
        mvf = [B.sbt("c_mvf%d" % i, [128, 256]) for i in range(2)]
        mkT = B.sbt("c_mkT", [128, 2, 256], BF16)
        mva = B.sbt("c_mva", [128, 2, 4, 80], BF16)
        Ec = B.sbt("c_E", [128, 32], BF16)
        accc = B.sbt("c_acc", [65, 4, 64])
        B.memset(POOL, mva.t[:], 0.0, [mva])
        B.memset(POOL, mva.t[:, :, :, 64:65], 1.0, [mva])
        pacc = pb[6]
        pc3 = pacc.t[0:65, 0:256].rearrange("p (h t) -> p h t", h=4)
        it = 0
        for b in range(16):
            for mt in range(2):
                f_ = mf[it % 2]; b_ = mb[it % 2]; v_ = mvf[it % 2]
                it += 1
                B.dma(SP, f_.t[:], B.cmk.t.ap()[b, mt * 128:(mt + 1) * 128, :], [B.cmk], [f_])
                B.dma(POOL, v_.t[:], B.cmv.t.ap()[b, mt * 128:(mt + 1) * 128, :], [B.cmv], [v_])
                B.cp(ACT, b_.t[:], f_.t[:], [f_], [b_])
                B.cp(POOL, mva.t[:, mt, :, 0:64], v_.t[:].rearrange("p (a b) -> p a b", a=4), [v_], [mva])
                for p_ in range(2):
                    B.tr(B.psTb.t[:, 4 + p_, :], b_.t[:, p_ * 128:(p_ + 1) * 128], B.identb.t[:], [b_, B.identb], [B.psTb])
                B.cp(DVE, mkT.t[:, :, mt * 128:(mt + 1) * 128], B.psTb.t[:, 4:6, :], [B.psTb], [mkT])
            pS = pb[1 + b % 2]
            for mt in range(2):
                for h in range(4):
                    lo = 64 * (h % 2)
                    B.mm(pS.t[:, mt * 16 + h * 4:mt * 16 + h * 4 + 4], mkT.t[lo:lo + 64, h // 2, mt * 128:(mt + 1) * 128],
                         z["qcT"].t[lo:lo + 64, h // 2, 4 * b:4 * b + 4], [mkT, z["qcT"]], [pS])
            B.act(Ec.t[:], pS.t[:, 0:32], AF.Exp, [pS], [Ec], scale=0.125)
            for h in range(4):
                for mt in range(2):
                    B.mm(pc3[:, h, 4 * b:4 * b + 4], mva.t[:, mt, h, 0:65], Ec.t[:, mt * 16 + h * 4:mt * 16 + h * 4 + 4], [mva, Ec], [pacc],
                         start=(mt == 0), stop=(mt == 1))
        B.cp(DVE, accc.t[:], pc3, [pacc], [accc])
        B.mm(pb[5].t[0:64, 0:256], B.onesf.t[64:65, 0:64], accc.t[64:65, :, :].rearrange("p h t -> p (h t)"), [B.onesf, accc], [pb[5]])
        B.P.op(DVE, lambda e: e.reciprocal(out=rec.t[:, 0:256], in_=pb[5].t[0:64, 0:256]), [pb[5].b], [rec.b])
        B.tt(DVE, z["ocT"].t[:].rearrange("p h t -> p (h t)"), accc.t[0:64, :, :].rearrange("p h t -> p (h t)"), rec.t[:, 0:256], ALU.mult, [accc, rec], [z["ocT"]])
        B.gdn_consts_s()
        B.gdn_alloc("h_")
        B.gsets = {0: B.gs}
        s = B.gs
        B.gm = {"triu": sc["s_triu"], "trils": sc["s_trils"], "negc": sc["s_negc"], "strictf": sc["s_strict"], "causal": sc["s_causal"]}
        stf = B.sbt("h_stf", [48, 768])
        qx = B.sbt("h_qx", [128, 6, 16, 7])
        Ss = B.sbt("h_Ss", [128, 16, 2, 64]); Ssb = B.sbt("h_Ssb", [128, 16, 2, 64], BF16)
        wTm = B.sbt("h_wTm", [128, 2, 16, 64], BF16); qTm = B.sbt("h_qTm", [128, 2, 16, 64], BF16)
        kdm = B.sbt("h_kdm", [128, 16, 256], BF16)
        gmk = B.sbt("h_gmk", [128, 16, 4]); egls = B.sbt("h_egls", [128, 16, 2])
        for p_ in range(2):
            B.dma(SP, Ss.t[:, :, p_, :], B.sgdn.t.ap()[:, 2 * p_:2 * p_ + 2, :, :].rearrange("b h k v -> (h k) b v"), [B.sgdn], [Ss])
        B.cp(ACT, Ssb.t[:], Ss.t[:], [Ss], [Ssb])
        B.dma(SP, stf.t[:], B.sgconv.t.ap().rearrange("b r n -> (b r) n"), [B.sgconv], [stf])

        def conv_fn():
            for c in range(6):
                B.mm(pb[3].t[:, c * 48:(c + 1) * 48], stf.t[:, c * 128:(c + 1) * 128], B.identf.t[0:48, 0:48], [stf, B.identf], [pb[3]])
            B.cp(DVE, qx.t[:, :, :, 0:3], pb[3].t[:, 0:288].rearrange("p (c b r) -> p c b r", c=6, b=16), [pb[3]], [qx])
            B.cp(POOL, qx.t[:, :, :, 3:7], z["qkvT"].t[:].rearrange("p c (b s) -> p c b s", b=16), [z["qkvT"]], [qx])
            B.memset(POOL, s["ca"].t[:], 0.0, [s["ca"]])
            for c in range(6):
                cav = s["ca"].t[:, c, 0:64].rearrange("p (b s) -> p b s", b=16)
                B.ts(DVE, cav, qx.t[:, c, :, 0:4], B.cw.t[:, c, 0:1], None, ALU.mult, None, [qx, B.cw], [s["ca"]])
                for k in range(1, 4):
                    B.stt(DVE, cav, qx.t[:, c, :, k:k + 4], B.cw.t[:, c, k:k + 1], cav, ALU.mult, ALU.add, [qx, B.cw, s["ca"]], [s["ca"]])
            B.cp(DVE, s["ab"].t[:], z["ab"].t[:], [z["ab"]], [s["ab"]])

        def state_fn(psB, psD, gate_fn):
            bm4 = bm.t[:].rearrange("p (b t) -> p b t", b=16)
            for p_ in range(2):
                B.tt(DVE, wTm.t[:, p_, :, :], bc(s["wT"].t[:, p_, 0:64], [128, 16, 64], 1), bm4, ALU.mult, [s["wT"], bm], [wTm])
                B.tt(DVE, qTm.t[:, p_, :, :], bc(s["qT"].t[:, p_, 0:64], [128, 16, 64], 1), bm4, ALU.mult, [s["qT"], bm], [qTm])
            pD = psD.t[0:64, 0:256].rearrange("p (a b) -> p a b", a=4)
            for h in range(4):
                lo = 64 * (h % 2)
                for b in range(16):
                    B.mm(pD[:, h, :], wTm.t[lo:lo + 64, h // 2, b, :], Ssb.t[lo:lo + 64, b, h // 2, :], [wTm, Ssb], [psD], start=(b == 0), stop=(b == 15))
            B.memset(POOL, s["vnew"].t[:], 0.0, [s["vnew"]])
            B.tt(DVE, s["vnew"].t[0:64, :, :], s["ub"].t[0:64, :, :], pD, ALU.subtract, [s["ub"], psD], [s["vnew"]])
            pQ = psB.t[0:64, 0:256].rearrange("p (a b) -> p a b", a=4)
            for h in range(4):
                lo = 64 * (h % 2)
                for b in range(16):
                    B.mm(pQ[:, h, :], qTm.t[lo:lo + 64, h // 2, b, :], Ssb.t[lo:lo + 64, b, h // 2, :], [qTm, Ssb], [psB], start=(b == 0), stop=(b == 15))
            B.memset(POOL, s["o"].t[:], 0.0, [s["o"]])
            B.tt(DVE, s["o"].t[0:64, :, :], pQ, bc(s["egc"].t[0:64, :], [64, 4, 64], 2), ALU.mult, [psB, s["egc"]], [s["o"]])
            pV = psD.t[:, 256:512].rearrange("p (a b) -> p a b", a=4)
            for h in range(4):
                B.mm(pV[:, h, :], s["AT"].t[:, h, :], s["vnew"].t[:, h, :], [s["AT"], s["vnew"]], [psD])
            B.tt(DVE, s["o"].t[0:64, :, :], s["o"].t[0:64, :, :], pV[0:64, :, :], ALU.add, [psD, s["o"]], [s["o"]])
            B.tt(DVE, gmk.t[:], bc(s["g"].t[:], [128, 16, 4], 1), bc(sc["s_rm"].t[:], [128, 16, 4], 2), ALU.mult, [s["g"], sc["s_rm"]], [gmk])
            B.mm(pb[3].t[:, 0:64], B.onesf.t[:], gmk.t[:].rearrange("p b h -> p (b h)"), [B.onesf, gmk], [pb[3]])
            g3 = pb[3].t[:, 0:64].rearrange("p (b h) -> p b h", b=16)
            B.act(egls.t[0:64, :, :], g3[0:64, :, 0:4:2], AF.Exp, [pb[3]], [egls])
            B.act(egls.t[64:128, :, :], g3[64:128, :, 1:4:2], AF.Exp, [pb[3]], [egls])
            B.tt(DVE, kdm.t[:], bc(s["kd"].t[:].rearrange("p a b -> p (a b)"), [128, 16, 256], 1), bc(sc["s_rm"].t[:], [128, 16, 256], 2), ALU.mult,
                 [s["kd"], sc["s_rm"]], [kdm])
            B.tt(DVE, Ss.t[:].rearrange("p b q v -> p (b q) v"), Ss.t[:].rearrange("p b q v -> p (b q) v"),
                 bc(egls.t[:].rearrange("p b q -> p (b q)"), [128, 32, 64], 2), ALU.mult, [Ss, egls], [Ss])
            for b in range(16):
                pk = pb[4 + b % 2]
                pk3 = pk.t[:, 0:256].rearrange("p (a b) -> p a b", a=4)
                for h in range(4):
                    B.mm(pk3[:, h, :], kdm.t[:, b, (h // 2) * 128:(h // 2 + 1) * 128], s["vnew"].t[:, h, :], [kdm, s["vnew"]], [pk])
                B.tt(DVE, Ss.t[0:64, b, :, :], Ss.t[0:64, b, :, :], pk3[0:64, 0:4:2, :], ALU.add, [Ss, pk], [Ss])
                B.tt(DVE, Ss.t[64:128, b, :, :], Ss.t[64:128, b, :, :], pk3[64:128, 1:4:2, :], ALU.add, [Ss, pk], [Ss])
            for p_ in range(2):
                B.dma(SP, B.o_gss.t.ap()[:, 2 * p_:2 * p_ + 2, :, :].rearrange("b h k v -> (h k) b v"), Ss.t[:, :, p_, :], [Ss], [B.o_gss])
            B.tt(DVE, s["o2"].t[:], s["o"].t[:], s["o"].t[:], ALU.mult, [s["o"]], [s["o2"]])
            B.P.op(DVE, lambda e: e.tensor_reduce(out=s["ms"].t[:], in_=s["o2"].t[:], axis=AX.X, op=ALU.add), [s["o2"].b], [s["ms"].b])
            B.act(s["ms"].t[:], s["ms"].t[:], AF.Sqrt, [s["ms"], B.epsc], [s["ms"]], bias=B.epsc.t[:], scale=1.0 / 64)
            B.P.op(DVE, lambda e: e.reciprocal(out=s["ms"].t[:], in_=s["ms"].t[:]), [s["ms"].b], [s["ms"].b])
            B.tt(DVE, s["o"].t[:], s["o"].t[:], bc(s["ms"].t[:], [128, 4, 64], 2), ALU.mult, [s["o"], s["ms"]], [s["o"]])
            B.tt(DVE, s["o"].t[:], s["o"].t[:], bc(B.onw.t[:], [128, 4, 64], 1), ALU.mult, [s["o"], B.onw], [s["o"]])
            B.act(s["gs"].t[:], z["gate"].t[:], AF.Silu, [z["gate"]], [s["gs"]])
            B.tt(DVE, s["obf"].t[:], s["o"].t[:].rearrange("p a b -> p (a b)"), s["gs"].t[:], ALU.mult, [s["o"], s["gs"]], [s["obf"]])
            for p_ in range(2):
                B.tr(B.psTb.t[:, p_, :], s["obf"].t[:, p_ * 128:(p_ + 1) * 128], B.identb.t[:], [s["obf"], B.identb], [B.psTb])
            B.cp(DVE, z["obT"].t[:], B.psTb.t[:, 0:2, 0:64], [B.psTb], [z["obT"]])

        for _ in B.gdn_tile(None, None, pb[3], pb[4], pb[5], pb[6], want_out=True, gate_fn=None, smode={"conv_fn": conv_fn, "state_fn": state_fn}, par=0):
            pass
        B.gm = {"triu": B.triu, "trils": B.trils, "negc": B.negc, "strictf": B.strictf, "causal": B.causalf}

    def gdn_consts_s(self):
        B = self
        cw = B.ins["conv_b_w"]; alog = B.ins["a_log_b"]; dtb = B.ins["dt_bias_b"]; onb = B.ins["out_norm_b"]
        B.cw = B.sbt("s_cw", [128, 6, 4])
        for k in range(4):
            B.dma(SP, B.cw.t[:, :, k], cw.t.ap()[k, :].rearrange("(c p) -> p c", p=128), [cw], [B.cw], allow_slow_non_contiguous=True)
        B.nega = B.sbt("s_nega", [128, 4])
        B.dma(SP, B.nega.t[:], alog.t.ap(), [alog], [B.nega])
        B.act(B.nega.t[:], B.nega.t[:], AF.Exp, [B.nega], [B.nega])
        B.ts(DVE, B.nega.t[:], B.nega.t[:], -1.0, None, ALU.mult, None, [B.nega], [B.nega])
        B.dtb = B.sbt("s_dtb", [128, 4])
        B.dma(SP, B.dtb.t[:], dtb.t.ap(), [dtb], [B.dtb])
        B.onw = B.sbt("s_onw", [128, 64])
        B.dma(SP, B.onw.t[:], onb.t.ap(), [onb], [B.onw])
```

```python
import numpy as np
import concourse.bass as bass
import concourse.mybir as mybir
from concourse.bass_utils import run_bass_kernel_spmd

F32 = mybir.dt.float32
BF16 = mybir.dt.bfloat16
AF = mybir.ActivationFunctionType
ALU = mybir.AluOpType
AX = mybir.AxisListType

PE, ACT, DVE, POOL, SP = 0, 1, 2, 3, 4
ENG_NAMES = ["tensor", "scalar", "vector", "gpsimd", "sync"]
NDSEM = 12
PE_NOSELF = False
ACC_NOSELF = True
MM_NOSELF = False

NCORES = 8
D = 1024
SEQ = 16384
OWN = 2048
N_IN = 2824
C_QA, C_KA, C_VA, C_QKV, C_GATE, C_A, C_B, C_QC = 0, 512, 1024, 1536, 2304, 2560, 2564, 2568
D_FF = 2816
EPS = 1e-6
NEGV = -30000.0


class Buf:
    __slots__ = ("name", "w", "r", "excl")

    def __init__(self, name=""):
        self.name = name
        self.w = None
        self.r = []
        self.excl = False


class Prog:
    def __init__(self, nc):
        self.nc = nc
        self.q = [[] for _ in range(5)]
        self.cnt = [0] * 5
        self.dcnt = [0] * 5
        self.seen = [dict() for _ in range(5)]
        self.sems = {}
        self._semstack = []
        self._stack = []
        self.ninstr = 0

    def sb(self, name, shape, dt=F32):
        g = self.nc.sbuf_tensor(name, list(shape), dt)
        t = g.__enter__()
        self._stack.append(g)
        return t

    def ps(self, name, shape, dt=F32):
        g = self.nc.psum_tensor(name, list(shape), dt)
        t = g.__enter__()
        self._stack.append(g)
        return t

    def mark(self):
        return len(self._stack)

    def release(self, mark):
        while len(self._stack) > mark:
            g = self._stack.pop()
            g.__exit__(None, None, None)

    def sem(self, key):
        if key not in self.sems:
            g = self.nc.semaphore("s_" + "_".join(str(k) for k in key))
            self.sems[key] = g.__enter__()
            self._semstack.append(g)
        return self.sems[key]

    def _deps(self, eng, reads, writes, noself=False):
        need = {}
        for b in reads:
            if b.w is not None:
                k, v = b.w
                if need.get(k, 0) < v:
                    need[k] = v
        for b in writes:
            if b.w is not None:
                k, v = b.w
                if need.get(k, 0) < v:
                    need[k] = v
            for (k, v) in b.r:
                if need.get(k, 0) < v:
                    need[k] = v
        waits = []
        seen = self.seen[eng]
        for k, v in need.items():
            if (PE_NOSELF or noself) and eng == PE and k == ("e", PE):
                continue
            if seen.get(k, 0) >= v:
                continue
            seen[k] = v
            waits.append((k, v))
        return waits

    def _commit(self, tok, reads, writes):
        for b in reads:
            b.r = [x for x in b.r if x[0] != tok[0]]
            b.r.append(tok)
        for b in writes:
            b.w = tok
            b.r = []

    def op(self, eng, fn, reads=(), writes=(), noself=False):
        if eng != PE:
            ex = [b for b in reads if b.excl]
            if ex:
                reads = [b for b in reads if not b.excl]
                writes = list(writes) + [b for b in ex if b not in writes]
        waits = self._deps(eng, reads, writes, noself)
        self.cnt[eng] += 1
        key = ("e", eng)
        tok = (key, self.cnt[eng])
        self.q[eng].append((fn, waits, (key, 1)))
        self._commit(tok, reads, writes)
        self.ninstr += 1
        return tok

    def dma(self, eng, fn, reads=(), writes=()):
        i = self.dcnt[eng]
        self.dcnt[eng] += 1
        slot = i % NDSEM
        key = ("d", eng, slot)
        val = 16 * (i // NDSEM + 1)
        waits = self._deps(eng, reads, writes)
        if i >= NDSEM:
            pv = val - 16
            if self.seen[eng].get(key, 0) < pv:
                self.seen[eng][key] = pv
                waits.append((key, pv))
        tok = (key, val)
        self.q[eng].append((fn, waits, (key, 16)))
        self._commit(tok, reads, writes)
        self.ninstr += 1
        return tok

    def barrier(self):
        targets = []
        for e in range(5):
            if self.cnt[e] > 0:
                targets.append((("e", e), self.cnt[e]))
            n = self.dcnt[e]
            for slot in range(min(n, NDSEM)):
                last = ((n - 1 - slot) // NDSEM) * NDSEM + slot
                targets.append((("d", e, slot), 16 * (last // NDSEM + 1)))
        for e in range(5):
            waits = []
            for (k, v) in targets:
                if k == ("e", e):
                    continue
                if self.seen[e].get(k, 0) < v:
                    self.seen[e][k] = v
                    waits.append((k, v))
            self.q[e].append((None, waits, None))

    def wait_all(self, eng, bufs):
        waits = self._deps(eng, bufs, ())
        self.q[eng].append((None, waits, None))

    def emit(self):
        nc = self.nc
        for e in range(5):
            for (fn, waits, inc) in self.q[e]:
                for (k, v) in waits:
                    self.sem(k)
                if inc is not None:
                    self.sem(inc[0])
        with nc.Block() as block:
            def mk(e):
                def body(engine):
                    for (fn, waits, inc) in self.q[e]:
                        for (k, v) in waits:
                            engine.wait_ge(self.sems[k], v)
                        if fn is not None:
                            ins = fn(engine)
                            ins.then_inc(self.sems[inc[0]], inc[1])
                return body
            for e, nm in enumerate(ENG_NAMES):
                if self.q[e]:
                    getattr(block, nm)(mk(e))
        self.q = [[] for _ in range(5)]

    def close(self):
        self.release(0)
        while self._semstack:
            self._semstack.pop().__exit__(None, None, None)


class T:
    __slots__ = ("t", "b")

    def __init__(self, t, name=""):
        self.t = t
        self.b = Buf(name)


def _consts():
    c = {}
    c["ident"] = np.eye(128, dtype=np.float32)
    j = np.arange(128)[:, None]
    i = np.arange(128)[None, :]
    c["triu"] = (j <= i).astype(np.float32)
    c["trils"] = (j > i).astype(np.float32)
    c["negc"] = np.where(i >= j, 0.0, NEGV).astype(np.float32)
    c["strict"] = (i > j).astype(np.float32)
    c["causal"] = (i >= j).astype(np.float32)
    c["ones"] = np.ones((128, 128), np.float32)
    bd = np.zeros((128, 128), np.float32)
    bd[:64, :64] = 1
    bd[64:, 64:] = 1
    c["bd"] = bd
    return c


class Builder:
    skip = ()
    NSETS = 3

    def __init__(self, NT=128, dbg=False):
        self.NT = NT
        self.L = NT * 128
        self.dbg = dbg
        self.nc = bass.Bass("TRN2", target_bir_lowering=False)
        self.P = Prog(self.nc)
        self.ins = {}
        self.outs = {}
        self.outbufs = []

    def inp(self, name, shape, dt=F32):
        t = self.nc.dram_tensor(name, list(shape), dt, kind="ExternalInput")
        self.ins[name] = T(t, name)
        return self.ins[name]

    def outp(self, name, shape, dt=F32):
        t = self.nc.dram_tensor(name, list(shape), dt, kind="ExternalOutput")
        self.outs[name] = T(t, name)
        self.outbufs.append(self.outs[name].b)
        return self.outs[name]

    def scratch(self, name, shape, dt=F32):
        t = self.nc.dram_tensor(name, list(shape), dt)
        return T(t, name)

    def sbt(self, name, shape, dt=F32):
        return T(self.P.sb(name, shape, dt), name)

    def pst(self, name, shape, dt=F32):
        t = T(self.P.ps(name, shape, dt), name)
        t.b.excl = True
        return t

    def dma(self, q, out, in_, r, w, **kw):
        self.P.dma(q, lambda e: e.dma_start(out=out, in_=in_, **kw), [x.b for x in r], [x.b for x in w])

    def mm(self, out, lhsT, rhs, r, w, start=True, stop=True):
        self.P.op(PE, lambda e: e.matmul(out, lhsT=lhsT, rhs=rhs, start=start, stop=stop),
                  [x.b for x in r], [x.b for x in w], noself=((not start) and ACC_NOSELF) or MM_NOSELF)

    def tr(self, out, in_, ident, r, w):
        self.P.op(PE, lambda e: e.transpose(out=out, in_=in_, identity=ident), [x.b for x in r], [x.b for x in w])

    def act(self, out, in_, func, r, w, eng=ACT, **kw):
        self.P.op(ACT, lambda e: e.activation(out=out, in_=in_, func=func, **kw), [x.b for x in r], [x.b for x in w])

    def cp(self, eng, out, in_, r, w):
        if eng == ACT:
            self.P.op(ACT, lambda e: e.activation(out=out, in_=in_, func=AF.Copy), [x.b for x in r], [x.b for x in w])
        else:
            self.P.op(eng, lambda e: e.tensor_copy(out=out, in_=in_), [x.b for x in r], [x.b for x in w])

    def tt(self, eng, out, in0, in1, op, r, w):
        if eng == POOL and getattr(self, "nopool", False):
            eng = DVE
        self.P.op(eng, lambda e: e.tensor_tensor(out=out, in0=in0, in1=in1, op=op), [x.b for x in r], [x.b for x in w])

    def ts(self, eng, out, in0, s1, s2, op0, op1, r, w):
        if eng == POOL and getattr(self, "nopool", False):
            eng = DVE
        if op1 is None:
            self.P.op(eng, lambda e: e.tensor_scalar(out=out, in0=in0, scalar1=s1, scalar2=None, op0=op0),
                      [x.b for x in r], [x.b for x in w])
        else:
            self.P.op(eng, lambda e: e.tensor_scalar(out=out, in0=in0, scalar1=s1, scalar2=s2, op0=op0, op1=op1),
                      [x.b for x in r], [x.b for x in w])

    def stt(self, eng, out, in0, scalar, in1, op0, op1, r, w):
        eng = DVE
        self.P.op(eng, lambda e: e.scalar_tensor_tensor(out=out, in0=in0, scalar=scalar, in1=in1, op0=op0, op1=op1),
                  [x.b for x in r], [x.b for x in w])

    def memset(self, eng, ap, val, w):
        self.P.op(eng, lambda e: e.memset(ap, val), [], [x.b for x in w])

    def rsqrt_inplace(self, t, ap, scale, r_extra=()):
        self.act(ap, ap, AF.Sqrt, [t, self.epsc] + list(r_extra), [t], scale=scale, bias=self.epsc.t[0:ap.shape[0], :])
        self.P.op(DVE, lambda e: e.reciprocal(out=ap, in_=ap), [t.b], [t.b])


def bc(ap, shape, axis):
    return ap.unsqueeze(axis).to_broadcast(list(shape))


class Builder(Builder):
    def setup_consts(self):
        B = self
        cn = _consts()
        self.cnp = cn
        for k, v in cn.items():
            B.inp("c_" + k, v.shape)
        B.epsc = B.sbt("epsc", [128, 1])
        B.memset(POOL, B.epsc.t[:], EPS, [B.epsc])
        B.onec = B.sbt("onec", [128, 1])
        B.memset(POOL, B.onec.t[:], 1.0, [B.onec])
        def ld(name, dt=F32):
            t = B.sbt("k_" + name + ("b" if dt == BF16 else "f"), [128, 128], dt)
            q = POOL if dt == BF16 else SP
            B.dma(q, t.t[:], B.ins["c_" + name].t.ap(), [B.ins["c_" + name]], [t])
            return t
        B.identf = ld("ident")
        B.identb = ld("ident", BF16)
        B.triu = ld("triu")
        B.trils = ld("trils")
        B.negc = ld("negc")
        B.strictf = ld("strict")
        B.causalf = ld("causal")
        B.onesf = ld("ones")
        B.bdb = ld("bd", BF16)
        B.gm = {"triu": B.triu, "trils": B.trils, "negc": B.negc, "strictf": B.strictf, "causal": B.causalf}

    def setup_weights_in(self):
        B = self
        w_in = B.inp("w_in", [D, N_IN])
        g1 = B.inp("norm1_g", [D])
        B.Wb = B.sbt("Wb", [128, 8, N_IN], BF16)
        B.g1 = B.sbt("g1", [128, 8])
        B.dma(SP, B.g1.t[:], g1.t.ap().rearrange("(k p) -> p k", p=128), [g1], [B.g1], allow_slow_non_contiguous=True)
        m = B.P.mark()
        st = [B.sbt("wst%d" % i, [128, 512]) for i in range(2)]
        i = 0
        for kc in range(8):
            for c0 in range(0, N_IN, 512):
                cn = min(512, N_IN - c0)
                s_ = st[i % 2]
                B.dma(SP if i % 2 == 0 else POOL, s_.t[:, 0:cn], w_in.t.ap()[kc * 128:(kc + 1) * 128, c0:c0 + cn], [w_in], [s_])
                if i % 2 == 0:
                    B.act(B.Wb.t[:, kc, c0:c0 + cn], s_.t[:, 0:cn], AF.Copy, [s_, B.g1], [B.Wb], scale=B.g1.t[:, kc:kc + 1])
                else:
                    B.ts(DVE, B.Wb.t[:, kc, c0:c0 + cn], s_.t[:, 0:cn], B.g1.t[:, kc:kc + 1], None, ALU.mult, None, [s_, B.g1], [B.Wb])
                i += 1
        return m

    def gdn_consts(self):
        B = self
        cw = B.inp("conv_b_w", [4, 768])
        alog = B.inp("a_log_b", [128, 4])
        dtb = B.inp("dt_bias_b", [128, 4])
        onb = B.inp("out_norm_b", [128, 64])
        B.cw = B.sbt("cw", [128, 6, 4])
        for k in range(4):
            B.dma(SP, B.cw.t[:, :, k], cw.t.ap()[k, :].rearrange("(c p) -> p c", p=128), [cw], [B.cw], allow_slow_non_contiguous=True)
        B.nega = B.sbt("nega", [128, 4])
        B.dma(SP, B.nega.t[:], alog.t.ap(), [alog], [B.nega])
        B.act(B.nega.t[:], B.nega.t[:], AF.Exp, [B.nega], [B.nega])
        B.ts(DVE, B.nega.t[:], B.nega.t[:], -1.0, None, ALU.mult, None, [B.nega], [B.nega])
        B.dtb = B.sbt("dtb", [128, 4])
        B.dma(SP, B.dtb.t[:], dtb.t.ap(), [dtb], [B.dtb])
        B.onw = B.sbt("onw", [128, 64])
        B.dma(SP, B.onw.t[:], onb.t.ap(), [onb], [B.onw])

    def gdn_alloc(self, pfx="g_", share=None):
        B = self
        s = {}
        def a(name, shape, dt=F32):
            if share is not None and name in share:
                s[name] = share[name]
            else:
                s[name] = B.sbt(pfx + name, shape, dt)
        a("qkvx", [128, 6, 131]); a("ca", [128, 6, 128]); a("cs", [128, 6, 128])
        a("sqb", [128, 4, 128], BF16); a("rn", [128, 4, 128])
        a("qT", [128, 2, 128], BF16); a("kT", [128, 2, 128], BF16); a("vTb", [128, 2, 128], BF16)
        a("ktok", [128, 4, 64]); a("vtok", [128, 4, 64])
        a("ab", [128, 8]); a("beta", [128, 4]); a("g", [128, 4]); a("ng", [128, 4])
        a("gc", [128, 4]); a("rev", [128, 4]); a("egc", [128, 4]); a("erev", [128, 4])
        a("G1", [128, 4, 128]); a("G2", [128, 4, 128])
        a("Ec", [128, 4, 128]); a("EsB", [128, 4, 128])
        a("X", [128, 4, 128]); a("AT", [128, 4, 128], BF16)
        s["Nn"] = s["G1"]; s["IX"] = s["G2"]
        yv = T(s["ca"].t[:, 0:4, :]); yv.b = s["ca"].b
        s["Y"] = yv
        a("ub", [128, 4, 64]); a("wb", [128, 4, 64], BF16); a("wT", [128, 2, 128], BF16)
        a("kd", [128, 4, 64], BF16); a("vnew", [128, 4, 64], BF16)
        a("S", [128, 2, 64]); a("Sb", [128, 2, 64], BF16); a("eglp", [128, 2])
        a("o", [128, 4, 64]); a("o2", [128, 4, 64]); a("ms", [128, 4]); a("gs", [128, 256]); a("obf", [128, 256], BF16)
        B.gs = s
        if not hasattr(B, "gsets"):
            B.gsets = {}
        B.memset(POOL, s["qkvx"].t[:], 0.0, [s["qkvx"]])
        B.memset(POOL, s["ca"].t[:], 0.0, [s["ca"]])
        B.memset(POOL, s["qT"].t[:], 0.0, [s["qT"]])
        if share is None or "S" not in share:
            B.memset(POOL, s["S"].t[:], 0.0, [s["S"]])
            B.memset(POOL, s["Sb"].t[:], 0.0, [s["Sb"]])

    def gdn_tile(self, psG, psAB, psA, psB, psC, psD, want_out, gate_fn=None, smode=None, par=0, first=False):
        B = self
        s = B.gsets[par]
        n = 128
        qkvx, ca, cs = s["qkvx"], s["ca"], s["cs"]
        if smode is None:
            if first:
                B.memset(POOL, qkvx.t[:, :, 0:3], 0.0, [qkvx])
            else:
                pq_ = B.gsets[(par - 1) % len(B.gsets)]["qkvx"]
                B.cp(POOL, qkvx.t[:, :, 0:3], pq_.t[:, :, 128:131], [pq_], [qkvx])
            B.cp(ACT, qkvx.t[:, 0:3, 3:131], psG[0].t[:], [psG[0]], [qkvx])
            B.cp(DVE, qkvx.t[:, 3:6, 3:131], psG[1].t[:], [psG[1]], [qkvx])
            B.cp(DVE, s["ab"].t[:], psAB.t[:, 0:8], [psAB], [s["ab"]])
        if getattr(B, 'stop_at', 99) <= 1:
            return
        yield
        if smode is not None:
            smode["conv_fn"]()
        for c in range(6 if smode is None else 0):
            if c < 2 and not want_out:
                continue
            eng = DVE if c < 3 else POOL
            B.ts(eng, ca.t[:, c, :], qkvx.t[:, c, 0:128], B.cw.t[:, c, 0:1], None, ALU.mult, None, [qkvx, B.cw], [ca])
            for k in range(1, 4):
                B.stt(eng, ca.t[:, c, :], qkvx.t[:, c, k:k + 128], B.cw.t[:, c, k:k + 1], ca.t[:, c, :], ALU.mult, ALU.add,
                      [qkvx, B.cw, ca], [ca])
        B.act(cs.t[:], ca.t[:], AF.Silu, [ca], [cs])
        if getattr(B, 'stop_at', 99) <= 2:
            return
        yield
        B.tt(POOL, s["sqb"].t[:], cs.t[:, 0:4, :], cs.t[:, 0:4, :], ALU.mult, [cs], [s["sqb"]])
        B.mm(psA.t[:], B.bdb.t[:], s["sqb"].t[:].rearrange("p a b -> p (a b)"), [B.bdb, s["sqb"]], [psA])
        B.act(s["rn"].t[:], psA.t[:].rearrange("p (a b) -> p a b", a=4), AF.Sqrt, [psA, B.epsc], [s["rn"]], bias=B.epsc.t[:], scale=1.0)
        B.P.op(DVE, lambda e: e.reciprocal(out=s["rn"].t[:], in_=s["rn"].t[:]), [s["rn"].b], [s["rn"].b])
        if want_out:
            B.stt(DVE, s["qT"].t[:], cs.t[:, 0:2, :], 0.125, s["rn"].t[:, 0:2, :], ALU.mult, ALU.mult, [cs, s["rn"]], [s["qT"]])
        B.tt(POOL, s["kT"].t[:], cs.t[:, 2:4, :], s["rn"].t[:, 2:4, :], ALU.mult, [cs, s["rn"]], [s["kT"]])
        B.cp(ACT, s["vTb"].t[:], cs.t[:, 4:6, :], [cs], [s["vTb"]])
        if getattr(B, 'stop_at', 99) <= 3:
            return
        yield
        pTb = B.psTb
        for a_ in range(2):
            B.tr(pTb.t[:, a_, :], s["kT"].t[:, a_, :], B.identb.t[:], [s["kT"], B.identb], [pTb])
            B.tr(pTb.t[:, 2 + a_, :], s["vTb"].t[:, a_, :], B.identb.t[:], [s["vTb"], B.identb], [pTb])
        if getattr(B, 'stop_at', 99) <= 3.1:
            return
        B.cp(DVE, s["ktok"].t[:].rearrange("p a b -> p (a b)"), pTb.t[:, 0:2, :].rearrange("p a b -> p (a b)"), [pTb], [s["ktok"]])
        if getattr(B, 'stop_at', 99) <= 3.2:
            return
        B.cp(DVE, s["vtok"].t[:].rearrange("p a b -> p (a b)"), pTb.t[:, 2:4, :].rearrange("p a b -> p (a b)"), [pTb], [s["vtok"]])
        if getattr(B, 'stop_at', 99) <= 4:
            return
        yield
        ab = s["ab"]
        B.act(s["beta"].t[:], ab.t[:, 4:8], AF.Sigmoid, [ab], [s["beta"]])
        B.tt(DVE, s["g"].t[:], ab.t[:, 0:4], B.dtb.t[:], ALU.add, [ab, B.dtb], [s["g"]])
        B.act(s["g"].t[:], s["g"].t[:], AF.Exp, [s["g"]], [s["g"]])
        B.act(s["g"].t[:], s["g"].t[:], AF.Ln, [s["g"], B.onec], [s["g"]], bias=B.onec.t[:], scale=1.0)
        B.tt(DVE, s["g"].t[:], s["g"].t[:], B.nega.t[:], ALU.mult, [s["g"], B.nega], [s["g"]])
        B.ts(DVE, s["ng"].t[:], s["g"].t[:], -1.0, None, ALU.mult, None, [s["g"]], [s["ng"]])
        if getattr(B, 'stop_at', 99) <= 5:
            return
        yield
        B.mm(psD.t[:, 0:4], B.gm["triu"].t[:], s["g"].t[:], [B.gm["triu"], s["g"]], [psD])
        B.mm(psD.t[:, 4:8], B.gm["trils"].t[:], s["g"].t[:], [B.gm["trils"], s["g"]], [psD])
        B.mm(psD.t[:, 8:12], B.onesf.t[:], s["g"].t[:], [B.onesf, s["g"]], [psD])
        B.act(s["egc"].t[:], psD.t[:, 0:4], AF.Exp, [psD], [s["egc"]])
        B.cp(ACT, s["gc"].t[:], psD.t[:, 0:4], [psD], [s["gc"]])
        B.act(s["erev"].t[:], psD.t[:, 4:8], AF.Exp, [psD], [s["erev"]])
        B.act(s["eglp"].t[0:64, :], psD.t[0:64, 8:12:2], AF.Exp, [psD], [s["eglp"]])
        B.act(s["eglp"].t[64:128, :], psD.t[64:128, 9:12:2], AF.Exp, [psD], [s["eglp"]])
        if getattr(B, 'stop_at', 99) <= 6:
            return
        yield
        B.tt(DVE, s["G1"].t[:], bc(B.onesf.t[:], [128, 4, 128], 1), bc(s["g"].t[:], [128, 4, 128], 2), ALU.mult, [B.onesf, s["g"]], [s["G1"]])
        pC = psC.t[:].rearrange("p (a b) -> p a b", a=4)
        for h in range(4):
            B.mm(pC[:, h, :], s["G1"].t[:, h, :], B.gm["triu"].t[:], [s["G1"], B.gm["triu"]], [psC], start=True, stop=True)
        B.tt(DVE, s["G2"].t[:], pC, bc(s["gc"].t[:], [128, 4, 128], 2), ALU.subtract, [psC, s["gc"]], [s["G2"]])
        B.ts(DVE, s["G2"].t[:], s["G2"].t[:], 0.0, None, ALU.min, None, [s["G2"]], [s["G2"]])
        B.act(s["Ec"].t[:], s["G2"].t[:], AF.Exp, [s["G2"]], [s["Ec"]])
        B.tt(POOL, s["Ec"].t[:], s["Ec"].t[:], bc(B.gm["causal"].t[:], [128, 4, 128], 1), ALU.mult, [s["Ec"], B.gm["causal"]], [s["Ec"]])
        if getattr(B, 'stop_at', 99) <= 7:
            return
        yield
        B.tt(POOL, s["EsB"].t[:], s["Ec"].t[:], bc(B.gm["strictf"].t[:], [128, 4, 128], 1), ALU.mult, [s["Ec"], B.gm["strictf"]], [s["EsB"]])
        B.tt(POOL, s["EsB"].t[:], s["EsB"].t[:], bc(s["beta"].t[:], [128, 4, 128], 2), ALU.mult, [s["EsB"], s["beta"]], [s["EsB"]])
        if getattr(B, 'stop_at', 99) <= 8:
            return
        yield
        pA = psA.t[:].rearrange("p (a b) -> p a b", a=4)
        pB = psB.t[:].rearrange("p (a b) -> p a b", a=4)
        for h in range(4):
            lo = 64 * (h % 2)
            kTh = s["kT"].t[lo:lo + 64, h // 2, :]
            B.mm(pA[:, h, :], kTh, kTh, [s["kT"]], [psA])
            if want_out:
                B.mm(pB[:, h, :], kTh, s["qT"].t[lo:lo + 64, h // 2, :], [s["kT"], s["qT"]], [psB])
        B.tt(DVE, s["X"].t[:], pA, s["EsB"].t[:], ALU.mult, [psA, s["EsB"]], [s["X"]])
        if want_out:
            B.tt(DVE, s["AT"].t[:], pB, s["Ec"].t[:], ALU.mult, [psB, s["Ec"]], [s["AT"]])
        yield
        Y = s["Y"]
        B.cp(ACT, Y.t[:, :, 0:64], s["vtok"].t[:], [s["vtok"]], [Y])
        B.tt(POOL, Y.t[:, :, 64:128], s["ktok"].t[:], bc(s["egc"].t[:], [128, 4, 64], 2), ALU.mult, [s["ktok"], s["egc"]], [Y])
        B.tt(POOL, s["kd"].t[:], s["ktok"].t[:], bc(s["erev"].t[:], [128, 4, 64], 2), ALU.mult, [s["ktok"], s["erev"]], [s["kd"]])
        if getattr(B, 'stop_at', 99) <= 10:
            return
        yield
        pTb = B.psTb
        for h in range(4):
            B.mm(pB[:, h, :], s["X"].t[:, h, :], B.identf.t[:], [s["X"], B.identf], [psB])
        B.cp(ACT, s["Nn"].t[:], pB, [psB], [s["Nn"]])
        identb4 = bc(B.identf.t[:], [128, 4, 128], 1)
        B.stt(POOL, s["IX"].t[:], s["X"].t[:], -1.0, identb4, ALU.mult, ALU.add, [s["X"], B.identf], [s["IX"]])
        if getattr(B, 'stop_at', 99) <= 11:
            return
        nlev = getattr(B, 'nlev', 7)
        for lev in range(nlev):
            for h in range(4):
                B.mm(pC[:, h, :], s["IX"].t[:, h, :], Y.t[:, h, :], [s["IX"], Y], [psC])
                if lev < nlev - 1:
                    B.mm(pA[:, h, :], s["Nn"].t[:, h, :], s["X"].t[:, h, :], [s["Nn"], s["X"]], [psA])
                    if lev < nlev - 2:
                        B.mm(pB[:, h, :], s["X"].t[:, h, :], s["Nn"].t[:, h, :], [s["Nn"], s["X"]], [psB])
            if lev < nlev - 1:
                B.cp(DVE, Y.t[:], pC, [psC], [Y])
                B.cp(ACT, s["X"].t[:], pA, [psA], [s["X"]])
                B.tt(POOL, s["IX"].t[:], s["X"].t[:], identb4, ALU.add, [s["X"], B.identf], [s["IX"]])
                if lev < nlev - 2:
                    B.cp(ACT, s["Nn"].t[:], pB, [psB], [s["Nn"]])
                yield
        if getattr(B, 'stop_at', 99) <= 12:
            return
        B.tt(DVE, s["ub"].t[:], pC[:, :, 0:64], bc(s["beta"].t[:], [128, 4, 64], 2), ALU.mult, [psC, s["beta"]], [s["ub"]])
        B.tt(DVE, s["wb"].t[:], pC[:, :, 64:128], bc(s["beta"].t[:], [128, 4, 64], 2), ALU.mult, [psC, s["beta"]], [s["wb"]])
        for p_ in range(2):
            B.tr(pTb.t[:, 4 + p_, :], s["wb"].t[:, 2 * p_:2 * p_ + 2, :].rearrange("p a b -> p (a b)"), B.identb.t[:], [s["wb"], B.identb], [pTb])
        B.cp(DVE, s["wT"].t[:], pTb.t[:, 4:6, :], [pTb], [s["wT"]])
        if getattr(B, 'stop_at', 99) <= 13:
            return
        if smode is not None:
            smode["state_fn"](psB, psD, gate_fn)
            return
        yield
        pD = psD.t[:, 0:256].rearrange("p (a b) -> p a b", a=4)
        pD2 = psD.t[:, 256:512].rearrange("p (a b) -> p a b", a=4)
        pQ = psB.t[:, 0:256].rearrange("p (a b) -> p a b", a=4)
        pV = psB.t[:, 256:512].rearrange("p (a b) -> p a b", a=4)
        for h in range(4):
            lo = 64 * (h % 2)
            B.mm(pD[:, h, :], s["wT"].t[lo:lo + 64, h // 2, :], s["Sb"].t[lo:lo + 64, h // 2, :], [s["wT"], s["Sb"]], [psD])
            if want_out:
                B.mm(pQ[:, h, :], s["qT"].t[lo:lo + 64, h // 2, :], s["Sb"].t[lo:lo + 64, h // 2, :], [s["qT"], s["Sb"]], [psB])
        B.tt(DVE, s["vnew"].t[:], s["ub"].t[:], pD, ALU.subtract, [s["ub"], psD], [s["vnew"]])
        if want_out:
            B.tt(DVE, s["o"].t[:], pQ, bc(s["egc"].t[:], [128, 4, 64], 2), ALU.mult, [psB, s["egc"]], [s["o"]])
        for h in range(4):
            if want_out:
                B.mm(pV[:, h, :], s["AT"].t[:, h, :], s["vnew"].t[:, h, :], [s["AT"], s["vnew"]], [psB])
            B.mm(pD2[:, h, :], s["kd"].t[:, 2 * (h // 2):2 * (h // 2) + 2, :].rearrange("p a b -> p (a b)"), s["vnew"].t[:, h, :], [s["kd"], s["vnew"]], [psD])
        if want_out:
            B.tt(DVE, s["o"].t[:], s["o"].t[:], pV, ALU.add, [psB, s["o"]], [s["o"]])
        B.tt(POOL, s["S"].t[:], s["S"].t[:], bc(s["eglp"].t[:], [128, 2, 64], 2), ALU.mult, [s["S"], s["eglp"]], [s["S"]])
        B.tt(DVE, s["S"].t[0:64, :, :], s["S"].t[0:64, :, :], pD2[0:64, 0:4:2, :], ALU.add, [s["S"], psD], [s["S"]])
        B.tt(DVE, s["S"].t[64:128, :, :], s["S"].t[64:128, :, :], pD2[64:128, 1:4:2, :], ALU.add, [s["S"], psD], [s["S"]])
        B.cp(ACT, s["Sb"].t[:], s["S"].t[:], [s["S"]], [s["Sb"]])
        if want_out:
            B.tt(POOL, s["o2"].t[:], s["o"].t[:], s["o"].t[:], ALU.mult, [s["o"]], [s["o2"]])
            B.P.op(DVE, lambda e: e.tensor_reduce(out=s["ms"].t[:], in_=s["o2"].t[:], axis=AX.X, op=ALU.add), [s["o2"].b], [s["ms"].b])
            B.act(s["ms"].t[:], s["ms"].t[:], AF.Sqrt, [s["ms"], B.epsc], [s["ms"]], bias=B.epsc.t[:], scale=1.0 / 64)
            B.P.op(DVE, lambda e: e.reciprocal(out=s["ms"].t[:], in_=s["ms"].t[:]), [s["ms"].b], [s["ms"].b])
            B.tt(DVE, s["o"].t[:], s["o"].t[:], bc(s["ms"].t[:], [128, 4, 64], 2), ALU.mult, [s["o"], s["ms"]], [s["o"]])
            B.tt(POOL, s["o"].t[:], s["o"].t[:], bc(B.onw.t[:], [128, 4, 64], 1), ALU.mult, [s["o"], B.onw], [s["o"]])
            psGate = gate_fn()
            B.act(s["gs"].t[:], psGate.t[:, 0:256], AF.Silu, [psGate], [s["gs"]])
            B.tt(DVE, s["obf"].t[:], s["o"].t[:].rearrange("p a b -> p (a b)"), s["gs"].t[:], ALU.mult, [s["o"], s["gs"]], [s["obf"]])


def alibi_masks():
    slopes = 2.0 ** (-8.0 * np.arange(1, 9) / 8.0)
    k = np.arange(128)[:, None]
    q = np.arange(128)[None, :]
    m = np.zeros((4, 128, 17, 2, 128), np.float64)
    for o in range(17):
        d = 128 * o + q - k
        c = ((d >= 0) & (d <= 128)).astype(np.float64) + ((d >= 0) & (d <= 512) & (d % 4 == 0)) + ((d >= 0) & (d <= 2048) & (d % 16 == 0))
        for h in range(8):
            m[h // 2, :, o, h % 2, :] = c * np.exp(-slopes[h] * np.maximum(d, 0))
    return m.astype(np.float32)


class Builder(Builder):
    def headnorm_fm(self, ps, npairs, ntok, gcol, dst, dstT, ps2, tmp_sq, tmp_rn):
        B = self
        n = npairs * ntok
        B.act(tmp_sq.t[:, 0:n], ps.t[:, 0:n], AF.Square, [ps], [tmp_sq])
        B.mm(ps2.t[:, 0:n], B.bdb.t[:], tmp_sq.t[:, 0:n], [B.bdb, tmp_sq], [ps2])
        B.act(tmp_rn.t[:, 0:n], ps2.t[:, 0:n], AF.Sqrt, [ps2, B.epsc], [tmp_rn], bias=B.epsc.t[:], scale=1.0 / 64)
        B.P.op(DVE, lambda e: e.reciprocal(out=tmp_rn.t[:, 0:n], in_=tmp_rn.t[:, 0:n]), [tmp_rn.b], [tmp_rn.b])
        B.stt(DVE, dst, ps.t[:, 0:n], gcol, tmp_rn.t[:, 0:n], ALU.mult, ALU.mult, [ps, tmp_rn] + [v[1] for v in B.gw.values()], [dstT])

    def rmsnorm_tile(self, x, sqj, ss, xb, gate_scale=None):
        B = self
        B.act(sqj.t[:], x.t[:], AF.Square, [x], [sqj, ss], accum_out=ss.t[:])
        np_ = x.t.shape[0]
        B.act(ss.t[:], ss.t[:], AF.Sqrt, [ss, B.epsc], [ss], bias=B.epsc.t[0:np_, :], scale=1.0 / D)
        B.P.op(DVE, lambda e: e.reciprocal(out=ss.t[:], in_=ss.t[:]), [ss.b], [ss.b])
        B.act(xb.t[:], x.t[:], AF.Copy, [x, ss], [xb], scale=ss.t[:])

    def tok_headnorm(self, ps, nh, gw, out, kss, np_=128):
        B = self
        n = nh * 64
        P_ = slice(0, np_)
        B.act(out.t[P_, 0:n], ps.t[P_, 0:n], AF.Square, [ps], [out])
        B.P.op(DVE, lambda e: e.tensor_reduce(out=kss.t[P_, 0:nh], in_=out.t[P_, 0:n].rearrange("p (a b) -> p a b", a=nh), axis=AX.X, op=ALU.add), [out.b], [kss.b])
        B.act(kss.t[P_, 0:nh], kss.t[P_, 0:nh], AF.Sqrt, [kss, B.epsc], [kss], bias=B.epsc.t[P_, :], scale=1.0 / 64)
        B.P.op(DVE, lambda e: e.reciprocal(out=kss.t[P_, 0:nh], in_=kss.t[P_, 0:nh]), [kss.b], [kss.b])
        o3 = out.t[P_, 0:n].rearrange("p (a b) -> p a b", a=nh)
        B.tt(DVE, o3, ps.t[P_, 0:n].rearrange("p (a b) -> p a b", a=nh), bc(kss.t[P_, 0:nh], [np_, nh, 64], 2), ALU.mult, [ps, kss], [out])
        B.tt(DVE, o3, o3, bc(gw.t[P_, :], [np_, nh, 64], 1), ALU.mult, [out, gw], [out])

    def prompt_setup(self, NOWN, sample=False):
        B = self
        NT = B.NT
        B.NOWN = NOWN
        B.th = NT - NOWN - 1
        assert B.th >= 0
        B.NF = NOWN + 1
        B.tw0 = max(0, B.th - 16)
        B.KW = NT - B.tw0
        B.xloc = B.inp("xloc", [B.L, D])
        B.valid = B.inp("valid", [128, NT])
        B.flag = B.inp("flag", [128, 1])
        B.amask = B.inp("amask", [4, 128, 17 * 2 * 128])
        for nm in ["q_norm_a", "k_norm_a", "q_norm_c", "k_norm_c"]:
            B.inp(nm, [128, 64])
            B.inp(nm + "_col", [128, 1])
        B.winK = B.outp("win_k_p", [NOWN * 128, 512])
        B.winV = B.outp("win_v_p", [NOWN * 128, 512])
        B.gst = B.outp("gdn_state_p", [4, 64, 64])
        B.gcv = B.outp("gdn_conv_p", [3, 768])
        B.memk_o = B.outp("mem_k_p", [256, 256])
        B.memv_o = B.outp("mem_v_p", [256, 256])
        B.y_o = B.outp("y_p", [NOWN * 128, D])
        B.fcv = B.outp("ffn_conv_p", [2, 2 * D_FF])
        B.kTs = B.scratch("kTs", [4, 128, B.KW * 128], BF16)
        B.qTs = B.scratch("qTs", [4, 128, B.NF * 128], BF16)
        B.vaug = B.scratch("vaug", [B.KW, 128, 8 * 80], BF16)
        B.x1s = B.scratch("x1s", [B.NF * 128, D])
        B.gw = {}
        for nm in ["q_norm_a", "k_norm_a", "q_norm_c", "k_norm_c"]:
            t = B.sbt("gw_" + nm, [128, 64])
            B.dma(SP, t.t[:], B.ins[nm].t.ap(), [B.ins[nm]], [t])
            c = B.sbt("gc_" + nm, [128, 1])
            B.dma(SP, c.t[:], B.ins[nm + "_col"].t.ap(), [B.ins[nm + "_col"]], [c])
            B.gw[nm] = (t, c)
        B.validsb = B.sbt("validsb", [128, NT])
        B.dma(SP, B.validsb.t[:], B.valid.t.ap(), [B.valid], [B.validsb])
        B.flagsb = B.sbt("flagsb", [128, 1])
        B.dma(SP, B.flagsb.t[:], B.flag.t.ap(), [B.flag], [B.flagsb])
        B.pb = [None] + [B.pst("pb%d" % i, [128, 512]) for i in range(1, 7)]
        B.psTb = B.pst("psTb", [128, 8, 128], BF16)
        B.psT = B.pst("psT", [128, 8, 128], BF16)
        if sample:
            B.sample_setup()
        B.m_mix = B.P.mark()
        B.mixTs = B.scratch("mixTs", [8, 128, B.NF * 128], BF16)

    def mem_kv(self):
        B = self
        pb = B.pb
        mem = B.inp("mem_prompt", [256, D])
        wm = B.inp("w_mem_kv", [D, 512])
        gm = B.inp("mem_norm_g", [D])
        B.mkT = B.sbt("mkT", [128, 2, 256], BF16)
        B.mvaug = B.sbt("mvaug", [128, 2, 4, 80], BF16)
        m = B.P.mark()
        B.Wm = B.sbt("Wm", [128, 8, 512], BF16)
        gmc = B.sbt("gmc", [128, 8])
        B.dma(SP, gmc.t[:], gm.t.ap().rearrange("(k p) -> p k", p=128), [gm], [gmc], allow_slow_non_contiguous=True)
        st = B.sbt("wmst", [128, 512])
        for kc in range(8):
            B.dma(SP, st.t[:], wm.t.ap()[kc * 128:(kc + 1) * 128, :], [wm], [st])
            B.ts(DVE, B.Wm.t[:, kc, :], st.t[:], gmc.t[:, kc:kc + 1], None, ALU.mult, None, [st, gmc], [B.Wm])
        x = B.sbt("memx", [128, D]); sqj = B.sbt("memsq", [128, D]); ss = B.sbt("memss", [128, 1]); xb = B.sbt("memxb", [128, D], BF16)
        mT = B.sbt("memT", [128, 8, 256], BF16)
        tsq = B.sbt("mem_tsq", [128, 512], BF16); trn = B.sbt("mem_trn", [128, 512])
        tokf = B.sbt("mem_tokf", [128, 256]); kss = B.sbt("mem_kss", [128, 8])
        for t in range(2):
            B.dma(SP, x.t[:], mem.t.ap()[t * 128:(t + 1) * 128, :], [mem], [x])
            B.rmsnorm_tile(x, sqj, ss, xb)
            for c in range(8):
                B.tr(B.psT.t[:, c, :], xb.t[:, c * 128:(c + 1) * 128], B.identb.t[:], [xb, B.identb], [B.psT])
            B.cp(DVE, mT.t[:, :, t * 128:(t + 1) * 128], B.psT.t[:], [B.psT], [mT])
        for p_ in range(2):
            for kc in range(8):
                B.mm(pb[3].t[:, p_ * 256:(p_ + 1) * 256], B.Wm.t[:, kc, p_ * 128:(p_ + 1) * 128], mT.t[:, kc, :], [B.Wm, mT], [pb[3]], start=(kc == 0), stop=(kc == 7))
        B.headnorm_fm(pb[3], 2, 256, B.gw["k_norm_c"][1].t[:], B.mkT.t[:].rearrange("p a b -> p (a b)"), B.mkT, pb[4], tsq, trn)
        B.memset(POOL, B.mvaug.t[:], 0.0, [B.mvaug])
        B.memset(POOL, B.mvaug.t[:, :, :, 64:65], 1.0, [B.mvaug])
        for t in range(2):
            for kc in range(8):
                B.mm(pb[5].t[:, 0:512], mT.t[:, kc, t * 128:(t + 1) * 128], B.Wm.t[:, kc, :], [B.Wm, mT], [pb[5]], start=(kc == 0), stop=(kc == 7))
            pk = T(pb[5].t[:, 0:256]); pk.b = pb[5].b
            B.tok_headnorm(pk, 4, B.gw["k_norm_c"][0], tokf, kss)
            B.dma(SP, B.memk_o.t.ap()[t * 128:(t + 1) * 128, :], tokf.t[:], [tokf], [B.memk_o])
            tokv = B.sbt("mem_tokv%d" % t, [128, 256])
            B.cp(ACT, tokv.t[:], pb[5].t[:, 256:512], [pb[5]], [tokv])
            B.dma(SP, B.memv_o.t.ap()[t * 128:(t + 1) * 128, :], tokv.t[:], [tokv], [B.memv_o])
            B.cp(DVE, B.mvaug.t[:, t, :, 0:64], pb[5].t[:, 256:512].rearrange("p (a b) -> p a b", a=4), [pb[5]], [B.mvaug])
        return m

    def prompt_loop(self):
        B = self
        NT, NOWN, th, tw0 = B.NT, B.NOWN, B.th, B.tw0
        pb = B.pb
        psT = B.psT
        xt = [B.sbt("xt%d" % i, [128, D]) for i in range(B.NSETS)]
        sqj = B.sbt("sqj", [128, D], BF16)
        NS = B.NSETS
        ssL = [B.sbt("ss%d" % i, [128, 1]) for i in range(NS)]
        xbL = [B.sbt("xb%d" % i, [128, D], BF16) for i in range(NS)]
        xnTL = [B.sbt("xnT%d" % i, [128, 8, 128], BF16) for i in range(NS)]
        kv = B.sbt("kvtok", [128, 512]); kv2 = B.sbt("kv2", [128, 512]); kss = B.sbt("kss", [128, 8])
        tsq = B.sbt("tsq", [128, 512], BF16); trn = B.sbt("trn", [128, 512])
        kst = B.sbt("kst", [128, 4, 128], BF16); qst = B.sbt("qst", [128, 4, 128], BF16)
        vst = B.sbt("vst", [128, 8, 80], BF16)
        qcT = B.sbt("qcT", [128, 2, 128], BF16)
        pcs = B.sbt("pcs", [128, 4, 128], BF16)
        rc = B.sbt("rc", [128, 2]); ocp = B.sbt("ocp", [128, 2, 64], BF16)
        mixst = B.sbt("mixst", [128, 4, 128], BF16)
        B.memset(POOL, vst.t[:], 0.0, [vst])
        B.gdn_alloc("g0_"); B.gsets[0] = B.gs
        shared = {k: B.gsets[0][k] for k in ["S", "Sb", "o", "o2", "ms", "gs", "obf"]}
        for i in range(1, NS):
            B.gdn_alloc("g%d_" % i, share=shared); B.gsets[i] = B.gs

        def tile_gen(t):
            par = t % NS
            s = B.gsets[par]
            ss = ssL[par]; xb = xbL[par]; xnT = xnTL[par]
            win = t >= tw0
            F = t >= th
            own = t > th
            f = t - th
            kw = t - tw0
            x = xt[t % NS]
            B.dma(SP, x.t[:], B.xloc.t.ap()[t * 128:(t + 1) * 128, :], [B.xloc], [x])
            B.rmsnorm_tile(x, sqj, ss, xb)
            for c in range(8):
                B.tr(psT.t[:, c, :], xb.t[:, c * 128:(c + 1) * 128], B.identb.t[:], [xb, B.identb], [psT])
            B.cp(DVE, xnT.t[:], psT.t[:], [psT], [xnT])
            if win and 'win' not in B.skip:
                for p_ in range(4):
                    if 'kproj' in B.skip:
                        break
                    for kc in range(8):
                        B.mm(pb[3].t[:, p_ * 128:(p_ + 1) * 128], B.Wb.t[:, kc, C_KA + p_ * 128:C_KA + (p_ + 1) * 128], xnT.t[:, kc, :],
                             [B.Wb, xnT], [pb[3]], start=(kc == 0), stop=(kc == 7))
                if 'kproj' not in B.skip:
                  B.headnorm_fm(pb[3], 4, 128, B.gw["k_norm_a"][1].t[:], kst.t[:].rearrange("p a b -> p (a b)"), kst, pb[4], tsq, trn)
                if 'ktsdma' not in B.skip:
                    B.dma(SP, B.kTs.t.ap()[:, :, kw * 128:(kw + 1) * 128].rearrange("a p n -> p a n"), kst.t[:], [kst], [B.kTs])
                for kc in range(8):
                    B.mm(pb[4].t[:], xnT.t[:, kc, :], B.Wb.t[:, kc, C_VA:C_VA + 512], [B.Wb, xnT], [pb[4]], start=(kc == 0), stop=(kc == 7))
                if 'vts' not in B.skip:
                    if 'novalid' in B.skip:
                        B.cp(DVE, vst.t[:, :, 0:64], pb[4].t[:].rearrange("p (a b) -> p a b", a=8), [pb[4]], [vst])
                    elif 'actv' in B.skip:
                        B.act(vst.t[:, :, 0:64], pb[4].t[:].rearrange("p (a b) -> p a b", a=8), AF.Copy, [pb[4], B.validsb], [vst], scale=B.validsb.t[:, t:t + 1])
                    else:
                        B.tt(DVE, vst.t[:, :, 0:64], pb[4].t[:].rearrange("p (a b) -> p a b", a=8),
                             B.validsb.t[:, t:t + 1].unsqueeze(1).to_broadcast([128, 8, 64]), ALU.mult, [pb[4], B.validsb], [vst])
                if 'vstcp' not in B.skip:
                    B.cp(DVE, vst.t[:, :, 64:65], bc(B.validsb.t[:, t:t + 1], [128, 8, 1], 1), [B.validsb], [vst])
                if 'vaugdma' not in B.skip:
                    B.dma(SP, B.vaug.t.ap()[kw, :, :], vst.t[:].rearrange("p a b -> p (a b)"), [vst], [B.vaug])
                if own:
                    r0 = (f - 1) * 128
                    B.cp(ACT, kv2.t[:], pb[4].t[:], [pb[4]], [kv2])
                    B.dma(SP, B.winV.t.ap()[r0:r0 + 128, :], kv2.t[:], [kv2], [B.winV])
                    for kc in range(8):
                        B.mm(pb[5].t[:], xnT.t[:, kc, :], B.Wb.t[:, kc, C_KA:C_KA + 512], [B.Wb, xnT], [pb[5]], start=(kc == 0), stop=(kc == 7))
                    B.tok_headnorm(pb[5], 8, B.gw["k_norm_a"][0], kv, kss)
                    B.dma(SP, B.winK.t.ap()[r0:r0 + 128, :], kv.t[:], [kv], [B.winK])
            if F and 'fq' not in B.skip:
                for p_ in range(4):
                    for kc in range(8):
                        B.mm(pb[3].t[:, p_ * 128:(p_ + 1) * 128], B.Wb.t[:, kc, C_QA + p_ * 128:C_QA + (p_ + 1) * 128], xnT.t[:, kc, :],
                             [B.Wb, xnT], [pb[3]], start=(kc == 0), stop=(kc == 7))
                B.headnorm_fm(pb[3], 4, 128, B.gw["q_norm_a"][1].t[:], qst.t[:].rearrange("p a b -> p (a b)"), qst, pb[4], tsq, trn)
                B.dma(SP, B.qTs.t.ap()[:, :, f * 128:(f + 1) * 128].rearrange("a p n -> p a n"), qst.t[:], [qst], [B.qTs])
            if F and 'cross' not in B.skip:
                for p_ in range(2):
                    for kc in range(8):
                        B.mm(pb[3].t[:, p_ * 128:(p_ + 1) * 128], B.Wb.t[:, kc, C_QC + p_ * 128:C_QC + (p_ + 1) * 128], xnT.t[:, kc, :],
                             [B.Wb, xnT], [pb[3]], start=(kc == 0), stop=(kc == 7))
                B.headnorm_fm(pb[3], 2, 128, B.gw["q_norm_c"][1].t[:], qcT.t[:].rearrange("p a b -> p (a b)"), qcT, pb[4], tsq, trn)
                for p_ in range(2):
                    p5 = pb[5].t[:].rearrange("p (a b) -> p a b", a=4)
                    for mt in range(2):
                        for h2 in range(2):
                            lo = 64 * h2
                            B.mm(p5[:, mt * 2 + h2, :], B.mkT.t[lo:lo + 64, p_, mt * 128:(mt + 1) * 128], qcT.t[lo:lo + 64, p_, :],
                                 [B.mkT, qcT], [pb[5]])
                    B.act(pcs.t[:], p5, AF.Exp, [pb[5]], [pcs], scale=0.125)
                    for h2 in range(2):
                        for mt in range(2):
                            B.mm(pb[6].t[:, h2 * 128:h2 * 128 + 65], pcs.t[:, mt * 2 + h2, :], B.mvaug.t[:, mt, 2 * p_ + h2, 0:65],
                                 [pcs, B.mvaug], [pb[6]], start=(mt == 0), stop=(mt == 1))
                    B.P.op(DVE, lambda e: e.reciprocal(out=rc.t[:], in_=pb[6].t[:, 64:256:128]), [pb[6].b], [rc.b])
                    B.tt(DVE, ocp.t[:], pb[6].t[:, 0:256].rearrange("p (a b) -> p a b", a=2)[:, :, 0:64], bc(rc.t[:], [128, 2, 64], 2), ALU.mult,
                         [pb[6], rc], [ocp])
                    B.tr(B.psTb.t[:, 0, :], ocp.t[:].rearrange("p a b -> p (a b)"), B.identb.t[:], [ocp, B.identb], [B.psTb])
                    B.cp(DVE, mixst.t[:, 2 + p_, :], B.psTb.t[:, 0, :], [B.psTb], [mixst])
                B.dma(SP, B.mixTs.t.ap()[6:8, :, f * 128:(f + 1) * 128].rearrange("a p n -> p a n"), mixst.t[:, 2:4, :], [mixst], [B.mixTs])
            for ct in (0, 3, 1, 4, 2, 5):
                dst = pb[1 + ct // 3]
                for kc in range(8):
                    B.mm(dst.t[:, (ct % 3) * 128:(ct % 3 + 1) * 128], B.Wb.t[:, kc, C_QKV + ct * 128:C_QKV + (ct + 1) * 128],
                         xnT.t[:, kc, :], [B.Wb, xnT], [dst], start=(kc == 0), stop=(kc == 7))
            for kc in range(8):
                B.mm(pb[1].t[:, 384:392], xnT.t[:, kc, :], B.Wb.t[:, kc, C_A:C_A + 8], [B.Wb, xnT], [pb[1]], start=(kc == 0), stop=(kc == 7))
            psG = [T(pb[1].t[:, 0:384].rearrange("p (a b) -> p a b", a=3)), T(pb[2].t[:, 0:384].rearrange("p (a b) -> p a b", a=3))]
            psG[0].b = pb[1].b
            psG[1].b = pb[2].b
            psAB = T(pb[1].t[:, 384:392])
            psAB.b = pb[1].b

            def gate_fn():
                for kc in range(8):
                    B.mm(pb[3].t[:, 0:256], xnT.t[:, kc, :], B.Wb.t[:, kc, C_GATE:C_GATE + 256], [B.Wb, xnT], [pb[3]], start=(kc == 0), stop=(kc == 7))
                return pb[3]
            yield from B.gdn_tile(psG, psAB, pb[3], pb[4], pb[5], pb[6], want_out=F and 'gout' not in B.skip, gate_fn=gate_fn, par=par, first=(t == 0))
            if F and 'gout' not in B.skip:
                for p_ in range(2):
                    B.tr(B.psTb.t[:, p_, :], s["obf"].t[:, p_ * 128:(p_ + 1) * 128], B.identb.t[:], [s["obf"], B.identb], [B.psTb])
                B.cp(DVE, mixst.t[:, 0:2, :], B.psTb.t[:, 0:2, :], [B.psTb], [mixst])
                B.dma(SP, B.mixTs.t.ap()[4:6, :, f * 128:(f + 1) * 128].rearrange("a p n -> p a n"), mixst.t[:, 0:2, :], [mixst], [B.mixTs])

        active = []
        nxt = 0
        while active or nxt < NT:
            if len(active) < NS and nxt < NT:
                active.append(tile_gen(nxt)); nxt += 1
            for g_ in list(active):
                try:
                    next(g_)
                except StopIteration:
                    active.remove(g_)
        s = B.gsets[(NT - 1) % NS]
        B.dma(SP, B.gst.t.ap().rearrange("(p h) k v -> (h k) p v", h=2), s["S"].t[:], [s["S"]], [B.gst])
        gct = B.sbt("gct", [3, 768])
        for c in range(4):
            B.mm(pb[3].t[0:3, c * 128:(c + 1) * 128], s["qkvx"].t[:, c, 128:131], B.identf.t[:], [s["qkvx"], B.identf], [pb[3]])
        B.cp(DVE, gct.t[:, 0:512], pb[3].t[0:3, 0:512], [pb[3]], [gct])
        for c in range(4, 6):
            B.mm(pb[4].t[0:3, (c - 4) * 128:(c - 3) * 128], s["qkvx"].t[:, c, 128:131], B.identf.t[:], [s["qkvx"], B.identf], [pb[4]])
        B.cp(DVE, gct.t[:, 512:768], pb[4].t[0:3, 0:256], [pb[4]], [gct])
        B.dma(SP, B.gcv.t.ap(), gct.t[:], [gct], [B.gcv])

    def attention_phase(self):
        B = self
        NF, th, tw0, KW = B.NF, B.th, B.tw0, B.KW
        pb = B.pb
        vall = B.sbt("vall", [128, KW, 8 * 80], BF16)
        B.dma(SP, vall.t[:], B.vaug.t.ap().rearrange("k p n -> p k n"), [B.vaug], [vall])
        kTb = [B.sbt("kTb%d" % i, [128, KW * 128], BF16) for i in range(2)]
        qTb = [B.sbt("qTb%d" % i, [128, NF * 128], BF16) for i in range(2)]
        msk = [B.sbt("msk%d" % i, [128, 17 * 2 * 128]) for i in range(2)]
        Eb = [B.sbt("Eb%d" % i, [128, 4, 128]) for i in range(2)]
        Pm = [B.sbt("Pm%d" % i, [128, 4, 128], BF16) for i in range(2)]
        rc = B.sbt("arc", [128, 2]); oap = B.sbt("oap", [128, 2, 64], BF16)
        amix = [B.sbt("amix%d" % i, [128, 128], BF16) for i in range(2)]
        it = 0
        for hp in range(4):
            kT = kTb[hp % 2]; qT = qTb[hp % 2]; mk = msk[hp % 2]
            B.dma(SP, kT.t[:], B.kTs.t.ap()[hp], [B.kTs], [kT])
            B.dma(SP, qT.t[:], B.qTs.t.ap()[hp], [B.qTs], [qT])
            B.dma(POOL, mk.t[:], B.amask.t.ap()[hp], [B.amask], [mk])
            mk4 = mk.t[:].rearrange("p (o h q) -> p o h q", o=17, h=2)
            jobs = []
            for f in range(NF):
                tq = th + f
                offs = [o for o in range(17) if tq - o >= tw0]
                groups = [offs[i:i + 2] for i in range(0, len(offs), 2)]
                for gi, g in enumerate(groups):
                    jobs.append((f, g, offs, gi == len(groups) - 1))

            def stageA(n):
                f, g, offs, last = jobs[n]
                tq = th + f
                psS = pb[1 + n % 2]
                pS = psS.t[:].rearrange("p (a b) -> p a b", a=4)
                for oi, o in enumerate(g):
                    kw = tq - o - tw0
                    for h2 in range(2):
                        lo = 64 * h2
                        B.mm(pS[:, oi * 2 + h2, :], kT.t[lo:lo + 64, kw * 128:(kw + 1) * 128], qT.t[lo:lo + 64, f * 128:(f + 1) * 128],
                             [kT, qT], [psS])

            def stageB(n):
                f, g, offs, last = jobs[n]
                psS = pb[1 + n % 2]
                pS = psS.t[:].rearrange("p (a b) -> p a b", a=4)
                E = Eb[n % 2]; P_ = Pm[n % 2]
                ns = 2 * len(g)
                B.act(E.t[:, 0:ns, :], pS[:, 0:ns, :], AF.Exp, [psS], [E], scale=0.125)
                B.tt(DVE, P_.t[:, 0:ns, :], E.t[:, 0:ns, :], mk4[:, g[0]:g[0] + len(g), :, :].rearrange("p o h q -> p (o h) q"), ALU.mult, [E, mk], [P_])

            def stageC(n):
                f, g, offs, last = jobs[n]
                tq = th + f
                P_ = Pm[n % 2]
                pab = 5 if f % 2 == 0 else 3
                for oi, o in enumerate(g):
                    kw = tq - o - tw0
                    for h2 in range(2):
                        h = 2 * hp + h2
                        pa_ = pb[pab + h2]
                        B.mm(pa_.t[:, 0:65], P_.t[:, oi * 2 + h2, :], vall.t[:, kw, h * 80:h * 80 + 65], [P_, vall], [pa_],
                             start=(o == offs[0]), stop=(o == offs[-1]))
                if not last:
                    return
                for h2 in range(2):
                    pa_ = pb[pab + h2]
                    B.ts(DVE, rc.t[:, h2:h2 + 1], pa_.t[:, 64:65], 1e-30, None, ALU.add, None, [pa_], [rc])
                B.P.op(DVE, lambda e: e.reciprocal(out=rc.t[:], in_=rc.t[:]), [rc.b], [rc.b])
                for h2 in range(2):
                    pa_ = pb[pab + h2]
                    B.tt(DVE, oap.t[:, h2, :], pa_.t[:, 0:64], rc.t[:, h2:h2 + 1].to_broadcast([128, 64]), ALU.mult, [pa_, rc], [oap])
                B.tr(B.psTb.t[:, 0, :], oap.t[:].rearrange("p a b -> p (a b)"), B.identb.t[:], [oap, B.identb], [B.psTb])
                ms_ = amix[f % 2]
                B.cp(DVE, ms_.t[:], B.psTb.t[:, 0, :], [B.psTb], [ms_])
                B.dma(SP, B.mixTs.t.ap()[hp, :, f * 128:(f + 1) * 128], ms_.t[:], [ms_], [B.mixTs])

            stageA(0)
            for n in range(len(jobs)):
                if n + 1 < len(jobs):
                    stageA(n + 1)
                stageB(n)
                stageC(n)

    def outproj_phase(self):
        B = self
        NF, th = B.NF, B.th
        pb = B.pb
        w_out = B.inp("w_out", [D, D])
        B.x1nTs = B.scratch("x1nTs", [8, 128, NF * 128], BF16)
        Wo = B.sbt("Wo", [128, 8, D], BF16)
        st = [B.sbt("wost%d" % i, [128, D]) for i in range(2)]
        for kc in range(8):
            B.dma(SP if kc % 2 == 0 else POOL, st[kc % 2].t[:], w_out.t.ap()[kc * 128:(kc + 1) * 128, :], [w_out], [st[kc % 2]])
            B.cp(ACT if kc % 2 == 0 else DVE, Wo.t[:, kc, :], st[kc % 2].t[:], [st[kc % 2]], [Wo])
        xt = [B.sbt("oxt%d" % i, [128, D]) for i in range(2)]
        x1 = [B.sbt("ox1%d" % i, [128, D]) for i in range(2)]
        sqj = B.sbt("osqj", [128, D]); ss = B.sbt("oss", [128, 1]); xb = B.sbt("oxb", [128, D], BF16)
        xnT = B.sbt("oxnT", [128, 8, 128], BF16)
        omix = [B.sbt("omix%d" % i, [128, 8, 128], BF16) for i in range(2)]
        for f in range(NF):
            t = th + f
            x = xt[f % 2]; y = x1[f % 2]
            B.dma(SP, x.t[:], B.xloc.t.ap()[t * 128:(t + 1) * 128, :], [B.xloc], [x])
            mx_ = omix[f % 2]
            B.dma(POOL, mx_.t[:], B.mixTs.t.ap()[:, :, f * 128:(f + 1) * 128].rearrange("a p n -> p a n"), [B.mixTs], [mx_])
            for half in range(2):
                ps = pb[1 + half]
                for kc in range(8):
                    B.mm(ps.t[:], mx_.t[:, kc, :], Wo.t[:, kc, half * 512:(half + 1) * 512], [mx_, Wo], [ps],
                         start=(kc == 0), stop=(kc == 7))
                B.tt(DVE, y.t[:, half * 512:(half + 1) * 512], x.t[:, half * 512:(half + 1) * 512], ps.t[:], ALU.add, [x, ps], [y])
            B.dma(SP, B.x1s.t.ap()[f * 128:(f + 1) * 128, :], y.t[:], [y], [B.x1s])
            B.rmsnorm_tile(y, sqj, ss, xb)
            for c in range(8):
                B.tr(B.psT.t[:, c, :], xb.t[:, c * 128:(c + 1) * 128], B.identb.t[:], [xb, B.identb], [B.psT])
            B.cp(DVE, xnT.t[:], B.psT.t[:], [B.psT], [xnT])
            B.dma(SP, B.x1nTs.t.ap()[:, :, f * 128:(f + 1) * 128].rearrange("a p n -> p a n"), xnT.t[:], [xnT], [B.x1nTs])
        if hasattr(B, "z"):
            z = B.z
            WoA = B.sbt("WoA", [64, 8, D], BF16); WoC = B.sbt("WoC", [64, 4, D], BF16)
            for h in range(12):
                s_ = st[h % 2]
                r0 = h * 64 if h < 8 else 768 + (h - 8) * 64
                B.dma(SP if h % 2 == 0 else POOL, s_.t[0:64, :], w_out.t.ap()[r0:r0 + 64, :], [w_out], [s_])
                dst = WoA.t[:, h, :] if h < 8 else WoC.t[:, h - 8, :]
                B.cp(ACT if h % 2 == 0 else DVE, dst, s_.t[0:64, :], [s_], [WoA if h < 8 else WoC])
            xs_ = B.sbt("oxs", [64, D]); sqs = B.sbt("osqs", [64, D], BF16); sss = B.sbt("osss", [64, 1]); xbs = B.sbt("oxbs", [64, D], BF16)
            B.dma(SP, xs_.t[:], B.xs.t.ap(), [B.xs], [xs_])
            for half in range(2):
                ps = pb[1 + half]
                hs = slice(half * 512, (half + 1) * 512)
                for h in range(8):
                    B.mm(ps.t[0:64, :], z["oaT"].t[:, h, :], WoA.t[:, h, hs], [z["oaT"], WoA], [ps], start=(h == 0), stop=False)
                for p_ in range(2):
                    B.mm(ps.t[0:64, :], z["obT"].t[:, p_, :], Wo.t[:, 4 + p_, hs], [z["obT"], Wo], [ps], start=False, stop=False)
                for h in range(4):
                    B.mm(ps.t[0:64, :], z["ocT"].t[:, h, :], WoC.t[:, h, hs], [z["ocT"], WoC], [ps], start=False, stop=(h == 3))
                B.tt(DVE, z["x1"].t[:, hs], xs_.t[:, hs], ps.t[0:64, :], ALU.add, [xs_, ps], [z["x1"]])
            B.rmsnorm_tile(z["x1"], sqs, sss, xbs)
            for c in range(8):
                B.tr(B.psT.t[:, c, 0:64], xbs.t[:, c * 128:(c + 1) * 128], B.identb.t[0:64, 0:64], [xbs, B.identb], [B.psT])
            B.cp(DVE, z["x1nT"].t[:], B.psT.t[:, :, 0:64], [B.psT], [z["x1nT"]])

    def ffn_phase(self):
        B = self
        NF, NOWN = B.NF, B.NOWN
        pb = B.pb
        g2 = B.inp("norm2_g", [D])
        w_up = B.inp("w_up", [D, 2 * D_FF])
        cfw = B.inp("conv_ffn_w", [3, 2 * D_FF])
        w_down = B.inp("w_down", [D_FF, D])
        g2c = B.sbt("g2c", [128, 8])
        B.dma(SP, g2c.t[:], g2.t.ap().rearrange("(k p) -> p k", p=128), [g2], [g2c], allow_slow_non_contiguous=True)
        cfc = B.sbt("cfc", [128, 44, 3])
        for k in range(3):
            B.dma(SP, cfc.t[:, :, k], cfw.t.ap()[k, :].rearrange("(c p) -> p c", p=128), [cfw], [cfc], allow_slow_non_contiguous=True)
        Wd = B.sbt("Wd", [128, 22, D], BF16)
        x1t = [B.sbt("fx1t%d" % i, [128, D]) for i in range(2)]
        yt = [B.sbt("fyt%d" % i, [128, D]) for i in range(2)]
        st = x1t
        for j in range(22):
            B.dma(SP if j % 2 == 0 else POOL, st[j % 2].t[:], w_down.t.ap()[j * 128:(j + 1) * 128, :], [w_down], [st[j % 2]])
            B.cp(ACT if j % 2 == 0 else DVE, Wd.t[:, j, :], st[j % 2].t[:], [st[j % 2]], [Wd])
        n0 = (NF + 2) // 3
        groups = [(0, min(n0, NF)), (min(n0, NF), min(2 * n0, NF)), (min(2 * n0, NF), NF)]
        NG = n0 * 128
        x1nT = B.sbt("fx1nT", [128, 8, NG], BF16)
        hT = B.sbt("fhT", [128, 22, NG], BF16)
        ug = [B.sbt("fug%d" % i, [128, NG + 2]) for i in range(2)]
        cg = [B.sbt("fcg%d" % i, [128, NG]) for i in range(2)]
        sg = B.sbt("fsg", [128, NG])
        wst = [[B.sbt("fwst%d_%d" % (i, k), [128, 8, 128]) for k in range(2)] for i in range(2)]
        wbf = [[B.sbt("fwbf%d_%d" % (i, k), [128, 8, 128], BF16) for k in range(2)] for i in range(2)]
        hsave = B.sbt("fhsave", [128, 44, 2])
        B.memset(POOL, hsave.t[:], 0.0, [hsave])
        wups = [B.scratch("wups%d" % i, [128, 8 * 128], BF16) for i in range(44)]
        SAMP = hasattr(B, "z")
        if SAMP:
            z = B.z
            fst = B.sbt("s_fst", [128, 44, 16, 2])
            fstg = [B.sbt("s_fstg%d" % i, [32, 512]) for i in range(2)]
            sfv = B.sfconv.t.ap().rearrange("b r n -> (b r) n")
            for c4 in range(11):
                sg_ = fstg[c4 % 2]
                B.dma(SP, sg_.t[:], sfv[:, c4 * 512:(c4 + 1) * 512], [B.sfconv], [sg_])
                ps = pb[5 + c4 % 2]
                for i in range(4):
                    B.mm(ps.t[:, i * 32:(i + 1) * 32], sg_.t[:, i * 128:(i + 1) * 128], B.identf.t[0:32, 0:32], [sg_, B.identf], [ps])
                B.cp(DVE, fst.t[:, c4 * 4:(c4 + 1) * 4, :, :], ps.t[:, 0:128].rearrange("p (c b r) -> p c b r", c=4, b=16), [ps], [fst])
            us = [B.sbt("s_us%d" % i, [128, 16, 6]) for i in range(2)]
            cs_ = [B.sbt("s_cs%d" % i, [128, 16, 4]) for i in range(2)]
            sgs = B.sbt("s_sgs", [128, 64])
            hTs = B.sbt("s_hTs", [128, 22, 64], BF16)
            usel = B.sbt("s_usel", [128, 16, 2])
            tks = [B.sbt("s_tks%d" % i, [32, 128]) for i in range(2)]
            ofv = B.o_fcs.t.ap().rearrange("b r n -> (b r) n")
        it = 0
        for gi, (f0, f1) in enumerate(groups):
            ntok = (f1 - f0) * 128
            if ntok == 0:
                continue
            B.dma(SP, x1nT.t[:, :, 0:ntok], B.x1nTs.t.ap()[:, :, f0 * 128:f1 * 128].rearrange("a p n -> p a n"), [B.x1nTs], [x1nT])
            chunks = [(c0, min(512, ntok - c0)) for c0 in range(0, ntok, 512)]
            for j in range(22):
                for k in range(2):
                    col = k * D_FF + j * 128
                    ws = wst[it % 2][k]; wb_ = wbf[it % 2][k]
                    wsl = wups[k * 22 + j]
                    if gi == 0:
                        B.dma(SP if k == 0 else POOL, ws.t[:], w_up.t.ap()[:, col:col + 128].rearrange("(a p) n -> p a n", p=128), [w_up], [ws])
                        B.tt(DVE, wb_.t[:], ws.t[:], bc(g2c.t[:], [128, 8, 128], 2), ALU.mult, [ws, g2c], [wb_])
                        B.dma(SP if k == 1 else POOL, wsl.t.ap(), wb_.t[:].rearrange("p a n -> p (a n)"), [wb_], [wsl])
                    else:
                        B.dma(SP if k == 0 else POOL, wb_.t[:].rearrange("p a n -> p (a n)"), wsl.t.ap(), [wsl], [wb_])
                    u = ug[k]
                    B.cp(POOL, u.t[:, 0:2], hsave.t[:, k * 22 + j, :], [hsave], [u])
                    for ci, (c0, cn) in enumerate(chunks):
                        ps = pb[1 + (ci + k) % 2]
                        for kc in range(8):
                            B.mm(ps.t[:, 0:cn], wb_.t[:, kc, :], x1nT.t[:, kc, c0:c0 + cn], [wb_, x1nT], [ps], start=(kc == 0), stop=(kc == 7))
                        B.cp(ACT, u.t[:, 2 + c0:2 + c0 + cn], ps.t[:, 0:cn], [ps], [u])
                    if gi == 0:
                        B.ts(DVE, u.t[:, 2 + 126:2 + 128], u.t[:, 2 + 126:2 + 128], B.flagsb.t[:, 0:1], None, ALU.mult, None, [u, B.flagsb], [u])
                    B.cp(POOL, hsave.t[:, k * 22 + j, :], u.t[:, ntok:ntok + 2], [u], [hsave])
                    c_ = cg[k]
                    cc = k * 22 + j
                    B.ts(DVE, c_.t[:, 0:ntok], u.t[:, 0:ntok], cfc.t[:, cc, 0:1], None, ALU.mult, None, [u, cfc], [c_])
                    B.stt(DVE, c_.t[:, 0:ntok], u.t[:, 1:ntok + 1], cfc.t[:, cc, 1:2], c_.t[:, 0:ntok], ALU.mult, ALU.add, [u, cfc, c_], [c_])
                    B.stt(DVE, c_.t[:, 0:ntok], u.t[:, 2:ntok + 2], cfc.t[:, cc, 2:3], c_.t[:, 0:ntok], ALU.mult, ALU.add, [u, cfc, c_], [c_])
                    if SAMP and gi == 0:
                        ps = pb[5 + k]
                        for kc in range(8):
                            B.mm(ps.t[:, 0:64], wb_.t[:, kc, :], z["x1nT"].t[:, kc, :], [wb_, z["x1nT"]], [ps], start=(kc == 0), stop=(kc == 7))
                        us_ = us[k]
                        B.cp(POOL, us_.t[:, :, 0:2], fst.t[:, cc, :, :], [fst], [us_])
                        B.cp(ACT, us_.t[:, :, 2:6], ps.t[:, 0:64].rearrange("p (b s) -> p b s", b=16), [ps], [us_])
                        B.cp(POOL, usel.t[:], us_.t[:, :, 4:6], [us_], [usel])
                        tk_ = tks[cc % 2]
                        pq = pb[3 + cc % 2]
                        B.mm(pq.t[0:32, 0:128], usel.t[:].rearrange("p b r -> p (b r)"), B.identf.t[:], [usel, B.identf], [pq])
                        B.cp(DVE, tk_.t[:], pq.t[0:32, 0:128], [pq], [tk_])
                        B.dma(SP, ofv[:, k * D_FF + j * 128:k * D_FF + (j + 1) * 128], tk_.t[:], [tk_], [B.o_fcs])
                        cq = cs_[k]
                        B.ts(DVE, cq.t[:], us_.t[:, :, 0:4], cfc.t[:, cc, 0:1], None, ALU.mult, None, [us_, cfc], [cq])
                        B.stt(DVE, cq.t[:], us_.t[:, :, 1:5], cfc.t[:, cc, 1:2], cq.t[:], ALU.mult, ALU.add, [us_, cfc, cq], [cq])
                        B.stt(DVE, cq.t[:], us_.t[:, :, 2:6], cfc.t[:, cc, 2:3], cq.t[:], ALU.mult, ALU.add, [us_, cfc, cq], [cq])
                if SAMP and gi == 0:
                    B.act(sgs.t[:], cs_[0].t[:].rearrange("p b s -> p (b s)"), AF.Silu, [cs_[0]], [sgs])
                    B.tt(DVE, hTs.t[:, j, :], sgs.t[:], cs_[1].t[:].rearrange("p b s -> p (b s)"), ALU.mult, [sgs, cs_[1]], [hTs])
                it += 1
                B.act(sg.t[:, 0:ntok], cg[0].t[:, 0:ntok], AF.Silu, [cg[0]], [sg])
                B.tt(DVE, hT.t[:, j, 0:ntok], sg.t[:, 0:ntok], cg[1].t[:, 0:ntok], ALU.mult, [sg, cg[1]], [hT])
            for f in range(f0, f1):
                if f == 0:
                    continue
                lt = f - f0
                xx = x1t[f % 2]; yy = yt[f % 2]
                B.dma(POOL, xx.t[:], B.x1s.t.ap()[f * 128:(f + 1) * 128, :], [B.x1s], [xx])
                for half in range(2):
                    ps = pb[3 + half]
                    for j in range(22):
                        B.mm(ps.t[:], hT.t[:, j, lt * 128:(lt + 1) * 128], Wd.t[:, j, half * 512:(half + 1) * 512], [hT, Wd], [ps],
                             start=(j == 0), stop=(j == 21))
                    B.tt(DVE, yy.t[:, half * 512:(half + 1) * 512], xx.t[:, half * 512:(half + 1) * 512], ps.t[:], ALU.add, [xx, ps], [yy])
                B.dma(SP, B.y_o.t.ap()[(f - 1) * 128:f * 128, :], yy.t[:], [yy], [B.y_o])
        if SAMP:
            ys_ = B.sbt("s_ys", [64, D])
            for half in range(2):
                ps = pb[3 + half]
                hs = slice(half * 512, (half + 1) * 512)
                for j in range(22):
                    B.mm(ps.t[0:64, :], hTs.t[:, j, :], Wd.t[:, j, hs], [hTs, Wd], [ps], start=(j == 0), stop=(j == 21))
                B.tt(DVE, ys_.t[:, hs], z["x1"].t[:, hs], ps.t[0:64, :], ALU.add, [z["x1"], ps], [ys_])
            B.dma(SP, B.o_ys.t.ap(), ys_.t[:], [ys_], [B.o_ys])
        fct = [B.sbt("fct%d" % i, [2, 512]) for i in range(2)]
        for c4 in range(11):
            ps = pb[5 + c4 % 2]
            for i in range(4):
                c = c4 * 4 + i
                B.mm(ps.t[0:2, i * 128:(i + 1) * 128], hsave.t[:, c, :], B.identf.t[:], [hsave, B.identf], [ps])
            B.cp(DVE, fct[c4 % 2].t[:], ps.t[0:2, :], [ps], [fct[c4 % 2]])
            B.dma(SP, B.fcv.t.ap()[:, c4 * 512:(c4 + 1) * 512], fct[c4 % 2].t[:], [fct[c4 % 2]], [B.fcv])

    def finish(self):
        B = self
        B.P.wait_all(SP, B.outbufs)
        B.P.emit()
        B.P.close()

    def build_prompt(self, NOWN, sample=False):
        B = self
        B.setup_consts()
        m0 = B.P.mark()
        B.prompt_setup(NOWN, sample=sample)
        m1 = B.P.mark()
        mw = B.setup_weights_in()
        if sample:
            B.sample_inproj()
        B.P.barrier(); B.P.emit(); B.P.release(mw)
        if 'memkv' not in B.skip:
            mm_ = B.mem_kv()
            B.P.barrier(); B.P.emit(); B.P.release(mm_)
        B.gdn_consts()
        B.prompt_loop()
        if getattr(B, "nphase", 9) < 2:
            return
        B.P.barrier(); B.P.emit(); B.P.release(m1)
        B.attention_phase()
        if getattr(B, "nphase", 9) < 3:
            return
        if sample:
            B.P.barrier(); B.P.emit(); B.P.release(m1)
            B.sample_phase()
        B.P.barrier(); B.P.emit(); B.P.release(m1)
        B.outproj_phase()
        if getattr(B, "nphase", 9) < 4:
            return
        B.P.barrier(); B.P.emit(); B.P.release(B.m_mix)
        B.ffn_phase()


_CACHE = {}


def _build():
    if "B" in _CACHE:
        return _CACHE["B"]
    B = Builder(NT=128)
    B.build_prompt(16, sample=True)
    B.finish()
    _CACHE["B"] = B
    return B


def kernel(x_prompt, x_sample, cache_win_k, cache_win_v, state_gdn, state_gdn_conv, state_ffn_conv,
           cache_mem_k, cache_mem_v, mem_prompt, norm1_g, w_in, q_norm_a, k_norm_a, conv_b_w, a_log_b,
           dt_bias_b, out_norm_b, mem_norm_g, w_mem_kv, q_norm_c, k_norm_c, w_out, norm2_g, w_up,
           conv_ffn_w, w_down):
    f = lambda a: np.ascontiguousarray(np.asarray(a, dtype=np.float32))
    B = _build()
    xp = f(x_prompt)[0]
    rep = lambda v, n: np.ascontiguousarray(np.broadcast_to(f(v).reshape(1, -1), (128, n)))
    col = lambda v: np.ascontiguousarray(np.tile(f(v).reshape(-1), 2).reshape(128, 1))
    am = alibi_masks().reshape(4, 128, -1)
    shared = {
        "amask": am, "w_in": f(w_in)[0], "norm1_g": f(norm1_g)[0], "conv_b_w": f(conv_b_w)[0],
        "a_log_b": rep(a_log_b, 4), "dt_bias_b": rep(dt_bias_b, 4), "out_norm_b": rep(out_norm_b, 64),
        "mem_prompt": f(mem_prompt)[0], "w_mem_kv": f(w_mem_kv)[0], "mem_norm_g": f(mem_norm_g)[0],
        "w_out": f(w_out)[0], "norm2_g": f(norm2_g)[0], "w_up": f(w_up)[0], "conv_ffn_w": f(conv_ffn_w)[0], "w_down": f(w_down)[0],
    }
    for nm, v in (("q_norm_a", q_norm_a), ("k_norm_a", k_norm_a), ("q_norm_c", q_norm_c), ("k_norm_c", k_norm_c)):
        shared[nm] = rep(v, 64)
        shared[nm + "_col"] = col(v)
    for k, v in B.cnp.items():
        shared["c_" + k] = v
    for k, v in B.scn.items():
        shared["c_" + k] = v
    xs_all = f(x_sample)
    cwk_all = np.asarray(cache_win_k, dtype=np.float32)[0]
    cwv_all = np.asarray(cache_win_v, dtype=np.float32)[0]
    sg_all = f(state_gdn)[0]; sgc_all = f(state_gdn_conv)[0]; sfc_all = f(state_ffn_conv)[0]
    cmk_all = f(cache_mem_k)[0]; cmv_all = f(cache_mem_v)[0]
    in_maps = []
    for c in range(NCORES):
        npad = (NCORES - 1 - c) * OWN
        xloc = np.zeros((SEQ, D), np.float32)
        xloc[npad:] = xp[:(c + 1) * OWN]
        valid = np.zeros((SEQ,), np.float32)
        valid[npad:] = 1.0
        im = dict(shared)
        im["xloc"] = xloc
        im["valid"] = np.ascontiguousarray(valid.reshape(128, 128).T)
        im["flag"] = np.full((128, 1), 0.0 if c == 0 else 1.0, np.float32)
        bs = slice(16 * c, 16 * c + 16)
        im["xs"] = np.ascontiguousarray(xs_all[bs].reshape(64, D))
        im["cwk"] = np.ascontiguousarray(cwk_all[bs].reshape(16, 2048, 512))
        im["cwv"] = np.ascontiguousarray(cwv_all[bs].reshape(16, 2048, 512))
        im["sgdn"] = np.ascontiguousarray(sg_all[bs]); im["sgconv"] = np.ascontiguousarray(sgc_all[bs]); im["sfconv"] = np.ascontiguousarray(sfc_all[bs])
        im["cmk"] = np.ascontiguousarray(cmk_all[bs].reshape(16, 256, 256)); im["cmv"] = np.ascontiguousarray(cmv_all[bs].reshape(16, 256, 256))
        im = {k: v for k, v in im.items() if k in B.ins}
        in_maps.append(im)
    res = run_bass_kernel_spmd(B.nc, in_maps, core_ids=list(range(NCORES)))
    r = res.results
    z = lambda *s: np.zeros(s, np.float32)
    g = lambda c, n: np.asarray(r[c][n], np.float32)
    y_p = np.concatenate([g(c, "y_p") for c in range(NCORES)], 0).reshape(1, SEQ, D)
    win_k_p = g(7, "win_k_p").reshape(1, 1, 2048, 8, 64)
    win_v_p = g(7, "win_v_p").reshape(1, 1, 2048, 8, 64)
    gst_p = g(7, "gdn_state_p").reshape(1, 1, 4, 64, 64)
    gconv_p = g(7, "gdn_conv_p").reshape(1, 1, 3, 768)
    fconv_p = g(7, "ffn_conv_p").reshape(1, 1, 2, 2 * D_FF)
    mem_k_p = g(0, "mem_k_p").reshape(1, 1, 256, 4, 64)
    mem_v_p = g(0, "mem_v_p").reshape(1, 1, 256, 4, 64)
    cat = lambda n: np.concatenate([g(c, n) for c in range(NCORES)], 0)
    y_s = cat("y_s").reshape(128, 4, D)
    win_k_s = cat("win_k_s").reshape(1, 128, 4, 8, 64)
    win_v_s = cat("win_v_s").reshape(1, 128, 4, 8, 64)
    gst_s = cat("gdn_state_s").reshape(1, 128, 4, 64, 64)
    gconv_s = cat("gdn_conv_s").reshape(1, 128, 3, 768)
    fconv_s = cat("ffn_conv_s").reshape(1, 128, 2, 2 * D_FF)
    return (y_p, y_s, win_k_p, win_v_p, win_k_s, win_v_s, gst_p, gst_s, gconv_p, gconv_s, fconv_p, fconv_s, mem_k_p, mem_v_p)


def sample_consts():
    slopes = 2.0 ** (-8.0 * np.arange(1, 9) / 8.0)
    def cmult(d):
        return ((d >= 0) & (d <= 128)).astype(np.float64) + ((d >= 0) & (d <= 512) & (d % 4 == 0)) + ((d >= 0) & (d <= 2048) & (d % 16 == 0))
    p = np.arange(128)
    rows = np.zeros((7, 128), np.int64)
    for j in range(4):
        rows[j] = 1536 + 128 * j + p
    for j in range(4, 7):
        rows[j] = 16 * (32 * (j - 4) + p % 32) + p // 32
    smask = np.zeros((128, 7, 8, 4), np.float64)
    for j in range(7):
        for s in range(4):
            d = 2048 + s - rows[j]
            c = cmult(d)
            for h in range(8):
                smask[:, j, h, s] = c * np.exp(-slopes[h] * np.maximum(d, 0))
    snew = np.zeros((128, 16, 8, 4), np.float64)
    for b in range(16):
        for sp in range(4):
            for s in range(4):
                d = s - sp
                if d >= 0:
                    snew[4 * b + sp, b, :, s] = cmult(np.array(d)) * np.exp(-slopes * d)
    grp = np.where(p < 64, p // 4, 100 + p)
    same = grp[:, None] == grp[None, :]
    j = p[:, None]; i = p[None, :]
    c = {}
    c["s_triu"] = (same & (j <= i)).astype(np.float32)
    c["s_trils"] = (same & (j > i)).astype(np.float32)
    c["s_negc"] = np.where(same & (i >= j), 0.0, NEGV).astype(np.float32)
    c["s_strict"] = (same & (i > j)).astype(np.float32)
    c["s_causal"] = (same & (i >= j)).astype(np.float32)
    bm = np.zeros((128, 16, 64), np.float32)
    for b in range(16):
        bm[:, b, 4 * b:4 * b + 4] = 1.0
    rm = np.zeros((128, 16), np.float32)
    for b in range(16):
        rm[4 * b:4 * b + 4, b] = 1.0
    c["s_smask"] = smask.reshape(128, -1).astype(np.float32)
    c["s_snew"] = snew.reshape(128, -1).astype(np.float32)
    c["s_bm"] = bm.reshape(128, -1)
    c["s_rm"] = rm
    return c


class Builder(Builder):
    def sample_setup(self):
        B = self
        B.scn = sample_consts()
        for k, v in B.scn.items():
            B.inp("c_" + k, v.shape)
        B.xs = B.inp("xs", [64, D])
        B.cwk = B.inp("cwk", [16, 2048, 512]); B.cwv = B.inp("cwv", [16, 2048, 512])
        B.sgdn = B.inp("sgdn", [16, 4, 64, 64]); B.sgconv = B.inp("sgconv", [16, 3, 768]); B.sfconv = B.inp("sfconv", [16, 2, 2 * D_FF])
        B.cmk = B.inp("cmk", [16, 256, 256]); B.cmv = B.inp("cmv", [16, 256, 256])
        B.o_wks = B.outp("win_k_s", [64, 512]); B.o_wvs = B.outp("win_v_s", [64, 512])
        B.o_gcs = B.outp("gdn_conv_s", [16, 3, 768]); B.o_gss = B.outp("gdn_state_s", [16, 4, 64, 64])
        B.o_fcs = B.outp("ffn_conv_s", [16, 2, 2 * D_FF]); B.o_ys = B.outp("y_s", [64, D])
        z = {}
        def a(name, shape, dt=F32):
            z[name] = B.sbt("z_" + name, shape, dt)
        a("qaT", [128, 4, 64], BF16); a("kaT", [128, 4, 64], BF16); a("qcT", [128, 2, 64], BF16)
        a("qkvT", [128, 6, 64]); a("ab", [128, 8]); a("gate", [128, 256]); a("vaug", [64, 8, 80], BF16)
        a("oaT", [64, 8, 64], BF16); a("obT", [128, 2, 64], BF16); a("ocT", [64, 4, 64], BF16)
        a("x1nT", [128, 8, 64], BF16); a("x1", [64, D])
        B.z = z

    def sample_inproj(self):
        B = self
        pb = B.pb
        z = B.z
        x = B.sbt("sx", [64, D]); sqj = B.sbt("ssqj", [64, D], BF16); ss = B.sbt("sss", [64, 1]); xb = B.sbt("sxb", [64, D], BF16)
        xnT = B.sbt("sxnT", [128, 8, 64], BF16)
        tsq = B.sbt("stsq", [128, 512], BF16); trn = B.sbt("strn", [128, 512])
        tok = B.sbt("stok", [64, 768]); tok2 = B.sbt("stok2", [64, 512]); kss = B.sbt("skss", [64, 8])
        B.dma(SP, x.t[:], B.xs.t.ap(), [B.xs], [x])
        B.rmsnorm_tile(x, sqj, ss, xb)
        for c in range(8):
            B.tr(B.psT.t[:, c, 0:64], xb.t[:, c * 128:(c + 1) * 128], B.identb.t[0:64, 0:64], [xb, B.identb], [B.psT])
        B.cp(DVE, xnT.t[:], B.psT.t[:, :, 0:64], [B.psT], [xnT])

        def fm(col0, npairs, dst_ps):
            for p_ in range(npairs):
                for kc in range(8):
                    B.mm(dst_ps.t[:, p_ * 64:(p_ + 1) * 64], B.Wb.t[:, kc, col0 + p_ * 128:col0 + (p_ + 1) * 128], xnT.t[:, kc, :],
                         [B.Wb, xnT], [dst_ps], start=(kc == 0), stop=(kc == 7))
        fm(C_QA, 4, pb[3])
        B.headnorm_fm(pb[3], 4, 64, B.gw["q_norm_a"][1].t[:], z["qaT"].t[:].rearrange("p a b -> p (a b)"), z["qaT"], pb[4], tsq, trn)
        fm(C_KA, 4, pb[3])
        B.headnorm_fm(pb[3], 4, 64, B.gw["k_norm_a"][1].t[:], z["kaT"].t[:].rearrange("p a b -> p (a b)"), z["kaT"], pb[4], tsq, trn)
        fm(C_QC, 2, pb[3])
        B.headnorm_fm(pb[3], 2, 64, B.gw["q_norm_c"][1].t[:], z["qcT"].t[:].rearrange("p a b -> p (a b)"), z["qcT"], pb[4], tsq, trn)
        fm(C_QKV, 6, pb[3])
        B.cp(DVE, z["qkvT"].t[:].rearrange("p a b -> p (a b)"), pb[3].t[:, 0:384], [pb[3]], [z["qkvT"]])

        def tm(col0, ncol, dst_ps, c_off=0):
            for kc in range(8):
                B.mm(dst_ps.t[0:64, c_off:c_off + ncol], xnT.t[:, kc, :], B.Wb.t[:, kc, col0:col0 + ncol], [B.Wb, xnT], [dst_ps], start=(kc == 0), stop=(kc == 7))
        tm(C_KA, 512, pb[5])
        B.tok_headnorm(pb[5], 8, B.gw["k_norm_a"][0], tok2, kss, np_=64)
        B.dma(SP, B.o_wks.t.ap(), tok2.t[:], [tok2], [B.o_wks])
        tm(C_VA, 512, pb[6])
        tokv = B.sbt("stokv", [64, 512])
        B.cp(ACT, tokv.t[:], pb[6].t[0:64, :], [pb[6]], [tokv])
        B.dma(SP, B.o_wvs.t.ap(), tokv.t[:], [tokv], [B.o_wvs])
        B.memset(POOL, z["vaug"].t[:], 0.0, [z["vaug"]])
        B.memset(POOL, z["vaug"].t[:, :, 64:65], 1.0, [z["vaug"]])
        B.cp(DVE, z["vaug"].t[:, :, 0:64], tokv.t[:].rearrange("p (a b) -> p a b", a=8), [tokv], [z["vaug"]])
        tm(C_QKV, 512, pb[5])
        tm(C_QKV + 512, 256, pb[6])
        B.cp(ACT, tok.t[:, 0:512], pb[5].t[0:64, :], [pb[5]], [tok])
        B.cp(ACT, tok.t[:, 512:768], pb[6].t[0:64, 0:256], [pb[6]], [tok])
        for b in range(16):
            B.dma(SP if b % 2 == 0 else POOL, B.o_gcs.t.ap()[b, :, :], tok.t[4 * b + 1:4 * b + 4, :], [tok], [B.o_gcs])
        B.memset(POOL, z["ab"].t[:], 0.0, [z["ab"]])
        B.memset(POOL, z["gate"].t[:], 0.0, [z["gate"]])
        tm(C_A, 8, pb[5])
        B.cp(DVE, z["ab"].t[0:64, :], pb[5].t[0:64, 0:8], [pb[5]], [z["ab"]])
        tm(C_GATE, 256, pb[6])
        B.cp(DVE, z["gate"].t[0:64, :], pb[6].t[0:64, 0:256], [pb[6]], [z["gate"]])

    def sample_phase(self):
        B = self
        pb = B.pb
        z = B.z
        sc = {}
        for k in ["s_triu", "s_trils", "s_negc", "s_strict", "s_causal", "s_rm"]:
            shp = list(B.scn[k].shape)
            sc[k] = B.sbt("k_" + k, shp)
            B.dma(SP, sc[k].t[:], B.ins["c_" + k].t.ap(), [B.ins["c_" + k]], [sc[k]])
        smask = B.sbt("k_smask", [128, 7 * 32]); B.dma(SP, smask.t[:], B.ins["c_s_smask"].t.ap(), [B.ins["c_s_smask"]], [smask])
        snew = B.sbt("k_snew", [128, 16 * 32]); B.dma(SP, snew.t[:], B.ins["c_s_snew"].t.ap(), [B.ins["c_s_snew"]], [snew])
        bm = B.sbt("k_bm", [128, 16 * 64], BF16); B.dma(POOL, bm.t[:], B.ins["c_s_bm"].t.ap(), [B.ins["c_s_bm"]], [bm])
        kf = [B.sbt("a_kf%d" % i, [128, 512]) for i in range(2)]
        kb = [B.sbt("a_kb%d" % i, [128, 512], BF16) for i in range(2)]
        vf = [B.sbt("a_vf%d" % i, [128, 512]) for i in range(2)]
        kT = B.sbt("a_kT", [128, 4, 7 * 128], BF16)
        va = B.sbt("a_va", [128, 7, 8, 80], BF16)
        E = B.sbt("a_E", [128, 7 * 32]); Pb = B.sbt("a_P", [128, 7 * 32], BF16)
        En = B.sbt("a_En", [64, 32]); Pn = B.sbt("a_Pn", [64, 32], BF16)
        accs = B.sbt("a_acc", [65, 8, 64])
        B.memset(POOL, va.t[:], 0.0, [va])
        B.memset(POOL, va.t[:, :, :, 64:65], 1.0, [va])
        pacc = pb[6]
        pa3 = pacc.t[0:65, :].rearrange("p (h t) -> p h t", h=8)
        it = 0
        for b in range(16):
            for j in range(7):
                k_ = kf[it % 2]; v_ = vf[it % 2]; kb_ = kb[it % 2]
                it += 1
                if j < 4:
                    B.dma(SP, k_.t[:], B.cwk.t.ap()[b, 1536 + 128 * j:1536 + 128 * (j + 1), :], [B.cwk], [k_])
                    B.dma(POOL, v_.t[:], B.cwv.t.ap()[b, 1536 + 128 * j:1536 + 128 * (j + 1), :], [B.cwv], [v_])
                else:
                    m0 = 32 * (j - 4)
                    for r_ in range(4):
                        B.dma(SP, k_.t[32 * r_:32 * r_ + 32, :], B.cwk.t.ap()[b, 16 * m0 + r_:16 * (m0 + 32):16, :], [B.cwk], [k_])
                        B.dma(POOL, v_.t[32 * r_:32 * r_ + 32, :], B.cwv.t.ap()[b, 16 * m0 + r_:16 * (m0 + 32):16, :], [B.cwv], [v_])
                B.cp(ACT, kb_.t[:], k_.t[:], [k_], [kb_])
                B.cp(POOL, va.t[:, j, :, 0:64], v_.t[:].rearrange("p (a b) -> p a b", a=8), [v_], [va])
                for p_ in range(4):
                    B.tr(B.psTb.t[:, p_, :], kb_.t[:, p_ * 128:(p_ + 1) * 128], B.identb.t[:], [kb_, B.identb], [B.psTb])
                B.cp(DVE, kT.t[:, :, j * 128:(j + 1) * 128], B.psTb.t[:, 0:4, :], [B.psTb], [kT])
            pS = pb[1 + b % 2]
            for j in range(7):
                for h in range(8):
                    lo = 64 * (h % 2)
                    B.mm(pS.t[:, j * 32 + h * 4:j * 32 + h * 4 + 4], kT.t[lo:lo + 64, h // 2, j * 128:(j + 1) * 128],
                         z["qaT"].t[lo:lo + 64, h // 2, 4 * b:4 * b + 4], [kT, z["qaT"]], [pS])
            B.act(E.t[:], pS.t[:, 0:224], AF.Exp, [pS], [E], scale=0.125)
            B.tt(DVE, Pb.t[:], E.t[:], smask.t[:], ALU.mult, [E, smask], [Pb])
            pN = pb[3 + b % 2]
            for h in range(8):
                lo = 64 * (h % 2)
                B.mm(pN.t[0:64, h * 4:h * 4 + 4], z["kaT"].t[lo:lo + 64, h // 2, :], z["qaT"].t[lo:lo + 64, h // 2, 4 * b:4 * b + 4],
                     [z["kaT"], z["qaT"]], [pN])
            B.act(En.t[:], pN.t[0:64, 0:32], AF.Exp, [pN], [En], scale=0.125)
            B.tt(DVE, Pn.t[:], En.t[:], snew.t[0:64, b * 32:(b + 1) * 32], ALU.mult, [En, snew], [Pn])
            for h in range(8):
                for j in range(7):
                    B.mm(pa3[:, h, 4 * b:4 * b + 4], va.t[:, j, h, 0:65], Pb.t[:, j * 32 + h * 4:j * 32 + h * 4 + 4], [va, Pb], [pacc],
                         start=(j == 0), stop=False)
                B.mm(pa3[:, h, 4 * b:4 * b + 4], z["vaug"].t[:, h, 0:65], Pn.t[:, h * 4:h * 4 + 4], [z["vaug"], Pn], [pacc], start=False, stop=True)
        B.cp(DVE, accs.t[:], pa3, [pacc], [accs])
        B.mm(pb[5].t[0:64, :], B.onesf.t[64:65, 0:64], accs.t[64:65, :, :].rearrange("p h t -> p (h t)"), [B.onesf, accs], [pb[5]])
        rec = B.sbt("a_rec", [64, 512])
        B.P.op(DVE, lambda e: e.reciprocal(out=rec.t[:], in_=pb[5].t[0:64, :]), [pb[5].b], [rec.b])
        B.tt(DVE, z["oaT"].t[:].rearrange("p h t -> p (h t)"), accs.t[0:64, :, :].rearrange("p h t -> p (h t)"), rec.t[:], ALU.mult, [accs, rec], [z["oaT"]])
        mf = [B.sbt("c_mf%d" % i, [128, 256]) for i in range(2)]
        mb = [B.sbt("c_mb%d" % i, [128, 256], BF16) for i in range(2)]
        mvf = [B.sbt("c_mvf%d" % i, [128, 256]) for i in range(2)]
        mkT = B.sbt("c_mkT", [128, 2, 256], BF16)
        mva = B.sbt("c_mva", [128, 2, 4, 80], BF16)
        Ec = B.sbt("c_E", [128, 32], BF16)
        accc = B.sbt("c_acc", [65, 4, 64])
        B.memset(POOL, mva.t[:], 0.0, [mva])
        B.memset(POOL, mva.t[:, :, :, 64:65], 1.0, [mva])
        pacc = pb[6]
        pc3 = pacc.t[0:65, 0:256].rearrange("p (h t) -> p h t", h=4)
        it = 0
        for b in range(16):
            for mt in range(2):
                f_ = mf[it % 2]; b_ = mb[it % 2]; v_ = mvf[it % 2]
                it += 1
                B.dma(SP, f_.t[:], B.cmk.t.ap()[b, mt * 128:(mt + 1) * 128, :], [B.cmk], [f_])
                B.dma(POOL, v_.t[:], B.cmv.t.ap()[b, mt * 128:(mt + 1) * 128, :], [B.cmv], [v_])
                B.cp(ACT, b_.t[:], f_.t[:], [f_], [b_])
                B.cp(POOL, mva.t[:, mt, :, 0:64], v_.t[:].rearrange("p (a b) -> p a b", a=4), [v_], [mva])
                for p_ in range(2):
                    B.tr(B.psTb.t[:, 4 + p_, :], b_.t[:, p_ * 128:(p_ + 1) * 128], B.identb.t[:], [b_, B.identb], [B.psTb])
                B.cp(DVE, mkT.t[:, :, mt * 128:(mt + 1) * 128], B.psTb.t[:, 4:6, :], [B.psTb], [mkT])
            pS = pb[1 + b % 2]
            for mt in range(2):
                for h in range(4):
                    lo = 64 * (h % 2)
                    B.mm(pS.t[:, mt * 16 + h * 4:mt * 16 + h * 4 + 4], mkT.t[lo:lo + 64, h // 2, mt * 128:(mt + 1) * 128],
                         z["qcT"].t[lo:lo + 64, h // 2, 4 * b:4 * b + 4], [mkT, z["qcT"]], [pS])
            B.act(Ec.t[:], pS.t[:, 0:32], AF.Exp, [pS], [Ec], scale=0.125)
            for h in range(4):
                for mt in range(2):
                    B.mm(pc3[:, h, 4 * b:4 * b + 4], mva.t[:, mt, h, 0:65], Ec.t[:, mt * 16 + h * 4:mt * 16 + h * 4 + 4], [mva, Ec], [pacc],
                         start=(mt == 0), stop=(mt == 1))
        B.cp(DVE, accc.t[:], pc3, [pacc], [accc])
        B.mm(pb[5].t[0:64, 0:256], B.onesf.t[64:65, 0:64], accc.t[64:65, :, :].rearrange("p h t -> p (h t)"), [B.onesf, accc], [pb[5]])
        B.P.op(DVE, lambda e: e.reciprocal(out=rec.t[:, 0:256], in_=pb[5].t[0:64, 0:256]), [pb[5].b], [rec.b])
        B.tt(DVE, z["ocT"].t[:].rearrange("p h t -> p (h t)"), accc.t[0:64, :, :].rearrange("p h t -> p (h t)"), rec.t[:, 0:256], ALU.mult, [accc, rec], [z["ocT"]])
        B.gdn_consts_s()
        B.gdn_alloc("h_")
        B.gsets = {0: B.gs}
        s = B.gs
        B.gm = {"triu": sc["s_triu"], "trils": sc["s_trils"], "negc": sc["s_negc"], "strictf": sc["s_strict"], "causal": sc["s_causal"]}
        stf = B.sbt("h_stf", [48, 768])
        qx = B.sbt("h_qx", [128, 6, 16, 7])
        Ss = B.sbt("h_Ss", [128, 16, 2, 64]); Ssb = B.sbt("h_Ssb", [128, 16, 2, 64], BF16)
        wTm = B.sbt("h_wTm", [128, 2, 16, 64], BF16); qTm = B.sbt("h_qTm", [128, 2, 16, 64], BF16)
        kdm = B.sbt("h_kdm", [128, 16, 256], BF16)
        gmk = B.sbt("h_gmk", [128, 16, 4]); egls = B.sbt("h_egls", [128, 16, 2])
        for p_ in range(2):
            B.dma(SP, Ss.t[:, :, p_, :], B.sgdn.t.ap()[:, 2 * p_:2 * p_ + 2, :, :].rearrange("b h k v -> (h k) b v"), [B.sgdn], [Ss])
        B.cp(ACT, Ssb.t[:], Ss.t[:], [Ss], [Ssb])
        B.dma(SP, stf.t[:], B.sgconv.t.ap().rearrange("b r n -> (b r) n"), [B.sgconv], [stf])

        def conv_fn():
            for c in range(6):
                B.mm(pb[3].t[:, c * 48:(c + 1) * 48], stf.t[:, c * 128:(c + 1) * 128], B.identf.t[0:48, 0:48], [stf, B.identf], [pb[3]])
            B.cp(DVE, qx.t[:, :, :, 0:3], pb[3].t[:, 0:288].rearrange("p (c b r) -> p c b r", c=6, b=16), [pb[3]], [qx])
            B.cp(POOL, qx.t[:, :, :, 3:7], z["qkvT"].t[:].rearrange("p c (b s) -> p c b s", b=16), [z["qkvT"]], [qx])
            B.memset(POOL, s["ca"].t[:], 0.0, [s["ca"]])
            for c in range(6):
                cav = s["ca"].t[:, c, 0:64].rearrange("p (b s) -> p b s", b=16)
                B.ts(DVE, cav, qx.t[:, c, :, 0:4], B.cw.t[:, c, 0:1], None, ALU.mult, None, [qx, B.cw], [s["ca"]])
                for k in range(1, 4):
                    B.stt(DVE, cav, qx.t[:, c, :, k:k + 4], B.cw.t[:, c, k:k + 1], cav, ALU.mult, ALU.add, [qx, B.cw, s["ca"]], [s["ca"]])
            B.cp(DVE, s["ab"].t[:], z["ab"].t[:], [z["ab"]], [s["ab"]])

        def state_fn(psB, psD, gate_fn):
            bm4 = bm.t[:].rearrange("p (b t) -> p b t", b=16)
            for p_ in range(2):
                B.tt(DVE, wTm.t[:, p_, :, :], bc(s["wT"].t[:, p_, 0:64], [128, 16, 64], 1), bm4, ALU.mult, [s["wT"], bm], [wTm])
                B.tt(DVE, qTm.t[:, p_, :, :], bc(s["qT"].t[:, p_, 0:64], [128, 16, 64], 1), bm4, ALU.mult, [s["qT"], bm], [qTm])
            pD = psD.t[0:64, 0:256].rearrange("p (a b) -> p a b", a=4)
            for h in range(4):
                lo = 64 * (h % 2)
                for b in range(16):
                    B.mm(pD[:, h, :], wTm.t[lo:lo + 64, h // 2, b, :], Ssb.t[lo:lo + 64, b, h // 2, :], [wTm, Ssb], [psD], start=(b == 0), stop=(b == 15))
            B.memset(POOL, s["vnew"].t[:], 0.0, [s["vnew"]])
            B.tt(DVE, s["vnew"].t[0:64, :, :], s["ub"].t[0:64, :, :], pD, ALU.subtract, [s["ub"], psD], [s["vnew"]])
            pQ = psB.t[0:64, 0:256].rearrange("p (a b) -> p a b", a=4)
            for h in range(4):
                lo = 64 * (h % 2)
                for b in range(16):
                    B.mm(pQ[:, h, :], qTm.t[lo:lo + 64, h // 2, b, :], Ssb.t[lo:lo + 64, b, h // 2, :], [qTm, Ssb], [psB], start=(b == 0), stop=(b == 15))
            B.memset(POOL, s["o"].t[:], 0.0, [s["o"]])
            B.tt(DVE, s["o"].t[0:64, :, :], pQ, bc(s["egc"].t[0:64, :], [64, 4, 64], 2), ALU.mult, [psB, s["egc"]], [s["o"]])
            pV = psD.t[:, 256:512].rearrange("p (a b) -> p a b", a=4)
            for h in range(4):
                B.mm(pV[:, h, :], s["AT"].t[:, h, :], s["vnew"].t[:, h, :], [s["AT"], s["vnew"]], [psD])
            B.tt(DVE, s["o"].t[0:64, :, :], s["o"].t[0:64, :, :], pV[0:64, :, :], ALU.add, [psD, s["o"]], [s["o"]])
            B.tt(DVE, gmk.t[:], bc(s["g"].t[:], [128, 16, 4], 1), bc(sc["s_rm"].t[:], [128, 16, 4], 2), ALU.mult, [s["g"], sc["s_rm"]], [gmk])
            B.mm(pb[3].t[:, 0:64], B.onesf.t[:], gmk.t[:].rearrange("p b h -> p (b h)"), [B.onesf, gmk], [pb[3]])
            g3 = pb[3].t[:, 0:64].rearrange("p (b h) -> p b h", b=16)
            B.act(egls.t[0:64, :, :], g3[0:64, :, 0:4:2], AF.Exp, [pb[3]], [egls])
            B.act(egls.t[64:128, :, :], g3[64:128, :, 1:4:2], AF.Exp, [pb[3]], [egls])
            B.tt(DVE, kdm.t[:], bc(s["kd"].t[:].rearrange("p a b -> p (a b)"), [128, 16, 256], 1), bc(sc["s_rm"].t[:], [128, 16, 256], 2), ALU.mult,
                 [s["kd"], sc["s_rm"]], [kdm])
            B.tt(DVE, Ss.t[:].rearrange("p b q v -> p (b q) v"), Ss.t[:].rearrange("p b q v -> p (b q) v"),
                 bc(egls.t[:].rearrange("p b q -> p (b q)"), [128, 32, 64], 2), ALU.mult, [Ss, egls], [Ss])
            for b in range(16):
                pk = pb[4 + b % 2]
                pk3 = pk.t[:, 0:256].rearrange("p (a b) -> p a b", a=4)
                for h in range(4):
                    B.mm(pk3[:, h, :], kdm.t[:, b, (h // 2) * 128:(h // 2 + 1) * 128], s["vnew"].t[:, h, :], [kdm, s["vnew"]], [pk])
                B.tt(DVE, Ss.t[0:64, b, :, :], Ss.t[0:64, b, :, :], pk3[0:64, 0:4:2, :], ALU.add, [Ss, pk], [Ss])
                B.tt(DVE, Ss.t[64:128, b, :, :], Ss.t[64:128, b, :, :], pk3[64:128, 1:4:2, :], ALU.add, [Ss, pk], [Ss])
            for p_ in range(2):
                B.dma(SP, B.o_gss.t.ap()[:, 2 * p_:2 * p_ + 2, :, :].rearrange("b h k v -> (h k) b v"), Ss.t[:, :, p_, :], [Ss], [B.o_gss])
            B.tt(DVE, s["o2"].t[:], s["o"].t[:], s["o"].t[:], ALU.mult, [s["o"]], [s["o2"]])
            B.P.op(DVE, lambda e: e.tensor_reduce(out=s["ms"].t[:], in_=s["o2"].t[:], axis=AX.X, op=ALU.add), [s["o2"].b], [s["ms"].b])
            B.act(s["ms"].t[:], s["ms"].t[:], AF.Sqrt, [s["ms"], B.epsc], [s["ms"]], bias=B.epsc.t[:], scale=1.0 / 64)
            B.P.op(DVE, lambda e: e.reciprocal(out=s["ms"].t[:], in_=s["ms"].t[:]), [s["ms"].b], [s["ms"].b])
            B.tt(DVE, s["o"].t[:], s["o"].t[:], bc(s["ms"].t[:], [128, 4, 64], 2), ALU.mult, [s["o"], s["ms"]], [s["o"]])
            B.tt(DVE, s["o"].t[:], s["o"].t[:], bc(B.onw.t[:], [128, 4, 64], 1), ALU.mult, [s["o"], B.onw], [s["o"]])
            B.act(s["gs"].t[:], z["gate"].t[:], AF.Silu, [z["gate"]], [s["gs"]])
            B.tt(DVE, s["obf"].t[:], s["o"].t[:].rearrange("p a b -> p (a b)"), s["gs"].t[:], ALU.mult, [s["o"], s["gs"]], [s["obf"]])
            for p_ in range(2):
                B.tr(B.psTb.t[:, p_, :], s["obf"].t[:, p_ * 128:(p_ + 1) * 128], B.identb.t[:], [s["obf"], B.identb], [B.psTb])
            B.cp(DVE, z["obT"].t[:], B.psTb.t[:, 0:2, 0:64], [B.psTb], [z["obT"]])

        for _ in B.gdn_tile(None, None, pb[3], pb[4], pb[5], pb[6], want_out=True, gate_fn=None, smode={"conv_fn": conv_fn, "state_fn": state_fn}, par=0):
            pass
        B.gm = {"triu": B.triu, "trils": B.trils, "negc": B.negc, "strictf": B.strictf, "causal": B.causalf}

    def gdn_consts_s(self):
        B = self
        cw = B.ins["conv_b_w"]; alog = B.ins["a_log_b"]; dtb = B.ins["dt_bias_b"]; onb = B.ins["out_norm_b"]
        B.cw = B.sbt("s_cw", [128, 6, 4])
        for k in range(4):
            B.dma(SP, B.cw.t[:, :, k], cw.t.ap()[k, :].rearrange("(c p) -> p c", p=128), [cw], [B.cw], allow_slow_non_contiguous=True)
        B.nega = B.sbt("s_nega", [128, 4])
        B.dma(SP, B.nega.t[:], alog.t.ap(), [alog], [B.nega])
        B.act(B.nega.t[:], B.nega.t[:], AF.Exp, [B.nega], [B.nega])
        B.ts(DVE, B.nega.t[:], B.nega.t[:], -1.0, None, ALU.mult, None, [B.nega], [B.nega])
        B.dtb = B.sbt("s_dtb", [128, 4])
        B.dma(SP, B.dtb.t[:], dtb.t.ap(), [dtb], [B.dtb])
        B.onw = B.sbt("s_onw", [128, 64])
        B.dma(SP, B.onw.t[:], onb.t.ap(), [onb], [B.onw])
```
